# Optimizing a Trainium2 kernel written in Bass

```python
import math
import jax
import jax.numpy as jnp
from jax import lax
import numpy as np

D_MODEL = 1024
BATCH = 8
SEQ = 2048
DEPTH = 4

N_MIXERS = 4
NORM_EPS = 1e-6
MLA_HEADS = 16
MLA_Q_RANK = 384
MLA_KV_RANK = 256
MLA_NOPE = 64
MLA_ROPE = 32
MLA_V = 64
MLA_QK = MLA_NOPE + MLA_ROPE
MLA_WIDTH = MLA_HEADS * MLA_V
MLA_IN = MLA_Q_RANK + MLA_KV_RANK + MLA_ROPE + MLA_WIDTH
ROPE_THETA = 10000.0
Q_BLOCK = 128
GLA_HEADS = 4
GLA_DK = D_MODEL // (2 * GLA_HEADS)
GLA_DV = D_MODEL // GLA_HEADS
GLA_KEY = GLA_HEADS * GLA_DK
GLA_VAL = GLA_HEADS * GLA_DV
GLA_GATE_RANK = 16
GLA_TAU = 16.0
GLA_CHUNK = 64
GLA_IN = 2 * GLA_KEY + 2 * GLA_VAL + GLA_GATE_RANK
LRU_WIDTH = 5 * D_MODEL // 4
LRU_BLOCKS = 10
LRU_BLOCK = LRU_WIDTH // LRU_BLOCKS
LRU_C = 8.0
CONV_W = 4
SSD_INNER = 2 * D_MODEL
SSD_HEADDIM = 64
SSD_HEADS = SSD_INNER // SSD_HEADDIM
SSD_GROUPS = 8
SSD_HPG = SSD_HEADS // SSD_GROUPS
SSD_STATE = 128
SSD_CHUNK = 64
SSD_CONV_DIM = SSD_INNER + 2 * SSD_GROUPS * SSD_STATE
SSD_IN = SSD_INNER + SSD_CONV_DIM + SSD_HEADS

kernel_name = 'hybrid_mla_gla_rglru_ssd_trunk'


def _layers_of(m):
    return len(range(m, DEPTH, N_MIXERS))


def rmsnorm(x, g):
    xf = x.astype(jnp.float32)
    y = xf * lax.rsqrt(jnp.mean(xf * xf, axis=-1, keepdims=True) + NORM_EPS)
    return (y * g.astype(jnp.float32)).astype(x.dtype)


def rope_tables(positions):
    inv_freq = ROPE_THETA ** (-jnp.arange(0, MLA_ROPE, 2, dtype=jnp.float32) / MLA_ROPE)
    ang = positions.astype(jnp.float32)[..., None] * inv_freq
    return jnp.cos(ang), jnp.sin(ang)


def apply_rope(t, cos, sin):
    t1, t2 = jnp.split(t.astype(jnp.float32), 2, axis=-1)
    return jnp.concatenate([t1 * cos - t2 * sin, t2 * cos + t1 * sin], axis=-1).astype(t.dtype)


def causal_depthwise_conv(u, w, b):
    y = lax.conv_general_dilated(u, w[:, None, :].astype(u.dtype), window_strides=(1,),
                                 padding=[(CONV_W - 1, 0)], dimension_numbers=('NWC', 'WIO', 'NWC'),
                                 feature_group_count=u.shape[-1])
    return y + b.astype(u.dtype)


def mla_mixer(h, cos, sin, w_in, g_q, w_uq, g_kv, w_ukv, w_out):
    B, S, _ = h.shape
    c_q, c_kv, k_r, gate = jnp.split(h @ w_in, [MLA_Q_RANK, MLA_Q_RANK + MLA_KV_RANK,
                                                MLA_Q_RANK + MLA_KV_RANK + MLA_ROPE], axis=-1)
    q = (rmsnorm(c_q, g_q) @ w_uq).reshape(B, S, MLA_HEADS, MLA_QK)
    q_nope = q[..., :MLA_NOPE]
    q_rope = apply_rope(q[..., MLA_NOPE:], cos[:, :, None], sin[:, :, None])
    kv = (rmsnorm(c_kv, g_kv) @ w_ukv).reshape(B, S, MLA_HEADS, MLA_NOPE + MLA_V)
    k_nope, v = kv[..., :MLA_NOPE], kv[..., MLA_NOPE:]
    k_rope = apply_rope(k_r, cos, sin)
    scale = MLA_QK ** -0.5
    outs = []
    for start in range(0, S, Q_BLOCK):
        end = start + Q_BLOCK
        s = (jnp.einsum('bqhd,bkhd->bhqk', q_nope[:, start:end], k_nope[:, :end])
             + jnp.einsum('bqhr,bkr->bhqk', q_rope[:, start:end], k_rope[:, :end])).astype(jnp.float32) * scale
        mask = jnp.arange(start, end)[:, None] >= jnp.arange(end)[None, :]
        p = jax.nn.softmax(jnp.where(mask, s, -jnp.inf), axis=-1).astype(v.dtype)
        outs.append(jnp.einsum('bhqk,bkhd->bqhd', p, v[:, :end]))
    o = jnp.concatenate(outs, axis=1).reshape(B, S, MLA_WIDTH)
    return (o * jax.nn.silu(gate)) @ w_out


def gla_mixer(h, w_in, w_gk2, b_gk, g_o, w_out):
    B, S, _ = h.shape
    N, C = S // GLA_CHUNK, GLA_CHUNK
    f32 = jnp.float32
    q, k, v, gate, gk = jnp.split(h @ w_in, [GLA_KEY, 2 * GLA_KEY, 2 * GLA_KEY + GLA_VAL,
                                             2 * GLA_KEY + 2 * GLA_VAL], axis=-1)
    log_a = jax.nn.log_sigmoid((gk @ w_gk2 + b_gk).astype(f32)) / GLA_TAU

    def chunks(t, d):
        return t.reshape(B, N, C, GLA_HEADS, d).transpose(0, 3, 1, 2, 4).astype(f32)

    q = chunks(q, GLA_DK) * GLA_DK ** -0.5
    k = chunks(k, GLA_DK)
    v = chunks(v, GLA_DV)
    b = jnp.cumsum(chunks(log_a, GLA_DK), axis=3)
    q_t = q * jnp.exp(b)
    k_t = k * jnp.exp(-b)
    causal = jnp.tril(jnp.ones((C, C), dtype=bool))
    att = jnp.where(causal, jnp.einsum('bhnik,bhnjk->bhnij', q_t, k_t), 0.0)
    o_intra = jnp.einsum('bhnij,bhnjv->bhniv', att, v)
    b_last = b[:, :, :, -1]
    d_state = jnp.einsum('bhnck,bhncv->bhnkv', k * jnp.exp(b_last[:, :, :, None] - b), v)

    def step(s_prev, inp):
        decay, ds = inp
        return decay[..., None] * s_prev + ds, s_prev

    s0 = jnp.zeros((B, GLA_HEADS, GLA_DK, GLA_DV), f32)
    _, s_prev = lax.scan(step, s0, (jnp.moveaxis(jnp.exp(b_last), 2, 0), jnp.moveaxis(d_state, 2, 0)))
    s_prev = jnp.moveaxis(s_prev, 0, 2)
    o = o_intra + jnp.einsum('bhnck,bhnkv->bhncv', q_t, s_prev)
    o = o.transpose(0, 2, 3, 1, 4).reshape(B, S, GLA_HEADS, GLA_DV)
    o = rmsnorm(o, g_o).reshape(B, S, GLA_VAL).astype(h.dtype)
    return (o * jax.nn.silu(gate)) @ w_out


def rglru_mixer(h, w_in, conv_w, conv_b, w_a, b_a, w_x, b_x, lam, w_out):
    B, S, _ = h.shape
    f32 = jnp.float32
    gate, u = jnp.split(h @ w_in, 2, axis=-1)
    u = causal_depthwise_conv(u, conv_w, conv_b)
    ub = u.reshape(B, S, LRU_BLOCKS, LRU_BLOCK)
    r = jax.nn.sigmoid(jnp.einsum('bsni,nij->bsnj', ub, w_a).reshape(B, S, LRU_WIDTH) + b_a).astype(f32)
    i = jax.nn.sigmoid(jnp.einsum('bsni,nij->bsnj', ub, w_x).reshape(B, S, LRU_WIDTH) + b_x).astype(f32)
    log_a = -LRU_C * r * jax.nn.softplus(-lam.astype(f32))
    a = jnp.exp(log_a)
    b = jnp.sqrt(-jnp.expm1(2.0 * log_a)) * (i * u.astype(f32))

    def combine(left, right):
        a1, b1 = left
        a2, b2 = right
        return a1 * a2, a2 * b1 + b2

    _, hs = lax.associative_scan(combine, (a, b), axis=1)
    return (hs.astype(h.dtype) * jax.nn.silu(gate)) @ w_out


def ssd_mixer(h, w_in, conv_w, conv_b, dt_bias, a_log, d_skip, g_norm, w_out):
    B, S, _ = h.shape
    N, L, G, HG, P, NS = S // SSD_CHUNK, SSD_CHUNK, SSD_GROUPS, SSD_HPG, SSD_HEADDIM, SSD_STATE
    f32 = jnp.float32
    z, xbc, dt = jnp.split(h @ w_in, [SSD_INNER, SSD_INNER + SSD_CONV_DIM], axis=-1)
    xbc = jax.nn.silu(causal_depthwise_conv(xbc, conv_w, conv_b))
    x, bm, cm = jnp.split(xbc, [SSD_INNER, SSD_INNER + G * NS], axis=-1)
    dt = jax.nn.softplus(dt.astype(f32) + dt_bias.astype(f32))
    A = -jnp.exp(a_log.astype(f32))
    x = x.astype(f32).reshape(B, N, L, G, HG, P)
    bm = bm.astype(f32).reshape(B, N, L, G, NS)
    cm = cm.astype(f32).reshape(B, N, L, G, NS)
    dt_c = dt.reshape(B, N, L, G, HG)
    cs = jnp.cumsum(jnp.moveaxis(dt_c * A.reshape(G, HG), 2, -1), axis=-1)
    xdt = x * dt_c[..., None]
    causal = jnp.tril(jnp.ones((L, L), dtype=bool))
    seg = cs[..., :, None] - cs[..., None, :]
    lmat = jnp.exp(jnp.where(causal, seg, -jnp.inf))
    cb = jnp.einsum('bnigs,bnjgs->bngij', cm, bm)
    y_diag = jnp.einsum('bngij,bnghij,bnjghp->bnighp', cb, lmat, xdt)
    decay = jnp.exp(cs[..., -1:] - cs)
    states = jnp.einsum('bnjgs,bnghj,bnjghp->bnghps', bm, decay, xdt)

    def step(s_prev, inp):
        dec, st = inp
        return dec[..., None, None] * s_prev + st, s_prev

    s0 = jnp.zeros((B, G, HG, P, NS), f32)
    _, s_prev = lax.scan(step, s0, (jnp.moveaxis(jnp.exp(cs[..., -1]), 1, 0), jnp.moveaxis(states, 1, 0)))
    s_prev = jnp.moveaxis(s_prev, 0, 1)
    y_off = jnp.einsum('bnigs,bnghps,bnghi->bnighp', cm, s_prev, jnp.exp(cs))
    y = (y_diag + y_off).reshape(B, S, SSD_HEADS, P) + d_skip.astype(f32)[:, None] * x.reshape(B, S, SSD_HEADS, P)
    y = y.reshape(B, S, SSD_INNER) * jax.nn.silu(z.astype(f32))
    y = rmsnorm(y.reshape(B, S, G, SSD_INNER // G), g_norm.reshape(G, SSD_INNER // G))
    return y.reshape(B, S, SSD_INNER).astype(h.dtype) @ w_out


def setup_inputs(seed: int = 0) -> dict:
    key = jax.random.key(seed)
    ks = iter(jax.random.split(key, 48))
    f32 = jnp.float32

    def dense(shape, fan_in):
        return jax.random.normal(next(ks), shape, f32) * fan_in ** -0.5

    def gain(shape):
        return 1.0 + 0.02 * jax.random.normal(next(ks), shape, f32)

    def small(shape, scale=0.02):
        return scale * jax.random.normal(next(ks), shape, f32)

    nA, nB, nC, nD = (_layers_of(m) for m in range(N_MIXERS))
    x = jax.random.normal(next(ks), (BATCH, SEQ, D_MODEL), f32)
    positions = (jnp.arange(SEQ, dtype=jnp.int32)[None, :]
                 + jax.random.randint(next(ks), (BATCH, 1), 0, 4096, dtype=jnp.int32))
    norm_g = gain((DEPTH, D_MODEL))
    final_g = gain((D_MODEL,))
    mla_w_in = dense((nA, D_MODEL, MLA_IN), D_MODEL)
    mla_g_q = gain((nA, MLA_Q_RANK))
    mla_w_uq = dense((nA, MLA_Q_RANK, MLA_HEADS * MLA_QK), MLA_Q_RANK)
    mla_g_kv = gain((nA, MLA_KV_RANK))
    mla_w_ukv = dense((nA, MLA_KV_RANK, MLA_HEADS * (MLA_NOPE + MLA_V)), MLA_KV_RANK)
    mla_w_out = dense((nA, MLA_WIDTH, D_MODEL), MLA_WIDTH)
    gla_w_in = dense((nB, D_MODEL, GLA_IN), D_MODEL)
    gla_w_gk2 = dense((nB, GLA_GATE_RANK, GLA_KEY), GLA_GATE_RANK)
    gla_b_gk = small((nB, GLA_KEY), 0.1)
    gla_g_o = gain((nB, GLA_DV))
    gla_w_out = dense((nB, GLA_VAL, D_MODEL), GLA_VAL)
    lru_w_in = dense((nC, D_MODEL, 2 * LRU_WIDTH), D_MODEL)
    lru_conv_w = dense((nC, CONV_W, LRU_WIDTH), CONV_W)
    lru_conv_b = small((nC, LRU_WIDTH))
    lru_w_a = dense((nC, LRU_BLOCKS, LRU_BLOCK, LRU_BLOCK), LRU_BLOCK)
    lru_b_a = small((nC, LRU_WIDTH))
    lru_w_x = dense((nC, LRU_BLOCKS, LRU_BLOCK, LRU_BLOCK), LRU_BLOCK)
    lru_b_x = small((nC, LRU_WIDTH))
    a0 = jax.random.uniform(next(ks), (nC, LRU_WIDTH), f32, 0.9, 0.999) ** (1.0 / LRU_C)
    lru_lam = jnp.log(a0) - jnp.log1p(-a0)
    lru_w_out = dense((nC, LRU_WIDTH, D_MODEL), LRU_WIDTH)
    ssd_w_in = dense((nD, D_MODEL, SSD_IN), D_MODEL)
    ssd_conv_w = dense((nD, CONV_W, SSD_CONV_DIM), CONV_W)
    ssd_conv_b = small((nD, SSD_CONV_DIM))
    dt0 = jnp.exp(jax.random.uniform(next(ks), (nD, SSD_HEADS), f32, math.log(1e-3), math.log(1e-1)))
    ssd_dt_bias = dt0 + jnp.log(-jnp.expm1(-dt0))
    ssd_a_log = jnp.log(jax.random.uniform(next(ks), (nD, SSD_HEADS), f32, 1.0, 16.0))
    ssd_d = gain((nD, SSD_HEADS))
    ssd_g_norm = gain((nD, SSD_INNER))
    ssd_w_out = dense((nD, SSD_INNER, D_MODEL), SSD_INNER)
    return {'x': x, 'positions': positions, 'norm_g': norm_g, 'final_g': final_g,
            'mla_w_in': mla_w_in, 'mla_g_q': mla_g_q, 'mla_w_uq': mla_w_uq, 'mla_g_kv': mla_g_kv,
            'mla_w_ukv': mla_w_ukv, 'mla_w_out': mla_w_out,
            'gla_w_in': gla_w_in, 'gla_w_gk2': gla_w_gk2, 'gla_b_gk': gla_b_gk, 'gla_g_o': gla_g_o,
            'gla_w_out': gla_w_out,
            'lru_w_in': lru_w_in, 'lru_conv_w': lru_conv_w, 'lru_conv_b': lru_conv_b, 'lru_w_a': lru_w_a,
            'lru_b_a': lru_b_a, 'lru_w_x': lru_w_x, 'lru_b_x': lru_b_x, 'lru_lam': lru_lam,
            'lru_w_out': lru_w_out,
            'ssd_w_in': ssd_w_in, 'ssd_conv_w': ssd_conv_w, 'ssd_conv_b': ssd_conv_b,
            'ssd_dt_bias': ssd_dt_bias, 'ssd_a_log': ssd_a_log, 'ssd_d': ssd_d, 'ssd_g_norm': ssd_g_norm,
            'ssd_w_out': ssd_w_out}


def reference(x, positions, norm_g, final_g,
              mla_w_in, mla_g_q, mla_w_uq, mla_g_kv, mla_w_ukv, mla_w_out,
              gla_w_in, gla_w_gk2, gla_b_gk, gla_g_o, gla_w_out,
              lru_w_in, lru_conv_w, lru_conv_b, lru_w_a, lru_b_a, lru_w_x, lru_b_x, lru_lam, lru_w_out,
              ssd_w_in, ssd_conv_w, ssd_conv_b, ssd_dt_bias, ssd_a_log, ssd_d, ssd_g_norm, ssd_w_out):
    cos, sin = rope_tables(positions)
    h = x
    for i in range(DEPTH):
        m, j = i % N_MIXERS, i // N_MIXERS
        u = rmsnorm(h, norm_g[i])
        if m == 0:
            y = mla_mixer(u, cos, sin, mla_w_in[j], mla_g_q[j], mla_w_uq[j], mla_g_kv[j], mla_w_ukv[j], mla_w_out[j])
        elif m == 1:
            y = gla_mixer(u, gla_w_in[j], gla_w_gk2[j], gla_b_gk[j], gla_g_o[j], gla_w_out[j])
        elif m == 2:
            y = rglru_mixer(u, lru_w_in[j], lru_conv_w[j], lru_conv_b[j], lru_w_a[j], lru_b_a[j],
                            lru_w_x[j], lru_b_x[j], lru_lam[j], lru_w_out[j])
        else:
            y = ssd_mixer(u, ssd_w_in[j], ssd_conv_w[j], ssd_conv_b[j], ssd_dt_bias[j], ssd_a_log[j],
                          ssd_d[j], ssd_g_norm[j], ssd_w_out[j])
        h = h + y
    return rmsnorm(h, final_g)
```

```python
import numpy as np
import ml_dtypes
import concourse.bass as bass
import concourse.mybir as mybir
from concourse.bass_utils import run_bass_kernel_spmd
from contextlib import ExitStack

F32 = mybir.dt.float32
BF16 = mybir.dt.bfloat16
I32 = mybir.dt.int32
AF = mybir.ActivationFunctionType
ALU = mybir.AluOpType

S_ = 2048
D_ = 1024
NT = 16
EPS = 1e-6
LAYER_FNS = {}
DEBUG = False

ENGS = ("tensor", "vector", "scalar", "gpsimd", "sync")
EPOCH = 6000
NDMA = 24


_BASE = {"r": {}}


class Dep:
    __slots__ = ("w", "r")

    def __init__(self):
        self.w = None
        self.r = dict(_BASE["r"])


def deps(n):
    return [Dep() for _ in range(n)]


class Sched:
    def __init__(self, nc, stack):
        self.nc = nc
        self.stack = stack
        self.q = {e: [] for e in ENGS}
        self.cnt = {e: 0 for e in ENGS}
        self.sems = {}
        self.known = {e: {} for e in ENGS}
        self.dma_sems = [stack.enter_context(nc.semaphore(f"dma{i}")) for i in range(NDMA)]
        self.dma_i = 0

    def _esem(self, eng, epoch):
        k = (eng, epoch)
        if k not in self.sems:
            self.sems[k] = self.stack.enter_context(self.nc.semaphore(f"s_{eng}_{epoch}"))
        return self.sems[k]

    def _need_wait(self, weng, tok):
        kn = self.known[weng]
        if tok[0] == "c":
            _, eng, ep, val = tok
            cur = kn.get(("c", eng), (-1, 0))
            if cur >= (ep, val):
                return None
            kn[("c", eng)] = (ep, val)
            return (self._esem(eng, ep), val)
        _, idx, val = tok
        cur = kn.get(("d", idx), 0)
        if cur >= val:
            return None
        kn[("d", idx)] = val
        return (self.dma_sems[idx], val)

    def op(self, eng, fn, reads=(), writes=(), dma=False):
        waits = []
        toks = []
        for d in reads:
            if d.w is not None:
                toks.append((d.w, "raw"))
        for d in writes:
            if d.w is not None:
                toks.append((d.w, "waw"))
            for t in d.r.values():
                toks.append((t, "war"))
        for tok, kind in toks:
            if not dma and tok[0] == "c" and tok[1] == eng:
                if eng == "tensor" or kind != "raw":
                    continue
            w = self._need_wait(eng, tok)
            if w is not None:
                waits.append(w)
        if dma:
            idx = self.dma_i % NDMA
            rnd = self.dma_i // NDMA
            self.dma_i += 1
            if rnd > 0:
                w = self._need_wait(eng, ("d", idx, 16 * rnd))
                if w is not None:
                    waits.append(w)
            mytok = ("d", idx, 16 * (rnd + 1))
            inc = (self.dma_sems[idx], 16)
        else:
            c = self.cnt[eng]
            self.cnt[eng] = c + 1
            ep, val = c // EPOCH, c % EPOCH + 1
            mytok = ("c", eng, ep, val)
            inc = (self._esem(eng, ep), 1)
        self.q[eng].append((fn, waits, inc))
        rk = mytok[:2]
        for d in reads:
            d.r[rk] = mytok
        for d in writes:
            d.w = mytok
            d.r = {}
        return mytok

    def snapshot(self):
        snap = {}
        for e in ENGS:
            c = self.cnt[e]
            if c > 0:
                c -= 1
                snap[("c", e)] = ("c", e, c // EPOCH, c % EPOCH + 1)
        for i in range(min(self.dma_i, NDMA)):
            n_i = (self.dma_i - 1 - i) // NDMA
            snap[("d", i)] = ("d", i, 16 * (n_i + 1))
        return snap

    def wait_tok(self, eng, tok):
        w = self._need_wait(eng, tok)
        if w is not None:
            self.q[eng].append((None, [w], None))

    def emit(self):
        nc = self.nc
        with nc.Block() as block:
            def run(engname):
                def body(e):
                    for fn, waits, inc in self.q[engname]:
                        for sem, val in waits:
                            e.wait_ge(sem, val)
                        if fn is not None:
                            ins = fn(e)
                            ins.then_inc(inc[0], inc[1])
                return body
            block.tensor(run("tensor"))
            block.vector(run("vector"))
            block.scalar(run("scalar"))
            block.gpsimd(run("gpsimd"))
            block.sync(run("sync"))


class Ctx:
    def __init__(self, nc, st):
        self.nc = nc
        self.st = st
        self.S = Sched(nc, st)
        self.ps = st.enter_context(nc.psum_tensor("ps", [128, 8 * 512], F32))
        self.psd = deps(8)
        self.uid = 0
        self.out_toks = []

    def release(self):
        _BASE["r"] = self.S.snapshot()

    def sb(self, stack, shape, dt):
        self.uid += 1
        return stack.enter_context(self.nc.sbuf_tensor(f"t{self.uid}", list(shape), dt))

    def pb(self, i, n=512, off=0):
        return self.ps[:, i * 512 + off:i * 512 + off + n]

    def pbh(self, i):
        return self.ps[:, i * 512:(i + 1) * 512].bitcast(BF16)

    def T(self, fn, r=(), w=()):
        return self.S.op("tensor", fn, r, w)

    def V(self, fn, r=(), w=()):
        return self.S.op("vector", fn, r, w)

    def A(self, fn, r=(), w=()):
        return self.S.op("scalar", fn, r, w)

    def G(self, fn, r=(), w=()):
        return self.S.op("gpsimd", fn, r, w)

    def D(self, fn, r=(), w=()):
        return self.S.op("sync", fn, r, w, dma=True)

    def DG(self, fn, r=(), w=()):
        return self.S.op("gpsimd", fn, r, w, dma=True)

    def mm(self, out, lhsT, rhs, start, stop, r=(), w=()):
        return self.T(lambda e: e.matmul(out, lhsT=lhsT, rhs=rhs, start=start, stop=stop), r, w)

    def tr(self, out, in_, ident, r=(), w=()):
        return self.T(lambda e: e.transpose(out=out, in_=in_, identity=ident), r, w)

    def act(self, out, in_, func, r=(), w=(), **kw):
        return self.A(lambda e: e.activation(out=out, in_=in_, func=func, **kw), r, w)

    def dbg(self, name, ap, shape, dt, r):
        if not DEBUG:
            return
        d = self.nc.dram_tensor("dbg_" + name, list(shape), dt, kind="ExternalOutput").ap()
        tok = self.D(lambda e: e.dma_start(out=d, in_=ap), r=r)
        self.out_toks.append(tok)

    def load_w(self, dst, src2d, dep):
        n = src2d.shape[1]
        tok = None
        for c0 in range(0, n, 512):
            c1 = min(n, c0 + 512)
            assert (c1 - c0) in (16, 32, 64, 128, 256, 512), (c0, c1)
            tok = self.DG(lambda e, c0=c0, c1=c1: e.dma_start(
                out=dst[:, :, c0:c1], in_=src2d[:, c0:c1].rearrange("(c p) n -> p c n", p=128)), w=[dep])
        return tok

    def load_bc(self, dst, src1d, dep, n=128):
        return self.DG(lambda e: e.dma_start(out=dst, in_=src1d.partition_broadcast(n)), w=[dep])

    def load_col(self, dst, src1d, dep, nchunks):
        return self.DG(lambda e: e.dma_start(out=dst, in_=src1d.rearrange("(c p) -> p c", p=128),
                                             allow_slow_non_contiguous=True), w=[dep])


def rstd_ops(C, rs_ap, ss_ap, n, r, w):
    C.act(rs_ap, ss_ap, AF.Ln, r=r, w=w, scale=1.0 / n, bias=EPS)
    C.act(rs_ap, rs_ap, AF.Exp, r=w, w=w, scale=-0.5)


def norm_T(C, stk, h_ap, g_row, uT, d_uT, d_h, banks=(0, 1)):
    with ExitStack() as s:
        gt = C.sb(s, [128, D_], F32); d_gt = Dep()
        C.load_bc(gt[:], g_row, d_gt)
        hb = [C.sb(s, [128, D_], F32) for _ in range(2)]; d_hb = deps(2)
        junk = C.sb(s, [128, D_], BF16); d_junk = Dep()
        ss = C.sb(s, [128, NT], F32); rs = C.sb(s, [128, NT], F32)
        d_ss = deps(NT); d_rs = deps(NT)
        xn = [C.sb(s, [128, D_], BF16) for _ in range(2)]; d_xn = deps(2)
        for t in range(NT):
            b = t % 2
            C.D(lambda e, t=t, b=b: e.dma_start(out=hb[b][:], in_=h_ap[t * 128:(t + 1) * 128, :]),
                r=[d_h[t]], w=[d_hb[b]])
            C.act(junk[:], hb[b][:], AF.Square, r=[d_hb[b]], w=[d_junk, d_ss[t]], accum_out=ss[:, t:t + 1])
            rstd_ops(C, rs[:, t:t + 1], ss[:, t:t + 1], D_, [d_ss[t]], [d_rs[t]])
            C.V(lambda e, t=t, b=b: e.scalar_tensor_tensor(out=xn[b][:], in0=hb[b][:], scalar=rs[:, t:t + 1],
                                                           in1=gt[:], op0=ALU.mult, op1=ALU.mult),
                r=[d_hb[b], d_rs[t], d_gt], w=[d_xn[b]])
            bk = banks[t % len(banks)]
            pv = C.pbh(bk)
            for c in range(8):
                C.tr(pv[:, c * 128:(c + 1) * 128], xn[b][:, c * 128:(c + 1) * 128], C.ident_bf[:],
                     r=[d_xn[b], C.d_const], w=[C.psd[bk]])
            C.V(lambda e, t=t, pv=pv: e.tensor_copy(out=uT[:, :, t * 128:(t + 1) * 128],
                                                   in_=pv.rearrange("p (c n) -> p c n", c=8)),
                r=[C.psd[bk]], w=[d_uT[t]])
        C.release()


def out_proj(C, goT, d_go, nk, w_out, h_in, d_hin, h_out, d_hout, banks=(0, 1, 2, 3), wo_pre=None):
    with ExitStack() as s:
        if wo_pre is None:
            wo = C.sb(s, [128, nk, D_], BF16); d_wo = Dep()
            half = nk // 2
            C.load_w(wo[:, 0:half, :], w_out[0:half * 128, :], d_wo)
            C.load_w(wo[:, half:nk, :], w_out[half * 128:nk * 128, :], d_wo)
        else:
            wo, d_wo = wo_pre
        hb = [C.sb(s, [128, D_], F32) for _ in range(2)]; d_hb = deps(2)
        ho = [C.sb(s, [128, D_], F32) for _ in range(2)]; d_ho = deps(2)
        i = 0
        for t in range(NT):
            b = t % 2
            C.D(lambda e, t=t, b=b: e.dma_start(out=hb[b][:], in_=h_in[t * 128:(t + 1) * 128, :]),
                r=[d_hin[t]], w=[d_hb[b]])
            for nb in range(2):
                bk = banks[i % len(banks)]; i += 1
                for c in range(nk):
                    C.mm(C.pb(bk), goT[:, c, t * 128:(t + 1) * 128], wo[:, c, nb * 512:(nb + 1) * 512],
                         c == 0, c == nk - 1, r=[d_go[t], d_wo], w=[C.psd[bk]])
                C.V(lambda e, b=b, nb=nb, bk=bk: e.tensor_tensor(out=ho[b][:, nb * 512:(nb + 1) * 512], in0=C.pb(bk),
                                                               in1=hb[b][:, nb * 512:(nb + 1) * 512], op=ALU.add),
                    r=[C.psd[bk], d_hb[b]], w=[d_ho[b]])
            tok = C.D(lambda e, t=t, b=b: e.dma_start(out=h_out[t * 128:(t + 1) * 128, :], in_=ho[b][:]),
                      r=[d_ho[b]], w=[d_hout[t]])
            C.last_store = tok
        C.release()


def final_norm(C, h_ap, d_h, g_row, out_ap):
    with ExitStack() as s:
        gt = C.sb(s, [128, D_], F32); d_gt = Dep()
        C.load_bc(gt[:], g_row, d_gt)
        hb = [C.sb(s, [128, D_], F32) for _ in range(2)]; d_hb = deps(2)
        junk = C.sb(s, [128, D_], BF16); d_junk = Dep()
        ss = C.sb(s, [128, NT], F32); rs = C.sb(s, [128, NT], F32)
        d_ss = deps(NT); d_rs = deps(NT)
        ob = [C.sb(s, [128, D_], F32) for _ in range(2)]; d_ob = deps(2)
        for t in range(NT):
            b = t % 2
            C.D(lambda e, t=t, b=b: e.dma_start(out=hb[b][:], in_=h_ap[t * 128:(t + 1) * 128, :]),
                r=[d_h[t]], w=[d_hb[b]])
            C.act(junk[:], hb[b][:], AF.Square, r=[d_hb[b]], w=[d_junk, d_ss[t]], accum_out=ss[:, t:t + 1])
            rstd_ops(C, rs[:, t:t + 1], ss[:, t:t + 1], D_, [d_ss[t]], [d_rs[t]])
            C.V(lambda e, t=t, b=b: e.scalar_tensor_tensor(out=ob[b][:], in0=hb[b][:], scalar=rs[:, t:t + 1],
                                                           in1=gt[:], op0=ALU.mult, op1=ALU.mult),
                r=[d_hb[b], d_rs[t], d_gt], w=[d_ob[b]])
            tok = C.D(lambda e, t=t, b=b: e.dma_start(out=out_ap[t * 128:(t + 1) * 128, :], in_=ob[b][:]),
                      r=[d_ob[b]])
            C.out_toks.append(tok)
        C.release()


def layer_lru(C, W, h_in, d_hin, h_out, d_hout):
    NCH = 10
    with ExitStack() as L:
        uT = C.sb(L, [128, 8, S_], BF16); d_uT = deps(NT)
        norm_T(C, L, h_in, W["norm_g"], uT, d_uT, d_hin)
        d_uTb = [[d_uT[4 * tb + i] for i in range(4)] for tb in range(4)]
        goT = C.sb(L, [128, NCH, S_], BF16); d_go = deps(NT)
        cw = C.sb(L, [128, 4, NCH], F32); d_cw = Dep()
        for k in range(4):
            C.load_col(cw[:, k, :], W["lru_conv_w"][k, :], d_cw, NCH)
        cb = C.sb(L, [128, NCH], F32); ba = C.sb(L, [128, NCH], F32); bx = C.sb(L, [128, NCH], F32)
        lam = C.sb(L, [128, NCH], F32); d_p = Dep()
        C.load_col(cb[:], W["lru_conv_b"], d_p, NCH)
        C.load_col(ba[:], W["lru_b_a"], d_p, NCH)
        C.load_col(bx[:], W["lru_b_x"], d_p, NCH)
        C.load_col(lam[:], W["lru_lam"], d_p, NCH)
        cA = C.sb(L, [128, NCH], F32); cA2 = C.sb(L, [128, NCH], F32); d_cA = Dep()
        C.act(cA[:], lam[:], AF.Exp, r=[d_p], w=[d_cA], scale=-1.0)
        C.act(cA[:], cA[:], AF.Ln, r=[d_cA], w=[d_cA], bias=1.0)
        C.V(lambda e: e.tensor_scalar(out=cA2[:], in0=cA[:], scalar1=-16.0, scalar2=None, op0=ALU.mult),
            r=[d_cA], w=[d_cA])
        C.V(lambda e: e.tensor_scalar(out=cA[:], in0=cA[:], scalar1=-8.0, scalar2=None, op0=ALU.mult),
            r=[d_cA], w=[d_cA])
        wa = C.sb(L, [128, NCH, 128], BF16); wx = C.sb(L, [128, NCH, 128], BF16); d_wg = Dep()
        C.DG(lambda e: e.dma_start(out=wa[:], in_=W["lru_w_a"].rearrange("n i j -> i n j")), w=[d_wg])
        C.DG(lambda e: e.dma_start(out=wx[:], in_=W["lru_w_x"].rearrange("n i j -> i n j")), w=[d_wg])
        dg = C.sb(L, [128, NCH, 4, 128], BF16); d_dg = Dep()
        for c in range(NCH):
            for k in range(4):
                C.V(lambda e, c=c, k=k: e.tensor_scalar(out=dg[:, c, k, :], in0=C.ident_bf[:],
                                                        scalar1=cw[:, k, c:c + 1], scalar2=None, op0=ALU.mult),
                    r=[d_cw, C.d_const], w=[d_dg])
        wu = [C.sb(L, [128, 8, 128], BF16) for _ in range(2)]; d_wu = deps(2)
        wg = [C.sb(L, [128, 8, 128], BF16) for _ in range(2)]; d_wgt = deps(2)
        ub = C.sb(L, [128, 3 + S_], BF16); d_ub = Dep()
        C.V(lambda e: e.memset(ub[:, 0:3], 0.0), w=[d_ub])
        uc = C.sb(L, [128, S_], F32); d_uc = Dep()
        ucb = C.sb(L, [128, S_], BF16); d_ucb = Dep()
        rr = C.sb(L, [128, S_], F32); d_rr = Dep()
        ig = C.sb(L, [128, S_], F32); d_ig = Dep()
        tmp = C.sb(L, [128, S_], F32); d_tmp = Dep()
        hs = C.sb(L, [128, S_], BF16); d_hs = Dep()
        sg = C.sb(L, [128, S_], BF16); d_sg = Dep()
        w_in = W["lru_w_in"]
        for c in range(NCH):
            b = c % 2
            C.load_w(wu[b][:], w_in[:, 1280 + c * 128:1280 + (c + 1) * 128], d_wu[b])
            C.load_w(wg[b][:], w_in[:, c * 128:(c + 1) * 128], d_wgt[b])
            for tb in range(4):
                bk = tb
                for kc in range(8):
                    C.mm(C.pb(bk), wu[b][:, kc, :], uT[:, kc, tb * 512:(tb + 1) * 512], kc == 0, kc == 7,
                         r=d_uTb[tb] + [d_wu[b]], w=[C.psd[bk]])
                C.act(ub[:, 3 + tb * 512:3 + (tb + 1) * 512], C.pb(bk), AF.Copy, r=[C.psd[bk]], w=[d_ub])
            for tb in range(4):
                bk = 4 + tb
                for k in range(4):
                    C.mm(C.pb(bk), dg[:, c, k, :], ub[:, tb * 512 + k:tb * 512 + k + 512], k == 0, k == 3,
                         r=[d_ub, d_dg], w=[C.psd[bk]])
                C.act(uc[:, tb * 512:(tb + 1) * 512], C.pb(bk), AF.Identity, r=[C.psd[bk], d_p], w=[d_uc],
                      bias=cb[:, c:c + 1])
            C.G(lambda e: e.tensor_copy(out=ucb[:], in_=uc[:]), r=[d_uc], w=[d_ucb])
            for tb in range(4):
                bk = tb
                C.mm(C.pb(bk), wa[:, c, :], ucb[:, tb * 512:(tb + 1) * 512], True, True,
                     r=[d_ucb, d_wg], w=[C.psd[bk]])
                C.act(rr[:, tb * 512:(tb + 1) * 512], C.pb(bk), AF.Sigmoid, r=[C.psd[bk], d_p], w=[d_rr],
                      bias=ba[:, c:c + 1])
            for tb in range(4):
                bk = 4 + tb
                C.mm(C.pb(bk), wx[:, c, :], ucb[:, tb * 512:(tb + 1) * 512], True, True,
                     r=[d_ucb, d_wg], w=[C.psd[bk]])
                C.act(ig[:, tb * 512:(tb + 1) * 512], C.pb(bk), AF.Sigmoid, r=[C.psd[bk], d_p], w=[d_ig],
                      bias=bx[:, c:c + 1])
            C.act(tmp[:], rr[:], AF.Exp, r=[d_rr, d_cA], w=[d_tmp], scale=cA2[:, c:c + 1])
            C.act(rr[:], rr[:], AF.Exp, r=[d_rr, d_cA], w=[d_rr], scale=cA[:, c:c + 1])
            C.act(tmp[:], tmp[:], AF.Sqrt, r=[d_tmp], w=[d_tmp], scale=-1.0, bias=1.0)
            C.V(lambda e: e.tensor_tensor(out=ig[:], in0=ig[:], in1=uc[:], op=ALU.mult), r=[d_ig, d_uc], w=[d_ig])
            C.G(lambda e: e.tensor_tensor(out=tmp[:], in0=tmp[:], in1=ig[:], op=ALU.mult), r=[d_tmp, d_ig], w=[d_tmp])
            C.V(lambda e: e.tensor_tensor_scan(out=hs[:], data0=rr[:], data1=tmp[:], initial=0.0,
                                               op0=ALU.mult, op1=ALU.add), r=[d_rr, d_tmp], w=[d_hs])
            for tb in range(4):
                bk = tb
                for kc in range(8):
                    C.mm(C.pb(bk), wg[b][:, kc, :], uT[:, kc, tb * 512:(tb + 1) * 512], kc == 0, kc == 7,
                         r=d_uTb[tb] + [d_wgt[b]], w=[C.psd[bk]])
                C.act(sg[:, tb * 512:(tb + 1) * 512], C.pb(bk), AF.Silu, r=[C.psd[bk]], w=[d_sg])
            C.G(lambda e, c=c: e.tensor_tensor(out=goT[:, c, :], in0=hs[:], in1=sg[:], op=ALU.mult),
                r=[d_hs, d_sg], w=d_go)
        out_proj(C, goT, d_go, NCH, W["lru_w_out"], h_in, d_hin, h_out, d_hout)
        C.release()


def layer_mla(C, W, h_in, d_hin, h_out, d_hout):
    H = 16
    PI = float(np.pi)
    with ExitStack() as L:
        cqnT = C.sb(L, [128, 3, S_], BF16); d_cq = deps(NT)
        ckvT = C.sb(L, [128, 2, S_], BF16); d_ckv = deps(NT)
        sgT = C.sb(L, [128, 8, S_], BF16); d_sg4 = deps(4)
        kr_all = C.sb(L, [128, NT, 32], F32); d_kr = Dep()
        w_in = W["mla_w_in"]
        with ExitStack() as PA:
            uT = C.sb(PA, [128, 8, S_], BF16); d_uT = deps(NT)
            norm_T(C, PA, h_in, W["norm_g"], uT, d_uT, d_hin)
            d_uTb = [[d_uT[4 * tb + i] for i in range(4)] for tb in range(4)]
            w1 = C.sb(PA, [128, 8, 768], BF16); d_w1 = Dep()
            C.load_w(w1[:], w_in[:, 0:768], d_w1)
            gq = C.sb(PA, [128, 384], F32); gkv = C.sb(PA, [128, 256], F32); d_g = Dep()
            C.load_bc(gq[:], W["mla_g_q"], d_g)
            C.load_bc(gkv[:], W["mla_g_kv"], d_g)
            ctok = [C.sb(PA, [128, 672], F32) for _ in range(2)]; d_ct = deps(2)
            ss2 = C.sb(PA, [128, NT, 2], F32); rs2 = C.sb(PA, [128, NT, 2], F32)
            d_ss = deps(NT); d_rs = deps(NT)
            junk = C.sb(PA, [128, 384], BF16); d_junk = Dep()
            cn = [C.sb(PA, [128, 640], BF16) for _ in range(2)]; d_cn = deps(2)
            wgt = [C.sb(PA, [128, 8, 512], BF16) for _ in range(2)]; d_wgt = deps(2)
            for hf in range(2):
                C.load_w(wgt[hf][:], w_in[:, 672 + hf * 512:672 + (hf + 1) * 512], d_wgt[hf])
            for t in range(NT):
                b = t % 2
                for (c0, c1, bk) in ((0, 512, 2), (512, 672, 3)):
                    for kc in range(8):
                        C.mm(C.pb(bk, c1 - c0), uT[:, kc, t * 128:(t + 1) * 128], w1[:, kc, c0:c1], kc == 0, kc == 7,
                             r=[d_uT[t], d_w1], w=[C.psd[bk]])
                    C.act(ctok[b][:, c0:c1], C.pb(bk, c1 - c0), AF.Copy, r=[C.psd[bk]], w=[d_ct[b]])
                C.act(junk[:, 0:384], ctok[b][:, 0:384], AF.Square, r=[d_ct[b]], w=[d_junk, d_ss[t]],
                      accum_out=ss2[:, t, 0:1])
                C.act(junk[:, 0:256], ctok[b][:, 384:640], AF.Square, r=[d_ct[b]], w=[d_junk, d_ss[t]],
                      accum_out=ss2[:, t, 1:2])
                rstd_ops(C, rs2[:, t, 0:1], ss2[:, t, 0:1], 384, [d_ss[t]], [d_rs[t]])
                rstd_ops(C, rs2[:, t, 1:2], ss2[:, t, 1:2], 256, [d_ss[t]], [d_rs[t]])
                C.V(lambda e, t=t, b=b: e.scalar_tensor_tensor(out=cn[b][:, 0:384], in0=ctok[b][:, 0:384],
                                                               scalar=rs2[:, t, 0:1], in1=gq[:], op0=ALU.mult, op1=ALU.mult),
                    r=[d_ct[b], d_rs[t], d_g], w=[d_cn[b]])
                C.V(lambda e, t=t, b=b: e.scalar_tensor_tensor(out=cn[b][:, 384:640], in0=ctok[b][:, 384:640],
                                                               scalar=rs2[:, t, 1:2], in1=gkv[:], op0=ALU.mult, op1=ALU.mult),
                    r=[d_ct[b], d_rs[t], d_g], w=[d_cn[b]])
                C.G(lambda e, t=t, b=b: e.tensor_copy(out=kr_all[:, t, :], in_=ctok[b][:, 640:672]),
                    r=[d_ct[b]], w=[d_kr])
                bk = 4 + b
                pv = C.pbh(bk)
                for c in range(5):
                    C.tr(pv[:, c * 128:(c + 1) * 128], cn[b][:, c * 128:(c + 1) * 128], C.ident_bf[:],
                         r=[d_cn[b], C.d_const], w=[C.psd[bk]])
                C.V(lambda e, t=t, pv=pv: e.tensor_copy(out=cqnT[:, :, t * 128:(t + 1) * 128],
                                                       in_=pv[:, 0:384].rearrange("p (c n) -> p c n", c=3)),
                    r=[C.psd[bk]], w=[d_cq[t]])
                C.V(lambda e, t=t, pv=pv: e.tensor_copy(out=ckvT[:, :, t * 128:(t + 1) * 128],
                                                       in_=pv[:, 384:640].rearrange("p (c n) -> p c n", c=2)),
                    r=[C.psd[bk]], w=[d_ckv[t]])
            C.dbg("ctok", ctok[1][:], [128, 672], F32, [d_ct[1]])
            C.dbg("w1", w1[:], [128, 8, 768], BF16, [d_w1])
            C.dbg("cn", cn[1][:], [128, 640], BF16, [d_cn[1]])
            C.dbg("rs2", rs2[:], [128, NT, 2], F32, d_rs)
            i = 0
            for c in range(8):
                for tb in range(4):
                    bk = (6, 7, 0, 1)[i % 4]; i += 1
                    for kc in range(8):
                        C.mm(C.pb(bk), wgt[c // 4][:, kc, (c % 4) * 128:(c % 4 + 1) * 128],
                             uT[:, kc, tb * 512:(tb + 1) * 512], kc == 0, kc == 7,
                             r=d_uTb[tb] + [d_wgt[c // 4]], w=[C.psd[bk]])
                    C.act(sgT[:, c, tb * 512:(tb + 1) * 512], C.pb(bk), AF.Silu, r=[C.psd[bk]], w=[d_sg4[tb]])
            C.release()
        C.dbg("cqnT", cqnT[:], [128, 3, S_], BF16, d_cq)
        C.dbg("ckvT", ckvT[:], [128, 2, S_], BF16, d_ckv)
        C.dbg("sgT", sgT[:], [128, 8, S_], BF16, d_sg4)
        C.dbg("kr", kr_all[:], [128, NT, 32], F32, [d_kr])
        with ExitStack() as PB:
            wuq = C.sb(PB, [128, 3, 1536], BF16); d_wuq = Dep()
            C.load_w(wuq[:], W["mla_w_uq"], d_wuq)
            wukv = C.sb(PB, [128, 2, 2048], BF16); d_wukv = Dep()
            C.load_w(wukv[:], W["mla_w_ukv"], d_wukv)
            wo = C.sb(PB, [128, 8, D_], BF16); d_wo = Dep()
            C.load_w(wo[:, 0:4, :], W["mla_w_out"][0:512, :], d_wo)
            C.load_w(wo[:, 4:8, :], W["mla_w_out"][512:1024, :], d_wo)
            tri = C.sb(PB, [128, 128], BF16); ones64 = C.sb(PB, [128, 64], BF16)
            invf = C.sb(PB, [128, 16], F32); posi = C.sb(PB, [128, NT], I32); d_k = Dep()
            C.D(lambda e: e.dma_start(out=tri[:], in_=C.CD["tri_bf"]), w=[d_k])
            C.D(lambda e: e.dma_start(out=ones64[:], in_=C.CD["ones_bf"]), w=[d_k])
            C.D(lambda e: e.dma_start(out=invf[:], in_=C.CD["invf"]), w=[d_k])
            C.D(lambda e: e.dma_start(out=posi[:], in_=C.pos), w=[d_k])
            posf = C.sb(PB, [128, NT], F32); d_t = Dep()
            ang = C.sb(PB, [128, NT, 16], F32); tmpf = C.sb(PB, [128, NT, 16], F32)
            ki = C.sb(PB, [128, NT, 16], I32); kff = C.sb(PB, [128, NT, 16], F32)
            rr = C.sb(PB, [128, NT, 16], F32); yy = C.sb(PB, [128, NT, 16], F32); mm_ = C.sb(PB, [128, NT, 16], F32)
            cos_t = C.sb(PB, [128, NT, 16], F32); sin_t = C.sb(PB, [128, NT, 16], F32); d_cs = Dep()
            C.V(lambda e: e.tensor_copy(out=posf[:], in_=posi[:]), r=[d_k], w=[d_t])
            C.V(lambda e: e.tensor_tensor(out=ang[:], in0=posf[:].unsqueeze(2).to_broadcast([128, NT, 16]),
                                          in1=invf[:].unsqueeze(1).to_broadcast([128, NT, 16]), op=ALU.mult),
                r=[d_t, d_k], w=[d_t])
            C.V(lambda e: e.tensor_scalar(out=tmpf[:], in0=ang[:], scalar1=1.0 / (2 * PI), scalar2=None, op0=ALU.mult),
                r=[d_t], w=[d_t])
            C.V(lambda e: e.tensor_copy(out=ki[:], in_=tmpf[:]), r=[d_t], w=[d_t])
            C.V(lambda e: e.tensor_copy(out=kff[:], in_=ki[:]), r=[d_t], w=[d_t])
            C1 = 6.28125
            C2 = float(2 * np.pi - 6.28125)
            C.V(lambda e: e.scalar_tensor_tensor(out=rr[:], in0=kff[:], scalar=-C1, in1=ang[:], op0=ALU.mult, op1=ALU.add),
                r=[d_t], w=[d_t])
            C.V(lambda e: e.scalar_tensor_tensor(out=rr[:], in0=kff[:], scalar=-C2, in1=rr[:], op0=ALU.mult, op1=ALU.add),
                r=[d_t], w=[d_t])
            for dst, shift in ((sin_t, 0.0), (cos_t, PI / 2)):
                C.V(lambda e, shift=shift: e.tensor_scalar(out=yy[:], in0=rr[:], scalar1=shift, scalar2=None, op0=ALU.add),
                    r=[d_t], w=[d_t])
                C.V(lambda e: e.tensor_scalar(out=mm_[:], in0=yy[:], scalar1=PI, scalar2=-2 * PI, op0=ALU.is_gt, op1=ALU.mult),
                    r=[d_t], w=[d_t])
                C.V(lambda e: e.tensor_tensor(out=yy[:], in0=yy[:], in1=mm_[:], op=ALU.add), r=[d_t], w=[d_t])
                C.V(lambda e: e.tensor_scalar(out=mm_[:], in0=yy[:], scalar1=-PI, scalar2=2 * PI, op0=ALU.is_lt, op1=ALU.mult),
                    r=[d_t], w=[d_t])
                C.V(lambda e: e.tensor_tensor(out=yy[:], in0=yy[:], in1=mm_[:], op=ALU.add), r=[d_t], w=[d_t])
                C.act(dst[:], yy[:], AF.Sin, r=[d_t], w=[d_cs, d_t])
            qT = [C.sb(PB, [128, S_], BF16) for _ in range(2)]; d_qT = deps(2)
            kT = [C.sb(PB, [128, S_], BF16) for _ in range(2)]; d_kT = deps(2)
            Vh = [C.sb(PB, [128, NT, 64], BF16) for _ in range(2)]; d_V = deps(2)
            for b in range(2):
                C.G(lambda e, b=b: e.memset(qT[b][:], 0.0), w=[d_qT[b]])
                C.G(lambda e, b=b: e.memset(kT[b][:], 0.0), w=[d_kT[b]])
            qtok = C.sb(PB, [128, NT, 96], F32); d_qtok = Dep()
            ta = C.sb(PB, [128, NT, 16], F32); tb_ = C.sb(PB, [128, NT, 16], F32)
            tc = C.sb(PB, [128, NT, 16], F32); td = C.sb(PB, [128, NT, 16], F32); d_tt = Dep()
            krr = C.sb(PB, [128, NT, 96], F32); d_krr = Dep()
            C.G(lambda e: e.memset(krr[:], 0.0), w=[d_krr])
            k1 = kr_all[:, :, 0:16]; k2 = kr_all[:, :, 16:32]
            C.V(lambda e: e.tensor_tensor(out=ta[:], in0=k1, in1=cos_t[:], op=ALU.mult), r=[d_kr, d_cs], w=[d_tt])
            C.V(lambda e: e.tensor_tensor(out=tb_[:], in0=k2, in1=sin_t[:], op=ALU.mult), r=[d_kr, d_cs], w=[d_tt])
            C.V(lambda e: e.tensor_tensor(out=krr[:, :, 64:80], in0=ta[:], in1=tb_[:], op=ALU.subtract), r=[d_tt], w=[d_krr])
            C.V(lambda e: e.tensor_tensor(out=tc[:], in0=k2, in1=cos_t[:], op=ALU.mult), r=[d_kr, d_cs], w=[d_tt])
            C.V(lambda e: e.tensor_tensor(out=td[:], in0=k1, in1=sin_t[:], op=ALU.mult), r=[d_kr, d_cs], w=[d_tt])
            C.V(lambda e: e.tensor_tensor(out=krr[:, :, 80:96], in0=tc[:], in1=td[:], op=ALU.add), r=[d_tt], w=[d_krr])
            for t4 in range(4):
                bk = 7
                for i in range(4):
                    C.tr(C.pb(bk)[0:96, i * 128:(i + 1) * 128], krr[:, t4 * 4 + i, :], C.ident_f[:],
                         r=[d_krr, C.d_const], w=[C.psd[bk]])
                for b in range(2):
                    C.V(lambda e, b=b, t4=t4, bk=bk: e.tensor_copy(out=kT[b][64:96, t4 * 512:(t4 + 1) * 512],
                                                                 in_=C.pb(bk)[64:96, :]),
                        r=[C.psd[bk]], w=[d_kT[b]])
            lnd = C.sb(PB, [128, 512], F32); rden = C.sb(PB, [128, 512], F32); ot = C.sb(PB, [128, 512], F32)
            d_nrm = Dep()
            PT = [C.sb(PB, [128, 512], BF16) for _ in range(3)]; d_PT = deps(3)
            QS = 96 ** -0.5

            def proj_units(h, b):
                for t4 in range(4):
                    bk = 7
                    for i in range(4):
                        t = t4 * 4 + i
                        for kc in range(3):
                            C.mm(C.pb(bk)[:, i * 96:(i + 1) * 96], cqnT[:, kc, t * 128:(t + 1) * 128],
                                 wuq[:, kc, h * 96:(h + 1) * 96], kc == 0, kc == 2,
                                 r=[d_cq[t], d_wuq], w=[C.psd[bk]])
                    C.V(lambda e, t4=t4, bk=bk: e.tensor_scalar(
                        out=qtok[:, t4 * 4:(t4 + 1) * 4, :], in0=C.pb(bk)[:, 0:384].rearrange("p (a c) -> p a c", a=4),
                        scalar1=QS, scalar2=None, op0=ALU.mult), r=[C.psd[bk]], w=[d_qtok])
                    yield
                q1 = qtok[:, :, 64:80]; q2 = qtok[:, :, 80:96]
                C.V(lambda e: e.tensor_tensor(out=ta[:], in0=q1, in1=cos_t[:], op=ALU.mult), r=[d_qtok, d_cs], w=[d_tt])
                C.V(lambda e: e.tensor_tensor(out=tb_[:], in0=q2, in1=sin_t[:], op=ALU.mult), r=[d_qtok, d_cs], w=[d_tt])
                C.V(lambda e: e.tensor_tensor(out=tc[:], in0=q2, in1=cos_t[:], op=ALU.mult), r=[d_qtok, d_cs], w=[d_tt])
                C.V(lambda e: e.tensor_tensor(out=td[:], in0=q1, in1=sin_t[:], op=ALU.mult), r=[d_qtok, d_cs], w=[d_tt])
                C.V(lambda e: e.tensor_tensor(out=q1, in0=ta[:], in1=tb_[:], op=ALU.subtract), r=[d_tt], w=[d_qtok])
                C.V(lambda e: e.tensor_tensor(out=q2, in0=tc[:], in1=td[:], op=ALU.add), r=[d_tt], w=[d_qtok])
                yield
                for t4 in range(4):
                    bk = 7
                    for i in range(4):
                        C.tr(C.pb(bk)[0:96, i * 128:(i + 1) * 128], qtok[:, t4 * 4 + i, :], C.ident_f[:],
                             r=[d_qtok, C.d_const], w=[C.psd[bk]])
                    C.V(lambda e, t4=t4, bk=bk: e.tensor_copy(out=qT[b][0:96, t4 * 512:(t4 + 1) * 512],
                                                            in_=C.pb(bk)[0:96, :]), r=[C.psd[bk]], w=[d_qT[b]])
                    yield
                for tb in range(4):
                    bk = 7
                    for kc in range(2):
                        C.mm(C.pb(bk)[0:64, :], wukv[:, kc, h * 128:h * 128 + 64], ckvT[:, kc, tb * 512:(tb + 1) * 512],
                             kc == 0, kc == 1, r=[d_ckv[4 * tb + i] for i in range(4)] + [d_wukv], w=[C.psd[bk]])
                    C.V(lambda e, tb=tb, bk=bk: e.tensor_copy(out=kT[b][0:64, tb * 512:(tb + 1) * 512],
                                                            in_=C.pb(bk)[0:64, :]), r=[C.psd[bk]], w=[d_kT[b]])
                    yield
                for t8 in range(2):
                    bk = 7
                    for i in range(8):
                        t = t8 * 8 + i
                        for kc in range(2):
                            C.mm(C.pb(bk)[:, i * 64:(i + 1) * 64], ckvT[:, kc, t * 128:(t + 1) * 128],
                                 wukv[:, kc, h * 128 + 64:h * 128 + 128], kc == 0, kc == 1,
                                 r=[d_ckv[t], d_wukv], w=[C.psd[bk]])
                    C.V(lambda e, t8=t8, bk=bk: e.tensor_copy(out=Vh[b][:, t8 * 8:(t8 + 1) * 8, :],
                                                            in_=C.pb(bk).rearrange("p (a c) -> p a c", a=8)),
                        r=[C.psd[bk]], w=[d_V[b]])
                    yield

            cnt = {"s": 0, "n": 0}

            def attn(h, b, units):
                r0 = (h % 2) * 64
                c = h // 2
                for qc in range(4):
                    n = cnt["n"]; cnt["n"] += 1
                    bO = 3 + n % 2; bD = 5 + n % 2
                    nj = 4 * qc + 4
                    for j in range(nj):
                        col0 = max(0, j * 128 - qc * 512)
                        si = cnt["s"]; cnt["s"] += 1
                        bS = si % 3
                        pt = PT[si % 3]; d_pt = d_PT[si % 3]
                        C.mm(C.pb(bS)[:, col0:512], kT[b][:, j * 128:(j + 1) * 128],
                             qT[b][:, qc * 512 + col0:(qc + 1) * 512], True, True,
                             r=[d_kT[b], d_qT[b]], w=[C.psd[bS]])
                        C.act(pt[:, col0:512], C.pb(bS)[:, col0:512], AF.Exp, r=[C.psd[bS]], w=[d_pt])
                        if j >= 4 * qc:
                            C.G(lambda e, pt=pt, col0=col0: e.tensor_tensor(out=pt[:, col0:col0 + 128],
                                                                          in0=pt[:, col0:col0 + 128], in1=tri[:], op=ALU.mult),
                                r=[d_pt, d_k], w=[d_pt])
                        C.mm(C.pb(bO)[r0:r0 + 64, col0:512], Vh[b][:, j, :], pt[:, col0:512], j == 0, j == nj - 1,
                             r=[d_V[b], d_pt], w=[C.psd[bO]])
                        C.mm(C.pb(bD)[r0:r0 + 64, col0:512], ones64[:], pt[:, col0:512], j == 0, j == nj - 1,
                             r=[d_k, d_pt], w=[C.psd[bD]])
                        if units is not None:
                            next(units, None)
                    rs = slice(r0, r0 + 64)
                    C.act(lnd[rs, :], C.pb(bD)[rs, :], AF.Ln, r=[C.psd[bD]], w=[d_nrm])
                    C.act(rden[rs, :], lnd[rs, :], AF.Exp, r=[d_nrm], w=[d_nrm], scale=-1.0)
                    C.V(lambda e, rs=rs, bO=bO: e.tensor_tensor(out=ot[rs, :], in0=C.pb(bO)[rs, :], in1=rden[rs, :], op=ALU.mult),
                        r=[C.psd[bO], d_nrm], w=[d_nrm])
                    C.V(lambda e, rs=rs, c=c, qc=qc: e.tensor_tensor(out=sgT[rs, c, qc * 512:(qc + 1) * 512], in0=ot[rs, :],
                                                                    in1=sgT[rs, c, qc * 512:(qc + 1) * 512], op=ALU.mult),
                        r=[d_nrm, d_sg4[qc]], w=[d_sg4[qc]])

            for _ in proj_units(0, 0):
                pass
            C.dbg("cos", cos_t[:], [128, NT, 16], F32, [d_cs])
            C.dbg("sin", sin_t[:], [128, NT, 16], F32, [d_cs])
            C.dbg("qT0", qT[0][:], [128, S_], BF16, [d_qT[0]])
            C.dbg("kT0", kT[0][:], [128, S_], BF16, [d_kT[0]])
            C.dbg("V0", Vh[0][:], [128, NT, 64], BF16, [d_V[0]])
            for h in range(H):
                units = proj_units(h + 1, (h + 1) % 2) if h + 1 < H else None
                attn(h, h % 2, units)
                if units is not None:
                    for _ in units:
                        pass
            d_go = [d_sg4[t // 4] for t in range(NT)]
            out_proj(C, sgT, d_go, 8, None, h_in, d_hin, h_out, d_hout, banks=(0, 1, 2, 7), wo_pre=(wo, d_wo))
            C.release()
        C.release()


LAYER_FNS[0] = layer_mla

def layer_gla(C, W, h_in, d_hin, h_out, d_hout):
    w_in = W["gla_w_in"]
    with ExitStack() as L:
        qtT = C.sb(L, [128, 4, S_], BF16); d_qt = deps(4)
        ktT = C.sb(L, [128, 4, S_], BF16); d_kt = deps(4)
        vtok = C.sb(L, [128, NT, 1024], BF16); d_v = deps(NT)
        gsg = C.sb(L, [128, NT, 1024], BF16); d_gsg = deps(NT)
        elast = C.sb(L, [128, 4, 32], F32); d_el = Dep()
        with ExitStack() as PA:
            uT = C.sb(PA, [128, 8, S_], BF16); d_uT = deps(NT)
            norm_T(C, PA, h_in, W["norm_g"], uT, d_uT, d_hin)
            d_uTb = [[d_uT[4 * tb + i] for i in range(4)] for tb in range(4)]
            wgk = C.sb(PA, [128, 8, 16], BF16); d_wgk = Dep()
            C.load_w(wgk[:], w_in[:, 3072:3088], d_wgk)
            gkT = C.sb(PA, [32, S_], BF16); d_gkT = Dep()
            wgk2 = C.sb(PA, [32, 512], BF16); d_wgk2 = Dep()
            C.V(lambda e: e.memset(gkT[:], 0.0), w=[d_gkT])
            C.V(lambda e: e.memset(wgk2[:], 0.0), w=[d_wgk2])
            C.DG(lambda e: e.dma_start(out=wgk2[0:16, :], in_=W["gla_w_gk2"]), w=[d_wgk2])
            for tb in range(4):
                bk = 2 + tb % 2
                for kc in range(8):
                    C.mm(C.pb(bk)[0:16, :], wgk[:, kc, :], uT[:, kc, tb * 512:(tb + 1) * 512], kc == 0, kc == 7,
                         r=d_uTb[tb] + [d_wgk], w=[C.psd[bk]])
                C.V(lambda e, tb=tb, bk=bk: e.tensor_copy(out=gkT[0:16, tb * 512:(tb + 1) * 512], in_=C.pb(bk)[0:16, :]),
                    r=[C.psd[bk]], w=[d_gkT])
            nbgk = C.sb(PA, [128, 4], F32); d_nb = Dep()
            C.load_col(nbgk[:], W["gla_b_gk"], d_nb, 4)
            C.V(lambda e: e.tensor_scalar(out=nbgk[:], in0=nbgk[:], scalar1=-1.0, scalar2=None, op0=ALU.mult),
                r=[d_nb], w=[d_nb])
            rmask = C.sb(PA, [128, S_], BF16); d_rm = Dep()
            C.D(lambda e: e.dma_start(out=rmask[:], in_=C.CD["rmask"]), w=[d_rm])
            Lh = C.sb(PA, [128, S_], F32); d_Lh = Dep()
            Bp = C.sb(PA, [128, S_], F32); d_Bp = Dep()
            wq = [C.sb(PA, [128, 8, 128], BF16) for _ in range(2)]; d_wq = deps(2)
            wk = [C.sb(PA, [128, 8, 128], BF16) for _ in range(2)]; d_wk = deps(2)
            QS = 128 ** -0.5
            for h in range(4):
                b = h % 2
                C.load_w(wq[b][:], w_in[:, h * 128:(h + 1) * 128], d_wq[b])
                C.load_w(wk[b][:], w_in[:, 512 + h * 128:512 + (h + 1) * 128], d_wk[b])
                for tb in range(4):
                    bk = 2 + tb % 2
                    C.mm(C.pb(bk), wgk2[:, h * 128:(h + 1) * 128], gkT[:, tb * 512:(tb + 1) * 512], True, True,
                         r=[d_wgk2, d_gkT], w=[C.psd[bk]])
                    C.act(Lh[:, tb * 512:(tb + 1) * 512], C.pb(bk), AF.Exp, r=[C.psd[bk], d_nb], w=[d_Lh],
                          scale=-1.0, bias=nbgk[:, h:h + 1])
                C.act(Lh[:], Lh[:], AF.Ln, r=[d_Lh], w=[d_Lh], bias=1.0)
                C.V(lambda e: e.tensor_tensor_scan(out=Bp[:], data0=rmask[:], data1=Lh[:], initial=0.0,
                                                   op0=ALU.mult, op1=ALU.add), r=[d_Lh, d_rm], w=[d_Bp])
                C.act(Lh[:], Bp[:], AF.Exp, r=[d_Bp], w=[d_Lh], scale=-1.0 / 16)
                C.act(Bp[:], Bp[:], AF.Exp, r=[d_Bp], w=[d_Bp], scale=1.0 / 16)
                C.G(lambda e, h=h: e.tensor_copy(out=elast[:, h, :],
                                                 in_=Lh[:].rearrange("p (n c) -> p n c", c=64)[:, :, 63]),
                    r=[d_Lh], w=[d_el])
                for tb in range(4):
                    bq = 4 + tb % 2; bkk = 6 + tb % 2
                    for kc in range(8):
                        C.mm(C.pb(bq), wq[b][:, kc, :], uT[:, kc, tb * 512:(tb + 1) * 512], kc == 0, kc == 7,
                             r=d_uTb[tb] + [d_wq[b]], w=[C.psd[bq]])
                    C.V(lambda e, h=h, tb=tb, bq=bq: e.scalar_tensor_tensor(
                        out=qtT[:, h, tb * 512:(tb + 1) * 512], in0=C.pb(bq), scalar=QS,
                        in1=Lh[:, tb * 512:(tb + 1) * 512], op0=ALU.mult, op1=ALU.mult),
                        r=[C.psd[bq], d_Lh], w=[d_qt[tb]])
                    for kc in range(8):
                        C.mm(C.pb(bkk), wk[b][:, kc, :], uT[:, kc, tb * 512:(tb + 1) * 512], kc == 0, kc == 7,
                             r=d_uTb[tb] + [d_wk[b]], w=[C.psd[bkk]])
                    C.V(lambda e, h=h, tb=tb, bkk=bkk: e.tensor_tensor(
                        out=ktT[:, h, tb * 512:(tb + 1) * 512], in0=C.pb(bkk), in1=Bp[:, tb * 512:(tb + 1) * 512],
                        op=ALU.mult), r=[C.psd[bkk], d_Bp], w=[d_kt[tb]])
            wv = [C.sb(PA, [128, 8, 512], BF16) for _ in range(2)]; d_wv = deps(2)
            for i in range(2):
                C.load_w(wv[i][:], w_in[:, 1024 + i * 512:1024 + (i + 1) * 512], d_wv[i])
            i = 0
            for t in range(NT):
                for nb in range(2):
                    bk = (0, 1, 2, 3)[i % 4]; i += 1
                    for kc in range(8):
                        C.mm(C.pb(bk), uT[:, kc, t * 128:(t + 1) * 128], wv[nb][:, kc, :], kc == 0, kc == 7,
                             r=[d_uT[t], d_wv[nb]], w=[C.psd[bk]])
                    C.act(vtok[:, t, nb * 512:(nb + 1) * 512], C.pb(bk), AF.Copy, r=[C.psd[bk]], w=[d_v[t]])
            for i in range(2):
                C.load_w(wv[i][:], w_in[:, 2048 + i * 512:2048 + (i + 1) * 512], d_wv[i])
            go_bc = C.sb(PA, [128, 256], F32); d_gob = Dep()
            C.load_bc(go_bc[:], W["gla_g_o"], d_gob)
            sgt = [C.sb(PA, [128, 1024], F32) for _ in range(2)]; d_sgt = deps(2)
            i = 0
            for t in range(NT):
                b = t % 2
                for nb in range(2):
                    bk = (4, 5, 6, 7)[i % 4]; i += 1
                    for kc in range(8):
                        C.mm(C.pb(bk), uT[:, kc, t * 128:(t + 1) * 128], wv[nb][:, kc, :], kc == 0, kc == 7,
                             r=[d_uT[t], d_wv[nb]], w=[C.psd[bk]])
                    C.act(sgt[b][:, nb * 512:(nb + 1) * 512], C.pb(bk), AF.Silu, r=[C.psd[bk]], w=[d_sgt[b]])
                C.G(lambda e, t=t, b=b: e.tensor_tensor(
                    out=gsg[:, t, :].rearrange("p (a c) -> p a c", a=4), in0=sgt[b][:].rearrange("p (a c) -> p a c", a=4),
                    in1=go_bc[:].unsqueeze(1).to_broadcast([128, 4, 256]), op=ALU.mult),
                    r=[d_sgt[b], d_gob], w=[d_gsg[t]])
            C.release()
        with ExitStack() as PB:
            goT = C.sb(PB, [128, 8, S_], BF16); d_go = deps(NT)
            wo = C.sb(PB, [128, 8, D_], BF16); d_wo = Dep()
            C.load_w(wo[:, 0:4, :], W["gla_w_out"][0:512, :], d_wo)
            C.load_w(wo[:, 4:8, :], W["gla_w_out"][512:1024, :], d_wo)
            mask2 = C.sb(PB, [128, 128], BF16); d_m2 = Dep()
            C.D(lambda e: e.dma_start(out=mask2[:], in_=C.CD["mask2"]), w=[d_m2])
            Sst = C.sb(PB, [128, 4, 256], F32); d_S = Dep()
            tmpS = C.sb(PB, [128, 4, 256], F32); d_tmpS = Dep()
            Sbf = [C.sb(PB, [128, 4, 256], BF16) for _ in range(2)]; d_Sbf = deps(2)
            C.V(lambda e: e.memset(Sst[:], 0.0), w=[d_S])
            C.V(lambda e: e.memset(Sbf[0][:], 0.0), w=[d_Sbf[0]])
            attm = [C.sb(PB, [128, 4, 128], BF16) for _ in range(2)]; d_attm = deps(2)
            ktk = [C.sb(PB, [128, 4, 128], BF16) for _ in range(2)]; d_ktk = deps(2)
            ss = C.sb(PB, [128, NT, 4], F32); rs = C.sb(PB, [128, NT, 4], F32); d_ss = deps(NT); d_rs = deps(NT)
            junk = C.sb(PB, [128, 256], BF16); d_junk = Dep()
            gotok = [C.sb(PB, [128, 1024], BF16) for _ in range(2)]; d_gotok = deps(2)
            for t in range(NT):
                b = t % 2
                tb = t // 4
                tsl = slice(t * 128, (t + 1) * 128)
                pv = C.pbh(0)
                for h in range(4):
                    C.tr(pv[:, h * 128:(h + 1) * 128], ktT[:, h, tsl], C.ident_bf[:], r=[d_kt[tb], C.d_const], w=[C.psd[0]])
                C.V(lambda e, b=b, pv=pv: e.tensor_copy(out=ktk[b][:], in_=pv[:, 0:512].rearrange("p (a c) -> p a c", a=4)),
                    r=[C.psd[0]], w=[d_ktk[b]])
                for h in range(4):
                    C.mm(C.pb(1)[:, h * 128:(h + 1) * 128], ktT[:, h, tsl], qtT[:, h, tsl], True, True,
                         r=[d_kt[tb], d_qt[tb]], w=[C.psd[1]])
                C.V(lambda e, b=b: e.tensor_tensor(out=attm[b][:], in0=C.pb(1).rearrange("p (a c) -> p a c", a=4),
                                                   in1=mask2[:].unsqueeze(1).to_broadcast([128, 4, 128]), op=ALU.mult),
                    r=[C.psd[1], d_m2], w=[d_attm[b]])
                for h in range(4):
                    C.mm(C.pb(2 + h)[:, 0:256], attm[b][:, h, :], vtok[:, t, h * 256:(h + 1) * 256], True, False,
                         r=[d_attm[b], d_v[t]], w=[C.psd[2 + h]])
                for half in range(2):
                    r0 = half * 64
                    n = 2 * t + half
                    for h in range(4):
                        C.mm(C.pb(2 + h)[r0:r0 + 64, 0:256], qtT[:, h, t * 128 + r0:t * 128 + r0 + 64], Sbf[half][:, h, :],
                             False, half == 1, r=[d_qt[tb], d_Sbf[half]], w=[C.psd[2 + h]])
                    for h in range(4):
                        bkS = 6 + h // 2
                        C.mm(C.pb(bkS)[:, (h % 2) * 256:(h % 2 + 1) * 256], ktk[b][r0:r0 + 64, h, :],
                             vtok[r0:r0 + 64, t, h * 256:(h + 1) * 256], True, True,
                             r=[d_ktk[b], d_v[t]], w=[C.psd[bkS]])
                    C.V(lambda e: e.tensor_tensor(out=tmpS[:], in0=C.ps[:, 6 * 512:8 * 512].rearrange("p (a c) -> p a c", a=4),
                                                  in1=Sst[:], op=ALU.add), r=[C.psd[6], C.psd[7], d_S], w=[d_tmpS])
                    C.V(lambda e, n=n: e.tensor_tensor(out=Sst[:], in0=tmpS[:],
                                                       in1=elast[:, :, n:n + 1].to_broadcast([128, 4, 256]), op=ALU.mult),
                        r=[d_tmpS, d_el], w=[d_S])
                    nxt = 1 - half
                    C.act(Sbf[nxt][:], Sst[:], AF.Copy, r=[d_S], w=[d_Sbf[nxt]])
                for h in range(4):
                    C.act(junk[:], C.pb(2 + h)[:, 0:256], AF.Square, r=[C.psd[2 + h]], w=[d_junk, d_ss[t]],
                          accum_out=ss[:, t, h:h + 1])
                rstd_ops(C, rs[:, t, :], ss[:, t, :], 256, [d_ss[t]], [d_rs[t]])
                for h in range(4):
                    C.V(lambda e, t=t, b=b, h=h: e.scalar_tensor_tensor(
                        out=gotok[b][:, h * 256:(h + 1) * 256], in0=C.pb(2 + h)[:, 0:256], scalar=rs[:, t, h:h + 1],
                        in1=gsg[:, t, h * 256:(h + 1) * 256], op0=ALU.mult, op1=ALU.mult),
                        r=[C.psd[2 + h], d_rs[t], d_gsg[t]], w=[d_gotok[b]])
                pv = C.pbh(0)
                for c in range(8):
                    C.tr(pv[:, c * 128:(c + 1) * 128], gotok[b][:, c * 128:(c + 1) * 128], C.ident_bf[:],
                         r=[d_gotok[b], C.d_const], w=[C.psd[0]])
                C.V(lambda e, t=t, pv=pv: e.tensor_copy(out=goT[:, :, t * 128:(t + 1) * 128],
                                                       in_=pv.rearrange("p (c n) -> p c n", c=8)),
                    r=[C.psd[0]], w=[d_go[t]])
            out_proj(C, goT, d_go, 8, None, h_in, d_hin, h_out, d_hout, banks=(1, 2, 3, 4), wo_pre=(wo, d_wo))
            C.release()
        C.release()


LAYER_FNS[1] = layer_gla

def layer_ssd(C, W, h_in, d_hin, h_out, d_hout):
    w_in = W["ssd_w_in"]
    G_ = 8
    with ExitStack() as L:
        goT = C.sb(L, [128, 16, S_], BF16); d_go = deps(NT)
        with ExitStack() as PA:
            uT = C.sb(PA, [128, 8, S_], BF16); d_uT = deps(NT)
            norm_T(C, PA, h_in, W["norm_g"], uT, d_uT, d_hin)
            d_uTb = [[d_uT[4 * tb + i] for i in range(4)] for tb in range(4)]
            U2b = C.sb(PA, [128, 128], BF16); V2b = C.sb(PA, [128, 128], BF16)
            U2f = C.sb(PA, [128, 128], F32); V2f = C.sb(PA, [128, 128], F32)
            onA = C.sb(PA, [128, 128], F32); onB = C.sb(PA, [128, 128], F32); d_k = Dep()
            for dst, nm in ((U2b, "U2b"), (V2b, "mask2"), (U2f, "U2f"), (V2f, "V2f"), (onA, "onesA"), (onB, "onesB")):
                C.D(lambda e, dst=dst, nm=nm: e.dma_start(out=dst[:], in_=C.CD[nm]), w=[d_k])
            dtb = C.sb(PA, [128, 32], F32); alog = C.sb(PA, [128, 32], F32); dsk = C.sb(PA, [128, 32], F32)
            gn = C.sb(PA, [128, 2048], F32); d_p = Dep()
            C.load_bc(dtb[:], W["ssd_dt_bias"], d_p)
            C.load_bc(alog[:], W["ssd_a_log"], d_p)
            C.load_bc(dsk[:], W["ssd_d"], d_p)
            C.load_bc(gn[:], W["ssd_g_norm"], d_p)
            cw = C.sb(PA, [128, 4, 32], F32); cb = C.sb(PA, [128, 32], F32); d_cw = Dep()
            for k in range(4):
                C.load_col(cw[:, k, :], W["ssd_conv_w"][k, :], d_cw, 32)
            C.load_col(cb[:], W["ssd_conv_b"], d_cw, 32)
            wdt = C.sb(PA, [128, 8, 32], BF16); d_wdt = Dep()
            C.load_w(wdt[:], w_in[:, 6144:6176], d_wdt)
            dtt = C.sb(PA, [128, NT, 32], F32); a_tok = C.sb(PA, [128, NT, 32], F32)
            ecs = C.sb(PA, [128, NT, 32], F32); dec = C.sb(PA, [128, NT, 32], F32)
            elast = C.sb(PA, [128, 32, 32], F32); eA = C.sb(PA, [128, 32], F32)
            d_dt = Dep(); d_a = Dep(); d_ecs = Dep(); d_dec = Dep(); d_el = Dep()
            for t in range(NT):
                for kc in range(8):
                    C.mm(C.pb(0)[:, t * 32:(t + 1) * 32], uT[:, kc, t * 128:(t + 1) * 128], wdt[:, kc, :], kc == 0, kc == 7,
                         r=[d_uT[t], d_wdt], w=[C.psd[0]])
            C.V(lambda e: e.tensor_tensor(out=dtt[:], in0=C.pb(0).rearrange("p (a c) -> p a c", a=NT),
                                          in1=dtb[:].unsqueeze(1).to_broadcast([128, NT, 32]), op=ALU.add),
                r=[C.psd[0], d_p], w=[d_dt])
            C.act(dtt[:], dtt[:], AF.Exp, r=[d_dt], w=[d_dt])
            C.act(dtt[:], dtt[:], AF.Ln, r=[d_dt], w=[d_dt], bias=1.0)
            C.act(eA[:], alog[:], AF.Exp, r=[d_p], w=[d_a])
            C.V(lambda e: e.scalar_tensor_tensor(out=a_tok[:], in0=dtt[:], scalar=-1.0,
                                                 in1=eA[:].unsqueeze(1).to_broadcast([128, NT, 32]),
                                                 op0=ALU.mult, op1=ALU.mult), r=[d_dt, d_a], w=[d_a])
            for t in range(NT):
                C.mm(C.pb(1)[:, t * 32:(t + 1) * 32], V2f[:], a_tok[:, t, :], True, True, r=[d_k, d_a], w=[C.psd[1]])
            C.act(ecs[:], C.pb(1).rearrange("p (a c) -> p a c", a=NT), AF.Exp, r=[C.psd[1]], w=[d_ecs])
            for t in range(NT):
                C.mm(C.pb(2)[:, t * 32:(t + 1) * 32], U2f[:], a_tok[:, t, :], True, True, r=[d_k, d_a], w=[C.psd[2]])
            C.act(dec[:], C.pb(2).rearrange("p (a c) -> p a c", a=NT), AF.Exp, r=[C.psd[2]], w=[d_dec])
            for t in range(NT):
                for half in range(2):
                    n = 2 * t + half
                    bk = 3 + n // 16
                    C.mm(C.pb(bk)[:, (n % 16) * 32:(n % 16 + 1) * 32], (onA, onB)[half][:], a_tok[:, t, :], True, True,
                         r=[d_k, d_a], w=[C.psd[bk]])
            C.act(elast[:], C.ps[:, 3 * 512:5 * 512].rearrange("p (a c) -> p a c", a=32), AF.Exp,
                  r=[C.psd[3], C.psd[4]], w=[d_el])
            xTc = [C.sb(PA, [128, S_], BF16) for _ in range(2)]
            BT = C.sb(PA, [128, S_], BF16); CT = C.sb(PA, [128, S_], BF16)
            d_xT = deps(2); d_BT = Dep(); d_CT = Dep()
            x_tok = C.sb(PA, [128, NT, 256], BF16); d_xtok = deps(4)
            B_tok = C.sb(PA, [128, NT, 128], BF16); d_Btok = deps(2)
            sz = C.sb(PA, [128, NT, 256], BF16); d_sz = deps(NT)
            ub = [C.sb(PA, [128, 3 + S_], BF16) for _ in range(2)]; d_ub = deps(2)
            for b in range(2):
                C.V(lambda e, b=b: e.memset(ub[b][:, 0:3], 0.0), w=[d_ub[b]])
            wc = [C.sb(PA, [128, 8, 128], BF16) for _ in range(2)]; d_wc = deps(2)
            dg = [C.sb(PA, [128, 4, 128], BF16) for _ in range(2)]; d_dg = deps(2)
            wz = C.sb(PA, [128, 8, 256], BF16); d_wz = Dep()
            Sst = C.sb(PA, [128, 4, 64], F32); tmpS = C.sb(PA, [128, 4, 64], F32); d_S = Dep(); d_tmpS = Dep()
            Sbf = [C.sb(PA, [128, 256], BF16) for _ in range(2)]; d_Sbf = deps(2)
            cbm = [C.sb(PA, [128, 128], F32) for _ in range(2)]; d_cbm = deps(2)
            aV = [C.sb(PA, [128, 4, 128], BF16) for _ in range(2)]; d_aV = deps(2)
            lm = [C.sb(PA, [128, 4, 128], F32) for _ in range(2)]; d_lm = deps(2)
            Mh = [C.sb(PA, [128, 4, 128], BF16) for _ in range(2)]; d_M = deps(2)
            xdt = [C.sb(PA, [128, 4, 64], BF16) for _ in range(2)]; d_xdt = deps(2)
            xdd = [C.sb(PA, [128, 4, 64], BF16) for _ in range(2)]; d_xdd = deps(2)
            xD = [C.sb(PA, [128, 4, 64], BF16) for _ in range(2)]; d_xD = deps(2)
            t1 = [C.sb(PA, [128, 4, 64], F32) for _ in range(2)]; d_t1 = deps(2)
            yz = [C.sb(PA, [128, 256], F32) for _ in range(2)]; d_yz = deps(2)
            yn = [C.sb(PA, [128, 256], BF16) for _ in range(2)]; d_yn = deps(2)
            ss = C.sb(PA, [128, G_, NT], F32); rs = C.sb(PA, [128, G_, NT], F32)
            junk = C.sb(PA, [128, 256], BF16); d_junk = Dep()
            ci = 0
            for g in range(G_):
                hs = slice(4 * g, 4 * g + 4)
                for cc, dst, d_dst in ((2 * g, xTc[0], d_xT[0]), (2 * g + 1, xTc[1], d_xT[1]),
                                       (16 + g, BT, d_BT), (24 + g, CT, d_CT)):
                    b = ci % 2; ci += 1
                    C.load_w(wc[b][:], w_in[:, 2048 + cc * 128:2048 + (cc + 1) * 128], d_wc[b])
                    for k in range(4):
                        C.V(lambda e, b=b, k=k, cc=cc: e.tensor_scalar(out=dg[b][:, k, :], in0=C.ident_bf[:],
                                                                       scalar1=cw[:, k, cc:cc + 1], scalar2=None, op0=ALU.mult),
                            r=[d_cw, C.d_const], w=[d_dg[b]])
                    for tb in range(4):
                        bk = tb
                        for kc in range(8):
                            C.mm(C.pb(bk), wc[b][:, kc, :], uT[:, kc, tb * 512:(tb + 1) * 512], kc == 0, kc == 7,
                                 r=d_uTb[tb] + [d_wc[b]], w=[C.psd[bk]])
                        C.act(ub[b][:, 3 + tb * 512:3 + (tb + 1) * 512], C.pb(bk), AF.Copy, r=[C.psd[bk]], w=[d_ub[b]])
                    for tb in range(4):
                        bk = 4 + tb
                        for k in range(4):
                            C.mm(C.pb(bk), dg[b][:, k, :], ub[b][:, tb * 512 + k:tb * 512 + k + 512], k == 0, k == 3,
                                 r=[d_ub[b], d_dg[b]], w=[C.psd[bk]])
                        C.act(dst[:, tb * 512:(tb + 1) * 512], C.pb(bk), AF.Silu, r=[C.psd[bk], d_cw], w=[d_dst],
                              bias=cb[:, cc:cc + 1])
                for t4 in range(4):
                    bk = t4 % 2
                    pv = C.pbh(bk)
                    for i in range(4):
                        for c in range(2):
                            C.tr(pv[:, (i * 2 + c) * 128:(i * 2 + c + 1) * 128], xTc[c][:, (t4 * 4 + i) * 128:(t4 * 4 + i + 1) * 128],
                                 C.ident_bf[:], r=[d_xT[c], C.d_const], w=[C.psd[bk]])
                    C.V(lambda e, t4=t4, pv=pv: e.tensor_copy(out=x_tok[:, t4 * 4:(t4 + 1) * 4, :],
                                                             in_=pv.rearrange("p (a c) -> p a c", a=4)),
                        r=[C.psd[bk]], w=[d_xtok[t4]])
                for t8 in range(2):
                    bk = 2 + t8
                    pv = C.pbh(bk)
                    for i in range(8):
                        C.tr(pv[:, i * 128:(i + 1) * 128], BT[:, (t8 * 8 + i) * 128:(t8 * 8 + i + 1) * 128], C.ident_bf[:],
                             r=[d_BT, C.d_const], w=[C.psd[bk]])
                    C.V(lambda e, t8=t8, pv=pv: e.tensor_copy(out=B_tok[:, t8 * 8:(t8 + 1) * 8, :],
                                                             in_=pv.rearrange("p (a c) -> p a c", a=8)),
                        r=[C.psd[bk]], w=[d_Btok[t8]])
                C.load_w(wz[:], w_in[:, g * 256:(g + 1) * 256], d_wz)
                for t2 in range(8):
                    bk = 4 + t2 % 4
                    for i in range(2):
                        t = t2 * 2 + i
                        for kc in range(8):
                            C.mm(C.pb(bk)[:, i * 256:(i + 1) * 256], uT[:, kc, t * 128:(t + 1) * 128], wz[:, kc, :],
                                 kc == 0, kc == 7, r=[d_uT[t], d_wz], w=[C.psd[bk]])
                    C.act(sz[:, t2 * 2:t2 * 2 + 2, :], C.pb(bk).rearrange("p (a c) -> p a c", a=2), AF.Silu,
                          r=[C.psd[bk]], w=[d_sz[t2 * 2], d_sz[t2 * 2 + 1]])
                C.V(lambda e: e.memset(Sst[:], 0.0), w=[d_S])
                C.V(lambda e: e.memset(Sbf[0][:], 0.0), w=[d_Sbf[0]])
                for t in range(NT):
                    b = t % 2
                    tsl = slice(t * 128, (t + 1) * 128)
                    xt_ = x_tok[:, t, :].rearrange("p (a c) -> p a c", a=4)
                    d_x = d_xtok[t // 4]
                    C.G(lambda e, hs=hs, b=b, t=t: e.tensor_tensor(out=aV[b][:], in0=a_tok[:, t, hs].unsqueeze(2).to_broadcast([128, 4, 128]),
                                                           in1=V2b[:].unsqueeze(1).to_broadcast([128, 4, 128]), op=ALU.mult),
                        r=[d_a, d_k], w=[d_aV[b]])
                    C.G(lambda e, hs=hs, b=b, t=t, xt_=xt_: e.tensor_tensor(out=xdt[b][:], in0=xt_,
                                                                   in1=dtt[:, t, hs].unsqueeze(2).to_broadcast([128, 4, 64]), op=ALU.mult),
                        r=[d_x, d_dt], w=[d_xdt[b]])
                    C.G(lambda e, hs=hs, b=b, t=t: e.tensor_tensor(out=xdd[b][:], in0=xdt[b][:],
                                                           in1=dec[:, t, hs].unsqueeze(2).to_broadcast([128, 4, 64]), op=ALU.mult),
                        r=[d_xdt[b], d_dec], w=[d_xdd[b]])
                    C.G(lambda e, hs=hs, b=b, xt_=xt_: e.tensor_tensor(out=xD[b][:], in0=xt_,
                                                              in1=dsk[:, hs].unsqueeze(2).to_broadcast([128, 4, 64]), op=ALU.mult),
                        r=[d_x, d_p], w=[d_xD[b]])
                    C.mm(C.pb(0)[:, 0:128], BT[:, tsl], CT[:, tsl], True, True, r=[d_BT, d_CT], w=[C.psd[0]])
                    C.V(lambda e, b=b: e.tensor_tensor(out=cbm[b][:], in0=C.pb(0)[:, 0:128], in1=V2b[:], op=ALU.mult),
                        r=[C.psd[0], d_k], w=[d_cbm[b]])
                    C.mm(C.pb(1), U2b[:], aV[b][:].rearrange("p a c -> p (a c)"), True, True, r=[d_k, d_aV[b]], w=[C.psd[1]])
                    C.act(lm[b][:], C.pb(1).rearrange("p (a c) -> p a c", a=4), AF.Exp, r=[C.psd[1]], w=[d_lm[b]])
                    C.G(lambda e, b=b: e.tensor_tensor(out=Mh[b][:], in0=lm[b][:],
                                                       in1=cbm[b][:].unsqueeze(1).to_broadcast([128, 4, 128]), op=ALU.mult),
                        r=[d_lm[b], d_cbm[b]], w=[d_M[b]])
                    for half in range(2):
                        r0 = half * 64
                        C.mm(C.pb(4 + half)[:, 0:256], B_tok[r0:r0 + 64, t, :], xdd[b][r0:r0 + 64, :, :].rearrange("p a c -> p (a c)"),
                             True, True, r=[d_Btok[t // 8], d_xdd[b]], w=[C.psd[4 + half]])
                    C.mm(C.pb(2)[:, 0:256], C.ident_bf[:], xD[b][:].rearrange("p a c -> p (a c)"), True, False,
                         r=[C.d_const, d_xD[b]], w=[C.psd[2]])
                    for h in range(4):
                        C.mm(C.pb(2)[:, h * 64:(h + 1) * 64], Mh[b][:, h, :], xdt[b][:, h, :], False, h == 3,
                             r=[d_M[b], d_xdt[b]], w=[C.psd[2]])
                    for half in range(2):
                        r0 = half * 64
                        n = 2 * t + half
                        C.mm(C.pb(3)[r0:r0 + 64, 0:256], CT[:, t * 128 + r0:t * 128 + r0 + 64], Sbf[half][:], True, True,
                             r=[d_CT, d_Sbf[half]], w=[C.psd[3]])
                        C.V(lambda e, hs=hs, n=n: e.tensor_tensor(out=tmpS[:], in0=Sst[:],
                                                           in1=elast[:, n, hs].unsqueeze(2).to_broadcast([128, 4, 64]), op=ALU.mult),
                            r=[d_S, d_el], w=[d_tmpS])
                        C.V(lambda e, half=half: e.tensor_tensor(out=Sst[:], in0=C.pb(4 + half)[:, 0:256].rearrange("p (a c) -> p a c", a=4),
                                                                in1=tmpS[:], op=ALU.add),
                            r=[C.psd[4 + half], d_tmpS], w=[d_S])
                        C.act(Sbf[1 - half][:], Sst[:].rearrange("p a c -> p (a c)"), AF.Copy, r=[d_S], w=[d_Sbf[1 - half]])
                    C.V(lambda e, hs=hs, b=b, t=t: e.tensor_tensor(out=t1[b][:], in0=C.pb(3)[:, 0:256].rearrange("p (a c) -> p a c", a=4),
                                                           in1=ecs[:, t, hs].unsqueeze(2).to_broadcast([128, 4, 64]), op=ALU.mult),
                        r=[C.psd[3], d_ecs], w=[d_t1[b]])
                    C.V(lambda e, b=b: e.tensor_tensor(out=t1[b][:], in0=C.pb(2)[:, 0:256].rearrange("p (a c) -> p a c", a=4),
                                                       in1=t1[b][:], op=ALU.add), r=[C.psd[2], d_t1[b]], w=[d_t1[b]])
                    C.G(lambda e, b=b, t=t: e.tensor_tensor(out=yz[b][:], in0=t1[b][:].rearrange("p a c -> p (a c)"),
                                                           in1=sz[:, t, :], op=ALU.mult), r=[d_t1[b], d_sz[t]], w=[d_yz[b]])
                    C.act(junk[:], yz[b][:], AF.Square, r=[d_yz[b]], w=[d_junk], accum_out=ss[:, g, t:t + 1])
                    rstd_ops(C, rs[:, g, t:t + 1], ss[:, g, t:t + 1], 256, [d_junk], [d_junk])
                    C.V(lambda e, b=b, t=t, g=g: e.scalar_tensor_tensor(out=yn[b][:], in0=yz[b][:], scalar=rs[:, g, t:t + 1],
                                                                       in1=gn[:, g * 256:(g + 1) * 256], op0=ALU.mult, op1=ALU.mult),
                        r=[d_yz[b], d_junk, d_p], w=[d_yn[b]])
                    bk = 6 + (t // 4) % 2
                    pv = C.pbh(bk)
                    i = t % 4
                    for c in range(2):
                        C.tr(pv[:, (c * 4 + i) * 128:(c * 4 + i + 1) * 128], yn[b][:, c * 128:(c + 1) * 128], C.ident_bf[:],
                             r=[d_yn[b], C.d_const], w=[C.psd[bk]])
                    if i == 3:
                        t4 = t // 4
                        C.V(lambda e, t4=t4, pv=pv, g=g: e.tensor_copy(out=goT[:, 2 * g:2 * g + 2, t4 * 512:(t4 + 1) * 512],
                                                                      in_=pv.rearrange("p (c m) -> p c m", c=2)),
                            r=[C.psd[bk]], w=[d_go[4 * t4 + j] for j in range(4)])
            C.dbg("dtt", dtt[:], [128, NT, 32], F32, [d_dt])
            C.dbg("a_tok", a_tok[:], [128, NT, 32], F32, [d_a])
            C.dbg("ecs", ecs[:], [128, NT, 32], F32, [d_ecs])
            C.dbg("dec", dec[:], [128, NT, 32], F32, [d_dec])
            C.dbg("elast", elast[:], [128, 32, 32], F32, [d_el])
            C.dbg("x_tok", x_tok[:], [128, NT, 256], BF16, d_xtok)
            C.dbg("B_tok", B_tok[:], [128, NT, 128], BF16, d_Btok)
            C.dbg("CT", CT[:], [128, S_], BF16, [d_CT])
            C.dbg("sz", sz[:], [128, NT, 256], BF16, d_sz)
            C.dbg("goT", goT[:], [128, 16, S_], BF16, d_go)
            C.dbg("Mh", Mh[1][:], [128, 4, 128], BF16, [d_M[1]])
            C.dbg("cbm", cbm[1][:], [128, 128], F32, [d_cbm[1]])
            C.dbg("lm", lm[1][:], [128, 4, 128], F32, [d_lm[1]])
            C.dbg("yz", yz[1][:], [128, 256], F32, [d_yz[1]])
            C.dbg("Sst", Sst[:], [128, 4, 64], F32, [d_S])
            C.release()
        out_proj(C, goT, d_go, 16, W["ssd_w_out"], h_in, d_hin, h_out, d_hout)
        C.release()


LAYER_FNS[3] = layer_ssd

WSPEC = {
    "norm_g": [4, 1024], "final_g": [1024],
    "mla_w_in": [1024, 1696], "mla_g_q": [384], "mla_w_uq": [384, 1536], "mla_g_kv": [256],
    "mla_w_ukv": [256, 2048], "mla_w_out": [1024, 1024],
    "gla_w_in": [1024, 3088], "gla_w_gk2": [16, 512], "gla_b_gk": [512], "gla_g_o": [256],
    "gla_w_out": [1024, 1024],
    "lru_w_in": [1024, 2560], "lru_conv_w": [4, 1280], "lru_conv_b": [1280], "lru_w_a": [10, 128, 128],
    "lru_b_a": [1280], "lru_w_x": [10, 128, 128], "lru_b_x": [1280], "lru_lam": [1280],
    "lru_w_out": [1280, 1024],
    "ssd_w_in": [1024, 6176], "ssd_conv_w": [4, 4096], "ssd_conv_b": [4096], "ssd_dt_bias": [32],
    "ssd_a_log": [32], "ssd_d": [32], "ssd_g_norm": [2048], "ssd_w_out": [2048, 1024],
}


def host_consts():
    c = {}
    c["ident_bf"] = np.eye(128, dtype=np.float32).astype(ml_dtypes.bfloat16)
    c["ident_f"] = np.eye(128, dtype=np.float32)
    k = np.arange(128)
    c["tri_bf"] = (k[None, :] >= k[:, None]).astype(np.float32).astype(ml_dtypes.bfloat16)
    c["ones_bf"] = np.ones((128, 64), np.float32).astype(ml_dtypes.bfloat16)
    invf = (np.float32(10000.0) ** (-np.arange(0, 32, 2, dtype=np.float32) / np.float32(32))).astype(np.float32)
    rm = np.ones((128, S_), np.float32); rm[:, ::64] = 0.0
    c["rmask"] = rm.astype(ml_dtypes.bfloat16)
    m2 = ((k[None, :] >= k[:, None]) & ((k[None, :] // 64) == (k[:, None] // 64))).astype(np.float32)
    c["mask2"] = m2.astype(ml_dtypes.bfloat16)
    same = (k[None, :] // 64) == (k[:, None] // 64)
    u2 = ((k[:, None] > k[None, :]) & same).astype(np.float32)
    c["U2b"] = u2.astype(ml_dtypes.bfloat16)
    c["U2f"] = u2
    c["V2f"] = m2.astype(np.float32)
    c["onesA"] = np.ascontiguousarray(np.broadcast_to((k[:, None] < 64).astype(np.float32), (128, 128)))
    c["onesB"] = np.ascontiguousarray(np.broadcast_to((k[:, None] >= 64).astype(np.float32), (128, 128)))
    c["invf"] = np.ascontiguousarray(np.broadcast_to(invf[None, :], (128, 16))).astype(np.float32)
    return c


CSPEC = {"ident_bf": ([128, 128], BF16), "ident_f": ([128, 128], F32), "tri_bf": ([128, 128], BF16),
         "ones_bf": ([128, 64], BF16), "invf": ([128, 16], F32), "rmask": ([128, S_], BF16),
         "mask2": ([128, 128], BF16), "U2b": ([128, 128], BF16), "U2f": ([128, 128], F32),
         "V2f": ([128, 128], F32), "onesA": ([128, 128], F32), "onesB": ([128, 128], F32)}

def build(layers=(0, 1, 2, 3), final=True):
    nc = bass.Bass("TRN2", target_bir_lowering=False)
    x = nc.dram_tensor("x", [S_, D_], F32, kind="ExternalInput").ap()
    pos = nc.dram_tensor("pos", [128, NT], I32, kind="ExternalInput").ap()
    W = {k: nc.dram_tensor(k, v, F32, kind="ExternalInput").ap() for k, v in WSPEC.items()}
    CD = {k: nc.dram_tensor(k, v[0], v[1], kind="ExternalInput").ap() for k, v in CSPEC.items()}
    out = nc.dram_tensor("out", [S_, D_], F32, kind="ExternalOutput").ap()
    hbuf = [nc.dram_tensor(f"hbuf{i}", [S_, D_], F32).ap() for i in range(2)]
    _BASE["r"] = {}
    with ExitStack() as st:
        C = Ctx(nc, st)
        C.pos = pos
        C.CD = CD
        C.d_const = Dep()
        C.ident_bf = C.sb(st, [128, 128], BF16)
        C.ident_f = C.sb(st, [128, 128], F32)
        C.D(lambda e: e.dma_start(out=C.ident_bf[:], in_=CD["ident_bf"]), w=[C.d_const])
        C.D(lambda e: e.dma_start(out=C.ident_f[:], in_=CD["ident_f"]), w=[C.d_const])
        h_cur, d_cur = x, deps(NT)
        nxt = 0
        for li in layers:
            Wl = dict(W)
            Wl["norm_g"] = W["norm_g"][li, :]
            h_nxt, d_nxt = hbuf[nxt], deps(NT)
            if not final and li == layers[-1]:
                h_nxt = out
            LAYER_FNS[li](C, Wl, h_cur, d_cur, h_nxt, d_nxt)
            h_cur, d_cur = h_nxt, d_nxt
            nxt ^= 1
        if final:
            final_norm(C, h_cur, d_cur, W["final_g"], out)
        else:
            C.out_toks.append(C.last_store)
            for t in range(NT):
                C.out_toks.append(d_cur[t].w)
        for tok in C.out_toks:
            C.S.wait_tok("sync", tok)
        C.S.emit()
    return nc


LAYER_FNS[2] = layer_lru


def make_in_maps(inputs):
    consts = host_consts()
    maps = []
    for b in range(8):
        m = {"x": np.ascontiguousarray(inputs["x"][b]),
             "pos": np.ascontiguousarray(np.asarray(inputs["positions"][b]).astype(np.int32).reshape(NT, 128).T)}
        for k in WSPEC:
            a = np.asarray(inputs[k], dtype=np.float32)
            if k not in ("norm_g", "final_g"):
                a = a[0]
            m[k] = np.ascontiguousarray(a)
        m.update(consts)
        maps.append(m)
    return maps


_NC_CACHE = {}


def kernel(**inputs):
    if "nc" not in _NC_CACHE:
        _NC_CACHE["nc"] = build()
    nc = _NC_CACHE["nc"]
    maps = make_in_maps(inputs)
    res = run_bass_kernel_spmd(nc, maps, core_ids=list(range(8)))
    return np.stack([np.asarray(r["out"], dtype=np.float32) for r in res.results], axis=0)
```

```python
import numpy as np
import ml_dtypes
import concourse.bass as bass
import concourse.mybir as mybir
from concourse.bass_utils import run_bass_kernel_spmd
from contextlib import ExitStack

F32 = mybir.dt.float32
BF16 = mybir.dt.bfloat16
I32 = mybir.dt.int32
AF = mybir.ActivationFunctionType
ALU = mybir.AluOpType

S_ = 2048
D_ = 1024
NT = 16
EPS = 1e-6
LAYER_FNS = {}
DEBUG = False
import os
SSD_SKEW = int(os.environ.get('SSD_SKEW', '1'))

ENGS = ("tensor", "vector", "scalar", "gpsimd", "sync")
EPOCH = 6000
NDMA = 24


_BASE = {"r": {}}


class Dep:
    __slots__ = ("w", "r")

    def __init__(self):
        self.w = None
        self.r = dict(_BASE["r"])


def deps(n):
    return [Dep() for _ in range(n)]


class Sched:
    def __init__(self, nc, stack):
        self.nc = nc
        self.stack = stack
        self.q = {e: [] for e in ENGS}
        self.cnt = {e: 0 for e in ENGS}
        self.sems = {}
        self.known = {e: {} for e in ENGS}
        self.dma_sems = [stack.enter_context(nc.semaphore(f"dma{i}")) for i in range(NDMA)]
        self.dma_i = 0

    def _esem(self, eng, epoch):
        k = (eng, epoch)
        if k not in self.sems:
            self.sems[k] = self.stack.enter_context(self.nc.semaphore(f"s_{eng}_{epoch}"))
        return self.sems[k]

    def _need_wait(self, weng, tok):
        kn = self.known[weng]
        if tok[0] == "c":
            _, eng, ep, val = tok
            cur = kn.get(("c", eng), (-1, 0))
            if cur >= (ep, val):
                return None
            kn[("c", eng)] = (ep, val)
            return (self._esem(eng, ep), val)
        _, idx, val = tok
        cur = kn.get(("d", idx), 0)
        if cur >= val:
            return None
        kn[("d", idx)] = val
        return (self.dma_sems[idx], val)

    def op(self, eng, fn, reads=(), writes=(), dma=False):
        waits = []
        toks = []
        for d in reads:
            if d.w is not None:
                toks.append((d.w, "raw"))
        for d in writes:
            if d.w is not None:
                toks.append((d.w, "waw"))
            for t in d.r.values():
                toks.append((t, "war"))
        for tok, kind in toks:
            if not dma and tok[0] == "c" and tok[1] == eng:
                if eng == "tensor" or kind != "raw":
                    continue
            w = self._need_wait(eng, tok)
            if w is not None:
                waits.append(w)
        if dma:
            idx = self.dma_i % NDMA
            rnd = self.dma_i // NDMA
            self.dma_i += 1
            if rnd > 0:
                w = self._need_wait(eng, ("d", idx, 16 * rnd))
                if w is not None:
                    waits.append(w)
            mytok = ("d", idx, 16 * (rnd + 1))
            inc = (self.dma_sems[idx], 16)
        else:
            c = self.cnt[eng]
            self.cnt[eng] = c + 1
            ep, val = c // EPOCH, c % EPOCH + 1
            mytok = ("c", eng, ep, val)
            inc = (self._esem(eng, ep), 1)
        self.q[eng].append((fn, waits, inc))
        rk = mytok[:2]
        for d in reads:
            d.r[rk] = mytok
        for d in writes:
            d.w = mytok
            d.r = {}
        return mytok

    def snapshot(self):
        snap = {}
        for e in ENGS:
            c = self.cnt[e]
            if c > 0:
                c -= 1
                snap[("c", e)] = ("c", e, c // EPOCH, c % EPOCH + 1)
        for i in range(min(self.dma_i, NDMA)):
            n_i = (self.dma_i - 1 - i) // NDMA
            snap[("d", i)] = ("d", i, 16 * (n_i + 1))
        return snap

    def wait_tok(self, eng, tok):
        w = self._need_wait(eng, tok)
        if w is not None:
            self.q[eng].append((None, [w], None))

    def emit(self):
        nc = self.nc
        with nc.Block() as block:
            def run(engname):
                def body(e):
                    for fn, waits, inc in self.q[engname]:
                        for sem, val in waits:
                            e.wait_ge(sem, val)
                        if fn is not None:
                            ins = fn(e)
                            ins.then_inc(inc[0], inc[1])
                return body
            block.tensor(run("tensor"))
            block.vector(run("vector"))
            block.scalar(run("scalar"))
            block.gpsimd(run("gpsimd"))
            block.sync(run("sync"))


class Ctx:
    def __init__(self, nc, st):
        self.nc = nc
        self.st = st
        self.S = Sched(nc, st)
        self.ps = st.enter_context(nc.psum_tensor("ps", [128, 8 * 512], F32))
        self.psd = deps(8)
        self.uid = 0
        self.out_toks = []

    def release(self):
        _BASE["r"] = self.S.snapshot()

    def sb(self, stack, shape, dt):
        self.uid += 1
        return stack.enter_context(self.nc.sbuf_tensor(f"t{self.uid}", list(shape), dt))

    def pb(self, i, n=512, off=0):
        return self.ps[:, i * 512 + off:i * 512 + off + n]

    def pbh(self, i):
        return self.ps[:, i * 512:(i + 1) * 512].bitcast(BF16)

    def T(self, fn, r=(), w=()):
        return self.S.op("tensor", fn, r, w)

    def V(self, fn, r=(), w=()):
        return self.S.op("vector", fn, r, w)

    def A(self, fn, r=(), w=()):
        return self.S.op("scalar", fn, r, w)

    def G(self, fn, r=(), w=()):
        return self.S.op("gpsimd", fn, r, w)

    def D(self, fn, r=(), w=()):
        return self.S.op("sync", fn, r, w, dma=True)

    def DG(self, fn, r=(), w=()):
        return self.S.op("gpsimd", fn, r, w, dma=True)

    def mm(self, out, lhsT, rhs, start, stop, r=(), w=()):
        return self.T(lambda e: e.matmul(out, lhsT=lhsT, rhs=rhs, start=start, stop=stop), r, w)

    def tr(self, out, in_, ident, r=(), w=()):
        return self.T(lambda e: e.transpose(out=out, in_=in_, identity=ident), r, w)

    def act(self, out, in_, func, r=(), w=(), **kw):
        return self.A(lambda e: e.activation(out=out, in_=in_, func=func, **kw), r, w)

    def dbg(self, name, ap, shape, dt, r):
        if not DEBUG:
            return
        d = self.nc.dram_tensor("dbg_" + name, list(shape), dt, kind="ExternalOutput").ap()
        tok = self.D(lambda e: e.dma_start(out=d, in_=ap), r=r)
        self.out_toks.append(tok)

    def load_w(self, dst, src2d, dep):
        n = src2d.shape[1]
        tok = None
        for c0 in range(0, n, 512):
            c1 = min(n, c0 + 512)
            assert (c1 - c0) in (16, 32, 64, 128, 256, 512), (c0, c1)
            tok = self.DG(lambda e, c0=c0, c1=c1: e.dma_start(
                out=dst[:, :, c0:c1], in_=src2d[:, c0:c1].rearrange("(c p) n -> p c n", p=128)), w=[dep])
        return tok

    def load_bc(self, dst, src1d, dep, n=128):
        return self.DG(lambda e: e.dma_start(out=dst, in_=src1d.partition_broadcast(n)), w=[dep])

    def load_col(self, dst, src1d, dep, nchunks):
        return self.DG(lambda e: e.dma_start(out=dst, in_=src1d.rearrange("(c p) -> p c", p=128),
                                             allow_slow_non_contiguous=True), w=[dep])


def rstd_ops(C, rs_ap, ss_ap, n, r, w):
    C.act(rs_ap, ss_ap, AF.Ln, r=r, w=w, scale=1.0 / n, bias=EPS)
    C.act(rs_ap, rs_ap, AF.Exp, r=w, w=w, scale=-0.5)


def norm_T(C, stk, h_ap, g_row, uT, d_uT, d_h, banks=(0, 1)):
    with ExitStack() as s:
        gt = C.sb(s, [128, D_], F32); d_gt = Dep()
        C.load_bc(gt[:], g_row, d_gt)
        hb = [C.sb(s, [128, D_], F32) for _ in range(2)]; d_hb = deps(2)
        junk = C.sb(s, [128, D_], BF16); d_junk = Dep()
        ss = C.sb(s, [128, NT], F32); rs = C.sb(s, [128, NT], F32)
        d_ss = deps(NT); d_rs = deps(NT)
        xn = [C.sb(s, [128, D_], BF16) for _ in range(2)]; d_xn = deps(2)
        for t in range(NT):
            b = t % 2
            C.D(lambda e, t=t, b=b: e.dma_start(out=hb[b][:], in_=h_ap[t * 128:(t + 1) * 128, :]),
                r=[d_h[t]], w=[d_hb[b]])
            C.act(junk[:], hb[b][:], AF.Square, r=[d_hb[b]], w=[d_junk, d_ss[t]], accum_out=ss[:, t:t + 1])
            rstd_ops(C, rs[:, t:t + 1], ss[:, t:t + 1], D_, [d_ss[t]], [d_rs[t]])
            C.V(lambda e, t=t, b=b: e.scalar_tensor_tensor(out=xn[b][:], in0=hb[b][:], scalar=rs[:, t:t + 1],
                                                           in1=gt[:], op0=ALU.mult, op1=ALU.mult),
                r=[d_hb[b], d_rs[t], d_gt], w=[d_xn[b]])
            bk = banks[t % len(banks)]
            pv = C.pbh(bk)
            for c in range(8):
                C.tr(pv[:, c * 128:(c + 1) * 128], xn[b][:, c * 128:(c + 1) * 128], C.ident_bf[:],
                     r=[d_xn[b], C.d_const], w=[C.psd[bk]])
            C.V(lambda e, t=t, pv=pv: e.tensor_copy(out=uT[:, :, t * 128:(t + 1) * 128],
                                                   in_=pv.rearrange("p (c n) -> p c n", c=8)),
                r=[C.psd[bk]], w=[d_uT[t]])
        C.release()


def out_proj(C, goT, d_go, nk, w_out, h_in, d_hin, h_out, d_hout, banks=(0, 1, 2, 3), wo_pre=None):
    with ExitStack() as s:
        if wo_pre is None:
            wo = C.sb(s, [128, nk, D_], BF16); d_wo = Dep()
            half = nk // 2
            C.load_w(wo[:, 0:half, :], w_out[0:half * 128, :], d_wo)
            C.load_w(wo[:, half:nk, :], w_out[half * 128:nk * 128, :], d_wo)
        else:
            wo, d_wo = wo_pre
        hb = [C.sb(s, [128, D_], F32) for _ in range(2)]; d_hb = deps(2)
        ho = [C.sb(s, [128, D_], F32) for _ in range(2)]; d_ho = deps(2)
        i = 0
        for t in range(NT):
            b = t % 2
            C.D(lambda e, t=t, b=b: e.dma_start(out=hb[b][:], in_=h_in[t * 128:(t + 1) * 128, :]),
                r=[d_hin[t]], w=[d_hb[b]])
            for nb in range(2):
                bk = banks[i % len(banks)]; i += 1
                for c in range(nk):
                    C.mm(C.pb(bk), goT[:, c, t * 128:(t + 1) * 128], wo[:, c, nb * 512:(nb + 1) * 512],
                         c == 0, c == nk - 1, r=[d_go[t], d_wo], w=[C.psd[bk]])
                C.V(lambda e, b=b, nb=nb, bk=bk: e.tensor_tensor(out=ho[b][:, nb * 512:(nb + 1) * 512], in0=C.pb(bk),
                                                               in1=hb[b][:, nb * 512:(nb + 1) * 512], op=ALU.add),
                    r=[C.psd[bk], d_hb[b]], w=[d_ho[b]])
            tok = C.D(lambda e, t=t, b=b: e.dma_start(out=h_out[t * 128:(t + 1) * 128, :], in_=ho[b][:]),
                      r=[d_ho[b]], w=[d_hout[t]])
            C.last_store = tok
        C.release()


def final_norm(C, h_ap, d_h, g_row, out_ap):
    with ExitStack() as s:
        gt = C.sb(s, [128, D_], F32); d_gt = Dep()
        C.load_bc(gt[:], g_row, d_gt)
        hb = [C.sb(s, [128, D_], F32) for _ in range(2)]; d_hb = deps(2)
        junk = C.sb(s, [128, D_], BF16); d_junk = Dep()
        ss = C.sb(s, [128, NT], F32); rs = C.sb(s, [128, NT], F32)
        d_ss = deps(NT); d_rs = deps(NT)
        ob = [C.sb(s, [128, D_], F32) for _ in range(2)]; d_ob = deps(2)
        for t in range(NT):
            b = t % 2
            C.D(lambda e, t=t, b=b: e.dma_start(out=hb[b][:], in_=h_ap[t * 128:(t + 1) * 128, :]),
                r=[d_h[t]], w=[d_hb[b]])
            C.act(junk[:], hb[b][:], AF.Square, r=[d_hb[b]], w=[d_junk, d_ss[t]], accum_out=ss[:, t:t + 1])
            rstd_ops(C, rs[:, t:t + 1], ss[:, t:t + 1], D_, [d_ss[t]], [d_rs[t]])
            C.V(lambda e, t=t, b=b: e.scalar_tensor_tensor(out=ob[b][:], in0=hb[b][:], scalar=rs[:, t:t + 1],
                                                           in1=gt[:], op0=ALU.mult, op1=ALU.mult),
                r=[d_hb[b], d_rs[t], d_gt], w=[d_ob[b]])
            tok = C.D(lambda e, t=t, b=b: e.dma_start(out=out_ap[t * 128:(t + 1) * 128, :], in_=ob[b][:]),
                      r=[d_ob[b]])
            C.out_toks.append(tok)
        C.release()


def layer_lru(C, W, h_in, d_hin, h_out, d_hout):
    NCH = 10
    with ExitStack() as L:
        uT = C.sb(L, [128, 8, S_], BF16); d_uT = deps(NT)
        norm_T(C, L, h_in, W["norm_g"], uT, d_uT, d_hin)
        d_uTb = [[d_uT[4 * tb + i] for i in range(4)] for tb in range(4)]
        goT = C.sb(L, [128, NCH, S_], BF16); d_go = deps(NT)
        cw = C.sb(L, [128, 4, NCH], F32); d_cw = Dep()
        for k in range(4):
            C.load_col(cw[:, k, :], W["lru_conv_w"][k, :], d_cw, NCH)
        cb = C.sb(L, [128, NCH], F32); ba = C.sb(L, [128, NCH], F32); bx = C.sb(L, [128, NCH], F32)
        lam = C.sb(L, [128, NCH], F32); d_p = Dep()
        C.load_col(cb[:], W["lru_conv_b"], d_p, NCH)
        C.load_col(ba[:], W["lru_b_a"], d_p, NCH)
        C.load_col(bx[:], W["lru_b_x"], d_p, NCH)
        C.load_col(lam[:], W["lru_lam"], d_p, NCH)
        cA = C.sb(L, [128, NCH], F32); cA2 = C.sb(L, [128, NCH], F32); d_cA = Dep()
        C.act(cA[:], lam[:], AF.Exp, r=[d_p], w=[d_cA], scale=-1.0)
        C.act(cA[:], cA[:], AF.Ln, r=[d_cA], w=[d_cA], bias=1.0)
        C.V(lambda e: e.tensor_scalar(out=cA2[:], in0=cA[:], scalar1=-16.0, scalar2=None, op0=ALU.mult),
            r=[d_cA], w=[d_cA])
        C.V(lambda e: e.tensor_scalar(out=cA[:], in0=cA[:], scalar1=-8.0, scalar2=None, op0=ALU.mult),
            r=[d_cA], w=[d_cA])
        wa = C.sb(L, [128, NCH, 128], BF16); wx = C.sb(L, [128, NCH, 128], BF16); d_wg = Dep()
        C.DG(lambda e: e.dma_start(out=wa[:], in_=W["lru_w_a"].rearrange("n i j -> i n j")), w=[d_wg])
        C.DG(lambda e: e.dma_start(out=wx[:], in_=W["lru_w_x"].rearrange("n i j -> i n j")), w=[d_wg])
        dg = C.sb(L, [128, NCH, 4, 128], BF16); d_dg = Dep()
        for c in range(NCH):
            for k in range(4):
                C.V(lambda e, c=c, k=k: e.tensor_scalar(out=dg[:, c, k, :], in0=C.ident_bf[:],
                                                        scalar1=cw[:, k, c:c + 1], scalar2=None, op0=ALU.mult),
                    r=[d_cw, C.d_const], w=[d_dg])
        wu = [C.sb(L, [128, 8, 128], BF16) for _ in range(2)]; d_wu = deps(2)
        wg = [C.sb(L, [128, 8, 128], BF16) for _ in range(2)]; d_wgt = deps(2)
        ub2 = [C.sb(L, [128, 3 + S_], BF16) for _ in range(2)]; d_ub2 = deps(2)
        for b_ in range(2):
            C.V(lambda e, b_=b_: e.memset(ub2[b_][:, 0:3], 0.0), w=[d_ub2[b_]])
        uc = C.sb(L, [128, S_], F32); d_uc = Dep()
        ucb = C.sb(L, [128, S_], BF16); d_ucb = Dep()
        rr = C.sb(L, [128, S_], F32); d_rr = Dep()
        ig = C.sb(L, [128, S_], F32); d_ig = Dep()
        tmp = C.sb(L, [128, S_], F32); d_tmp = Dep()
        hs = C.sb(L, [128, S_], BF16); d_hs = Dep()
        sg = C.sb(L, [128, S_], BF16); d_sg = Dep()
        w_in = W["lru_w_in"]

        def proj_u(c):
            b = c % 2
            C.load_w(wu[b][:], w_in[:, 1280 + c * 128:1280 + (c + 1) * 128], d_wu[b])
            C.load_w(wg[b][:], w_in[:, c * 128:(c + 1) * 128], d_wgt[b])
            for tb in range(4):
                bk = tb
                for kc in range(8):
                    C.mm(C.pb(bk), wu[b][:, kc, :], uT[:, kc, tb * 512:(tb + 1) * 512], kc == 0, kc == 7,
                         r=d_uTb[tb] + [d_wu[b]], w=[C.psd[bk]])
                C.act(ub2[b][:, 3 + tb * 512:3 + (tb + 1) * 512], C.pb(bk), AF.Copy, r=[C.psd[bk]], w=[d_ub2[b]])

        proj_u(0)
        for c in range(NCH):
            b = c % 2
            for tb in range(4):
                bk = 4 + tb
                for k in range(4):
                    C.mm(C.pb(bk), dg[:, c, k, :], ub2[b][:, tb * 512 + k:tb * 512 + k + 512], k == 0, k == 3,
                         r=[d_ub2[b], d_dg], w=[C.psd[bk]])
                C.act(uc[:, tb * 512:(tb + 1) * 512], C.pb(bk), AF.Identity, r=[C.psd[bk], d_p], w=[d_uc],
                      bias=cb[:, c:c + 1])
            C.V(lambda e: e.tensor_copy(out=ucb[:], in_=uc[:]), r=[d_uc], w=[d_ucb])
            for tb in range(4):
                bk = tb
                for kc in range(8):
                    C.mm(C.pb(bk), wg[b][:, kc, :], uT[:, kc, tb * 512:(tb + 1) * 512], kc == 0, kc == 7,
                         r=d_uTb[tb] + [d_wgt[b]], w=[C.psd[bk]])
                C.act(sg[:, tb * 512:(tb + 1) * 512], C.pb(bk), AF.Silu, r=[C.psd[bk]], w=[d_sg])
            for tb in range(4):
                bk = 4 + tb
                C.mm(C.pb(bk), wa[:, c, :], ucb[:, tb * 512:(tb + 1) * 512], True, True,
                     r=[d_ucb, d_wg], w=[C.psd[bk]])
                C.act(rr[:, tb * 512:(tb + 1) * 512], C.pb(bk), AF.Sigmoid, r=[C.psd[bk], d_p], w=[d_rr],
                      bias=ba[:, c:c + 1])
            for tb in range(4):
                bk = 4 + tb
                C.mm(C.pb(bk), wx[:, c, :], ucb[:, tb * 512:(tb + 1) * 512], True, True,
                     r=[d_ucb, d_wg], w=[C.psd[bk]])
                C.act(ig[:, tb * 512:(tb + 1) * 512], C.pb(bk), AF.Sigmoid, r=[C.psd[bk], d_p], w=[d_ig],
                      bias=bx[:, c:c + 1])
            if c + 1 < NCH:
                proj_u(c + 1)
            C.act(tmp[:], rr[:], AF.Exp, r=[d_rr, d_cA], w=[d_tmp], scale=cA2[:, c:c + 1])
            C.act(rr[:], rr[:], AF.Exp, r=[d_rr, d_cA], w=[d_rr], scale=cA[:, c:c + 1])
            C.act(tmp[:], tmp[:], AF.Sqrt, r=[d_tmp], w=[d_tmp], scale=-1.0, bias=1.0)
            C.V(lambda e: e.tensor_tensor(out=ig[:], in0=ig[:], in1=uc[:], op=ALU.mult), r=[d_ig, d_uc], w=[d_ig])
            C.V(lambda e: e.tensor_tensor(out=tmp[:], in0=tmp[:], in1=ig[:], op=ALU.mult), r=[d_tmp, d_ig], w=[d_tmp])
            C.V(lambda e: e.tensor_tensor_scan(out=hs[:], data0=rr[:], data1=tmp[:], initial=0.0,
                                               op0=ALU.mult, op1=ALU.add), r=[d_rr, d_tmp], w=[d_hs])
            C.G(lambda e, c=c: e.tensor_tensor(out=goT[:, c, :], in0=hs[:], in1=sg[:], op=ALU.mult),
                r=[d_hs, d_sg], w=d_go)
        out_proj(C, goT, d_go, NCH, W["lru_w_out"], h_in, d_hin, h_out, d_hout)
        C.release()


def layer_mla(C, W, h_in, d_hin, h_out, d_hout):
    H = 16
    PI = float(np.pi)
    with ExitStack() as L:
        cqnT = C.sb(L, [128, 3, S_], BF16); d_cq = deps(NT)
        ckvT = C.sb(L, [128, 2, S_], BF16); d_ckv = deps(NT)
        sgT = C.sb(L, [128, 8, S_], BF16); d_sg4 = deps(4)
        kr_all = C.sb(L, [128, NT, 32], F32); d_kr = Dep()
        w_in = W["mla_w_in"]
        with ExitStack() as PA:
            uT = C.sb(PA, [128, 8, S_], BF16); d_uT = deps(NT)
            norm_T(C, PA, h_in, W["norm_g"], uT, d_uT, d_hin)
            d_uTb = [[d_uT[4 * tb + i] for i in range(4)] for tb in range(4)]
            w1 = C.sb(PA, [128, 8, 768], BF16); d_w1 = Dep()
            C.load_w(w1[:], w_in[:, 0:768], d_w1)
            gq = C.sb(PA, [128, 384], F32); gkv = C.sb(PA, [128, 256], F32); d_g = Dep()
            C.load_bc(gq[:], W["mla_g_q"], d_g)
            C.load_bc(gkv[:], W["mla_g_kv"], d_g)
            ctok = [C.sb(PA, [128, 672], F32) for _ in range(2)]; d_ct = deps(2)
            ss2 = C.sb(PA, [128, NT, 2], F32); rs2 = C.sb(PA, [128, NT, 2], F32)
            d_ss = deps(NT); d_rs = deps(NT)
            junk = C.sb(PA, [128, 384], BF16); d_junk = Dep()
            cn = [C.sb(PA, [128, 640], BF16) for _ in range(2)]; d_cn = deps(2)
            wgt = [C.sb(PA, [128, 8, 512], BF16) for _ in range(2)]; d_wgt = deps(2)
            for hf in range(2):
                C.load_w(wgt[hf][:], w_in[:, 672 + hf * 512:672 + (hf + 1) * 512], d_wgt[hf])
            for t in range(NT):
                b = t % 2
                for (c0, c1, bk) in ((0, 512, 2), (512, 672, 3)):
                    for kc in range(8):
                        C.mm(C.pb(bk, c1 - c0), uT[:, kc, t * 128:(t + 1) * 128], w1[:, kc, c0:c1], kc == 0, kc == 7,
                             r=[d_uT[t], d_w1], w=[C.psd[bk]])
                    C.act(ctok[b][:, c0:c1], C.pb(bk, c1 - c0), AF.Copy, r=[C.psd[bk]], w=[d_ct[b]])
                C.act(junk[:, 0:384], ctok[b][:, 0:384], AF.Square, r=[d_ct[b]], w=[d_junk, d_ss[t]],
                      accum_out=ss2[:, t, 0:1])
                C.act(junk[:, 0:256], ctok[b][:, 384:640], AF.Square, r=[d_ct[b]], w=[d_junk, d_ss[t]],
                      accum_out=ss2[:, t, 1:2])
                rstd_ops(C, rs2[:, t, 0:1], ss2[:, t, 0:1], 384, [d_ss[t]], [d_rs[t]])
                rstd_ops(C, rs2[:, t, 1:2], ss2[:, t, 1:2], 256, [d_ss[t]], [d_rs[t]])
                C.V(lambda e, t=t, b=b: e.scalar_tensor_tensor(out=cn[b][:, 0:384], in0=ctok[b][:, 0:384],
                                                               scalar=rs2[:, t, 0:1], in1=gq[:], op0=ALU.mult, op1=ALU.mult),
                    r=[d_ct[b], d_rs[t], d_g], w=[d_cn[b]])
                C.V(lambda e, t=t, b=b: e.scalar_tensor_tensor(out=cn[b][:, 384:640], in0=ctok[b][:, 384:640],
                                                               scalar=rs2[:, t, 1:2], in1=gkv[:], op0=ALU.mult, op1=ALU.mult),
                    r=[d_ct[b], d_rs[t], d_g], w=[d_cn[b]])
                C.G(lambda e, t=t, b=b: e.tensor_copy(out=kr_all[:, t, :], in_=ctok[b][:, 640:672]),
                    r=[d_ct[b]], w=[d_kr])
                bk = 4 + b
                pv = C.pbh(bk)
                for c in range(5):
                    C.tr(pv[:, c * 128:(c + 1) * 128], cn[b][:, c * 128:(c + 1) * 128], C.ident_bf[:],
                         r=[d_cn[b], C.d_const], w=[C.psd[bk]])
                C.V(lambda e, t=t, pv=pv: e.tensor_copy(out=cqnT[:, :, t * 128:(t + 1) * 128],
                                                       in_=pv[:, 0:384].rearrange("p (c n) -> p c n", c=3)),
                    r=[C.psd[bk]], w=[d_cq[t]])
                C.V(lambda e, t=t, pv=pv: e.tensor_copy(out=ckvT[:, :, t * 128:(t + 1) * 128],
                                                       in_=pv[:, 384:640].rearrange("p (c n) -> p c n", c=2)),
                    r=[C.psd[bk]], w=[d_ckv[t]])
            C.dbg("ctok", ctok[1][:], [128, 672], F32, [d_ct[1]])
            C.dbg("w1", w1[:], [128, 8, 768], BF16, [d_w1])
            C.dbg("cn", cn[1][:], [128, 640], BF16, [d_cn[1]])
            C.dbg("rs2", rs2[:], [128, NT, 2], F32, d_rs)
            i = 0
            for c in range(8):
                for tb in range(4):
                    bk = (6, 7, 0, 1)[i % 4]; i += 1
                    for kc in range(8):
                        C.mm(C.pb(bk), wgt[c // 4][:, kc, (c % 4) * 128:(c % 4 + 1) * 128],
                             uT[:, kc, tb * 512:(tb + 1) * 512], kc == 0, kc == 7,
                             r=d_uTb[tb] + [d_wgt[c // 4]], w=[C.psd[bk]])
                    C.act(sgT[:, c, tb * 512:(tb + 1) * 512], C.pb(bk), AF.Silu, r=[C.psd[bk]], w=[d_sg4[tb]])
            C.release()
        C.dbg("cqnT", cqnT[:], [128, 3, S_], BF16, d_cq)
        C.dbg("ckvT", ckvT[:], [128, 2, S_], BF16, d_ckv)
        C.dbg("sgT", sgT[:], [128, 8, S_], BF16, d_sg4)
        C.dbg("kr", kr_all[:], [128, NT, 32], F32, [d_kr])
        with ExitStack() as PB:
            wuq = C.sb(PB, [128, 3, 1536], BF16); d_wuq = Dep()
            C.load_w(wuq[:], W["mla_w_uq"], d_wuq)
            wukv = C.sb(PB, [128, 2, 2048], BF16); d_wukv = Dep()
            C.load_w(wukv[:], W["mla_w_ukv"], d_wukv)
            wo = C.sb(PB, [128, 8, D_], BF16); d_wo = Dep()
            C.load_w(wo[:, 0:4, :], W["mla_w_out"][0:512, :], d_wo)
            C.load_w(wo[:, 4:8, :], W["mla_w_out"][512:1024, :], d_wo)
            tri = C.sb(PB, [128, 128], BF16); ones64 = C.sb(PB, [128, 64], BF16)
            invf = C.sb(PB, [128, 16], F32); posi = C.sb(PB, [128, NT], I32); d_k = Dep()
            C.D(lambda e: e.dma_start(out=tri[:], in_=C.CD["tri_bf"]), w=[d_k])
            C.D(lambda e: e.dma_start(out=ones64[:], in_=C.CD["ones_bf"]), w=[d_k])
            C.D(lambda e: e.dma_start(out=invf[:], in_=C.CD["invf"]), w=[d_k])
            C.D(lambda e: e.dma_start(out=posi[:], in_=C.pos), w=[d_k])
            posf = C.sb(PB, [128, NT], F32); d_t = Dep()
            ang = C.sb(PB, [128, NT, 16], F32); tmpf = C.sb(PB, [128, NT, 16], F32)
            ki = C.sb(PB, [128, NT, 16], I32); kff = C.sb(PB, [128, NT, 16], F32)
            rr = C.sb(PB, [128, NT, 16], F32); yy = C.sb(PB, [128, NT, 16], F32); mm_ = C.sb(PB, [128, NT, 16], F32)
            cos_t = C.sb(PB, [128, NT, 16], F32); sin_t = C.sb(PB, [128, NT, 16], F32); d_cs = Dep()
            C.V(lambda e: e.tensor_copy(out=posf[:], in_=posi[:]), r=[d_k], w=[d_t])
            C.V(lambda e: e.tensor_tensor(out=ang[:], in0=posf[:].unsqueeze(2).to_broadcast([128, NT, 16]),
                                          in1=invf[:].unsqueeze(1).to_broadcast([128, NT, 16]), op=ALU.mult),
                r=[d_t, d_k], w=[d_t])
            C.V(lambda e: e.tensor_scalar(out=tmpf[:], in0=ang[:], scalar1=1.0 / (2 * PI), scalar2=None, op0=ALU.mult),
                r=[d_t], w=[d_t])
            C.V(lambda e: e.tensor_copy(out=ki[:], in_=tmpf[:]), r=[d_t], w=[d_t])
            C.V(lambda e: e.tensor_copy(out=kff[:], in_=ki[:]), r=[d_t], w=[d_t])
            C1 = 6.28125
            C2 = float(2 * np.pi - 6.28125)
            C.V(lambda e: e.scalar_tensor_tensor(out=rr[:], in0=kff[:], scalar=-C1, in1=ang[:], op0=ALU.mult, op1=ALU.add),
                r=[d_t], w=[d_t])
            C.V(lambda e: e.scalar_tensor_tensor(out=rr[:], in0=kff[:], scalar=-C2, in1=rr[:], op0=ALU.mult, op1=ALU.add),
                r=[d_t], w=[d_t])
            for dst, shift in ((sin_t, 0.0), (cos_t, PI / 2)):
                C.V(lambda e, shift=shift: e.tensor_scalar(out=yy[:], in0=rr[:], scalar1=shift, scalar2=None, op0=ALU.add),
                    r=[d_t], w=[d_t])
                C.V(lambda e: e.tensor_scalar(out=mm_[:], in0=yy[:], scalar1=PI, scalar2=-2 * PI, op0=ALU.is_gt, op1=ALU.mult),
                    r=[d_t], w=[d_t])
                C.V(lambda e: e.tensor_tensor(out=yy[:], in0=yy[:], in1=mm_[:], op=ALU.add), r=[d_t], w=[d_t])
                C.V(lambda e: e.tensor_scalar(out=mm_[:], in0=yy[:], scalar1=-PI, scalar2=2 * PI, op0=ALU.is_lt, op1=ALU.mult),
                    r=[d_t], w=[d_t])
                C.V(lambda e: e.tensor_tensor(out=yy[:], in0=yy[:], in1=mm_[:], op=ALU.add), r=[d_t], w=[d_t])
                C.act(dst[:], yy[:], AF.Sin, r=[d_t], w=[d_cs, d_t])
            qT = [C.sb(PB, [128, S_], BF16) for _ in range(2)]; d_qT = deps(2)
            kT = [C.sb(PB, [128, S_], BF16) for _ in range(2)]; d_kT = deps(2)
            Vh = [C.sb(PB, [128, NT, 64], BF16) for _ in range(2)]; d_V = deps(2)
            for b in range(2):
                C.G(lambda e, b=b: e.memset(qT[b][:], 0.0), w=[d_qT[b]])
                C.G(lambda e, b=b: e.memset(kT[b][:], 0.0), w=[d_kT[b]])
            qtok = C.sb(PB, [128, NT, 96], F32); d_qtok = Dep()
            ta = C.sb(PB, [128, NT, 16], F32); tb_ = C.sb(PB, [128, NT, 16], F32)
            tc = C.sb(PB, [128, NT, 16], F32); td = C.sb(PB, [128, NT, 16], F32); d_tt = Dep()
            krr = C.sb(PB, [128, NT, 96], F32); d_krr = Dep()
            C.G(lambda e: e.memset(krr[:], 0.0), w=[d_krr])
            k1 = kr_all[:, :, 0:16]; k2 = kr_all[:, :, 16:32]
            C.V(lambda e: e.tensor_tensor(out=ta[:], in0=k1, in1=cos_t[:], op=ALU.mult), r=[d_kr, d_cs], w=[d_tt])
            C.V(lambda e: e.tensor_tensor(out=tb_[:], in0=k2, in1=sin_t[:], op=ALU.mult), r=[d_kr, d_cs], w=[d_tt])
            C.V(lambda e: e.tensor_tensor(out=krr[:, :, 64:80], in0=ta[:], in1=tb_[:], op=ALU.subtract), r=[d_tt], w=[d_krr])
            C.V(lambda e: e.tensor_tensor(out=tc[:], in0=k2, in1=cos_t[:], op=ALU.mult), r=[d_kr, d_cs], w=[d_tt])
            C.V(lambda e: e.tensor_tensor(out=td[:], in0=k1, in1=sin_t[:], op=ALU.mult), r=[d_kr, d_cs], w=[d_tt])
            C.V(lambda e: e.tensor_tensor(out=krr[:, :, 80:96], in0=tc[:], in1=td[:], op=ALU.add), r=[d_tt], w=[d_krr])
            for t4 in range(4):
                bk = 7
                for i in range(4):
                    C.tr(C.pb(bk)[0:96, i * 128:(i + 1) * 128], krr[:, t4 * 4 + i, :], C.ident_f[:],
                         r=[d_krr, C.d_const], w=[C.psd[bk]])
                for b in range(2):
                    C.V(lambda e, b=b, t4=t4, bk=bk: e.tensor_copy(out=kT[b][64:96, t4 * 512:(t4 + 1) * 512],
                                                                 in_=C.pb(bk)[64:96, :]),
                        r=[C.psd[bk]], w=[d_kT[b]])
            lnd = [C.sb(PB, [128, 512], F32) for _ in range(2)]; rden = [C.sb(PB, [128, 512], F32) for _ in range(2)]
            ot = [C.sb(PB, [128, 512], F32) for _ in range(2)]
            d_nrm = deps(2)
            PT = [C.sb(PB, [128, 512], BF16) for _ in range(3)]; d_PT = deps(3)
            QS = 96 ** -0.5

            def proj_units(h, b):
                for t4 in range(4):
                    bk = 7
                    for i in range(4):
                        t = t4 * 4 + i
                        for kc in range(3):
                            C.mm(C.pb(bk)[:, i * 96:(i + 1) * 96], cqnT[:, kc, t * 128:(t + 1) * 128],
                                 wuq[:, kc, h * 96:(h + 1) * 96], kc == 0, kc == 2,
                                 r=[d_cq[t], d_wuq], w=[C.psd[bk]])
                    C.V(lambda e, t4=t4, bk=bk: e.tensor_scalar(
                        out=qtok[:, t4 * 4:(t4 + 1) * 4, :], in0=C.pb(bk)[:, 0:384].rearrange("p (a c) -> p a c", a=4),
                        scalar1=QS, scalar2=None, op0=ALU.mult), r=[C.psd[bk]], w=[d_qtok])
                    yield
                q1 = qtok[:, :, 64:80]; q2 = qtok[:, :, 80:96]
                C.V(lambda e: e.tensor_tensor(out=ta[:], in0=q1, in1=cos_t[:], op=ALU.mult), r=[d_qtok, d_cs], w=[d_tt])
                C.V(lambda e: e.tensor_tensor(out=tb_[:], in0=q2, in1=sin_t[:], op=ALU.mult), r=[d_qtok, d_cs], w=[d_tt])
                C.V(lambda e: e.tensor_tensor(out=tc[:], in0=q2, in1=cos_t[:], op=ALU.mult), r=[d_qtok, d_cs], w=[d_tt])
                C.V(lambda e: e.tensor_tensor(out=td[:], in0=q1, in1=sin_t[:], op=ALU.mult), r=[d_qtok, d_cs], w=[d_tt])
                C.V(lambda e: e.tensor_tensor(out=q1, in0=ta[:], in1=tb_[:], op=ALU.subtract), r=[d_tt], w=[d_qtok])
                C.V(lambda e: e.tensor_tensor(out=q2, in0=tc[:], in1=td[:], op=ALU.add), r=[d_tt], w=[d_qtok])
                yield
                for t4 in range(4):
                    bk = 7
                    for i in range(4):
                        C.tr(C.pb(bk)[0:96, i * 128:(i + 1) * 128], qtok[:, t4 * 4 + i, :], C.ident_f[:],
                             r=[d_qtok, C.d_const], w=[C.psd[bk]])
                    C.V(lambda e, t4=t4, bk=bk: e.tensor_copy(out=qT[b][0:96, t4 * 512:(t4 + 1) * 512],
                                                            in_=C.pb(bk)[0:96, :]), r=[C.psd[bk]], w=[d_qT[b]])
                    yield
                for tb in range(4):
                    bk = 7
                    for kc in range(2):
                        C.mm(C.pb(bk)[0:64, :], wukv[:, kc, h * 128:h * 128 + 64], ckvT[:, kc, tb * 512:(tb + 1) * 512],
                             kc == 0, kc == 1, r=[d_ckv[4 * tb + i] for i in range(4)] + [d_wukv], w=[C.psd[bk]])
                    C.V(lambda e, tb=tb, bk=bk: e.tensor_copy(out=kT[b][0:64, tb * 512:(tb + 1) * 512],
                                                            in_=C.pb(bk)[0:64, :]), r=[C.psd[bk]], w=[d_kT[b]])
                    yield
                for t8 in range(2):
                    bk = 7
                    for i in range(8):
                        t = t8 * 8 + i
                        for kc in range(2):
                            C.mm(C.pb(bk)[:, i * 64:(i + 1) * 64], ckvT[:, kc, t * 128:(t + 1) * 128],
                                 wukv[:, kc, h * 128 + 64:h * 128 + 128], kc == 0, kc == 1,
                                 r=[d_ckv[t], d_wukv], w=[C.psd[bk]])
                    C.V(lambda e, t8=t8, bk=bk: e.tensor_copy(out=Vh[b][:, t8 * 8:(t8 + 1) * 8, :],
                                                            in_=C.pb(bk).rearrange("p (a c) -> p a c", a=8)),
                        r=[C.psd[bk]], w=[d_V[b]])
                    yield

            items = []
            for h in range(H):
                for qc in range(4):
                    nj = 4 * qc + 4
                    for j in range(nj):
                        items.append((h, qc, j, nj))

            def stage_a(i):
                h, qc, j, nj = items[i]
                b = h % 2
                col0 = max(0, j * 128 - qc * 512)
                bS = i % 3
                pt = PT[i % 3]; d_pt = d_PT[i % 3]
                C.mm(C.pb(bS)[:, col0:512], kT[b][:, j * 128:(j + 1) * 128],
                     qT[b][:, qc * 512 + col0:(qc + 1) * 512], True, True,
                     r=[d_kT[b], d_qT[b]], w=[C.psd[bS]])
                C.act(pt[:, col0:512], C.pb(bS)[:, col0:512], AF.Exp, r=[C.psd[bS]], w=[d_pt])
                if j >= 4 * qc:
                    C.G(lambda e, pt=pt, col0=col0: e.tensor_tensor(out=pt[:, col0:col0 + 128],
                                                                  in0=pt[:, col0:col0 + 128], in1=tri[:], op=ALU.mult),
                        r=[d_pt, d_k], w=[d_pt])

            def stage_b(i):
                h, qc, j, nj = items[i]
                b = h % 2
                r0 = (h % 2) * 64
                c = h // 2
                n = h * 4 + qc
                bO = 3 + n % 2; bD = 5 + n % 2
                col0 = max(0, j * 128 - qc * 512)
                pt = PT[i % 3]; d_pt = d_PT[i % 3]
                C.mm(C.pb(bO)[r0:r0 + 64, col0:512], Vh[b][:, j, :], pt[:, col0:512], j == 0, j == nj - 1,
                     r=[d_V[b], d_pt], w=[C.psd[bO]])
                C.mm(C.pb(bD)[r0:r0 + 64, col0:512], ones64[:], pt[:, col0:512], j == 0, j == nj - 1,
                     r=[d_k, d_pt], w=[C.psd[bD]])
                if j == nj - 1:
                    rs = slice(r0, r0 + 64)
                    nb = n % 2
                    C.act(lnd[nb][rs, :], C.pb(bD)[rs, :], AF.Ln, r=[C.psd[bD]], w=[d_nrm[nb]])
                    C.act(rden[nb][rs, :], lnd[nb][rs, :], AF.Exp, r=[d_nrm[nb]], w=[d_nrm[nb]], scale=-1.0)
                    C.V(lambda e, rs=rs, bO=bO, nb=nb: e.tensor_tensor(out=ot[nb][rs, :], in0=C.pb(bO)[rs, :], in1=rden[nb][rs, :], op=ALU.mult),
                        r=[C.psd[bO], d_nrm[nb]], w=[d_nrm[nb]])
                    C.V(lambda e, rs=rs, c=c, qc=qc, nb=nb: e.tensor_tensor(out=sgT[rs, c, qc * 512:(qc + 1) * 512], in0=ot[nb][rs, :],
                                                                           in1=sgT[rs, c, qc * 512:(qc + 1) * 512], op=ALU.mult),
                        r=[d_nrm[nb], d_sg4[qc]], w=[d_sg4[qc]])

            for _ in proj_units(0, 0):
                pass
            NI = len(items)
            units = None
            LOOK = 2
            for i in range(min(LOOK, NI)):
                stage_a(i)
            for i in range(NI):
                h, qc, j, nj = items[i]
                if qc == 0 and j == 0 and h + 1 < H:
                    units = proj_units(h + 1, (h + 1) % 2)
                if i + LOOK < NI:
                    if items[i + LOOK][0] != h and units is not None:
                        for _ in units:
                            pass
                        units = None
                    stage_a(i + LOOK)
                stage_b(i)
                if units is not None:
                    if next(units, "done") == "done":
                        units = None
            d_go = [d_sg4[t // 4] for t in range(NT)]
            out_proj(C, sgT, d_go, 8, None, h_in, d_hin, h_out, d_hout, banks=(0, 1, 2, 7), wo_pre=(wo, d_wo))
            C.release()
        C.release()


LAYER_FNS[0] = layer_mla

def layer_gla(C, W, h_in, d_hin, h_out, d_hout):
    w_in = W["gla_w_in"]
    with ExitStack() as L:
        qtT = C.sb(L, [128, 4, S_], BF16); d_qt = deps(4)
        ktT = C.sb(L, [128, 4, S_], BF16); d_kt = deps(4)
        vtok = C.sb(L, [128, NT, 1024], BF16); d_v = deps(NT)
        gsg = C.sb(L, [128, NT, 1024], BF16); d_gsg = deps(NT)
        elast = C.sb(L, [128, 4, 32], F32); d_el = Dep()
        with ExitStack() as PA:
            uT = C.sb(PA, [128, 8, S_], BF16); d_uT = deps(NT)
            norm_T(C, PA, h_in, W["norm_g"], uT, d_uT, d_hin)
            d_uTb = [[d_uT[4 * tb + i] for i in range(4)] for tb in range(4)]
            wgk = C.sb(PA, [128, 8, 16], BF16); d_wgk = Dep()
            C.load_w(wgk[:], w_in[:, 3072:3088], d_wgk)
            gkT = C.sb(PA, [32, S_], BF16); d_gkT = Dep()
            wgk2 = C.sb(PA, [32, 512], BF16); d_wgk2 = Dep()
            C.V(lambda e: e.memset(gkT[:], 0.0), w=[d_gkT])
            C.V(lambda e: e.memset(wgk2[:], 0.0), w=[d_wgk2])
            C.DG(lambda e: e.dma_start(out=wgk2[0:16, :], in_=W["gla_w_gk2"]), w=[d_wgk2])
            for tb in range(4):
                bk = 2 + tb % 2
                for kc in range(8):
                    C.mm(C.pb(bk)[0:16, :], wgk[:, kc, :], uT[:, kc, tb * 512:(tb + 1) * 512], kc == 0, kc == 7,
                         r=d_uTb[tb] + [d_wgk], w=[C.psd[bk]])
                C.V(lambda e, tb=tb, bk=bk: e.tensor_copy(out=gkT[0:16, tb * 512:(tb + 1) * 512], in_=C.pb(bk)[0:16, :]),
                    r=[C.psd[bk]], w=[d_gkT])
            nbgk = C.sb(PA, [128, 4], F32); d_nb = Dep()
            C.load_col(nbgk[:], W["gla_b_gk"], d_nb, 4)
            C.V(lambda e: e.tensor_scalar(out=nbgk[:], in0=nbgk[:], scalar1=-1.0, scalar2=None, op0=ALU.mult),
                r=[d_nb], w=[d_nb])
            rmask = C.sb(PA, [128, S_], BF16); d_rm = Dep()
            C.D(lambda e: e.dma_start(out=rmask[:], in_=C.CD["rmask"]), w=[d_rm])
            Lh = C.sb(PA, [128, S_], F32); d_Lh = Dep()
            Bp = C.sb(PA, [128, S_], F32); d_Bp = Dep()
            wq = [C.sb(PA, [128, 8, 128], BF16) for _ in range(2)]; d_wq = deps(2)
            wk = [C.sb(PA, [128, 8, 128], BF16) for _ in range(2)]; d_wk = deps(2)
            QS = 128 ** -0.5
            for h in range(4):
                b = h % 2
                C.load_w(wq[b][:], w_in[:, h * 128:(h + 1) * 128], d_wq[b])
                C.load_w(wk[b][:], w_in[:, 512 + h * 128:512 + (h + 1) * 128], d_wk[b])
                for tb in range(4):
                    bk = 2 + tb % 2
                    C.mm(C.pb(bk), wgk2[:, h * 128:(h + 1) * 128], gkT[:, tb * 512:(tb + 1) * 512], True, True,
                         r=[d_wgk2, d_gkT], w=[C.psd[bk]])
                    C.act(Lh[:, tb * 512:(tb + 1) * 512], C.pb(bk), AF.Exp, r=[C.psd[bk], d_nb], w=[d_Lh],
                          scale=-1.0, bias=nbgk[:, h:h + 1])
                C.act(Lh[:], Lh[:], AF.Ln, r=[d_Lh], w=[d_Lh], bias=1.0)
                C.V(lambda e: e.tensor_tensor_scan(out=Bp[:], data0=rmask[:], data1=Lh[:], initial=0.0,
                                                   op0=ALU.mult, op1=ALU.add), r=[d_Lh, d_rm], w=[d_Bp])
                C.act(Lh[:], Bp[:], AF.Exp, r=[d_Bp], w=[d_Lh], scale=-1.0 / 16)
                C.act(Bp[:], Bp[:], AF.Exp, r=[d_Bp], w=[d_Bp], scale=1.0 / 16)
                C.G(lambda e, h=h: e.tensor_copy(out=elast[:, h, :],
                                                 in_=Lh[:].rearrange("p (n c) -> p n c", c=64)[:, :, 63]),
                    r=[d_Lh], w=[d_el])
                for tb in range(4):
                    bq = 4 + tb % 2; bkk = 6 + tb % 2
                    for kc in range(8):
                        C.mm(C.pb(bq), wq[b][:, kc, :], uT[:, kc, tb * 512:(tb + 1) * 512], kc == 0, kc == 7,
                             r=d_uTb[tb] + [d_wq[b]], w=[C.psd[bq]])
                    C.V(lambda e, h=h, tb=tb, bq=bq: e.scalar_tensor_tensor(
                        out=qtT[:, h, tb * 512:(tb + 1) * 512], in0=C.pb(bq), scalar=QS,
                        in1=Lh[:, tb * 512:(tb + 1) * 512], op0=ALU.mult, op1=ALU.mult),
                        r=[C.psd[bq], d_Lh], w=[d_qt[tb]])
                    for kc in range(8):
                        C.mm(C.pb(bkk), wk[b][:, kc, :], uT[:, kc, tb * 512:(tb + 1) * 512], kc == 0, kc == 7,
                             r=d_uTb[tb] + [d_wk[b]], w=[C.psd[bkk]])
                    C.V(lambda e, h=h, tb=tb, bkk=bkk: e.tensor_tensor(
                        out=ktT[:, h, tb * 512:(tb + 1) * 512], in0=C.pb(bkk), in1=Bp[:, tb * 512:(tb + 1) * 512],
                        op=ALU.mult), r=[C.psd[bkk], d_Bp], w=[d_kt[tb]])
            wv = [C.sb(PA, [128, 8, 512], BF16) for _ in range(2)]; d_wv = deps(2)
            for i in range(2):
                C.load_w(wv[i][:], w_in[:, 1024 + i * 512:1024 + (i + 1) * 512], d_wv[i])
            i = 0
            for t in range(NT):
                for nb in range(2):
                    bk = (0, 1, 2, 3)[i % 4]; i += 1
                    for kc in range(8):
                        C.mm(C.pb(bk), uT[:, kc, t * 128:(t + 1) * 128], wv[nb][:, kc, :], kc == 0, kc == 7,
                             r=[d_uT[t], d_wv[nb]], w=[C.psd[bk]])
                    C.act(vtok[:, t, nb * 512:(nb + 1) * 512], C.pb(bk), AF.Copy, r=[C.psd[bk]], w=[d_v[t]])
            for i in range(2):
                C.load_w(wv[i][:], w_in[:, 2048 + i * 512:2048 + (i + 1) * 512], d_wv[i])
            go_bc = C.sb(PA, [128, 256], F32); d_gob = Dep()
            C.load_bc(go_bc[:], W["gla_g_o"], d_gob)
            sgt = [C.sb(PA, [128, 1024], F32) for _ in range(2)]; d_sgt = deps(2)
            i = 0
            for t in range(NT):
                b = t % 2
                for nb in range(2):
                    bk = (4, 5, 6, 7)[i % 4]; i += 1
                    for kc in range(8):
                        C.mm(C.pb(bk), uT[:, kc, t * 128:(t + 1) * 128], wv[nb][:, kc, :], kc == 0, kc == 7,
                             r=[d_uT[t], d_wv[nb]], w=[C.psd[bk]])
                    C.act(sgt[b][:, nb * 512:(nb + 1) * 512], C.pb(bk), AF.Silu, r=[C.psd[bk]], w=[d_sgt[b]])
                C.G(lambda e, t=t, b=b: e.tensor_tensor(
                    out=gsg[:, t, :].rearrange("p (a c) -> p a c", a=4), in0=sgt[b][:].rearrange("p (a c) -> p a c", a=4),
                    in1=go_bc[:].unsqueeze(1).to_broadcast([128, 4, 256]), op=ALU.mult),
                    r=[d_sgt[b], d_gob], w=[d_gsg[t]])
            C.release()
        with ExitStack() as PB:
            goT = C.sb(PB, [128, 8, S_], BF16); d_go = deps(NT)
            wo = C.sb(PB, [128, 8, D_], BF16); d_wo = Dep()
            C.load_w(wo[:, 0:4, :], W["gla_w_out"][0:512, :], d_wo)
            C.load_w(wo[:, 4:8, :], W["gla_w_out"][512:1024, :], d_wo)
            mask2 = C.sb(PB, [128, 128], BF16); d_m2 = Dep()
            C.D(lambda e: e.dma_start(out=mask2[:], in_=C.CD["mask2"]), w=[d_m2])
            Sst = C.sb(PB, [128, 4, 256], F32); d_S = Dep()
            tmpS = C.sb(PB, [128, 4, 256], F32); d_tmpS = Dep()
            Sbf = [C.sb(PB, [128, 4, 256], BF16) for _ in range(2)]; d_Sbf = deps(2)
            C.V(lambda e: e.memset(Sst[:], 0.0), w=[d_S])
            C.V(lambda e: e.memset(Sbf[0][:], 0.0), w=[d_Sbf[0]])
            attm = [C.sb(PB, [128, 4, 128], BF16) for _ in range(2)]; d_attm = deps(2)
            ktk = [C.sb(PB, [128, 4, 128], BF16) for _ in range(2)]; d_ktk = deps(2)
            ss = C.sb(PB, [128, NT, 4], F32); rs = C.sb(PB, [128, NT, 4], F32); d_ss = deps(NT); d_rs = deps(NT)
            junk = C.sb(PB, [128, 256], BF16); d_junk = Dep()
            gotok = [C.sb(PB, [128, 1024], BF16) for _ in range(2)]; d_gotok = deps(2)
            for t in range(NT):
                b = t % 2
                tb = t // 4
                tsl = slice(t * 128, (t + 1) * 128)
                pv = C.pbh(0)
                for h in range(4):
                    C.tr(pv[:, h * 128:(h + 1) * 128], ktT[:, h, tsl], C.ident_bf[:], r=[d_kt[tb], C.d_const], w=[C.psd[0]])
                C.V(lambda e, b=b, pv=pv: e.tensor_copy(out=ktk[b][:], in_=pv[:, 0:512].rearrange("p (a c) -> p a c", a=4)),
                    r=[C.psd[0]], w=[d_ktk[b]])
                for h in range(4):
                    C.mm(C.pb(1)[:, h * 128:(h + 1) * 128], ktT[:, h, tsl], qtT[:, h, tsl], True, True,
                         r=[d_kt[tb], d_qt[tb]], w=[C.psd[1]])
                C.V(lambda e, b=b: e.tensor_tensor(out=attm[b][:], in0=C.pb(1).rearrange("p (a c) -> p a c", a=4),
                                                   in1=mask2[:].unsqueeze(1).to_broadcast([128, 4, 128]), op=ALU.mult),
                    r=[C.psd[1], d_m2], w=[d_attm[b]])
                for h in range(4):
                    C.mm(C.pb(2 + h)[:, 0:256], attm[b][:, h, :], vtok[:, t, h * 256:(h + 1) * 256], True, False,
                         r=[d_attm[b], d_v[t]], w=[C.psd[2 + h]])
                for half in range(2):
                    r0 = half * 64
                    n = 2 * t + half
                    for h in range(4):
                        C.mm(C.pb(2 + h)[r0:r0 + 64, 0:256], qtT[:, h, t * 128 + r0:t * 128 + r0 + 64], Sbf[half][:, h, :],
                             False, half == 1, r=[d_qt[tb], d_Sbf[half]], w=[C.psd[2 + h]])
                    for h in range(4):
                        bkS = 6 + h // 2
                        C.mm(C.pb(bkS)[:, (h % 2) * 256:(h % 2 + 1) * 256], ktk[b][r0:r0 + 64, h, :],
                             vtok[r0:r0 + 64, t, h * 256:(h + 1) * 256], True, True,
                             r=[d_ktk[b], d_v[t]], w=[C.psd[bkS]])
                    C.V(lambda e: e.tensor_tensor(out=tmpS[:], in0=C.ps[:, 6 * 512:8 * 512].rearrange("p (a c) -> p a c", a=4),
                                                  in1=Sst[:], op=ALU.add), r=[C.psd[6], C.psd[7], d_S], w=[d_tmpS])
                    C.V(lambda e, n=n: e.tensor_tensor(out=Sst[:], in0=tmpS[:],
                                                       in1=elast[:, :, n:n + 1].to_broadcast([128, 4, 256]), op=ALU.mult),
                        r=[d_tmpS, d_el], w=[d_S])
                    nxt = 1 - half
                    C.act(Sbf[nxt][:], Sst[:], AF.Copy, r=[d_S], w=[d_Sbf[nxt]])
                for h in range(4):
                    C.act(junk[:], C.pb(2 + h)[:, 0:256], AF.Square, r=[C.psd[2 + h]], w=[d_junk, d_ss[t]],
                          accum_out=ss[:, t, h:h + 1])
                rstd_ops(C, rs[:, t, :], ss[:, t, :], 256, [d_ss[t]], [d_rs[t]])
                for h in range(4):
                    C.V(lambda e, t=t, b=b, h=h: e.scalar_tensor_tensor(
                        out=gotok[b][:, h * 256:(h + 1) * 256], in0=C.pb(2 + h)[:, 0:256], scalar=rs[:, t, h:h + 1],
                        in1=gsg[:, t, h * 256:(h + 1) * 256], op0=ALU.mult, op1=ALU.mult),
                        r=[C.psd[2 + h], d_rs[t], d_gsg[t]], w=[d_gotok[b]])
                pv = C.pbh(0)
                for c in range(8):
                    C.tr(pv[:, c * 128:(c + 1) * 128], gotok[b][:, c * 128:(c + 1) * 128], C.ident_bf[:],
                         r=[d_gotok[b], C.d_const], w=[C.psd[0]])
                C.V(lambda e, t=t, pv=pv: e.tensor_copy(out=goT[:, :, t * 128:(t + 1) * 128],
                                                       in_=pv.rearrange("p (c n) -> p c n", c=8)),
                    r=[C.psd[0]], w=[d_go[t]])
            out_proj(C, goT, d_go, 8, None, h_in, d_hin, h_out, d_hout, banks=(1, 2, 3, 4), wo_pre=(wo, d_wo))
            C.release()
        C.release()


LAYER_FNS[1] = layer_gla

def layer_ssd(C, W, h_in, d_hin, h_out, d_hout):
    w_in = W["ssd_w_in"]
    G_ = 8
    with ExitStack() as L:
        goT = C.sb(L, [128, 16, S_], BF16); d_go = deps(NT)
        with ExitStack() as PA:
            uT = C.sb(PA, [128, 8, S_], BF16); d_uT = deps(NT)
            norm_T(C, PA, h_in, W["norm_g"], uT, d_uT, d_hin)
            d_uTb = [[d_uT[4 * tb + i] for i in range(4)] for tb in range(4)]
            U2b = C.sb(PA, [128, 128], BF16); V2b = C.sb(PA, [128, 128], BF16)
            U2f = C.sb(PA, [128, 128], F32); V2f = C.sb(PA, [128, 128], F32)
            onA = C.sb(PA, [128, 128], F32); onB = C.sb(PA, [128, 128], F32); d_k = Dep()
            for dst, nm in ((U2b, "U2b"), (V2b, "mask2"), (U2f, "U2f"), (V2f, "V2f"), (onA, "onesA"), (onB, "onesB")):
                C.D(lambda e, dst=dst, nm=nm: e.dma_start(out=dst[:], in_=C.CD[nm]), w=[d_k])
            dtb = C.sb(PA, [128, 32], F32); alog = C.sb(PA, [128, 32], F32); dsk = C.sb(PA, [128, 32], F32)
            gn = C.sb(PA, [128, 2048], F32); d_p = Dep()
            C.load_bc(dtb[:], W["ssd_dt_bias"], d_p)
            C.load_bc(alog[:], W["ssd_a_log"], d_p)
            C.load_bc(dsk[:], W["ssd_d"], d_p)
            C.load_bc(gn[:], W["ssd_g_norm"], d_p)
            cw = C.sb(PA, [128, 4, 32], F32); cb = C.sb(PA, [128, 32], F32); d_cw = Dep()
            for k in range(4):
                C.load_col(cw[:, k, :], W["ssd_conv_w"][k, :], d_cw, 32)
            C.load_col(cb[:], W["ssd_conv_b"], d_cw, 32)
            wdt = C.sb(PA, [128, 8, 32], BF16); d_wdt = Dep()
            C.load_w(wdt[:], w_in[:, 6144:6176], d_wdt)
            dtt = C.sb(PA, [128, NT, 32], F32); a_tok = C.sb(PA, [128, NT, 32], F32)
            ecs = C.sb(PA, [128, NT, 32], F32); dec = C.sb(PA, [128, NT, 32], F32)
            elast = C.sb(PA, [128, 32, 32], F32); eA = C.sb(PA, [128, 32], F32)
            d_dt = Dep(); d_a = Dep(); d_ecs = Dep(); d_dec = Dep(); d_el = Dep()
            for t in range(NT):
                for kc in range(8):
                    C.mm(C.pb(0)[:, t * 32:(t + 1) * 32], uT[:, kc, t * 128:(t + 1) * 128], wdt[:, kc, :], kc == 0, kc == 7,
                         r=[d_uT[t], d_wdt], w=[C.psd[0]])
            C.V(lambda e: e.tensor_tensor(out=dtt[:], in0=C.pb(0).rearrange("p (a c) -> p a c", a=NT),
                                          in1=dtb[:].unsqueeze(1).to_broadcast([128, NT, 32]), op=ALU.add),
                r=[C.psd[0], d_p], w=[d_dt])
            C.act(dtt[:], dtt[:], AF.Exp, r=[d_dt], w=[d_dt])
            C.act(dtt[:], dtt[:], AF.Ln, r=[d_dt], w=[d_dt], bias=1.0)
            C.act(eA[:], alog[:], AF.Exp, r=[d_p], w=[d_a])
            C.V(lambda e: e.scalar_tensor_tensor(out=a_tok[:], in0=dtt[:], scalar=-1.0,
                                                 in1=eA[:].unsqueeze(1).to_broadcast([128, NT, 32]),
                                                 op0=ALU.mult, op1=ALU.mult), r=[d_dt, d_a], w=[d_a])
            for t in range(NT):
                C.mm(C.pb(1)[:, t * 32:(t + 1) * 32], V2f[:], a_tok[:, t, :], True, True, r=[d_k, d_a], w=[C.psd[1]])
            C.act(ecs[:], C.pb(1).rearrange("p (a c) -> p a c", a=NT), AF.Exp, r=[C.psd[1]], w=[d_ecs])
            for t in range(NT):
                C.mm(C.pb(2)[:, t * 32:(t + 1) * 32], U2f[:], a_tok[:, t, :], True, True, r=[d_k, d_a], w=[C.psd[2]])
            C.act(dec[:], C.pb(2).rearrange("p (a c) -> p a c", a=NT), AF.Exp, r=[C.psd[2]], w=[d_dec])
            for t in range(NT):
                for half in range(2):
                    n = 2 * t + half
                    bk = 3 + n // 16
                    C.mm(C.pb(bk)[:, (n % 16) * 32:(n % 16 + 1) * 32], (onA, onB)[half][:], a_tok[:, t, :], True, True,
                         r=[d_k, d_a], w=[C.psd[bk]])
            C.act(elast[:], C.ps[:, 3 * 512:5 * 512].rearrange("p (a c) -> p a c", a=32), AF.Exp,
                  r=[C.psd[3], C.psd[4]], w=[d_el])
            xTc = [C.sb(PA, [128, S_], BF16) for _ in range(2)]
            BT = C.sb(PA, [128, S_], BF16); CT = C.sb(PA, [128, S_], BF16)
            d_xT = deps(2); d_BT = Dep(); d_CT = Dep()
            x_tok = C.sb(PA, [128, NT, 256], BF16); d_xtok = deps(4)
            B_tok = C.sb(PA, [128, NT, 128], BF16); d_Btok = deps(2)
            sz = C.sb(PA, [128, NT, 256], BF16); d_sz = deps(NT)
            ub = [C.sb(PA, [128, 3 + S_], BF16) for _ in range(2)]; d_ub = deps(2)
            for b in range(2):
                C.V(lambda e, b=b: e.memset(ub[b][:, 0:3], 0.0), w=[d_ub[b]])
            wc = [C.sb(PA, [128, 8, 128], BF16) for _ in range(2)]; d_wc = deps(2)
            dg = [C.sb(PA, [128, 4, 128], BF16) for _ in range(2)]; d_dg = deps(2)
            wz = C.sb(PA, [128, 8, 256], BF16); d_wz = Dep()
            Sst = C.sb(PA, [128, 4, 64], F32); tmpS = C.sb(PA, [128, 4, 64], F32); d_S = Dep(); d_tmpS = Dep()
            Sbf = [C.sb(PA, [128, 256], BF16) for _ in range(2)]; d_Sbf = deps(2)
            cbm = [C.sb(PA, [128, 128], F32) for _ in range(2)]; d_cbm = deps(2)
            aV = [C.sb(PA, [128, 4, 128], BF16) for _ in range(2)]; d_aV = deps(2)
            lm = [C.sb(PA, [128, 4, 128], F32) for _ in range(2)]; d_lm = deps(2)
            Mh = [C.sb(PA, [128, 4, 128], BF16) for _ in range(2)]; d_M = deps(2)
            xdt = [C.sb(PA, [128, 4, 64], BF16) for _ in range(2)]; d_xdt = deps(2)
            xdd = [C.sb(PA, [128, 4, 64], BF16) for _ in range(2)]; d_xdd = deps(2)
            xD = [C.sb(PA, [128, 4, 64], BF16) for _ in range(2)]; d_xD = deps(2)
            t1 = [C.sb(PA, [128, 4, 64], F32) for _ in range(2)]; d_t1 = deps(2)
            yz = [C.sb(PA, [128, 256], F32) for _ in range(2)]; d_yz = deps(2)
            yn = [C.sb(PA, [128, 256], BF16) for _ in range(2)]; d_yn = deps(2)
            ss = C.sb(PA, [128, G_, NT], F32); rs = C.sb(PA, [128, G_, NT], F32)
            junk = C.sb(PA, [128, 256], BF16); d_junk = Dep()
            ci = 0
            d_dS = [deps(2), deps(2)]
            for g in range(G_):
                hs = slice(4 * g, 4 * g + 4)
                chunks = ((2 * g, xTc[0], d_xT[0]), (2 * g + 1, xTc[1], d_xT[1]), (16 + g, BT, d_BT), (24 + g, CT, d_CT))

                def proj_c(k_, b):
                    cc = chunks[k_][0]
                    C.load_w(wc[b][:], w_in[:, 2048 + cc * 128:2048 + (cc + 1) * 128], d_wc[b])
                    for k in range(4):
                        C.V(lambda e, b=b, k=k, cc=cc: e.tensor_scalar(out=dg[b][:, k, :], in0=C.ident_bf[:],
                                                                       scalar1=cw[:, k, cc:cc + 1], scalar2=None, op0=ALU.mult),
                            r=[d_cw, C.d_const], w=[d_dg[b]])
                    for tb in range(4):
                        bk = tb
                        for kc in range(8):
                            C.mm(C.pb(bk), wc[b][:, kc, :], uT[:, kc, tb * 512:(tb + 1) * 512], kc == 0, kc == 7,
                                 r=d_uTb[tb] + [d_wc[b]], w=[C.psd[bk]])
                        C.act(ub[b][:, 3 + tb * 512:3 + (tb + 1) * 512], C.pb(bk), AF.Copy, r=[C.psd[bk]], w=[d_ub[b]])

                def conv_c(k_, b):
                    cc, dst, d_dst = chunks[k_]
                    for tb in range(4):
                        bk = 4 + tb
                        for k in range(4):
                            C.mm(C.pb(bk), dg[b][:, k, :], ub[b][:, tb * 512 + k:tb * 512 + k + 512], k == 0, k == 3,
                                 r=[d_ub[b], d_dg[b]], w=[C.psd[bk]])
                        C.act(dst[:, tb * 512:(tb + 1) * 512], C.pb(bk), AF.Silu, r=[C.psd[bk], d_cw], w=[d_dst],
                              bias=cb[:, cc:cc + 1])

                proj_c(0, 0)
                for k_ in range(4):
                    if k_ + 1 < 4:
                        proj_c(k_ + 1, (k_ + 1) % 2)
                    conv_c(k_, k_ % 2)
                for t4 in range(4):
                    bk = t4 % 2
                    pv = C.pbh(bk)
                    for i in range(4):
                        for c in range(2):
                            C.tr(pv[:, (i * 2 + c) * 128:(i * 2 + c + 1) * 128], xTc[c][:, (t4 * 4 + i) * 128:(t4 * 4 + i + 1) * 128],
                                 C.ident_bf[:], r=[d_xT[c], C.d_const], w=[C.psd[bk]])
                    C.V(lambda e, t4=t4, pv=pv: e.tensor_copy(out=x_tok[:, t4 * 4:(t4 + 1) * 4, :],
                                                             in_=pv.rearrange("p (a c) -> p a c", a=4)),
                        r=[C.psd[bk]], w=[d_xtok[t4]])
                for t8 in range(2):
                    bk = 2 + t8
                    pv = C.pbh(bk)
                    for i in range(8):
                        C.tr(pv[:, i * 128:(i + 1) * 128], BT[:, (t8 * 8 + i) * 128:(t8 * 8 + i + 1) * 128], C.ident_bf[:],
                             r=[d_BT, C.d_const], w=[C.psd[bk]])
                    C.V(lambda e, t8=t8, pv=pv: e.tensor_copy(out=B_tok[:, t8 * 8:(t8 + 1) * 8, :],
                                                             in_=pv.rearrange("p (a c) -> p a c", a=8)),
                        r=[C.psd[bk]], w=[d_Btok[t8]])
                C.load_w(wz[:], w_in[:, g * 256:(g + 1) * 256], d_wz)
                for t2 in range(8):
                    bk = 4 + t2 % 4
                    for i in range(2):
                        t = t2 * 2 + i
                        for kc in range(8):
                            C.mm(C.pb(bk)[:, i * 256:(i + 1) * 256], uT[:, kc, t * 128:(t + 1) * 128], wz[:, kc, :],
                                 kc == 0, kc == 7, r=[d_uT[t], d_wz], w=[C.psd[bk]])
                    C.act(sz[:, t2 * 2:t2 * 2 + 2, :], C.pb(bk).rearrange("p (a c) -> p a c", a=2), AF.Silu,
                          r=[C.psd[bk]], w=[d_sz[t2 * 2], d_sz[t2 * 2 + 1]])
                C.V(lambda e: e.memset(Sst[:], 0.0), w=[d_S])
                C.V(lambda e: e.memset(Sbf[0][:], 0.0), w=[d_Sbf[0]])
                def st0(t):
                    b = t % 2
                    tsl = slice(t * 128, (t + 1) * 128)
                    xt_ = x_tok[:, t, :].rearrange("p (a c) -> p a c", a=4)
                    d_x = d_xtok[t // 4]
                    C.G(lambda e, hs=hs, b=b, t=t: e.tensor_tensor(out=aV[b][:], in0=a_tok[:, t, hs].unsqueeze(2).to_broadcast([128, 4, 128]),
                                                           in1=V2b[:].unsqueeze(1).to_broadcast([128, 4, 128]), op=ALU.mult),
                        r=[d_a, d_k], w=[d_aV[b]])
                    C.G(lambda e, hs=hs, b=b, t=t, xt_=xt_: e.tensor_tensor(out=xdt[b][:], in0=xt_,
                                                                   in1=dtt[:, t, hs].unsqueeze(2).to_broadcast([128, 4, 64]), op=ALU.mult),
                        r=[d_x, d_dt], w=[d_xdt[b]])
                    C.G(lambda e, hs=hs, b=b, t=t: e.tensor_tensor(out=xdd[b][:], in0=xdt[b][:],
                                                           in1=dec[:, t, hs].unsqueeze(2).to_broadcast([128, 4, 64]), op=ALU.mult),
                        r=[d_xdt[b], d_dec], w=[d_xdd[b]])
                    C.G(lambda e, hs=hs, b=b, xt_=xt_: e.tensor_tensor(out=xD[b][:], in0=xt_,
                                                              in1=dsk[:, hs].unsqueeze(2).to_broadcast([128, 4, 64]), op=ALU.mult),
                        r=[d_x, d_p], w=[d_xD[b]])
                    C.mm(C.pb(0)[:, 0:128], BT[:, tsl], CT[:, tsl], True, True, r=[d_BT, d_CT], w=[C.psd[0]])
                    C.V(lambda e, b=b: e.tensor_tensor(out=cbm[b][:], in0=C.pb(0)[:, 0:128], in1=V2b[:], op=ALU.mult),
                        r=[C.psd[0], d_k], w=[d_cbm[b]])
                    C.mm(C.pb(1), U2b[:], aV[b][:].rearrange("p a c -> p (a c)"), True, True, r=[d_k, d_aV[b]], w=[C.psd[1]])
                    C.act(lm[b][:], C.pb(1).rearrange("p (a c) -> p a c", a=4), AF.Exp, r=[C.psd[1]], w=[d_lm[b]])
                    C.G(lambda e, b=b: e.tensor_tensor(out=Mh[b][:], in0=lm[b][:],
                                                       in1=cbm[b][:].unsqueeze(1).to_broadcast([128, 4, 128]), op=ALU.mult),
                        r=[d_lm[b], d_cbm[b]], w=[d_M[b]])
                    for half in range(2):
                        r0 = half * 64
                        C.mm(C.pb(4 + half)[:, b * 256:(b + 1) * 256], B_tok[r0:r0 + 64, t, :],
                             xdd[b][r0:r0 + 64, :, :].rearrange("p a c -> p (a c)"),
                             True, True, r=[d_Btok[t // 8], d_xdd[b]], w=[C.psd[4 + half]])

                def st1(t):
                    b = t % 2
                    bY = 2 + b
                    C.mm(C.pb(bY)[:, 0:256], C.ident_bf[:], xD[b][:].rearrange("p a c -> p (a c)"), True, False,
                         r=[C.d_const, d_xD[b]], w=[C.psd[bY]])
                    for h in range(4):
                        C.mm(C.pb(bY)[:, h * 64:(h + 1) * 64], Mh[b][:, h, :], xdt[b][:, h, :], False, h == 3,
                             r=[d_M[b], d_xdt[b]], w=[C.psd[bY]])
                    for half in range(2):
                        r0 = half * 64
                        n = 2 * t + half
                        C.mm(C.pb(bY)[r0:r0 + 64, 256:512], CT[:, t * 128 + r0:t * 128 + r0 + 64], Sbf[half][:], True, True,
                             r=[d_CT, d_Sbf[half]], w=[C.psd[bY]])
                        C.V(lambda e, hs=hs, n=n: e.tensor_tensor(out=tmpS[:], in0=Sst[:],
                                                           in1=elast[:, n, hs].unsqueeze(2).to_broadcast([128, 4, 64]), op=ALU.mult),
                            r=[d_S, d_el], w=[d_tmpS])
                        C.V(lambda e, half=half, b=b: e.tensor_tensor(
                            out=Sst[:], in0=C.pb(4 + half)[:, b * 256:(b + 1) * 256].rearrange("p (a c) -> p a c", a=4),
                            in1=tmpS[:], op=ALU.add), r=[C.psd[4 + half], d_tmpS], w=[d_S])
                        C.act(Sbf[1 - half][:], Sst[:].rearrange("p a c -> p (a c)"), AF.Copy, r=[d_S], w=[d_Sbf[1 - half]])

                def st2(t, g=g):
                    b = t % 2
                    bY = 2 + b
                    C.V(lambda e, hs=hs, b=b, t=t, bY=bY: e.tensor_tensor(out=t1[b][:], in0=C.pb(bY)[:, 256:512].rearrange("p (a c) -> p a c", a=4),
                                                           in1=ecs[:, t, hs].unsqueeze(2).to_broadcast([128, 4, 64]), op=ALU.mult),
                        r=[C.psd[bY], d_ecs], w=[d_t1[b]])
                    C.V(lambda e, b=b, bY=bY: e.tensor_tensor(out=t1[b][:], in0=C.pb(bY)[:, 0:256].rearrange("p (a c) -> p a c", a=4),
                                                       in1=t1[b][:], op=ALU.add), r=[C.psd[bY], d_t1[b]], w=[d_t1[b]])
                    C.G(lambda e, b=b, t=t: e.tensor_tensor(out=yz[b][:], in0=t1[b][:].rearrange("p a c -> p (a c)"),
                                                           in1=sz[:, t, :], op=ALU.mult), r=[d_t1[b], d_sz[t]], w=[d_yz[b]])
                    C.act(junk[:], yz[b][:], AF.Square, r=[d_yz[b]], w=[d_junk], accum_out=ss[:, g, t:t + 1])
                    rstd_ops(C, rs[:, g, t:t + 1], ss[:, g, t:t + 1], 256, [d_junk], [d_junk])
                    C.V(lambda e, b=b, t=t, g=g: e.scalar_tensor_tensor(out=yn[b][:], in0=yz[b][:], scalar=rs[:, g, t:t + 1],
                                                                       in1=gn[:, g * 256:(g + 1) * 256], op0=ALU.mult, op1=ALU.mult),
                        r=[d_yz[b], d_junk, d_p], w=[d_yn[b]])
                    bk = 6 + (t // 4) % 2
                    pv = C.pbh(bk)
                    i = t % 4
                    for c in range(2):
                        C.tr(pv[:, (c * 4 + i) * 128:(c * 4 + i + 1) * 128], yn[b][:, c * 128:(c + 1) * 128], C.ident_bf[:],
                             r=[d_yn[b], C.d_const], w=[C.psd[bk]])
                    if i == 3:
                        t4 = t // 4
                        C.V(lambda e, t4=t4, pv=pv, g=g: e.tensor_copy(out=goT[:, 2 * g:2 * g + 2, t4 * 512:(t4 + 1) * 512],
                                                                      in_=pv.rearrange("p (c m) -> p c m", c=2)),
                            r=[C.psd[bk]], w=[d_go[4 * t4 + j] for j in range(4)])

                if SSD_SKEW:
                    for s_ in range(NT + 2):
                        if 0 <= s_ - 2 < NT:
                            st2(s_ - 2)
                        if 0 <= s_ - 1 < NT:
                            st1(s_ - 1)
                        if s_ < NT:
                            st0(s_)
                else:
                    for s_ in range(NT):
                        st0(s_); st1(s_); st2(s_)
            C.dbg("dtt", dtt[:], [128, NT, 32], F32, [d_dt])
            C.dbg("a_tok", a_tok[:], [128, NT, 32], F32, [d_a])
            C.dbg("ecs", ecs[:], [128, NT, 32], F32, [d_ecs])
            C.dbg("dec", dec[:], [128, NT, 32], F32, [d_dec])
            C.dbg("elast", elast[:], [128, 32, 32], F32, [d_el])
            C.dbg("x_tok", x_tok[:], [128, NT, 256], BF16, d_xtok)
            C.dbg("B_tok", B_tok[:], [128, NT, 128], BF16, d_Btok)
            C.dbg("CT", CT[:], [128, S_], BF16, [d_CT])
            C.dbg("sz", sz[:], [128, NT, 256], BF16, d_sz)
            C.dbg("goT", goT[:], [128, 16, S_], BF16, d_go)
            C.dbg("Mh", Mh[1][:], [128, 4, 128], BF16, [d_M[1]])
            C.dbg("cbm", cbm[1][:], [128, 128], F32, [d_cbm[1]])
            C.dbg("lm", lm[1][:], [128, 4, 128], F32, [d_lm[1]])
            C.dbg("yz", yz[1][:], [128, 256], F32, [d_yz[1]])
            C.dbg("Sst", Sst[:], [128, 4, 64], F32, [d_S])
            C.release()
        out_proj(C, goT, d_go, 16, W["ssd_w_out"], h_in, d_hin, h_out, d_hout)
        C.release()


LAYER_FNS[3] = layer_ssd

WSPEC = {
    "norm_g": [4, 1024], "final_g": [1024],
    "mla_w_in": [1024, 1696], "mla_g_q": [384], "mla_w_uq": [384, 1536], "mla_g_kv": [256],
    "mla_w_ukv": [256, 2048], "mla_w_out": [1024, 1024],
    "gla_w_in": [1024, 3088], "gla_w_gk2": [16, 512], "gla_b_gk": [512], "gla_g_o": [256],
    "gla_w_out": [1024, 1024],
    "lru_w_in": [1024, 2560], "lru_conv_w": [4, 1280], "lru_conv_b": [1280], "lru_w_a": [10, 128, 128],
    "lru_b_a": [1280], "lru_w_x": [10, 128, 128], "lru_b_x": [1280], "lru_lam": [1280],
    "lru_w_out": [1280, 1024],
    "ssd_w_in": [1024, 6176], "ssd_conv_w": [4, 4096], "ssd_conv_b": [4096], "ssd_dt_bias": [32],
    "ssd_a_log": [32], "ssd_d": [32], "ssd_g_norm": [2048], "ssd_w_out": [2048, 1024],
}


def host_consts():
    c = {}
    c["ident_bf"] = np.eye(128, dtype=np.float32).astype(ml_dtypes.bfloat16)
    c["ident_f"] = np.eye(128, dtype=np.float32)
    k = np.arange(128)
    c["tri_bf"] = (k[None, :] >= k[:, None]).astype(np.float32).astype(ml_dtypes.bfloat16)
    c["ones_bf"] = np.ones((128, 64), np.float32).astype(ml_dtypes.bfloat16)
    invf = (np.float32(10000.0) ** (-np.arange(0, 32, 2, dtype=np.float32) / np.float32(32))).astype(np.float32)
    rm = np.ones((128, S_), np.float32); rm[:, ::64] = 0.0
    c["rmask"] = rm.astype(ml_dtypes.bfloat16)
    m2 = ((k[None, :] >= k[:, None]) & ((k[None, :] // 64) == (k[:, None] // 64))).astype(np.float32)
    c["mask2"] = m2.astype(ml_dtypes.bfloat16)
    same = (k[None, :] // 64) == (k[:, None] // 64)
    u2 = ((k[:, None] > k[None, :]) & same).astype(np.float32)
    c["U2b"] = u2.astype(ml_dtypes.bfloat16)
    c["U2f"] = u2
    c["V2f"] = m2.astype(np.float32)
    c["onesA"] = np.ascontiguousarray(np.broadcast_to((k[:, None] < 64).astype(np.float32), (128, 128)))
    c["onesB"] = np.ascontiguousarray(np.broadcast_to((k[:, None] >= 64).astype(np.float32), (128, 128)))
    c["invf"] = np.ascontiguousarray(np.broadcast_to(invf[None, :], (128, 16))).astype(np.float32)
    return c


CSPEC = {"ident_bf": ([128, 128], BF16), "ident_f": ([128, 128], F32), "tri_bf": ([128, 128], BF16),
         "ones_bf": ([128, 64], BF16), "invf": ([128, 16], F32), "rmask": ([128, S_], BF16),
         "mask2": ([128, 128], BF16), "U2b": ([128, 128], BF16), "U2f": ([128, 128], F32),
         "V2f": ([128, 128], F32), "onesA": ([128, 128], F32), "onesB": ([128, 128], F32)}

def build(layers=(0, 1, 2, 3), final=True):
    nc = bass.Bass("TRN2", target_bir_lowering=False)
    x = nc.dram_tensor("x", [S_, D_], F32, kind="ExternalInput").ap()
    pos = nc.dram_tensor("pos", [128, NT], I32, kind="ExternalInput").ap()
    W = {k: nc.dram_tensor(k, v, F32, kind="ExternalInput").ap() for k, v in WSPEC.items()}
    CD = {k: nc.dram_tensor(k, v[0], v[1], kind="ExternalInput").ap() for k, v in CSPEC.items()}
    out = nc.dram_tensor("out", [S_, D_], F32, kind="ExternalOutput").ap()
    hbuf = [nc.dram_tensor(f"hbuf{i}", [S_, D_], F32).ap() for i in range(2)]
    _BASE["r"] = {}
    with ExitStack() as st:
        C = Ctx(nc, st)
        C.pos = pos
        C.CD = CD
        C.d_const = Dep()
        C.ident_bf = C.sb(st, [128, 128], BF16)
        C.ident_f = C.sb(st, [128, 128], F32)
        C.D(lambda e: e.dma_start(out=C.ident_bf[:], in_=CD["ident_bf"]), w=[C.d_const])
        C.D(lambda e: e.dma_start(out=C.ident_f[:], in_=CD["ident_f"]), w=[C.d_const])
        h_cur, d_cur = x, deps(NT)
        nxt = 0
        for li in layers:
            Wl = dict(W)
            Wl["norm_g"] = W["norm_g"][li, :]
            h_nxt, d_nxt = hbuf[nxt], deps(NT)
            if not final and li == layers[-1]:
                h_nxt = out
            LAYER_FNS[li](C, Wl, h_cur, d_cur, h_nxt, d_nxt)
            h_cur, d_cur = h_nxt, d_nxt
            nxt ^= 1
        if final:
            final_norm(C, h_cur, d_cur, W["final_g"], out)
        else:
            C.out_toks.append(C.last_store)
            for t in range(NT):
                C.out_toks.append(d_cur[t].w)
        for tok in C.out_toks:
            C.S.wait_tok("sync", tok)
        C.S.emit()
    return nc


LAYER_FNS[2] = layer_lru


def make_in_maps(inputs):
    consts = host_consts()
    maps = []
    for b in range(8):
        m = {"x": np.ascontiguousarray(inputs["x"][b]),
             "pos": np.ascontiguousarray(np.asarray(inputs["positions"][b]).astype(np.int32).reshape(NT, 128).T)}
        for k in WSPEC:
            a = np.asarray(inputs[k], dtype=np.float32)
            if k not in ("norm_g", "final_g"):
                a = a[0]
            m[k] = np.ascontiguousarray(a)
        m.update(consts)
        maps.append(m)
    return maps


_NC_CACHE = {}


def kernel(**inputs):
    if "nc" not in _NC_CACHE:
        _NC_CACHE["nc"] = build()
    nc = _NC_CACHE["nc"]
    maps = make_in_maps(inputs)
    res = run_bass_kernel_spmd(nc, maps, core_ids=list(range(8)))
    return np.stack([np.asarray(r["out"], dtype=np.float32) for r in res.results], axis=0)
```

```python
import numpy as np
import ml_dtypes
import concourse.bass as bass
import concourse.mybir as mybir
from concourse.bass_utils import run_bass_kernel_spmd
from contextlib import ExitStack

F32 = mybir.dt.float32
BF16 = mybir.dt.bfloat16
I32 = mybir.dt.int32
AF = mybir.ActivationFunctionType
ALU = mybir.AluOpType

S_ = 2048
D_ = 1024
NT = 16
EPS = 1e-6
LAYER_FNS = {}
DEBUG = False
import os
SSD_SKEW = int(os.environ.get('SSD_SKEW', '1'))

ENGS = ("tensor", "vector", "scalar", "gpsimd", "sync")
EPOCH = 6000
NDMA = 24


_BASE = {"r": {}}


class Dep:
    __slots__ = ("w", "r")

    def __init__(self):
        self.w = None
        self.r = dict(_BASE["r"])


def deps(n):
    return [Dep() for _ in range(n)]


class Sched:
    def __init__(self, nc, stack):
        self.nc = nc
        self.stack = stack
        self.q = {e: [] for e in ENGS}
        self.cnt = {e: 0 for e in ENGS}
        self.sems = {}
        self.known = {e: {} for e in ENGS}
        self.dma_sems = [stack.enter_context(nc.semaphore(f"dma{i}")) for i in range(NDMA)]
        self.dma_i = 0

    def _esem(self, eng, epoch):
        k = (eng, epoch)
        if k not in self.sems:
            self.sems[k] = self.stack.enter_context(self.nc.semaphore(f"s_{eng}_{epoch}"))
        return self.sems[k]

    def _need_wait(self, weng, tok):
        kn = self.known[weng]
        if tok[0] == "c":
            _, eng, ep, val = tok
            cur = kn.get(("c", eng), (-1, 0))
            if cur >= (ep, val):
                return None
            kn[("c", eng)] = (ep, val)
            return (self._esem(eng, ep), val)
        _, idx, val = tok
        cur = kn.get(("d", idx), 0)
        if cur >= val:
            return None
        kn[("d", idx)] = val
        return (self.dma_sems[idx], val)

    def op(self, eng, fn, reads=(), writes=(), dma=False):
        waits = []
        toks = []
        for d in reads:
            if d.w is not None:
                toks.append((d.w, "raw"))
        for d in writes:
            if d.w is not None:
                toks.append((d.w, "waw"))
            for t in d.r.values():
                toks.append((t, "war"))
        for tok, kind in toks:
            if not dma and tok[0] == "c" and tok[1] == eng:
                if eng == "tensor" or kind != "raw":
                    continue
            w = self._need_wait(eng, tok)
            if w is not None:
                waits.append(w)
        if dma:
            idx = self.dma_i % NDMA
            rnd = self.dma_i // NDMA
            self.dma_i += 1
            if rnd > 0:
                w = self._need_wait(eng, ("d", idx, 16 * rnd))
                if w is not None:
                    waits.append(w)
            mytok = ("d", idx, 16 * (rnd + 1))
            inc = (self.dma_sems[idx], 16)
        else:
            c = self.cnt[eng]
            self.cnt[eng] = c + 1
            ep, val = c // EPOCH, c % EPOCH + 1
            mytok = ("c", eng, ep, val)
            inc = (self._esem(eng, ep), 1)
        self.q[eng].append((fn, waits, inc))
        rk = mytok[:2]
        for d in reads:
            d.r[rk] = mytok
        for d in writes:
            d.w = mytok
            d.r = {}
        return mytok

    def snapshot(self):
        snap = {}
        for e in ENGS:
            c = self.cnt[e]
            if c > 0:
                c -= 1
                snap[("c", e)] = ("c", e, c // EPOCH, c % EPOCH + 1)
        for i in range(min(self.dma_i, NDMA)):
            n_i = (self.dma_i - 1 - i) // NDMA
            snap[("d", i)] = ("d", i, 16 * (n_i + 1))
        return snap

    def wait_tok(self, eng, tok):
        w = self._need_wait(eng, tok)
        if w is not None:
            self.q[eng].append((None, [w], None))

    def emit(self):
        nc = self.nc
        with nc.Block() as block:
            def run(engname):
                def body(e):
                    for fn, waits, inc in self.q[engname]:
                        for sem, val in waits:
                            e.wait_ge(sem, val)
                        if fn is not None:
                            ins = fn(e)
                            ins.then_inc(inc[0], inc[1])
                return body
            block.tensor(run("tensor"))
            block.vector(run("vector"))
            block.scalar(run("scalar"))
            block.gpsimd(run("gpsimd"))
            block.sync(run("sync"))


class Ctx:
    def __init__(self, nc, st):
        self.nc = nc
        self.st = st
        self.S = Sched(nc, st)
        self.ps = st.enter_context(nc.psum_tensor("ps", [128, 8 * 512], F32))
        self.psd = deps(8)
        self.uid = 0
        self.out_toks = []

    def release(self):
        _BASE["r"] = self.S.snapshot()

    def sb(self, stack, shape, dt):
        self.uid += 1
        return stack.enter_context(self.nc.sbuf_tensor(f"t{self.uid}", list(shape), dt))

    def pb(self, i, n=512, off=0):
        return self.ps[:, i * 512 + off:i * 512 + off + n]

    def pbh(self, i):
        return self.ps[:, i * 512:(i + 1) * 512].bitcast(BF16)

    def T(self, fn, r=(), w=()):
        return self.S.op("tensor", fn, r, w)

    def V(self, fn, r=(), w=()):
        return self.S.op("vector", fn, r, w)

    def A(self, fn, r=(), w=()):
        return self.S.op("scalar", fn, r, w)

    def G(self, fn, r=(), w=()):
        return self.S.op("gpsimd", fn, r, w)

    def D(self, fn, r=(), w=()):
        return self.S.op("sync", fn, r, w, dma=True)

    def DG(self, fn, r=(), w=()):
        return self.S.op("gpsimd", fn, r, w, dma=True)

    def mm(self, out, lhsT, rhs, start, stop, r=(), w=()):
        return self.T(lambda e: e.matmul(out, lhsT=lhsT, rhs=rhs, start=start, stop=stop), r, w)

    def tr(self, out, in_, ident, r=(), w=()):
        return self.T(lambda e: e.transpose(out=out, in_=in_, identity=ident), r, w)

    def act(self, out, in_, func, r=(), w=(), **kw):
        return self.A(lambda e: e.activation(out=out, in_=in_, func=func, **kw), r, w)

    def dbg(self, name, ap, shape, dt, r):
        if not DEBUG:
            return
        d = self.nc.dram_tensor("dbg_" + name, list(shape), dt, kind="ExternalOutput").ap()
        tok = self.D(lambda e: e.dma_start(out=d, in_=ap), r=r)
        self.out_toks.append(tok)

    def load_w(self, dst, src2d, dep):
        n = src2d.shape[1]
        tok = None
        for c0 in range(0, n, 512):
            c1 = min(n, c0 + 512)
            assert (c1 - c0) in (16, 32, 64, 128, 256, 512), (c0, c1)
            tok = self.DG(lambda e, c0=c0, c1=c1: e.dma_start(
                out=dst[:, :, c0:c1], in_=src2d[:, c0:c1].rearrange("(c p) n -> p c n", p=128)), w=[dep])
        return tok

    def load_bc(self, dst, src1d, dep, n=128):
        return self.DG(lambda e: e.dma_start(out=dst, in_=src1d.partition_broadcast(n)), w=[dep])

    def load_col(self, dst, src1d, dep, nchunks):
        return self.DG(lambda e: e.dma_start(out=dst, in_=src1d.rearrange("(c p) -> p c", p=128),
                                             allow_slow_non_contiguous=True), w=[dep])


def rstd_ops(C, rs_ap, ss_ap, n, r, w):
    C.act(rs_ap, ss_ap, AF.Ln, r=r, w=w, scale=1.0 / n, bias=EPS)
    C.act(rs_ap, rs_ap, AF.Exp, r=w, w=w, scale=-0.5)


def norm_T(C, stk, h_ap, g_row, uT, d_uT, d_h, banks=(0, 1)):
    with ExitStack() as s:
        gt = C.sb(s, [128, D_], F32); d_gt = Dep()
        C.load_bc(gt[:], g_row, d_gt)
        hb = [C.sb(s, [128, D_], F32) for _ in range(2)]; d_hb = deps(2)
        junk = C.sb(s, [128, D_], BF16); d_junk = Dep()
        ss = C.sb(s, [128, NT], F32); rs = C.sb(s, [128, NT], F32)
        d_ss = deps(NT); d_rs = deps(NT)
        xn = [C.sb(s, [128, D_], BF16) for _ in range(2)]; d_xn = deps(2)
        for t in range(NT):
            b = t % 2
            C.D(lambda e, t=t, b=b: e.dma_start(out=hb[b][:], in_=h_ap[t * 128:(t + 1) * 128, :]),
                r=[d_h[t]], w=[d_hb[b]])
            C.act(junk[:], hb[b][:], AF.Square, r=[d_hb[b]], w=[d_junk, d_ss[t]], accum_out=ss[:, t:t + 1])
            rstd_ops(C, rs[:, t:t + 1], ss[:, t:t + 1], D_, [d_ss[t]], [d_rs[t]])
            C.V(lambda e, t=t, b=b: e.scalar_tensor_tensor(out=xn[b][:], in0=hb[b][:], scalar=rs[:, t:t + 1],
                                                           in1=gt[:], op0=ALU.mult, op1=ALU.mult),
                r=[d_hb[b], d_rs[t], d_gt], w=[d_xn[b]])
            bk = banks[t % len(banks)]
            pv = C.pbh(bk)
            for c in range(8):
                C.tr(pv[:, c * 128:(c + 1) * 128], xn[b][:, c * 128:(c + 1) * 128], C.ident_bf[:],
                     r=[d_xn[b], C.d_const], w=[C.psd[bk]])
            C.V(lambda e, t=t, pv=pv: e.tensor_copy(out=uT[:, :, t * 128:(t + 1) * 128],
                                                   in_=pv.rearrange("p (c n) -> p c n", c=8)),
                r=[C.psd[bk]], w=[d_uT[t]])
        C.release()


def out_proj(C, goT, d_go, nk, w_out, h_in, d_hin, h_out, d_hout, banks=(0, 1, 2, 3), wo_pre=None):
    with ExitStack() as s:
        if wo_pre is None:
            wo = C.sb(s, [128, nk, D_], BF16); d_wo = Dep()
            half = nk // 2
            C.load_w(wo[:, 0:half, :], w_out[0:half * 128, :], d_wo)
            C.load_w(wo[:, half:nk, :], w_out[half * 128:nk * 128, :], d_wo)
        else:
            wo, d_wo = wo_pre
        hb = [C.sb(s, [128, D_], F32) for _ in range(2)]; d_hb = deps(2)
        ho = [C.sb(s, [128, D_], F32) for _ in range(2)]; d_ho = deps(2)
        i = 0
        for t in range(NT):
            b = t % 2
            C.D(lambda e, t=t, b=b: e.dma_start(out=hb[b][:], in_=h_in[t * 128:(t + 1) * 128, :]),
                r=[d_hin[t]], w=[d_hb[b]])
            for nb in range(2):
                bk = banks[i % len(banks)]; i += 1
                for c in range(nk):
                    C.mm(C.pb(bk), goT[:, c, t * 128:(t + 1) * 128], wo[:, c, nb * 512:(nb + 1) * 512],
                         c == 0, c == nk - 1, r=[d_go[t], d_wo], w=[C.psd[bk]])
                C.V(lambda e, b=b, nb=nb, bk=bk: e.tensor_tensor(out=ho[b][:, nb * 512:(nb + 1) * 512], in0=C.pb(bk),
                                                               in1=hb[b][:, nb * 512:(nb + 1) * 512], op=ALU.add),
                    r=[C.psd[bk], d_hb[b]], w=[d_ho[b]])
            tok = C.D(lambda e, t=t, b=b: e.dma_start(out=h_out[t * 128:(t + 1) * 128, :], in_=ho[b][:]),
                      r=[d_ho[b]], w=[d_hout[t]])
            C.last_store = tok
        C.release()


def final_norm(C, h_ap, d_h, g_row, out_ap):
    with ExitStack() as s:
        gt = C.sb(s, [128, D_], F32); d_gt = Dep()
        C.load_bc(gt[:], g_row, d_gt)
        hb = [C.sb(s, [128, D_], F32) for _ in range(2)]; d_hb = deps(2)
        junk = C.sb(s, [128, D_], BF16); d_junk = Dep()
        ss = C.sb(s, [128, NT], F32); rs = C.sb(s, [128, NT], F32)
        d_ss = deps(NT); d_rs = deps(NT)
        ob = [C.sb(s, [128, D_], F32) for _ in range(2)]; d_ob = deps(2)
        for t in range(NT):
            b = t % 2
            C.D(lambda e, t=t, b=b: e.dma_start(out=hb[b][:], in_=h_ap[t * 128:(t + 1) * 128, :]),
                r=[d_h[t]], w=[d_hb[b]])
            C.act(junk[:], hb[b][:], AF.Square, r=[d_hb[b]], w=[d_junk, d_ss[t]], accum_out=ss[:, t:t + 1])
            rstd_ops(C, rs[:, t:t + 1], ss[:, t:t + 1], D_, [d_ss[t]], [d_rs[t]])
            C.V(lambda e, t=t, b=b: e.scalar_tensor_tensor(out=ob[b][:], in0=hb[b][:], scalar=rs[:, t:t + 1],
                                                           in1=gt[:], op0=ALU.mult, op1=ALU.mult),
                r=[d_hb[b], d_rs[t], d_gt], w=[d_ob[b]])
            tok = C.D(lambda e, t=t, b=b: e.dma_start(out=out_ap[t * 128:(t + 1) * 128, :], in_=ob[b][:]),
                      r=[d_ob[b]])
            C.out_toks.append(tok)
        C.release()


def layer_lru(C, W, h_in, d_hin, h_out, d_hout):
    NCH = 10
    with ExitStack() as L:
        uT = C.sb(L, [128, 8, S_], BF16); d_uT = deps(NT)
        norm_T(C, L, h_in, W["norm_g"], uT, d_uT, d_hin)
        d_uTb = [[d_uT[4 * tb + i] for i in range(4)] for tb in range(4)]
        goT = C.sb(L, [128, NCH, S_], BF16); d_go = deps(NT)
        cw = C.sb(L, [128, 4, NCH], F32); d_cw = Dep()
        for k in range(4):
            C.load_col(cw[:, k, :], W["lru_conv_w"][k, :], d_cw, NCH)
        cb = C.sb(L, [128, NCH], F32); ba = C.sb(L, [128, NCH], F32); bx = C.sb(L, [128, NCH], F32)
        lam = C.sb(L, [128, NCH], F32); d_p = Dep()
        C.load_col(cb[:], W["lru_conv_b"], d_p, NCH)
        C.load_col(ba[:], W["lru_b_a"], d_p, NCH)
        C.load_col(bx[:], W["lru_b_x"], d_p, NCH)
        C.load_col(lam[:], W["lru_lam"], d_p, NCH)
        cA = C.sb(L, [128, NCH], F32); cA2 = C.sb(L, [128, NCH], F32); d_cA = Dep()
        C.act(cA[:], lam[:], AF.Exp, r=[d_p], w=[d_cA], scale=-1.0)
        C.act(cA[:], cA[:], AF.Ln, r=[d_cA], w=[d_cA], bias=1.0)
        C.V(lambda e: e.tensor_scalar(out=cA2[:], in0=cA[:], scalar1=-16.0, scalar2=None, op0=ALU.mult),
            r=[d_cA], w=[d_cA])
        C.V(lambda e: e.tensor_scalar(out=cA[:], in0=cA[:], scalar1=-8.0, scalar2=None, op0=ALU.mult),
            r=[d_cA], w=[d_cA])
        wa = C.sb(L, [128, NCH, 128], BF16); wx = C.sb(L, [128, NCH, 128], BF16); d_wg = Dep()
        C.DG(lambda e: e.dma_start(out=wa[:], in_=W["lru_w_a"].rearrange("n i j -> i n j")), w=[d_wg])
        C.DG(lambda e: e.dma_start(out=wx[:], in_=W["lru_w_x"].rearrange("n i j -> i n j")), w=[d_wg])
        dg = C.sb(L, [128, NCH, 4, 128], BF16); d_dg = Dep()
        for c in range(NCH):
            for k in range(4):
                C.V(lambda e, c=c, k=k: e.tensor_scalar(out=dg[:, c, k, :], in0=C.ident_bf[:],
                                                        scalar1=cw[:, k, c:c + 1], scalar2=None, op0=ALU.mult),
                    r=[d_cw, C.d_const], w=[d_dg])
        wu = [C.sb(L, [128, 8, 128], BF16) for _ in range(2)]; d_wu = deps(2)
        wg = [C.sb(L, [128, 8, 128], BF16) for _ in range(2)]; d_wgt = deps(2)
        ub2 = [C.sb(L, [128, 3 + S_], BF16) for _ in range(2)]; d_ub2 = deps(2)
        for b_ in range(2):
            C.V(lambda e, b_=b_: e.memset(ub2[b_][:, 0:3], 0.0), w=[d_ub2[b_]])
        uc = C.sb(L, [128, S_], F32); d_uc = Dep()
        ucb = C.sb(L, [128, S_], BF16); d_ucb = Dep()
        rr = C.sb(L, [128, S_], F32); d_rr = Dep()
        ig = C.sb(L, [128, S_], F32); d_ig = Dep()
        tmp = C.sb(L, [128, S_], F32); d_tmp = Dep()
        hs = C.sb(L, [128, S_], BF16); d_hs = Dep()
        sg = C.sb(L, [128, S_], BF16); d_sg = Dep()
        w_in = W["lru_w_in"]

        def proj_u(c):
            b = c % 2
            C.load_w(wu[b][:], w_in[:, 1280 + c * 128:1280 + (c + 1) * 128], d_wu[b])
            C.load_w(wg[b][:], w_in[:, c * 128:(c + 1) * 128], d_wgt[b])
            for tb in range(4):
                bk = tb
                for kc in range(8):
                    C.mm(C.pb(bk), wu[b][:, kc, :], uT[:, kc, tb * 512:(tb + 1) * 512], kc == 0, kc == 7,
                         r=d_uTb[tb] + [d_wu[b]], w=[C.psd[bk]])
                C.act(ub2[b][:, 3 + tb * 512:3 + (tb + 1) * 512], C.pb(bk), AF.Copy, r=[C.psd[bk]], w=[d_ub2[b]])

        proj_u(0)
        for c in range(NCH):
            b = c % 2
            for tb in range(4):
                bk = 4 + tb
                for k in range(4):
                    C.mm(C.pb(bk), dg[:, c, k, :], ub2[b][:, tb * 512 + k:tb * 512 + k + 512], k == 0, k == 3,
                         r=[d_ub2[b], d_dg], w=[C.psd[bk]])
                C.act(uc[:, tb * 512:(tb + 1) * 512], C.pb(bk), AF.Identity, r=[C.psd[bk], d_p], w=[d_uc],
                      bias=cb[:, c:c + 1])
            C.V(lambda e: e.tensor_copy(out=ucb[:], in_=uc[:]), r=[d_uc], w=[d_ucb])
            for tb in range(4):
                bk = tb
                for kc in range(8):
                    C.mm(C.pb(bk), wg[b][:, kc, :], uT[:, kc, tb * 512:(tb + 1) * 512], kc == 0, kc == 7,
                         r=d_uTb[tb] + [d_wgt[b]], w=[C.psd[bk]])
                C.act(sg[:, tb * 512:(tb + 1) * 512], C.pb(bk), AF.Silu, r=[C.psd[bk]], w=[d_sg])
            for tb in range(4):
                bk = 4 + tb
                C.mm(C.pb(bk), wa[:, c, :], ucb[:, tb * 512:(tb + 1) * 512], True, True,
                     r=[d_ucb, d_wg], w=[C.psd[bk]])
                C.act(rr[:, tb * 512:(tb + 1) * 512], C.pb(bk), AF.Sigmoid, r=[C.psd[bk], d_p], w=[d_rr],
                      bias=ba[:, c:c + 1])
            for tb in range(4):
                bk = 4 + tb
                C.mm(C.pb(bk), wx[:, c, :], ucb[:, tb * 512:(tb + 1) * 512], True, True,
                     r=[d_ucb, d_wg], w=[C.psd[bk]])
                C.act(ig[:, tb * 512:(tb + 1) * 512], C.pb(bk), AF.Sigmoid, r=[C.psd[bk], d_p], w=[d_ig],
                      bias=bx[:, c:c + 1])
            if c + 1 < NCH:
                proj_u(c + 1)
            C.act(tmp[:], rr[:], AF.Exp, r=[d_rr, d_cA], w=[d_tmp], scale=cA2[:, c:c + 1])
            C.act(rr[:], rr[:], AF.Exp, r=[d_rr, d_cA], w=[d_rr], scale=cA[:, c:c + 1])
            C.act(tmp[:], tmp[:], AF.Sqrt, r=[d_tmp], w=[d_tmp], scale=-1.0, bias=1.0)
            C.V(lambda e: e.tensor_tensor(out=ig[:], in0=ig[:], in1=uc[:], op=ALU.mult), r=[d_ig, d_uc], w=[d_ig])
            C.V(lambda e: e.tensor_tensor(out=tmp[:], in0=tmp[:], in1=ig[:], op=ALU.mult), r=[d_tmp, d_ig], w=[d_tmp])
            C.V(lambda e: e.tensor_tensor_scan(out=hs[:], data0=rr[:], data1=tmp[:], initial=0.0,
                                               op0=ALU.mult, op1=ALU.add), r=[d_rr, d_tmp], w=[d_hs])
            C.G(lambda e, c=c: e.tensor_tensor(out=goT[:, c, :], in0=hs[:], in1=sg[:], op=ALU.mult),
                r=[d_hs, d_sg], w=d_go)
        out_proj(C, goT, d_go, NCH, W["lru_w_out"], h_in, d_hin, h_out, d_hout)
        C.release()


def layer_mla(C, W, h_in, d_hin, h_out, d_hout):
    H = 16
    PI = float(np.pi)
    with ExitStack() as L:
        cqnT = C.sb(L, [128, 3, S_], BF16); d_cq = deps(NT)
        ckvT = C.sb(L, [128, 2, S_], BF16); d_ckv = deps(NT)
        sgT = C.sb(L, [128, 8, S_], BF16); d_sg4 = deps(4)
        kr_all = C.sb(L, [128, NT, 32], F32); d_kr = Dep()
        w_in = W["mla_w_in"]
        with ExitStack() as PA:
            uT = C.sb(PA, [128, 8, S_], BF16); d_uT = deps(NT)
            norm_T(C, PA, h_in, W["norm_g"], uT, d_uT, d_hin)
            d_uTb = [[d_uT[4 * tb + i] for i in range(4)] for tb in range(4)]
            w1 = C.sb(PA, [128, 8, 768], BF16); d_w1 = Dep()
            C.load_w(w1[:], w_in[:, 0:768], d_w1)
            gq = C.sb(PA, [128, 384], F32); gkv = C.sb(PA, [128, 256], F32); d_g = Dep()
            C.load_bc(gq[:], W["mla_g_q"], d_g)
            C.load_bc(gkv[:], W["mla_g_kv"], d_g)
            ctok = [C.sb(PA, [128, 672], F32) for _ in range(2)]; d_ct = deps(2)
            ss2 = C.sb(PA, [128, NT, 2], F32); rs2 = C.sb(PA, [128, NT, 2], F32)
            d_ss = deps(NT); d_rs = deps(NT)
            junk = C.sb(PA, [128, 384], BF16); d_junk = Dep()
            cn = [C.sb(PA, [128, 640], BF16) for _ in range(2)]; d_cn = deps(2)
            wgt = [C.sb(PA, [128, 8, 512], BF16) for _ in range(2)]; d_wgt = deps(2)
            for hf in range(2):
                C.load_w(wgt[hf][:], w_in[:, 672 + hf * 512:672 + (hf + 1) * 512], d_wgt[hf])
            for t in range(NT):
                b = t % 2
                for (c0, c1, bk) in ((0, 512, 2), (512, 672, 3)):
                    for kc in range(8):
                        C.mm(C.pb(bk, c1 - c0), uT[:, kc, t * 128:(t + 1) * 128], w1[:, kc, c0:c1], kc == 0, kc == 7,
                             r=[d_uT[t], d_w1], w=[C.psd[bk]])
                    C.act(ctok[b][:, c0:c1], C.pb(bk, c1 - c0), AF.Copy, r=[C.psd[bk]], w=[d_ct[b]])
                C.act(junk[:, 0:384], ctok[b][:, 0:384], AF.Square, r=[d_ct[b]], w=[d_junk, d_ss[t]],
                      accum_out=ss2[:, t, 0:1])
                C.act(junk[:, 0:256], ctok[b][:, 384:640], AF.Square, r=[d_ct[b]], w=[d_junk, d_ss[t]],
                      accum_out=ss2[:, t, 1:2])
                rstd_ops(C, rs2[:, t, 0:1], ss2[:, t, 0:1], 384, [d_ss[t]], [d_rs[t]])
                rstd_ops(C, rs2[:, t, 1:2], ss2[:, t, 1:2], 256, [d_ss[t]], [d_rs[t]])
                C.V(lambda e, t=t, b=b: e.scalar_tensor_tensor(out=cn[b][:, 0:384], in0=ctok[b][:, 0:384],
                                                               scalar=rs2[:, t, 0:1], in1=gq[:], op0=ALU.mult, op1=ALU.mult),
                    r=[d_ct[b], d_rs[t], d_g], w=[d_cn[b]])
                C.V(lambda e, t=t, b=b: e.scalar_tensor_tensor(out=cn[b][:, 384:640], in0=ctok[b][:, 384:640],
                                                               scalar=rs2[:, t, 1:2], in1=gkv[:], op0=ALU.mult, op1=ALU.mult),
                    r=[d_ct[b], d_rs[t], d_g], w=[d_cn[b]])
                C.G(lambda e, t=t, b=b: e.tensor_copy(out=kr_all[:, t, :], in_=ctok[b][:, 640:672]),
                    r=[d_ct[b]], w=[d_kr])
                bk = 4 + b
                pv = C.pbh(bk)
                for c in range(5):
                    C.tr(pv[:, c * 128:(c + 1) * 128], cn[b][:, c * 128:(c + 1) * 128], C.ident_bf[:],
                         r=[d_cn[b], C.d_const], w=[C.psd[bk]])
                C.V(lambda e, t=t, pv=pv: e.tensor_copy(out=cqnT[:, :, t * 128:(t + 1) * 128],
                                                       in_=pv[:, 0:384].rearrange("p (c n) -> p c n", c=3)),
                    r=[C.psd[bk]], w=[d_cq[t]])
                C.V(lambda e, t=t, pv=pv: e.tensor_copy(out=ckvT[:, :, t * 128:(t + 1) * 128],
                                                       in_=pv[:, 384:640].rearrange("p (c n) -> p c n", c=2)),
                    r=[C.psd[bk]], w=[d_ckv[t]])
            C.dbg("ctok", ctok[1][:], [128, 672], F32, [d_ct[1]])
            C.dbg("w1", w1[:], [128, 8, 768], BF16, [d_w1])
            C.dbg("cn", cn[1][:], [128, 640], BF16, [d_cn[1]])
            C.dbg("rs2", rs2[:], [128, NT, 2], F32, d_rs)
            i = 0
            for c in range(8):
                for tb in range(4):
                    bk = (6, 7, 0, 1)[i % 4]; i += 1
                    for kc in range(8):
                        C.mm(C.pb(bk), wgt[c // 4][:, kc, (c % 4) * 128:(c % 4 + 1) * 128],
                             uT[:, kc, tb * 512:(tb + 1) * 512], kc == 0, kc == 7,
                             r=d_uTb[tb] + [d_wgt[c // 4]], w=[C.psd[bk]])
                    C.act(sgT[:, c, tb * 512:(tb + 1) * 512], C.pb(bk), AF.Silu, r=[C.psd[bk]], w=[d_sg4[tb]])
            C.release()
        C.dbg("cqnT", cqnT[:], [128, 3, S_], BF16, d_cq)
        C.dbg("ckvT", ckvT[:], [128, 2, S_], BF16, d_ckv)
        C.dbg("sgT", sgT[:], [128, 8, S_], BF16, d_sg4)
        C.dbg("kr", kr_all[:], [128, NT, 32], F32, [d_kr])
        with ExitStack() as PB:
            wuq = C.sb(PB, [128, 3, 1536], BF16); d_wuq = Dep()
            C.load_w(wuq[:], W["mla_w_uq"], d_wuq)
            wukv = C.sb(PB, [128, 2, 2048], BF16); d_wukv = Dep()
            C.load_w(wukv[:], W["mla_w_ukv"], d_wukv)
            wo = C.sb(PB, [128, 8, D_], BF16); d_wo = Dep()
            C.load_w(wo[:, 0:4, :], W["mla_w_out"][0:512, :], d_wo)
            C.load_w(wo[:, 4:8, :], W["mla_w_out"][512:1024, :], d_wo)
            tri = C.sb(PB, [128, 128], BF16); ones64 = C.sb(PB, [128, 64], BF16)
            invf = C.sb(PB, [128, 16], F32); posi = C.sb(PB, [128, NT], I32); d_k = Dep()
            C.D(lambda e: e.dma_start(out=tri[:], in_=C.CD["tri_bf"]), w=[d_k])
            C.D(lambda e: e.dma_start(out=ones64[:], in_=C.CD["ones_bf"]), w=[d_k])
            C.D(lambda e: e.dma_start(out=invf[:], in_=C.CD["invf"]), w=[d_k])
            C.D(lambda e: e.dma_start(out=posi[:], in_=C.pos), w=[d_k])
            posf = C.sb(PB, [128, NT], F32); d_t = Dep()
            ang = C.sb(PB, [128, NT, 16], F32); tmpf = C.sb(PB, [128, NT, 16], F32)
            ki = C.sb(PB, [128, NT, 16], I32); kff = C.sb(PB, [128, NT, 16], F32)
            rr = C.sb(PB, [128, NT, 16], F32); yy = C.sb(PB, [128, NT, 16], F32); mm_ = C.sb(PB, [128, NT, 16], F32)
            cos_t = C.sb(PB, [128, NT, 16], F32); sin_t = C.sb(PB, [128, NT, 16], F32); d_cs = Dep()
            C.V(lambda e: e.tensor_copy(out=posf[:], in_=posi[:]), r=[d_k], w=[d_t])
            C.V(lambda e: e.tensor_tensor(out=ang[:], in0=posf[:].unsqueeze(2).to_broadcast([128, NT, 16]),
                                          in1=invf[:].unsqueeze(1).to_broadcast([128, NT, 16]), op=ALU.mult),
                r=[d_t, d_k], w=[d_t])
            C.V(lambda e: e.tensor_scalar(out=tmpf[:], in0=ang[:], scalar1=1.0 / (2 * PI), scalar2=None, op0=ALU.mult),
                r=[d_t], w=[d_t])
            C.V(lambda e: e.tensor_copy(out=ki[:], in_=tmpf[:]), r=[d_t], w=[d_t])
            C.V(lambda e: e.tensor_copy(out=kff[:], in_=ki[:]), r=[d_t], w=[d_t])
            C1 = 6.28125
            C2 = float(2 * np.pi - 6.28125)
            C.V(lambda e: e.scalar_tensor_tensor(out=rr[:], in0=kff[:], scalar=-C1, in1=ang[:], op0=ALU.mult, op1=ALU.add),
                r=[d_t], w=[d_t])
            C.V(lambda e: e.scalar_tensor_tensor(out=rr[:], in0=kff[:], scalar=-C2, in1=rr[:], op0=ALU.mult, op1=ALU.add),
                r=[d_t], w=[d_t])
            for dst, shift in ((sin_t, 0.0), (cos_t, PI / 2)):
                C.V(lambda e, shift=shift: e.tensor_scalar(out=yy[:], in0=rr[:], scalar1=shift, scalar2=None, op0=ALU.add),
                    r=[d_t], w=[d_t])
                C.V(lambda e: e.tensor_scalar(out=mm_[:], in0=yy[:], scalar1=PI, scalar2=-2 * PI, op0=ALU.is_gt, op1=ALU.mult),
                    r=[d_t], w=[d_t])
                C.V(lambda e: e.tensor_tensor(out=yy[:], in0=yy[:], in1=mm_[:], op=ALU.add), r=[d_t], w=[d_t])
                C.V(lambda e: e.tensor_scalar(out=mm_[:], in0=yy[:], scalar1=-PI, scalar2=2 * PI, op0=ALU.is_lt, op1=ALU.mult),
                    r=[d_t], w=[d_t])
                C.V(lambda e: e.tensor_tensor(out=yy[:], in0=yy[:], in1=mm_[:], op=ALU.add), r=[d_t], w=[d_t])
                C.act(dst[:], yy[:], AF.Sin, r=[d_t], w=[d_cs, d_t])
            qT = [C.sb(PB, [128, S_], BF16) for _ in range(2)]; d_qT = deps(2)
            kT = [C.sb(PB, [128, S_], BF16) for _ in range(2)]; d_kT = deps(2)
            Vh = [C.sb(PB, [128, NT, 64], BF16) for _ in range(2)]; d_V = deps(2)
            for b in range(2):
                C.G(lambda e, b=b: e.memset(qT[b][:], 0.0), w=[d_qT[b]])
                C.G(lambda e, b=b: e.memset(kT[b][:], 0.0), w=[d_kT[b]])
            qtok = C.sb(PB, [128, NT, 96], F32); d_qtok = Dep()
            ta = C.sb(PB, [128, NT, 16], F32); tb_ = C.sb(PB, [128, NT, 16], F32)
            tc = C.sb(PB, [128, NT, 16], F32); td = C.sb(PB, [128, NT, 16], F32); d_tt = Dep()
            krr = C.sb(PB, [128, NT, 96], F32); d_krr = Dep()
            C.G(lambda e: e.memset(krr[:], 0.0), w=[d_krr])
            k1 = kr_all[:, :, 0:16]; k2 = kr_all[:, :, 16:32]
            C.V(lambda e: e.tensor_tensor(out=ta[:], in0=k1, in1=cos_t[:], op=ALU.mult), r=[d_kr, d_cs], w=[d_tt])
            C.V(lambda e: e.tensor_tensor(out=tb_[:], in0=k2, in1=sin_t[:], op=ALU.mult), r=[d_kr, d_cs], w=[d_tt])
            C.V(lambda e: e.tensor_tensor(out=krr[:, :, 64:80], in0=ta[:], in1=tb_[:], op=ALU.subtract), r=[d_tt], w=[d_krr])
            C.V(lambda e: e.tensor_tensor(out=tc[:], in0=k2, in1=cos_t[:], op=ALU.mult), r=[d_kr, d_cs], w=[d_tt])
            C.V(lambda e: e.tensor_tensor(out=td[:], in0=k1, in1=sin_t[:], op=ALU.mult), r=[d_kr, d_cs], w=[d_tt])
            C.V(lambda e: e.tensor_tensor(out=krr[:, :, 80:96], in0=tc[:], in1=td[:], op=ALU.add), r=[d_tt], w=[d_krr])
            for t4 in range(4):
                bk = 7
                for i in range(4):
                    C.tr(C.pb(bk)[0:96, i * 128:(i + 1) * 128], krr[:, t4 * 4 + i, :], C.ident_f[:],
                         r=[d_krr, C.d_const], w=[C.psd[bk]])
                for b in range(2):
                    C.V(lambda e, b=b, t4=t4, bk=bk: e.tensor_copy(out=kT[b][64:96, t4 * 512:(t4 + 1) * 512],
                                                                 in_=C.pb(bk)[64:96, :]),
                        r=[C.psd[bk]], w=[d_kT[b]])
            lnd = [C.sb(PB, [128, 512], F32) for _ in range(2)]; rden = [C.sb(PB, [128, 512], F32) for _ in range(2)]
            ot = [C.sb(PB, [128, 512], F32) for _ in range(2)]
            d_nrm = deps(2)
            PT = [C.sb(PB, [128, 512], BF16) for _ in range(3)]; d_PT = deps(3)
            QS = 96 ** -0.5

            def proj_units(h, b):
                for t4 in range(4):
                    bk = 7
                    for i in range(4):
                        t = t4 * 4 + i
                        for kc in range(3):
                            C.mm(C.pb(bk)[:, i * 96:(i + 1) * 96], cqnT[:, kc, t * 128:(t + 1) * 128],
                                 wuq[:, kc, h * 96:(h + 1) * 96], kc == 0, kc == 2,
                                 r=[d_cq[t], d_wuq], w=[C.psd[bk]])
                    C.V(lambda e, t4=t4, bk=bk: e.tensor_scalar(
                        out=qtok[:, t4 * 4:(t4 + 1) * 4, :], in0=C.pb(bk)[:, 0:384].rearrange("p (a c) -> p a c", a=4),
                        scalar1=QS, scalar2=None, op0=ALU.mult), r=[C.psd[bk]], w=[d_qtok])
                    yield
                q1 = qtok[:, :, 64:80]; q2 = qtok[:, :, 80:96]
                C.V(lambda e: e.tensor_tensor(out=ta[:], in0=q1, in1=cos_t[:], op=ALU.mult), r=[d_qtok, d_cs], w=[d_tt])
                C.V(lambda e: e.tensor_tensor(out=tb_[:], in0=q2, in1=sin_t[:], op=ALU.mult), r=[d_qtok, d_cs], w=[d_tt])
                C.V(lambda e: e.tensor_tensor(out=tc[:], in0=q2, in1=cos_t[:], op=ALU.mult), r=[d_qtok, d_cs], w=[d_tt])
                C.V(lambda e: e.tensor_tensor(out=td[:], in0=q1, in1=sin_t[:], op=ALU.mult), r=[d_qtok, d_cs], w=[d_tt])
                C.V(lambda e: e.tensor_tensor(out=q1, in0=ta[:], in1=tb_[:], op=ALU.subtract), r=[d_tt], w=[d_qtok])
                C.V(lambda e: e.tensor_tensor(out=q2, in0=tc[:], in1=td[:], op=ALU.add), r=[d_tt], w=[d_qtok])
                yield
                for t4 in range(4):
                    bk = 7
                    for i in range(4):
                        C.tr(C.pb(bk)[0:96, i * 128:(i + 1) * 128], qtok[:, t4 * 4 + i, :], C.ident_f[:],
                             r=[d_qtok, C.d_const], w=[C.psd[bk]])
                    C.V(lambda e, t4=t4, bk=bk: e.tensor_copy(out=qT[b][0:96, t4 * 512:(t4 + 1) * 512],
                                                            in_=C.pb(bk)[0:96, :]), r=[C.psd[bk]], w=[d_qT[b]])
                    yield
                for tb in range(4):
                    bk = 7
                    for kc in range(2):
                        C.mm(C.pb(bk)[0:64, :], wukv[:, kc, h * 128:h * 128 + 64], ckvT[:, kc, tb * 512:(tb + 1) * 512],
                             kc == 0, kc == 1, r=[d_ckv[4 * tb + i] for i in range(4)] + [d_wukv], w=[C.psd[bk]])
                    C.V(lambda e, tb=tb, bk=bk: e.tensor_copy(out=kT[b][0:64, tb * 512:(tb + 1) * 512],
                                                            in_=C.pb(bk)[0:64, :]), r=[C.psd[bk]], w=[d_kT[b]])
                    yield
                for t8 in range(2):
                    bk = 7
                    for i in range(8):
                        t = t8 * 8 + i
                        for kc in range(2):
                            C.mm(C.pb(bk)[:, i * 64:(i + 1) * 64], ckvT[:, kc, t * 128:(t + 1) * 128],
                                 wukv[:, kc, h * 128 + 64:h * 128 + 128], kc == 0, kc == 1,
                                 r=[d_ckv[t], d_wukv], w=[C.psd[bk]])
                    C.V(lambda e, t8=t8, bk=bk: e.tensor_copy(out=Vh[b][:, t8 * 8:(t8 + 1) * 8, :],
                                                            in_=C.pb(bk).rearrange("p (a c) -> p a c", a=8)),
                        r=[C.psd[bk]], w=[d_V[b]])
                    yield

            items = []
            for h in range(H):
                for qc in range(4):
                    nj = 4 * qc + 4
                    for j in range(nj):
                        items.append((h, qc, j, nj))

            def stage_a(i):
                h, qc, j, nj = items[i]
                b = h % 2
                col0 = max(0, j * 128 - qc * 512)
                bS = i % 3
                pt = PT[i % 3]; d_pt = d_PT[i % 3]
                C.mm(C.pb(bS)[:, col0:512], kT[b][:, j * 128:(j + 1) * 128],
                     qT[b][:, qc * 512 + col0:(qc + 1) * 512], True, True,
                     r=[d_kT[b], d_qT[b]], w=[C.psd[bS]])
                C.act(pt[:, col0:512], C.pb(bS)[:, col0:512], AF.Exp, r=[C.psd[bS]], w=[d_pt])
                if j >= 4 * qc:
                    C.G(lambda e, pt=pt, col0=col0: e.tensor_tensor(out=pt[:, col0:col0 + 128],
                                                                  in0=pt[:, col0:col0 + 128], in1=tri[:], op=ALU.mult),
                        r=[d_pt, d_k], w=[d_pt])

            def stage_b(i):
                h, qc, j, nj = items[i]
                b = h % 2
                r0 = (h % 2) * 64
                c = h // 2
                n = h * 4 + qc
                bO = 3 + n % 2; bD = 5 + n % 2
                col0 = max(0, j * 128 - qc * 512)
                pt = PT[i % 3]; d_pt = d_PT[i % 3]
                C.mm(C.pb(bO)[r0:r0 + 64, col0:512], Vh[b][:, j, :], pt[:, col0:512], j == 0, j == nj - 1,
                     r=[d_V[b], d_pt], w=[C.psd[bO]])
                C.mm(C.pb(bD)[r0:r0 + 64, col0:512], ones64[:], pt[:, col0:512], j == 0, j == nj - 1,
                     r=[d_k, d_pt], w=[C.psd[bD]])
                if j == nj - 1:
                    rs = slice(r0, r0 + 64)
                    nb = n % 2
                    C.act(lnd[nb][rs, :], C.pb(bD)[rs, :], AF.Ln, r=[C.psd[bD]], w=[d_nrm[nb]])
                    C.act(rden[nb][rs, :], lnd[nb][rs, :], AF.Exp, r=[d_nrm[nb]], w=[d_nrm[nb]], scale=-1.0)
                    C.V(lambda e, rs=rs, bO=bO, nb=nb: e.tensor_tensor(out=ot[nb][rs, :], in0=C.pb(bO)[rs, :], in1=rden[nb][rs, :], op=ALU.mult),
                        r=[C.psd[bO], d_nrm[nb]], w=[d_nrm[nb]])
                    C.V(lambda e, rs=rs, c=c, qc=qc, nb=nb: e.tensor_tensor(out=sgT[rs, c, qc * 512:(qc + 1) * 512], in0=ot[nb][rs, :],
                                                                           in1=sgT[rs, c, qc * 512:(qc + 1) * 512], op=ALU.mult),
                        r=[d_nrm[nb], d_sg4[qc]], w=[d_sg4[qc]])

            for _ in proj_units(0, 0):
                pass
            NI = len(items)
            units = None
            LOOK = 2
            for i in range(min(LOOK, NI)):
                stage_a(i)
            for i in range(NI):
                h, qc, j, nj = items[i]
                if qc == 0 and j == 0 and h + 1 < H:
                    units = proj_units(h + 1, (h + 1) % 2)
                if i + LOOK < NI:
                    if items[i + LOOK][0] != h and units is not None:
                        for _ in units:
                            pass
                        units = None
                    stage_a(i + LOOK)
                stage_b(i)
                if units is not None:
                    if next(units, "done") == "done":
                        units = None
            d_go = [d_sg4[t // 4] for t in range(NT)]
            out_proj(C, sgT, d_go, 8, None, h_in, d_hin, h_out, d_hout, banks=(0, 1, 2, 7), wo_pre=(wo, d_wo))
            C.release()
        C.release()


LAYER_FNS[0] = layer_mla

def layer_gla(C, W, h_in, d_hin, h_out, d_hout):
    w_in = W["gla_w_in"]
    with ExitStack() as L:
        qtT = C.sb(L, [128, 4, S_], BF16); d_qt = deps(4)
        ktT = C.sb(L, [128, 4, S_], BF16); d_kt = deps(4)
        vtok = C.sb(L, [128, NT, 1024], BF16); d_v = deps(NT)
        gsg = C.sb(L, [128, NT, 1024], BF16); d_gsg = deps(NT)
        elast = C.sb(L, [128, 4, 32], F32); d_el = Dep()
        with ExitStack() as PA:
            uT = C.sb(PA, [128, 8, S_], BF16); d_uT = deps(NT)
            norm_T(C, PA, h_in, W["norm_g"], uT, d_uT, d_hin)
            d_uTb = [[d_uT[4 * tb + i] for i in range(4)] for tb in range(4)]
            wgk = C.sb(PA, [128, 8, 16], BF16); d_wgk = Dep()
            C.load_w(wgk[:], w_in[:, 3072:3088], d_wgk)
            gkT = C.sb(PA, [32, S_], BF16); d_gkT = Dep()
            wgk2 = C.sb(PA, [32, 512], BF16); d_wgk2 = Dep()
            C.V(lambda e: e.memset(gkT[:], 0.0), w=[d_gkT])
            C.V(lambda e: e.memset(wgk2[:], 0.0), w=[d_wgk2])
            C.DG(lambda e: e.dma_start(out=wgk2[0:16, :], in_=W["gla_w_gk2"]), w=[d_wgk2])
            for tb in range(4):
                bk = 2 + tb % 2
                for kc in range(8):
                    C.mm(C.pb(bk)[0:16, :], wgk[:, kc, :], uT[:, kc, tb * 512:(tb + 1) * 512], kc == 0, kc == 7,
                         r=d_uTb[tb] + [d_wgk], w=[C.psd[bk]])
                C.V(lambda e, tb=tb, bk=bk: e.tensor_copy(out=gkT[0:16, tb * 512:(tb + 1) * 512], in_=C.pb(bk)[0:16, :]),
                    r=[C.psd[bk]], w=[d_gkT])
            nbgk = C.sb(PA, [128, 4], F32); d_nb = Dep()
            C.load_col(nbgk[:], W["gla_b_gk"], d_nb, 4)
            C.V(lambda e: e.tensor_scalar(out=nbgk[:], in0=nbgk[:], scalar1=-1.0, scalar2=None, op0=ALU.mult),
                r=[d_nb], w=[d_nb])
            rmask = C.sb(PA, [128, S_], BF16); d_rm = Dep()
            C.D(lambda e: e.dma_start(out=rmask[:], in_=C.CD["rmask"]), w=[d_rm])
            Lh = C.sb(PA, [128, S_], F32); d_Lh = Dep()
            Bp = C.sb(PA, [128, S_], F32); d_Bp = Dep()
            wq = [C.sb(PA, [128, 8, 128], BF16) for _ in range(2)]; d_wq = deps(2)
            wk = [C.sb(PA, [128, 8, 128], BF16) for _ in range(2)]; d_wk = deps(2)
            QS = 128 ** -0.5
            for h in range(4):
                b = h % 2
                C.load_w(wq[b][:], w_in[:, h * 128:(h + 1) * 128], d_wq[b])
                C.load_w(wk[b][:], w_in[:, 512 + h * 128:512 + (h + 1) * 128], d_wk[b])
                for tb in range(4):
                    bk = 2 + tb % 2
                    C.mm(C.pb(bk), wgk2[:, h * 128:(h + 1) * 128], gkT[:, tb * 512:(tb + 1) * 512], True, True,
                         r=[d_wgk2, d_gkT], w=[C.psd[bk]])
                    C.act(Lh[:, tb * 512:(tb + 1) * 512], C.pb(bk), AF.Exp, r=[C.psd[bk], d_nb], w=[d_Lh],
                          scale=-1.0, bias=nbgk[:, h:h + 1])
                C.act(Lh[:], Lh[:], AF.Ln, r=[d_Lh], w=[d_Lh], bias=1.0)
                C.V(lambda e: e.tensor_tensor_scan(out=Bp[:], data0=rmask[:], data1=Lh[:], initial=0.0,
                                                   op0=ALU.mult, op1=ALU.add), r=[d_Lh, d_rm], w=[d_Bp])
                C.act(Lh[:], Bp[:], AF.Exp, r=[d_Bp], w=[d_Lh], scale=-1.0 / 16)
                C.act(Bp[:], Bp[:], AF.Exp, r=[d_Bp], w=[d_Bp], scale=1.0 / 16)
                C.G(lambda e, h=h: e.tensor_copy(out=elast[:, h, :],
                                                 in_=Lh[:].rearrange("p (n c) -> p n c", c=64)[:, :, 63]),
                    r=[d_Lh], w=[d_el])
                for tb in range(4):
                    bq = 4 + tb % 2; bkk = 6 + tb % 2
                    for kc in range(8):
                        C.mm(C.pb(bq), wq[b][:, kc, :], uT[:, kc, tb * 512:(tb + 1) * 512], kc == 0, kc == 7,
                             r=d_uTb[tb] + [d_wq[b]], w=[C.psd[bq]])
                    C.V(lambda e, h=h, tb=tb, bq=bq: e.scalar_tensor_tensor(
                        out=qtT[:, h, tb * 512:(tb + 1) * 512], in0=C.pb(bq), scalar=QS,
                        in1=Lh[:, tb * 512:(tb + 1) * 512], op0=ALU.mult, op1=ALU.mult),
                        r=[C.psd[bq], d_Lh], w=[d_qt[tb]])
                    for kc in range(8):
                        C.mm(C.pb(bkk), wk[b][:, kc, :], uT[:, kc, tb * 512:(tb + 1) * 512], kc == 0, kc == 7,
                             r=d_uTb[tb] + [d_wk[b]], w=[C.psd[bkk]])
                    C.V(lambda e, h=h, tb=tb, bkk=bkk: e.tensor_tensor(
                        out=ktT[:, h, tb * 512:(tb + 1) * 512], in0=C.pb(bkk), in1=Bp[:, tb * 512:(tb + 1) * 512],
                        op=ALU.mult), r=[C.psd[bkk], d_Bp], w=[d_kt[tb]])
            wv = [C.sb(PA, [128, 8, 512], BF16) for _ in range(2)]; d_wv = deps(2)
            for i in range(2):
                C.load_w(wv[i][:], w_in[:, 1024 + i * 512:1024 + (i + 1) * 512], d_wv[i])
            i = 0
            for t in range(NT):
                for nb in range(2):
                    bk = (0, 1, 2, 3)[i % 4]; i += 1
                    for kc in range(8):
                        C.mm(C.pb(bk), uT[:, kc, t * 128:(t + 1) * 128], wv[nb][:, kc, :], kc == 0, kc == 7,
                             r=[d_uT[t], d_wv[nb]], w=[C.psd[bk]])
                    C.act(vtok[:, t, nb * 512:(nb + 1) * 512], C.pb(bk), AF.Copy, r=[C.psd[bk]], w=[d_v[t]])
            for i in range(2):
                C.load_w(wv[i][:], w_in[:, 2048 + i * 512:2048 + (i + 1) * 512], d_wv[i])
            go_bc = C.sb(PA, [128, 256], F32); d_gob = Dep()
            C.load_bc(go_bc[:], W["gla_g_o"], d_gob)
            sgt = [C.sb(PA, [128, 1024], F32) for _ in range(2)]; d_sgt = deps(2)
            i = 0
            for t in range(NT):
                b = t % 2
                for nb in range(2):
                    bk = (4, 5, 6, 7)[i % 4]; i += 1
                    for kc in range(8):
                        C.mm(C.pb(bk), uT[:, kc, t * 128:(t + 1) * 128], wv[nb][:, kc, :], kc == 0, kc == 7,
                             r=[d_uT[t], d_wv[nb]], w=[C.psd[bk]])
                    C.act(sgt[b][:, nb * 512:(nb + 1) * 512], C.pb(bk), AF.Silu, r=[C.psd[bk]], w=[d_sgt[b]])
                C.G(lambda e, t=t, b=b: e.tensor_tensor(
                    out=gsg[:, t, :].rearrange("p (a c) -> p a c", a=4), in0=sgt[b][:].rearrange("p (a c) -> p a c", a=4),
                    in1=go_bc[:].unsqueeze(1).to_broadcast([128, 4, 256]), op=ALU.mult),
                    r=[d_sgt[b], d_gob], w=[d_gsg[t]])
            C.release()
        with ExitStack() as PB:
            goT = C.sb(PB, [128, 8, S_], BF16); d_go = deps(NT)
            wo = C.sb(PB, [128, 8, D_], BF16); d_wo = Dep()
            C.load_w(wo[:, 0:4, :], W["gla_w_out"][0:512, :], d_wo)
            C.load_w(wo[:, 4:8, :], W["gla_w_out"][512:1024, :], d_wo)
            mask2 = C.sb(PB, [128, 128], BF16); d_m2 = Dep()
            C.D(lambda e: e.dma_start(out=mask2[:], in_=C.CD["mask2"]), w=[d_m2])
            tmpS = C.sb(PB, [128, 4, 256], F32); d_tmpS = Dep()
            Sbf = [C.sb(PB, [128, 4, 256], BF16) for _ in range(2)]; d_Sbf = deps(2)
            C.V(lambda e: e.memset(Sbf[0][:], 0.0), w=[d_Sbf[0]])
            attm = [C.sb(PB, [128, 4, 128], BF16) for _ in range(2)]; d_attm = deps(2)
            ktk = [C.sb(PB, [128, 4, 128], BF16) for _ in range(2)]; d_ktk = deps(2)
            ss = C.sb(PB, [128, NT, 4], F32); rs = C.sb(PB, [128, NT, 4], F32); d_ss = deps(NT); d_rs = deps(NT)
            junk = C.sb(PB, [128, 256], BF16); d_junk = Dep()
            gotok = [C.sb(PB, [128, 1024], BF16) for _ in range(2)]; d_gotok = deps(2)
            osb = [C.sb(PB, [128, 1024], F32) for _ in range(2)]; d_osb = deps(2)

            def obank(t, h):
                return (2 + 2 * (t % 2) + h // 2, (h % 2) * 256)

            def g0(t):
                b = t % 2
                tb = t // 4
                tsl = slice(t * 128, (t + 1) * 128)
                pv = C.pbh(0)
                for h in range(4):
                    C.tr(pv[:, h * 128:(h + 1) * 128], ktT[:, h, tsl], C.ident_bf[:], r=[d_kt[tb], C.d_const], w=[C.psd[0]])
                for h in range(4):
                    C.mm(C.pb(1)[:, h * 128:(h + 1) * 128], ktT[:, h, tsl], qtT[:, h, tsl], True, True,
                         r=[d_kt[tb], d_qt[tb]], w=[C.psd[1]])
                yield
                C.V(lambda e, b=b, pv=pv: e.tensor_copy(out=ktk[b][:], in_=pv[:, 0:512].rearrange("p (a c) -> p a c", a=4)),
                    r=[C.psd[0]], w=[d_ktk[b]])
                C.V(lambda e, b=b: e.tensor_tensor(out=attm[b][:], in0=C.pb(1).rearrange("p (a c) -> p a c", a=4),
                                                   in1=mask2[:].unsqueeze(1).to_broadcast([128, 4, 128]), op=ALU.mult),
                    r=[C.psd[1], d_m2], w=[d_attm[b]])
                yield
                for h in range(4):
                    bk, c0 = obank(t, h)
                    C.T(lambda e, bk=bk, c0=c0, b=b, h=h, t=t: e.matmul(
                        C.pb(bk)[:, c0:c0 + 256], lhsT=attm[b][:, h, :], rhs=vtok[:, t, h * 256:(h + 1) * 256],
                        start=(h % 2 == 0), stop=False, skip_group_check=True), r=[d_attm[b], d_v[t]], w=[C.psd[bk]])

            def g1(t):
                b = t % 2
                tb = t // 4
                for half in range(2):
                    r0 = half * 64
                    n = 2 * t + half
                    for h in range(4):
                        bk, c0 = obank(t, h)
                        C.T(lambda e, bk=bk, c0=c0, h=h, t=t, r0=r0, half=half: e.matmul(
                            C.pb(bk)[r0:r0 + 64, c0:c0 + 256], lhsT=qtT[:, h, t * 128 + r0:t * 128 + r0 + 64],
                            rhs=Sbf[half][:, h, :], start=False, stop=(half == 1), skip_group_check=True),
                            r=[d_qt[tb], d_Sbf[half]], w=[C.psd[bk]])
                    for h in range(4):
                        bkS = 6 + h // 2
                        C.mm(C.pb(bkS)[:, (h % 2) * 256:(h % 2 + 1) * 256], ktk[b][r0:r0 + 64, h, :],
                             vtok[r0:r0 + 64, t, h * 256:(h + 1) * 256], True, True,
                             r=[d_ktk[b], d_v[t]], w=[C.psd[bkS]])
                    C.V(lambda e, half=half: e.tensor_tensor(out=tmpS[:], in0=C.ps[:, 6 * 512:8 * 512].rearrange("p (a c) -> p a c", a=4),
                                                            in1=Sbf[half][:], op=ALU.add), r=[C.psd[6], C.psd[7], d_Sbf[half]], w=[d_tmpS])
                    C.V(lambda e, n=n, half=half: e.tensor_tensor(out=Sbf[1 - half][:], in0=tmpS[:],
                                                                 in1=elast[:, :, n:n + 1].to_broadcast([128, 4, 256]), op=ALU.mult),
                        r=[d_tmpS, d_el], w=[d_Sbf[1 - half]])
                    yield
                for i_ in range(2):
                    bk = 2 + 2 * (t % 2) + i_
                    C.act(osb[b][:, i_ * 512:(i_ + 1) * 512], C.pb(bk), AF.Copy, r=[C.psd[bk]], w=[d_osb[b]])

            def g2(t):
                b = t % 2
                for h in range(4):
                    C.act(junk[:], osb[b][:, h * 256:(h + 1) * 256], AF.Square, r=[d_osb[b]], w=[d_junk, d_ss[t]],
                          accum_out=ss[:, t, h:h + 1])
                yield
                rstd_ops(C, rs[:, t, :], ss[:, t, :], 256, [d_ss[t]], [d_rs[t]])
                yield
                for h in range(4):
                    C.V(lambda e, t=t, b=b, h=h: e.scalar_tensor_tensor(
                        out=gotok[b][:, h * 256:(h + 1) * 256], in0=osb[b][:, h * 256:(h + 1) * 256], scalar=rs[:, t, h:h + 1],
                        in1=gsg[:, t, h * 256:(h + 1) * 256], op0=ALU.mult, op1=ALU.mult),
                        r=[d_osb[b], d_rs[t], d_gsg[t]], w=[d_gotok[b]])
                yield
                pv = C.pbh(0)
                for c in range(8):
                    C.tr(pv[:, c * 128:(c + 1) * 128], gotok[b][:, c * 128:(c + 1) * 128], C.ident_bf[:],
                         r=[d_gotok[b], C.d_const], w=[C.psd[0]])
                C.act(goT[:, :, t * 128:(t + 1) * 128], pv.rearrange("p (c n) -> p c n", c=8), AF.Copy,
                      r=[C.psd[0]], w=[d_go[t]])

            LV = {1: ("s1", "s0", "s2"), 2: ("s0", "s1", "s2"), 3: ("s1", "s0", "s2"), 4: ("s2",)}
            for s_ in range(NT + 2):
                gens = {}
                if 0 <= s_ - 2 < NT:
                    gens["s2"] = g2(s_ - 2)
                if 0 <= s_ - 1 < NT:
                    gens["s1"] = g1(s_ - 1)
                if s_ < NT:
                    gens["s0"] = g0(s_)
                for lvl in (1, 2, 3, 4):
                    for nm in LV[lvl]:
                        if nm in gens:
                            next(gens[nm], None)
                for gn_ in gens.values():
                    for _ in gn_:
                        pass
            out_proj(C, goT, d_go, 8, None, h_in, d_hin, h_out, d_hout, banks=(1, 2, 3, 4), wo_pre=(wo, d_wo))
            C.release()
        C.release()


LAYER_FNS[1] = layer_gla

def layer_ssd(C, W, h_in, d_hin, h_out, d_hout):
    w_in = W["ssd_w_in"]
    G_ = 8
    with ExitStack() as L:
        goT = C.sb(L, [128, 16, S_], BF16); d_go = deps(NT)
        with ExitStack() as PA:
            uT = C.sb(PA, [128, 8, S_], BF16); d_uT = deps(NT)
            norm_T(C, PA, h_in, W["norm_g"], uT, d_uT, d_hin)
            d_uTb = [[d_uT[4 * tb + i] for i in range(4)] for tb in range(4)]
            U2b = C.sb(PA, [128, 128], BF16); V2b = C.sb(PA, [128, 128], BF16)
            U2f = C.sb(PA, [128, 128], F32); V2f = C.sb(PA, [128, 128], F32)
            onA = C.sb(PA, [128, 128], F32); onB = C.sb(PA, [128, 128], F32); d_k = Dep()
            negm = C.sb(PA, [128, 512], BF16)
            C.D(lambda e: e.dma_start(out=negm[:], in_=C.CD["negm4"]), w=[d_k])
            for dst, nm in ((U2b, "U2b"), (V2b, "mask2"), (U2f, "U2f"), (V2f, "V2f"), (onA, "onesA"), (onB, "onesB")):
                C.D(lambda e, dst=dst, nm=nm: e.dma_start(out=dst[:], in_=C.CD[nm]), w=[d_k])
            dtb = C.sb(PA, [128, 32], F32); alog = C.sb(PA, [128, 32], F32); dsk = C.sb(PA, [128, 32], F32)
            gn = C.sb(PA, [128, 2048], F32); d_p = Dep()
            C.load_bc(dtb[:], W["ssd_dt_bias"], d_p)
            C.load_bc(alog[:], W["ssd_a_log"], d_p)
            C.load_bc(dsk[:], W["ssd_d"], d_p)
            C.load_bc(gn[:], W["ssd_g_norm"], d_p)
            cw = C.sb(PA, [128, 4, 32], F32); cb = C.sb(PA, [128, 32], F32); d_cw = Dep()
            for k in range(4):
                C.load_col(cw[:, k, :], W["ssd_conv_w"][k, :], d_cw, 32)
            C.load_col(cb[:], W["ssd_conv_b"], d_cw, 32)
            wdt = C.sb(PA, [128, 8, 32], BF16); d_wdt = Dep()
            C.load_w(wdt[:], w_in[:, 6144:6176], d_wdt)
            dtt = C.sb(PA, [128, NT, 32], F32); a_tok = C.sb(PA, [128, NT, 32], F32)
            ecs = C.sb(PA, [128, NT, 32], F32); dec = C.sb(PA, [128, NT, 32], F32)
            elast = C.sb(PA, [128, 32, 32], F32); eA = C.sb(PA, [128, 32], F32)
            d_dt = Dep(); d_a = Dep(); d_ecs = Dep(); d_dec = Dep(); d_el = Dep()
            for t in range(NT):
                for kc in range(8):
                    C.mm(C.pb(0)[:, t * 32:(t + 1) * 32], uT[:, kc, t * 128:(t + 1) * 128], wdt[:, kc, :], kc == 0, kc == 7,
                         r=[d_uT[t], d_wdt], w=[C.psd[0]])
            C.V(lambda e: e.tensor_tensor(out=dtt[:], in0=C.pb(0).rearrange("p (a c) -> p a c", a=NT),
                                          in1=dtb[:].unsqueeze(1).to_broadcast([128, NT, 32]), op=ALU.add),
                r=[C.psd[0], d_p], w=[d_dt])
            C.act(dtt[:], dtt[:], AF.Exp, r=[d_dt], w=[d_dt])
            C.act(dtt[:], dtt[:], AF.Ln, r=[d_dt], w=[d_dt], bias=1.0)
            C.act(eA[:], alog[:], AF.Exp, r=[d_p], w=[d_a])
            C.V(lambda e: e.scalar_tensor_tensor(out=a_tok[:], in0=dtt[:], scalar=-1.0,
                                                 in1=eA[:].unsqueeze(1).to_broadcast([128, NT, 32]),
                                                 op0=ALU.mult, op1=ALU.mult), r=[d_dt, d_a], w=[d_a])
            for t in range(NT):
                C.mm(C.pb(1)[:, t * 32:(t + 1) * 32], V2f[:], a_tok[:, t, :], True, True, r=[d_k, d_a], w=[C.psd[1]])
            C.act(ecs[:], C.pb(1).rearrange("p (a c) -> p a c", a=NT), AF.Exp, r=[C.psd[1]], w=[d_ecs])
            for t in range(NT):
                C.mm(C.pb(2)[:, t * 32:(t + 1) * 32], U2f[:], a_tok[:, t, :], True, True, r=[d_k, d_a], w=[C.psd[2]])
            C.act(dec[:], C.pb(2).rearrange("p (a c) -> p a c", a=NT), AF.Exp, r=[C.psd[2]], w=[d_dec])
            for t in range(NT):
                for half in range(2):
                    n = 2 * t + half
                    bk = 3 + n // 16
                    C.mm(C.pb(bk)[:, (n % 16) * 32:(n % 16 + 1) * 32], (onA, onB)[half][:], a_tok[:, t, :], True, True,
                         r=[d_k, d_a], w=[C.psd[bk]])
            C.act(elast[:], C.ps[:, 3 * 512:5 * 512].rearrange("p (a c) -> p a c", a=32), AF.Exp,
                  r=[C.psd[3], C.psd[4]], w=[d_el])
            xTc = [C.sb(PA, [128, S_], BF16) for _ in range(2)]
            BT = C.sb(PA, [128, S_], BF16); CT = C.sb(PA, [128, S_], BF16)
            d_xT = deps(2); d_BT = Dep(); d_CT = Dep()
            x_tok = C.sb(PA, [128, NT, 256], BF16); d_xtok = deps(4)
            B_tok = C.sb(PA, [128, NT, 128], BF16); d_Btok = deps(2)
            sz = C.sb(PA, [128, NT, 256], BF16); d_sz = deps(NT)
            ub = [C.sb(PA, [128, 3 + S_], BF16) for _ in range(2)]; d_ub = deps(2)
            for b in range(2):
                C.V(lambda e, b=b: e.memset(ub[b][:, 0:3], 0.0), w=[d_ub[b]])
            wc = [C.sb(PA, [128, 8, 128], BF16) for _ in range(2)]; d_wc = deps(2)
            dg = [C.sb(PA, [128, 4, 128], BF16) for _ in range(2)]; d_dg = deps(2)
            wz = C.sb(PA, [128, 8, 256], BF16); d_wz = Dep()
            Sst = C.sb(PA, [128, 4, 64], F32); tmpS = C.sb(PA, [128, 4, 64], F32); d_S = Dep(); d_tmpS = Dep()
            Sbf = [C.sb(PA, [128, 256], BF16) for _ in range(2)]; d_Sbf = deps(2)
            cbm = [C.sb(PA, [128, 128], F32) for _ in range(2)]; d_cbm = deps(2)
            aV = [C.sb(PA, [128, 4, 128], BF16) for _ in range(2)]; d_aV = deps(2)
            lm = [C.sb(PA, [128, 4, 128], F32) for _ in range(2)]; d_lm = deps(2)
            Mh = [C.sb(PA, [128, 4, 128], BF16) for _ in range(2)]; d_M = deps(2)
            xdt = [C.sb(PA, [128, 4, 64], BF16) for _ in range(2)]; d_xdt = deps(2)
            xdd = [C.sb(PA, [128, 4, 64], BF16) for _ in range(2)]; d_xdd = deps(2)
            xD = [C.sb(PA, [128, 4, 64], BF16) for _ in range(2)]; d_xD = deps(2)
            t1 = [C.sb(PA, [128, 4, 64], F32) for _ in range(2)]; d_t1 = deps(2)
            yz = [C.sb(PA, [128, 256], F32) for _ in range(2)]; d_yz = deps(2)
            yn = [C.sb(PA, [128, 256], BF16) for _ in range(2)]; d_yn = deps(2)
            ss = C.sb(PA, [128, G_, NT], F32); rs = C.sb(PA, [128, G_, NT], F32)
            junk = C.sb(PA, [128, 256], BF16); d_junk = Dep()
            ci = 0
            d_dS = [deps(2), deps(2)]
            for g in range(G_):
                hs = slice(4 * g, 4 * g + 4)
                chunks = ((2 * g, xTc[0], d_xT[0]), (2 * g + 1, xTc[1], d_xT[1]), (16 + g, BT, d_BT), (24 + g, CT, d_CT))

                def proj_c(k_, b):
                    cc = chunks[k_][0]
                    C.load_w(wc[b][:], w_in[:, 2048 + cc * 128:2048 + (cc + 1) * 128], d_wc[b])
                    for k in range(4):
                        C.V(lambda e, b=b, k=k, cc=cc: e.tensor_scalar(out=dg[b][:, k, :], in0=C.ident_bf[:],
                                                                       scalar1=cw[:, k, cc:cc + 1], scalar2=None, op0=ALU.mult),
                            r=[d_cw, C.d_const], w=[d_dg[b]])
                    for tb in range(4):
                        bk = tb
                        for kc in range(8):
                            C.mm(C.pb(bk), wc[b][:, kc, :], uT[:, kc, tb * 512:(tb + 1) * 512], kc == 0, kc == 7,
                                 r=d_uTb[tb] + [d_wc[b]], w=[C.psd[bk]])
                        C.act(ub[b][:, 3 + tb * 512:3 + (tb + 1) * 512], C.pb(bk), AF.Copy, r=[C.psd[bk]], w=[d_ub[b]])

                def conv_c(k_, b):
                    cc, dst, d_dst = chunks[k_]
                    for tb in range(4):
                        bk = 4 + tb
                        for k in range(4):
                            C.mm(C.pb(bk), dg[b][:, k, :], ub[b][:, tb * 512 + k:tb * 512 + k + 512], k == 0, k == 3,
                                 r=[d_ub[b], d_dg[b]], w=[C.psd[bk]])
                        C.act(dst[:, tb * 512:(tb + 1) * 512], C.pb(bk), AF.Silu, r=[C.psd[bk], d_cw], w=[d_dst],
                              bias=cb[:, cc:cc + 1])

                proj_c(0, 0)
                for k_ in range(4):
                    if k_ + 1 < 4:
                        proj_c(k_ + 1, (k_ + 1) % 2)
                    conv_c(k_, k_ % 2)
                for t4 in range(4):
                    bk = t4 % 2
                    pv = C.pbh(bk)
                    for i in range(4):
                        for c in range(2):
                            C.tr(pv[:, (i * 2 + c) * 128:(i * 2 + c + 1) * 128], xTc[c][:, (t4 * 4 + i) * 128:(t4 * 4 + i + 1) * 128],
                                 C.ident_bf[:], r=[d_xT[c], C.d_const], w=[C.psd[bk]])
                    C.act(x_tok[:, t4 * 4:(t4 + 1) * 4, :], pv.rearrange("p (a c) -> p a c", a=4), AF.Copy,
                          r=[C.psd[bk]], w=[d_xtok[t4]])
                for t8 in range(2):
                    bk = 2 + t8
                    pv = C.pbh(bk)
                    for i in range(8):
                        C.tr(pv[:, i * 128:(i + 1) * 128], BT[:, (t8 * 8 + i) * 128:(t8 * 8 + i + 1) * 128], C.ident_bf[:],
                             r=[d_BT, C.d_const], w=[C.psd[bk]])
                    C.V(lambda e, t8=t8, pv=pv: e.tensor_copy(out=B_tok[:, t8 * 8:(t8 + 1) * 8, :],
                                                             in_=pv.rearrange("p (a c) -> p a c", a=8)),
                        r=[C.psd[bk]], w=[d_Btok[t8]])
                C.load_w(wz[:], w_in[:, g * 256:(g + 1) * 256], d_wz)
                for t2 in range(8):
                    bk = 4 + t2 % 4
                    for i in range(2):
                        t = t2 * 2 + i
                        for kc in range(8):
                            C.mm(C.pb(bk)[:, i * 256:(i + 1) * 256], uT[:, kc, t * 128:(t + 1) * 128], wz[:, kc, :],
                                 kc == 0, kc == 7, r=[d_uT[t], d_wz], w=[C.psd[bk]])
                    C.act(sz[:, t2 * 2:t2 * 2 + 2, :], C.pb(bk).rearrange("p (a c) -> p a c", a=2), AF.Silu,
                          r=[C.psd[bk]], w=[d_sz[t2 * 2], d_sz[t2 * 2 + 1]])
                C.V(lambda e: e.memset(Sbf[0][:], 0.0), w=[d_Sbf[0]])
                def st0(t):
                    b = t % 2
                    tsl = slice(t * 128, (t + 1) * 128)
                    xt_ = x_tok[:, t, :].rearrange("p (a c) -> p a c", a=4)
                    d_x = d_xtok[t // 4]
                    for h_ in range(4):
                        C.act(aV[b][:, h_, :], V2b[:], AF.Copy, r=[d_a, d_k], w=[d_aV[b]],
                              scale=a_tok[:, t, 4 * g + h_:4 * g + h_ + 1])
                    C.G(lambda e, hs=hs, b=b, t=t, xt_=xt_: e.tensor_tensor(out=xdt[b][:], in0=xt_,
                                                                   in1=dtt[:, t, hs].unsqueeze(2).to_broadcast([128, 4, 64]), op=ALU.mult),
                        r=[d_x, d_dt], w=[d_xdt[b]])
                    C.G(lambda e, hs=hs, b=b, t=t: e.tensor_tensor(out=xdd[b][:], in0=xdt[b][:],
                                                           in1=dec[:, t, hs].unsqueeze(2).to_broadcast([128, 4, 64]), op=ALU.mult),
                        r=[d_xdt[b], d_dec], w=[d_xdd[b]])
                    C.G(lambda e, hs=hs, b=b, xt_=xt_: e.tensor_tensor(out=xD[b][:], in0=xt_,
                                                              in1=dsk[:, hs].unsqueeze(2).to_broadcast([128, 4, 64]), op=ALU.mult),
                        r=[d_x, d_p], w=[d_xD[b]])
                    C.mm(C.pb(0)[:, 0:128], BT[:, tsl], CT[:, tsl], True, True, r=[d_BT, d_CT], w=[C.psd[0]])
                    yield
                    C.mm(C.pb(1), U2b[:], aV[b][:].rearrange("p a c -> p (a c)"), True, False, r=[d_k, d_aV[b]], w=[C.psd[1]])
                    C.mm(C.pb(1), C.ident_bf[:], negm[:], False, True, r=[C.d_const, d_k], w=[C.psd[1]])
                    for half in range(2):
                        r0 = half * 64
                        C.mm(C.pb(4 + half)[:, b * 256:(b + 1) * 256], B_tok[r0:r0 + 64, t, :],
                             xdd[b][r0:r0 + 64, :, :].rearrange("p a c -> p (a c)"),
                             True, True, r=[d_Btok[t // 8], d_xdd[b]], w=[C.psd[4 + half]])
                    yield
                    C.act(lm[b][:], C.pb(1).rearrange("p (a c) -> p a c", a=4), AF.Exp, r=[C.psd[1]], w=[d_lm[b]])
                    yield
                    C.V(lambda e, b=b: e.tensor_tensor(out=Mh[b][:], in0=lm[b][:],
                                                       in1=C.pb(0)[:, 0:128].unsqueeze(1).to_broadcast([128, 4, 128]), op=ALU.mult),
                        r=[d_lm[b], C.psd[0]], w=[d_M[b]])

                def st1(t):
                    b = t % 2
                    bY = 2 + b
                    C.mm(C.pb(bY)[:, 0:256], C.ident_bf[:], xD[b][:].rearrange("p a c -> p (a c)"), True, False,
                         r=[C.d_const, d_xD[b]], w=[C.psd[bY]])
                    for h in range(4):
                        C.mm(C.pb(bY)[:, h * 64:(h + 1) * 64], Mh[b][:, h, :], xdt[b][:, h, :], False, h == 3,
                             r=[d_M[b], d_xdt[b]], w=[C.psd[bY]])
                    for half in range(2):
                        r0 = half * 64
                        n = 2 * t + half
                        C.mm(C.pb(bY)[r0:r0 + 64, 256:512], CT[:, t * 128 + r0:t * 128 + r0 + 64], Sbf[half][:], True, True,
                             r=[d_CT, d_Sbf[half]], w=[C.psd[bY]])
                        C.V(lambda e, hs=hs, n=n, half=half: e.tensor_tensor(
                            out=tmpS[:], in0=Sbf[half][:].rearrange("p (a c) -> p a c", a=4),
                            in1=elast[:, n, hs].unsqueeze(2).to_broadcast([128, 4, 64]), op=ALU.mult),
                            r=[d_Sbf[half], d_el], w=[d_tmpS])
                        C.V(lambda e, half=half, b=b: e.tensor_tensor(
                            out=Sbf[1 - half][:].rearrange("p (a c) -> p a c", a=4),
                            in0=C.pb(4 + half)[:, b * 256:(b + 1) * 256].rearrange("p (a c) -> p a c", a=4),
                            in1=tmpS[:], op=ALU.add), r=[C.psd[4 + half], d_tmpS], w=[d_Sbf[1 - half]])
                        if half == 0:
                            yield

                def st2(t, g=g):
                    b = t % 2
                    bY = 2 + b
                    C.V(lambda e, hs=hs, b=b, t=t, bY=bY: e.tensor_tensor(out=t1[b][:], in0=C.pb(bY)[:, 256:512].rearrange("p (a c) -> p a c", a=4),
                                                           in1=ecs[:, t, hs].unsqueeze(2).to_broadcast([128, 4, 64]), op=ALU.mult),
                        r=[C.psd[bY], d_ecs], w=[d_t1[b]])
                    C.V(lambda e, b=b, bY=bY: e.tensor_tensor(out=t1[b][:], in0=C.pb(bY)[:, 0:256].rearrange("p (a c) -> p a c", a=4),
                                                       in1=t1[b][:], op=ALU.add), r=[C.psd[bY], d_t1[b]], w=[d_t1[b]])
                    yield
                    C.G(lambda e, b=b, t=t: e.tensor_tensor(out=yz[b][:], in0=t1[b][:].rearrange("p a c -> p (a c)"),
                                                           in1=sz[:, t, :], op=ALU.mult), r=[d_t1[b], d_sz[t]], w=[d_yz[b]])
                    yield
                    C.act(junk[:], yz[b][:], AF.Square, r=[d_yz[b]], w=[d_junk], accum_out=ss[:, g, t:t + 1])
                    rstd_ops(C, rs[:, g, t:t + 1], ss[:, g, t:t + 1], 256, [d_junk], [d_junk])
                    yield
                    C.V(lambda e, b=b, t=t, g=g: e.scalar_tensor_tensor(out=yn[b][:], in0=yz[b][:], scalar=rs[:, g, t:t + 1],
                                                                       in1=gn[:, g * 256:(g + 1) * 256], op0=ALU.mult, op1=ALU.mult),
                        r=[d_yz[b], d_junk, d_p], w=[d_yn[b]])
                    bk = 6 + (t // 4) % 2
                    pv = C.pbh(bk)
                    i = t % 4
                    for c in range(2):
                        C.tr(pv[:, (c * 4 + i) * 128:(c * 4 + i + 1) * 128], yn[b][:, c * 128:(c + 1) * 128], C.ident_bf[:],
                             r=[d_yn[b], C.d_const], w=[C.psd[bk]])
                    if i == 3:
                        t4 = t // 4
                        C.act(goT[:, 2 * g:2 * g + 2, t4 * 512:(t4 + 1) * 512], pv.rearrange("p (c m) -> p c m", c=2), AF.Copy,
                              r=[C.psd[bk]], w=[d_go[4 * t4 + j] for j in range(4)])

                LV = {1: ("s1", "s0", "s2"), 2: ("s0", "s1", "s2"), 3: ("s0", "s2"), 4: ("s0", "s2")}
                for s_ in range(NT + 2):
                    gens = {}
                    if 0 <= s_ - 2 < NT:
                        gens["s2"] = st2(s_ - 2)
                    if 0 <= s_ - 1 < NT:
                        gens["s1"] = st1(s_ - 1)
                    if s_ < NT:
                        gens["s0"] = st0(s_)
                    for lvl in (1, 2, 3, 4):
                        for nm in LV[lvl]:
                            if nm in gens:
                                next(gens[nm], None)
                    for gn_ in gens.values():
                        for _ in gn_:
                            pass
            C.dbg("dtt", dtt[:], [128, NT, 32], F32, [d_dt])
            C.dbg("a_tok", a_tok[:], [128, NT, 32], F32, [d_a])
            C.dbg("ecs", ecs[:], [128, NT, 32], F32, [d_ecs])
            C.dbg("dec", dec[:], [128, NT, 32], F32, [d_dec])
            C.dbg("elast", elast[:], [128, 32, 32], F32, [d_el])
            C.dbg("x_tok", x_tok[:], [128, NT, 256], BF16, d_xtok)
            C.dbg("B_tok", B_tok[:], [128, NT, 128], BF16, d_Btok)
            C.dbg("CT", CT[:], [128, S_], BF16, [d_CT])
            C.dbg("sz", sz[:], [128, NT, 256], BF16, d_sz)
            C.dbg("goT", goT[:], [128, 16, S_], BF16, d_go)
            C.dbg("Mh", Mh[1][:], [128, 4, 128], BF16, [d_M[1]])
            C.dbg("lm", lm[1][:], [128, 4, 128], F32, [d_lm[1]])
            C.dbg("yz", yz[1][:], [128, 256], F32, [d_yz[1]])
            C.release()
        out_proj(C, goT, d_go, 16, W["ssd_w_out"], h_in, d_hin, h_out, d_hout)
        C.release()


LAYER_FNS[3] = layer_ssd

WSPEC = {
    "norm_g": [4, 1024], "final_g": [1024],
    "mla_w_in": [1024, 1696], "mla_g_q": [384], "mla_w_uq": [384, 1536], "mla_g_kv": [256],
    "mla_w_ukv": [256, 2048], "mla_w_out": [1024, 1024],
    "gla_w_in": [1024, 3088], "gla_w_gk2": [16, 512], "gla_b_gk": [512], "gla_g_o": [256],
    "gla_w_out": [1024, 1024],
    "lru_w_in": [1024, 2560], "lru_conv_w": [4, 1280], "lru_conv_b": [1280], "lru_w_a": [10, 128, 128],
    "lru_b_a": [1280], "lru_w_x": [10, 128, 128], "lru_b_x": [1280], "lru_lam": [1280],
    "lru_w_out": [1280, 1024],
    "ssd_w_in": [1024, 6176], "ssd_conv_w": [4, 4096], "ssd_conv_b": [4096], "ssd_dt_bias": [32],
    "ssd_a_log": [32], "ssd_d": [32], "ssd_g_norm": [2048], "ssd_w_out": [2048, 1024],
}


def host_consts():
    c = {}
    c["ident_bf"] = np.eye(128, dtype=np.float32).astype(ml_dtypes.bfloat16)
    c["ident_f"] = np.eye(128, dtype=np.float32)
    k = np.arange(128)
    c["tri_bf"] = (k[None, :] >= k[:, None]).astype(np.float32).astype(ml_dtypes.bfloat16)
    c["ones_bf"] = np.ones((128, 64), np.float32).astype(ml_dtypes.bfloat16)
    invf = (np.float32(10000.0) ** (-np.arange(0, 32, 2, dtype=np.float32) / np.float32(32))).astype(np.float32)
    rm = np.ones((128, S_), np.float32); rm[:, ::64] = 0.0
    c["rmask"] = rm.astype(ml_dtypes.bfloat16)
    m2 = ((k[None, :] >= k[:, None]) & ((k[None, :] // 64) == (k[:, None] // 64))).astype(np.float32)
    c["mask2"] = m2.astype(ml_dtypes.bfloat16)
    same = (k[None, :] // 64) == (k[:, None] // 64)
    u2 = ((k[:, None] > k[None, :]) & same).astype(np.float32)
    c["U2b"] = u2.astype(ml_dtypes.bfloat16)
    c["U2f"] = u2
    c["V2f"] = m2.astype(np.float32)
    c["onesA"] = np.ascontiguousarray(np.broadcast_to((k[:, None] < 64).astype(np.float32), (128, 128)))
    c["onesB"] = np.ascontiguousarray(np.broadcast_to((k[:, None] >= 64).astype(np.float32), (128, 128)))
    c["negm4"] = np.tile(np.where(m2 > 0, 0.0, -30000.0).astype(np.float32), (1, 4)).astype(ml_dtypes.bfloat16)
    c["invf"] = np.ascontiguousarray(np.broadcast_to(invf[None, :], (128, 16))).astype(np.float32)
    return c


CSPEC = {"ident_bf": ([128, 128], BF16), "ident_f": ([128, 128], F32), "tri_bf": ([128, 128], BF16),
         "ones_bf": ([128, 64], BF16), "invf": ([128, 16], F32), "rmask": ([128, S_], BF16),
         "mask2": ([128, 128], BF16), "U2b": ([128, 128], BF16), "U2f": ([128, 128], F32),
         "V2f": ([128, 128], F32), "negm4": ([128, 512], BF16), "onesA": ([128, 128], F32), "onesB": ([128, 128], F32)}

def build(layers=(0, 1, 2, 3), final=True):
    nc = bass.Bass("TRN2", target_bir_lowering=False)
    x = nc.dram_tensor("x", [S_, D_], F32, kind="ExternalInput").ap()
    pos = nc.dram_tensor("pos", [128, NT], I32, kind="ExternalInput").ap()
    W = {k: nc.dram_tensor(k, v, F32, kind="ExternalInput").ap() for k, v in WSPEC.items()}
    CD = {k: nc.dram_tensor(k, v[0], v[1], kind="ExternalInput").ap() for k, v in CSPEC.items()}
    out = nc.dram_tensor("out", [S_, D_], F32, kind="ExternalOutput").ap()
    hbuf = [nc.dram_tensor(f"hbuf{i}", [S_, D_], F32).ap() for i in range(2)]
    _BASE["r"] = {}
    with ExitStack() as st:
        C = Ctx(nc, st)
        C.pos = pos
        C.CD = CD
        C.d_const = Dep()
        C.ident_bf = C.sb(st, [128, 128], BF16)
        C.ident_f = C.sb(st, [128, 128], F32)
        C.D(lambda e: e.dma_start(out=C.ident_bf[:], in_=CD["ident_bf"]), w=[C.d_const])
        C.D(lambda e: e.dma_start(out=C.ident_f[:], in_=CD["ident_f"]), w=[C.d_const])
        h_cur, d_cur = x, deps(NT)
        nxt = 0
        for li in layers:
            Wl = dict(W)
            Wl["norm_g"] = W["norm_g"][li, :]
            h_nxt, d_nxt = hbuf[nxt], deps(NT)
            if not final and li == layers[-1]:
                h_nxt = out
            LAYER_FNS[li](C, Wl, h_cur, d_cur, h_nxt, d_nxt)
            h_cur, d_cur = h_nxt, d_nxt
            nxt ^= 1
        if final:
            final_norm(C, h_cur, d_cur, W["final_g"], out)
        else:
            C.out_toks.append(C.last_store)
            for t in range(NT):
                C.out_toks.append(d_cur[t].w)
        for tok in C.out_toks:
            C.S.wait_tok("sync", tok)
        C.S.emit()
    return nc


LAYER_FNS[2] = layer_lru


def make_in_maps(inputs):
    consts = host_consts()
    maps = []
    for b in range(8):
        m = {"x": np.ascontiguousarray(inputs["x"][b]),
             "pos": np.ascontiguousarray(np.asarray(inputs["positions"][b]).astype(np.int32).reshape(NT, 128).T)}
        for k in WSPEC:
            a = np.asarray(inputs[k], dtype=np.float32)
            if k not in ("norm_g", "final_g"):
                a = a[0]
            m[k] = np.ascontiguousarray(a)
        m.update(consts)
        maps.append(m)
    return maps


_NC_CACHE = {}


def kernel(**inputs):
    if "nc" not in _NC_CACHE:
        _NC_CACHE["nc"] = build()
    nc = _NC_CACHE["nc"]
    maps = make_in_maps(inputs)
    res = run_bass_kernel_spmd(nc, maps, core_ids=list(range(8)))
    return np.stack([np.asarray(r["out"], dtype=np.float32) for r in res.results], axis=0)
```

```python
import numpy as np
import ml_dtypes
import concourse.bass as bass
import concourse.mybir as mybir
from concourse.bass_utils import run_bass_kernel_spmd
from contextlib import ExitStack

F32 = mybir.dt.float32
BF16 = mybir.dt.bfloat16
I32 = mybir.dt.int32
AF = mybir.ActivationFunctionType
ALU = mybir.AluOpType

S_ = 2048
D_ = 1024
NT = 16
EPS = 1e-6
LAYER_FNS = {}
DEBUG = False
import os
SSD_SKEW = int(os.environ.get('SSD_SKEW', '1'))

ENGS = ("tensor", "vector", "scalar", "gpsimd", "sync")
EPOCH = 6000
NDMA = 24


_BASE = {"r": {}}


class Dep:
    __slots__ = ("w", "r")

    def __init__(self):
        self.w = None
        self.r = dict(_BASE["r"])


def deps(n):
    return [Dep() for _ in range(n)]


class Sched:
    def __init__(self, nc, stack):
        self.nc = nc
        self.stack = stack
        self.q = {e: [] for e in ENGS}
        self.cnt = {e: 0 for e in ENGS}
        self.sems = {}
        self.known = {e: {} for e in ENGS}
        self.dma_sems = [stack.enter_context(nc.semaphore(f"dma{i}")) for i in range(NDMA)]
        self.dma_i = 0

    def _esem(self, eng, epoch):
        k = (eng, epoch)
        if k not in self.sems:
            self.sems[k] = self.stack.enter_context(self.nc.semaphore(f"s_{eng}_{epoch}"))
        return self.sems[k]

    def _need_wait(self, weng, tok):
        kn = self.known[weng]
        if tok[0] == "c":
            _, eng, ep, val = tok
            cur = kn.get(("c", eng), (-1, 0))
            if cur >= (ep, val):
                return None
            kn[("c", eng)] = (ep, val)
            return (self._esem(eng, ep), val)
        _, idx, val = tok
        cur = kn.get(("d", idx), 0)
        if cur >= val:
            return None
        kn[("d", idx)] = val
        return (self.dma_sems[idx], val)

    def op(self, eng, fn, reads=(), writes=(), dma=False):
        waits = []
        toks = []
        for d in reads:
            if d.w is not None:
                toks.append((d.w, "raw"))
        for d in writes:
            if d.w is not None:
                toks.append((d.w, "waw"))
            for t in d.r.values():
                toks.append((t, "war"))
        for tok, kind in toks:
            if not dma and tok[0] == "c" and tok[1] == eng:
                if eng == "tensor" or kind != "raw":
                    continue
            w = self._need_wait(eng, tok)
            if w is not None:
                waits.append(w)
        if dma:
            idx = self.dma_i % NDMA
            rnd = self.dma_i // NDMA
            self.dma_i += 1
            if rnd > 0:
                w = self._need_wait(eng, ("d", idx, 16 * rnd))
                if w is not None:
                    waits.append(w)
            mytok = ("d", idx, 16 * (rnd + 1))
            inc = (self.dma_sems[idx], 16)
        else:
            c = self.cnt[eng]
            self.cnt[eng] = c + 1
            ep, val = c // EPOCH, c % EPOCH + 1
            mytok = ("c", eng, ep, val)
            inc = (self._esem(eng, ep), 1)
        self.q[eng].append((fn, waits, inc))
        rk = mytok[:2]
        for d in reads:
            d.r[rk] = mytok
        for d in writes:
            d.w = mytok
            d.r = {}
        return mytok

    def snapshot(self):
        snap = {}
        for e in ENGS:
            c = self.cnt[e]
            if c > 0:
                c -= 1
                snap[("c", e)] = ("c", e, c // EPOCH, c % EPOCH + 1)
        for i in range(min(self.dma_i, NDMA)):
            n_i = (self.dma_i - 1 - i) // NDMA
            snap[("d", i)] = ("d", i, 16 * (n_i + 1))
        return snap

    def wait_tok(self, eng, tok):
        w = self._need_wait(eng, tok)
        if w is not None:
            self.q[eng].append((None, [w], None))

    def emit(self):
        nc = self.nc
        with nc.Block() as block:
            def run(engname):
                def body(e):
                    for fn, waits, inc in self.q[engname]:
                        for sem, val in waits:
                            e.wait_ge(sem, val)
                        if fn is not None:
                            ins = fn(e)
                            ins.then_inc(inc[0], inc[1])
                return body
            block.tensor(run("tensor"))
            block.vector(run("vector"))
            block.scalar(run("scalar"))
            block.gpsimd(run("gpsimd"))
            block.sync(run("sync"))


class Ctx:
    def __init__(self, nc, st):
        self.nc = nc
        self.st = st
        self.S = Sched(nc, st)
        self.ps = st.enter_context(nc.psum_tensor("ps", [128, 8 * 512], F32))
        self.psd = deps(8)
        self.uid = 0
        self.out_toks = []

    def release(self):
        _BASE["r"] = self.S.snapshot()

    def sb(self, stack, shape, dt):
        self.uid += 1
        return stack.enter_context(self.nc.sbuf_tensor(f"t{self.uid}", list(shape), dt))

    def pb(self, i, n=512, off=0):
        return self.ps[:, i * 512 + off:i * 512 + off + n]

    def pbh(self, i):
        return self.ps[:, i * 512:(i + 1) * 512].bitcast(BF16)

    def T(self, fn, r=(), w=()):
        return self.S.op("tensor", fn, r, w)

    def V(self, fn, r=(), w=()):
        return self.S.op("vector", fn, r, w)

    def A(self, fn, r=(), w=()):
        return self.S.op("scalar", fn, r, w)

    def G(self, fn, r=(), w=()):
        return self.S.op("gpsimd", fn, r, w)

    def D(self, fn, r=(), w=()):
        return self.S.op("sync", fn, r, w, dma=True)

    def DG(self, fn, r=(), w=()):
        return self.S.op("gpsimd", fn, r, w, dma=True)

    def mm(self, out, lhsT, rhs, start, stop, r=(), w=()):
        return self.T(lambda e: e.matmul(out, lhsT=lhsT, rhs=rhs, start=start, stop=stop), r, w)

    def tr(self, out, in_, ident, r=(), w=()):
        return self.T(lambda e: e.transpose(out=out, in_=in_, identity=ident), r, w)

    def act(self, out, in_, func, r=(), w=(), **kw):
        return self.A(lambda e: e.activation(out=out, in_=in_, func=func, **kw), r, w)

    def dbg(self, name, ap, shape, dt, r):
        if not DEBUG:
            return
        d = self.nc.dram_tensor("dbg_" + name, list(shape), dt, kind="ExternalOutput").ap()
        tok = self.D(lambda e: e.dma_start(out=d, in_=ap), r=r)
        self.out_toks.append(tok)

    def load_w(self, dst, src2d, dep):
        n = src2d.shape[1]
        tok = None
        for c0 in range(0, n, 512):
            c1 = min(n, c0 + 512)
            assert (c1 - c0) in (16, 32, 64, 128, 256, 512), (c0, c1)
            tok = self.DG(lambda e, c0=c0, c1=c1: e.dma_start(
                out=dst[:, :, c0:c1], in_=src2d[:, c0:c1].rearrange("(c p) n -> p c n", p=128)), w=[dep])
        return tok

    def load_bc(self, dst, src1d, dep, n=128):
        return self.DG(lambda e: e.dma_start(out=dst, in_=src1d.partition_broadcast(n)), w=[dep])

    def load_col(self, dst, src1d, dep, nchunks):
        return self.DG(lambda e: e.dma_start(out=dst, in_=src1d.rearrange("(c p) -> p c", p=128),
                                             allow_slow_non_contiguous=True), w=[dep])


def rstd_ops(C, rs_ap, ss_ap, n, r, w):
    C.act(rs_ap, ss_ap, AF.Ln, r=r, w=w, scale=1.0 / n, bias=EPS)
    C.act(rs_ap, rs_ap, AF.Exp, r=w, w=w, scale=-0.5)


def norm_T(C, stk, h_ap, g_row, uT, d_uT, d_h, banks=(0, 1)):
    with ExitStack() as s:
        gt = C.sb(s, [128, D_], F32); d_gt = Dep()
        C.load_bc(gt[:], g_row, d_gt)
        hb = [C.sb(s, [128, D_], F32) for _ in range(2)]; d_hb = deps(2)
        junk = C.sb(s, [128, D_], BF16); d_junk = Dep()
        ss = C.sb(s, [128, NT], F32); rs = C.sb(s, [128, NT], F32)
        d_ss = deps(NT); d_rs = deps(NT)
        xn = [C.sb(s, [128, D_], BF16) for _ in range(2)]; d_xn = deps(2)
        for t in range(NT):
            b = t % 2
            C.D(lambda e, t=t, b=b: e.dma_start(out=hb[b][:], in_=h_ap[t * 128:(t + 1) * 128, :]),
                r=[d_h[t]], w=[d_hb[b]])
            C.act(junk[:], hb[b][:], AF.Square, r=[d_hb[b]], w=[d_junk, d_ss[t]], accum_out=ss[:, t:t + 1])
            rstd_ops(C, rs[:, t:t + 1], ss[:, t:t + 1], D_, [d_ss[t]], [d_rs[t]])
            C.V(lambda e, t=t, b=b: e.scalar_tensor_tensor(out=xn[b][:], in0=hb[b][:], scalar=rs[:, t:t + 1],
                                                           in1=gt[:], op0=ALU.mult, op1=ALU.mult),
                r=[d_hb[b], d_rs[t], d_gt], w=[d_xn[b]])
            bk = banks[t % len(banks)]
            pv = C.pbh(bk)
            for c in range(8):
                C.tr(pv[:, c * 128:(c + 1) * 128], xn[b][:, c * 128:(c + 1) * 128], C.ident_bf[:],
                     r=[d_xn[b], C.d_const], w=[C.psd[bk]])
            C.V(lambda e, t=t, pv=pv: e.tensor_copy(out=uT[:, :, t * 128:(t + 1) * 128],
                                                   in_=pv.rearrange("p (c n) -> p c n", c=8)),
                r=[C.psd[bk]], w=[d_uT[t]])
        C.release()


def out_proj(C, goT, d_go, nk, w_out, h_in, d_hin, h_out, d_hout, banks=(0, 1, 2, 3), wo_pre=None):
    with ExitStack() as s:
        if wo_pre is None:
            wo = C.sb(s, [128, nk, D_], BF16); d_wo = Dep()
            half = nk // 2
            C.load_w(wo[:, 0:half, :], w_out[0:half * 128, :], d_wo)
            C.load_w(wo[:, half:nk, :], w_out[half * 128:nk * 128, :], d_wo)
        else:
            wo, d_wo = wo_pre
        hb = [C.sb(s, [128, D_], F32) for _ in range(2)]; d_hb = deps(2)
        ho = [C.sb(s, [128, D_], F32) for _ in range(2)]; d_ho = deps(2)
        i = 0
        for t in range(NT):
            b = t % 2
            C.D(lambda e, t=t, b=b: e.dma_start(out=hb[b][:], in_=h_in[t * 128:(t + 1) * 128, :]),
                r=[d_hin[t]], w=[d_hb[b]])
            for nb in range(2):
                bk = banks[i % len(banks)]; i += 1
                for c in range(nk):
                    C.mm(C.pb(bk), goT[:, c, t * 128:(t + 1) * 128], wo[:, c, nb * 512:(nb + 1) * 512],
                         c == 0, c == nk - 1, r=[d_go[t], d_wo], w=[C.psd[bk]])
                C.V(lambda e, b=b, nb=nb, bk=bk: e.tensor_tensor(out=ho[b][:, nb * 512:(nb + 1) * 512], in0=C.pb(bk),
                                                               in1=hb[b][:, nb * 512:(nb + 1) * 512], op=ALU.add),
                    r=[C.psd[bk], d_hb[b]], w=[d_ho[b]])
            tok = C.D(lambda e, t=t, b=b: e.dma_start(out=h_out[t * 128:(t + 1) * 128, :], in_=ho[b][:]),
                      r=[d_ho[b]], w=[d_hout[t]])
            C.last_store = tok
        C.release()


def final_norm(C, h_ap, d_h, g_row, out_ap):
    with ExitStack() as s:
        gt = C.sb(s, [128, D_], F32); d_gt = Dep()
        C.load_bc(gt[:], g_row, d_gt)
        hb = [C.sb(s, [128, D_], F32) for _ in range(2)]; d_hb = deps(2)
        junk = C.sb(s, [128, D_], BF16); d_junk = Dep()
        ss = C.sb(s, [128, NT], F32); rs = C.sb(s, [128, NT], F32)
        d_ss = deps(NT); d_rs = deps(NT)
        ob = [C.sb(s, [128, D_], F32) for _ in range(2)]; d_ob = deps(2)
        for t in range(NT):
            b = t % 2
            C.D(lambda e, t=t, b=b: e.dma_start(out=hb[b][:], in_=h_ap[t * 128:(t + 1) * 128, :]),
                r=[d_h[t]], w=[d_hb[b]])
            C.act(junk[:], hb[b][:], AF.Square, r=[d_hb[b]], w=[d_junk, d_ss[t]], accum_out=ss[:, t:t + 1])
            rstd_ops(C, rs[:, t:t + 1], ss[:, t:t + 1], D_, [d_ss[t]], [d_rs[t]])
            C.V(lambda e, t=t, b=b: e.scalar_tensor_tensor(out=ob[b][:], in0=hb[b][:], scalar=rs[:, t:t + 1],
                                                           in1=gt[:], op0=ALU.mult, op1=ALU.mult),
                r=[d_hb[b], d_rs[t], d_gt], w=[d_ob[b]])
            tok = C.D(lambda e, t=t, b=b: e.dma_start(out=out_ap[t * 128:(t + 1) * 128, :], in_=ob[b][:]),
                      r=[d_ob[b]])
            C.out_toks.append(tok)
        C.release()


def layer_lru(C, W, h_in, d_hin, h_out, d_hout):
    NCH = 10
    with ExitStack() as L:
        uT = C.sb(L, [128, 8, S_], BF16); d_uT = deps(NT)
        norm_T(C, L, h_in, W["norm_g"], uT, d_uT, d_hin)
        d_uTb = [[d_uT[4 * tb + i] for i in range(4)] for tb in range(4)]
        goT = C.sb(L, [128, NCH, S_], BF16); d_go = deps(NT)
        cw = C.sb(L, [128, 4, NCH], F32); d_cw = Dep()
        for k in range(4):
            C.load_col(cw[:, k, :], W["lru_conv_w"][k, :], d_cw, NCH)
        cb = C.sb(L, [128, NCH], F32); ba = C.sb(L, [128, NCH], F32); bx = C.sb(L, [128, NCH], F32)
        lam = C.sb(L, [128, NCH], F32); d_p = Dep()
        C.load_col(cb[:], W["lru_conv_b"], d_p, NCH)
        C.load_col(ba[:], W["lru_b_a"], d_p, NCH)
        C.load_col(bx[:], W["lru_b_x"], d_p, NCH)
        C.load_col(lam[:], W["lru_lam"], d_p, NCH)
        cA = C.sb(L, [128, NCH], F32); cA2 = C.sb(L, [128, NCH], F32); d_cA = Dep()
        C.act(cA[:], lam[:], AF.Exp, r=[d_p], w=[d_cA], scale=-1.0)
        C.act(cA[:], cA[:], AF.Ln, r=[d_cA], w=[d_cA], bias=1.0)
        C.V(lambda e: e.tensor_scalar(out=cA2[:], in0=cA[:], scalar1=-16.0, scalar2=None, op0=ALU.mult),
            r=[d_cA], w=[d_cA])
        C.V(lambda e: e.tensor_scalar(out=cA[:], in0=cA[:], scalar1=-8.0, scalar2=None, op0=ALU.mult),
            r=[d_cA], w=[d_cA])
        wa = C.sb(L, [128, NCH, 128], BF16); wx = C.sb(L, [128, NCH, 128], BF16); d_wg = Dep()
        C.DG(lambda e: e.dma_start(out=wa[:], in_=W["lru_w_a"].rearrange("n i j -> i n j")), w=[d_wg])
        C.DG(lambda e: e.dma_start(out=wx[:], in_=W["lru_w_x"].rearrange("n i j -> i n j")), w=[d_wg])
        dg = C.sb(L, [128, NCH, 4, 128], BF16); d_dg = Dep()
        for c in range(NCH):
            for k in range(4):
                C.V(lambda e, c=c, k=k: e.tensor_scalar(out=dg[:, c, k, :], in0=C.ident_bf[:],
                                                        scalar1=cw[:, k, c:c + 1], scalar2=None, op0=ALU.mult),
                    r=[d_cw, C.d_const], w=[d_dg])
        wu = [C.sb(L, [128, 8, 128], BF16) for _ in range(2)]; d_wu = deps(2)
        wg = [C.sb(L, [128, 8, 128], BF16) for _ in range(2)]; d_wgt = deps(2)
        ub2 = [C.sb(L, [128, 3 + S_], BF16) for _ in range(2)]; d_ub2 = deps(2)
        for b_ in range(2):
            C.V(lambda e, b_=b_: e.memset(ub2[b_][:, 0:3], 0.0), w=[d_ub2[b_]])
        uc = C.sb(L, [128, S_], F32); d_uc = Dep()
        ucb = C.sb(L, [128, S_], BF16); d_ucb = Dep()
        rr = C.sb(L, [128, S_], F32); d_rr = Dep()
        ig = C.sb(L, [128, S_], F32); d_ig = Dep()
        tmp = C.sb(L, [128, S_], F32); d_tmp = Dep()
        hs = C.sb(L, [128, S_], BF16); d_hs = Dep()
        sg = C.sb(L, [128, S_], BF16); d_sg = Dep()
        w_in = W["lru_w_in"]

        def proj_u(c):
            b = c % 2
            C.load_w(wu[b][:], w_in[:, 1280 + c * 128:1280 + (c + 1) * 128], d_wu[b])
            C.load_w(wg[b][:], w_in[:, c * 128:(c + 1) * 128], d_wgt[b])
            for tb in range(4):
                bk = tb
                for kc in range(8):
                    C.mm(C.pb(bk), wu[b][:, kc, :], uT[:, kc, tb * 512:(tb + 1) * 512], kc == 0, kc == 7,
                         r=d_uTb[tb] + [d_wu[b]], w=[C.psd[bk]])
                C.act(ub2[b][:, 3 + tb * 512:3 + (tb + 1) * 512], C.pb(bk), AF.Copy, r=[C.psd[bk]], w=[d_ub2[b]])

        proj_u(0)
        for c in range(NCH):
            b = c % 2
            for tb in range(4):
                bk = 4 + tb
                for k in range(4):
                    C.mm(C.pb(bk), dg[:, c, k, :], ub2[b][:, tb * 512 + k:tb * 512 + k + 512], k == 0, k == 3,
                         r=[d_ub2[b], d_dg], w=[C.psd[bk]])
                C.act(uc[:, tb * 512:(tb + 1) * 512], C.pb(bk), AF.Identity, r=[C.psd[bk], d_p], w=[d_uc],
                      bias=cb[:, c:c + 1])
            C.V(lambda e: e.tensor_copy(out=ucb[:], in_=uc[:]), r=[d_uc], w=[d_ucb])
            for tb in range(4):
                bk = tb
                for kc in range(8):
                    C.mm(C.pb(bk), wg[b][:, kc, :], uT[:, kc, tb * 512:(tb + 1) * 512], kc == 0, kc == 7,
                         r=d_uTb[tb] + [d_wgt[b]], w=[C.psd[bk]])
                C.act(sg[:, tb * 512:(tb + 1) * 512], C.pb(bk), AF.Silu, r=[C.psd[bk]], w=[d_sg])
            for tb in range(4):
                bk = 4 + tb
                C.mm(C.pb(bk), wa[:, c, :], ucb[:, tb * 512:(tb + 1) * 512], True, True,
                     r=[d_ucb, d_wg], w=[C.psd[bk]])
                C.act(rr[:, tb * 512:(tb + 1) * 512], C.pb(bk), AF.Sigmoid, r=[C.psd[bk], d_p], w=[d_rr],
                      bias=ba[:, c:c + 1])
            for tb in range(4):
                bk = 4 + tb
                C.mm(C.pb(bk), wx[:, c, :], ucb[:, tb * 512:(tb + 1) * 512], True, True,
                     r=[d_ucb, d_wg], w=[C.psd[bk]])
                C.act(ig[:, tb * 512:(tb + 1) * 512], C.pb(bk), AF.Sigmoid, r=[C.psd[bk], d_p], w=[d_ig],
                      bias=bx[:, c:c + 1])
            if c + 1 < NCH:
                proj_u(c + 1)
            C.act(tmp[:], rr[:], AF.Exp, r=[d_rr, d_cA], w=[d_tmp], scale=cA2[:, c:c + 1])
            C.act(rr[:], rr[:], AF.Exp, r=[d_rr, d_cA], w=[d_rr], scale=cA[:, c:c + 1])
            C.act(tmp[:], tmp[:], AF.Sqrt, r=[d_tmp], w=[d_tmp], scale=-1.0, bias=1.0)
            C.V(lambda e: e.tensor_tensor(out=ig[:], in0=ig[:], in1=uc[:], op=ALU.mult), r=[d_ig, d_uc], w=[d_ig])
            C.V(lambda e: e.tensor_tensor(out=tmp[:], in0=tmp[:], in1=ig[:], op=ALU.mult), r=[d_tmp, d_ig], w=[d_tmp])
            C.V(lambda e: e.tensor_tensor_scan(out=hs[:], data0=rr[:], data1=tmp[:], initial=0.0,
                                               op0=ALU.mult, op1=ALU.add), r=[d_rr, d_tmp], w=[d_hs])
            C.G(lambda e, c=c: e.tensor_tensor(out=goT[:, c, :], in0=hs[:], in1=sg[:], op=ALU.mult),
                r=[d_hs, d_sg], w=d_go)
        out_proj(C, goT, d_go, NCH, W["lru_w_out"], h_in, d_hin, h_out, d_hout)
        C.release()


def layer_mla(C, W, h_in, d_hin, h_out, d_hout):
    H = 16
    PI = float(np.pi)
    with ExitStack() as L:
        cqnT = C.sb(L, [128, 3, S_], BF16); d_cq = deps(NT)
        ckvT = C.sb(L, [128, 2, S_], BF16); d_ckv = deps(NT)
        sgT = C.sb(L, [128, 8, S_], BF16); d_sg4 = deps(4)
        kr_all = C.sb(L, [128, NT, 32], F32); d_kr = Dep()
        w_in = W["mla_w_in"]
        with ExitStack() as PA:
            uT = C.sb(PA, [128, 8, S_], BF16); d_uT = deps(NT)
            norm_T(C, PA, h_in, W["norm_g"], uT, d_uT, d_hin)
            d_uTb = [[d_uT[4 * tb + i] for i in range(4)] for tb in range(4)]
            w1 = C.sb(PA, [128, 8, 768], BF16); d_w1 = Dep()
            C.load_w(w1[:], w_in[:, 0:768], d_w1)
            gq = C.sb(PA, [128, 384], F32); gkv = C.sb(PA, [128, 256], F32); d_g = Dep()
            C.load_bc(gq[:], W["mla_g_q"], d_g)
            C.load_bc(gkv[:], W["mla_g_kv"], d_g)
            ctok = [C.sb(PA, [128, 672], F32) for _ in range(2)]; d_ct = deps(2)
            ss2 = C.sb(PA, [128, NT, 2], F32); rs2 = C.sb(PA, [128, NT, 2], F32)
            d_ss = deps(NT); d_rs = deps(NT)
            junk = C.sb(PA, [128, 384], BF16); d_junk = Dep()
            cn = [C.sb(PA, [128, 640], BF16) for _ in range(2)]; d_cn = deps(2)
            wgt = [C.sb(PA, [128, 8, 512], BF16) for _ in range(2)]; d_wgt = deps(2)
            for hf in range(2):
                C.load_w(wgt[hf][:], w_in[:, 672 + hf * 512:672 + (hf + 1) * 512], d_wgt[hf])
            for t in range(NT):
                b = t % 2
                for (c0, c1, bk) in ((0, 512, 2), (512, 672, 3)):
                    for kc in range(8):
                        C.mm(C.pb(bk, c1 - c0), uT[:, kc, t * 128:(t + 1) * 128], w1[:, kc, c0:c1], kc == 0, kc == 7,
                             r=[d_uT[t], d_w1], w=[C.psd[bk]])
                    C.act(ctok[b][:, c0:c1], C.pb(bk, c1 - c0), AF.Copy, r=[C.psd[bk]], w=[d_ct[b]])
                C.act(junk[:, 0:384], ctok[b][:, 0:384], AF.Square, r=[d_ct[b]], w=[d_junk, d_ss[t]],
                      accum_out=ss2[:, t, 0:1])
                C.act(junk[:, 0:256], ctok[b][:, 384:640], AF.Square, r=[d_ct[b]], w=[d_junk, d_ss[t]],
                      accum_out=ss2[:, t, 1:2])
                rstd_ops(C, rs2[:, t, 0:1], ss2[:, t, 0:1], 384, [d_ss[t]], [d_rs[t]])
                rstd_ops(C, rs2[:, t, 1:2], ss2[:, t, 1:2], 256, [d_ss[t]], [d_rs[t]])
                C.V(lambda e, t=t, b=b: e.scalar_tensor_tensor(out=cn[b][:, 0:384], in0=ctok[b][:, 0:384],
                                                               scalar=rs2[:, t, 0:1], in1=gq[:], op0=ALU.mult, op1=ALU.mult),
                    r=[d_ct[b], d_rs[t], d_g], w=[d_cn[b]])
                C.V(lambda e, t=t, b=b: e.scalar_tensor_tensor(out=cn[b][:, 384:640], in0=ctok[b][:, 384:640],
                                                               scalar=rs2[:, t, 1:2], in1=gkv[:], op0=ALU.mult, op1=ALU.mult),
                    r=[d_ct[b], d_rs[t], d_g], w=[d_cn[b]])
                C.G(lambda e, t=t, b=b: e.tensor_copy(out=kr_all[:, t, :], in_=ctok[b][:, 640:672]),
                    r=[d_ct[b]], w=[d_kr])
                bk = 4 + b
                pv = C.pbh(bk)
                for c in range(5):
                    C.tr(pv[:, c * 128:(c + 1) * 128], cn[b][:, c * 128:(c + 1) * 128], C.ident_bf[:],
                         r=[d_cn[b], C.d_const], w=[C.psd[bk]])
                C.V(lambda e, t=t, pv=pv: e.tensor_copy(out=cqnT[:, :, t * 128:(t + 1) * 128],
                                                       in_=pv[:, 0:384].rearrange("p (c n) -> p c n", c=3)),
                    r=[C.psd[bk]], w=[d_cq[t]])
                C.V(lambda e, t=t, pv=pv: e.tensor_copy(out=ckvT[:, :, t * 128:(t + 1) * 128],
                                                       in_=pv[:, 384:640].rearrange("p (c n) -> p c n", c=2)),
                    r=[C.psd[bk]], w=[d_ckv[t]])
            C.dbg("ctok", ctok[1][:], [128, 672], F32, [d_ct[1]])
            C.dbg("w1", w1[:], [128, 8, 768], BF16, [d_w1])
            C.dbg("cn", cn[1][:], [128, 640], BF16, [d_cn[1]])
            C.dbg("rs2", rs2[:], [128, NT, 2], F32, d_rs)
            i = 0
            for c in range(8):
                for tb in range(4):
                    bk = (6, 7, 0, 1)[i % 4]; i += 1
                    for kc in range(8):
                        C.mm(C.pb(bk), wgt[c // 4][:, kc, (c % 4) * 128:(c % 4 + 1) * 128],
                             uT[:, kc, tb * 512:(tb + 1) * 512], kc == 0, kc == 7,
                             r=d_uTb[tb] + [d_wgt[c // 4]], w=[C.psd[bk]])
                    C.act(sgT[:, c, tb * 512:(tb + 1) * 512], C.pb(bk), AF.Silu, r=[C.psd[bk]], w=[d_sg4[tb]])
            C.release()
        C.dbg("cqnT", cqnT[:], [128, 3, S_], BF16, d_cq)
        C.dbg("ckvT", ckvT[:], [128, 2, S_], BF16, d_ckv)
        C.dbg("sgT", sgT[:], [128, 8, S_], BF16, d_sg4)
        C.dbg("kr", kr_all[:], [128, NT, 32], F32, [d_kr])
        with ExitStack() as PB:
            wuq = C.sb(PB, [128, 3, 1536], BF16); d_wuq = Dep()
            C.load_w(wuq[:], W["mla_w_uq"], d_wuq)
            wukv = C.sb(PB, [128, 2, 2048], BF16); d_wukv = Dep()
            C.load_w(wukv[:], W["mla_w_ukv"], d_wukv)
            wo = C.sb(PB, [128, 8, D_], BF16); d_wo = Dep()
            C.load_w(wo[:, 0:4, :], W["mla_w_out"][0:512, :], d_wo)
            C.load_w(wo[:, 4:8, :], W["mla_w_out"][512:1024, :], d_wo)
            tri = C.sb(PB, [128, 128], BF16); ones64 = C.sb(PB, [128, 64], BF16)
            invf = C.sb(PB, [128, 16], F32); posi = C.sb(PB, [128, NT], I32); d_k = Dep()
            C.D(lambda e: e.dma_start(out=tri[:], in_=C.CD["tri_bf"]), w=[d_k])
            C.D(lambda e: e.dma_start(out=ones64[:], in_=C.CD["ones_bf"]), w=[d_k])
            C.D(lambda e: e.dma_start(out=invf[:], in_=C.CD["invf"]), w=[d_k])
            C.D(lambda e: e.dma_start(out=posi[:], in_=C.pos), w=[d_k])
            posf = C.sb(PB, [128, NT], F32); d_t = Dep()
            ang = C.sb(PB, [128, NT, 16], F32); tmpf = C.sb(PB, [128, NT, 16], F32)
            ki = C.sb(PB, [128, NT, 16], I32); kff = C.sb(PB, [128, NT, 16], F32)
            rr = C.sb(PB, [128, NT, 16], F32); yy = C.sb(PB, [128, NT, 16], F32); mm_ = C.sb(PB, [128, NT, 16], F32)
            cos_t = C.sb(PB, [128, NT, 16], F32); sin_t = C.sb(PB, [128, NT, 16], F32); d_cs = Dep()
            C.V(lambda e: e.tensor_copy(out=posf[:], in_=posi[:]), r=[d_k], w=[d_t])
            C.V(lambda e: e.tensor_tensor(out=ang[:], in0=posf[:].unsqueeze(2).to_broadcast([128, NT, 16]),
                                          in1=invf[:].unsqueeze(1).to_broadcast([128, NT, 16]), op=ALU.mult),
                r=[d_t, d_k], w=[d_t])
            C.V(lambda e: e.tensor_scalar(out=tmpf[:], in0=ang[:], scalar1=1.0 / (2 * PI), scalar2=None, op0=ALU.mult),
                r=[d_t], w=[d_t])
            C.V(lambda e: e.tensor_copy(out=ki[:], in_=tmpf[:]), r=[d_t], w=[d_t])
            C.V(lambda e: e.tensor_copy(out=kff[:], in_=ki[:]), r=[d_t], w=[d_t])
            C1 = 6.28125
            C2 = float(2 * np.pi - 6.28125)
            C.V(lambda e: e.scalar_tensor_tensor(out=rr[:], in0=kff[:], scalar=-C1, in1=ang[:], op0=ALU.mult, op1=ALU.add),
                r=[d_t], w=[d_t])
            C.V(lambda e: e.scalar_tensor_tensor(out=rr[:], in0=kff[:], scalar=-C2, in1=rr[:], op0=ALU.mult, op1=ALU.add),
                r=[d_t], w=[d_t])
            for dst, shift in ((sin_t, 0.0), (cos_t, PI / 2)):
                C.V(lambda e, shift=shift: e.tensor_scalar(out=yy[:], in0=rr[:], scalar1=shift, scalar2=None, op0=ALU.add),
                    r=[d_t], w=[d_t])
                C.V(lambda e: e.tensor_scalar(out=mm_[:], in0=yy[:], scalar1=PI, scalar2=-2 * PI, op0=ALU.is_gt, op1=ALU.mult),
                    r=[d_t], w=[d_t])
                C.V(lambda e: e.tensor_tensor(out=yy[:], in0=yy[:], in1=mm_[:], op=ALU.add), r=[d_t], w=[d_t])
                C.V(lambda e: e.tensor_scalar(out=mm_[:], in0=yy[:], scalar1=-PI, scalar2=2 * PI, op0=ALU.is_lt, op1=ALU.mult),
                    r=[d_t], w=[d_t])
                C.V(lambda e: e.tensor_tensor(out=yy[:], in0=yy[:], in1=mm_[:], op=ALU.add), r=[d_t], w=[d_t])
                C.act(dst[:], yy[:], AF.Sin, r=[d_t], w=[d_cs, d_t])
            qT = [C.sb(PB, [128, S_], BF16) for _ in range(2)]; d_qT = deps(2)
            kT = [C.sb(PB, [128, S_], BF16) for _ in range(2)]; d_kT = deps(2)
            Vh = [C.sb(PB, [128, NT, 128], BF16) for _ in range(2)]; d_V = deps(2)
            for b_ in range(2):
                C.G(lambda e, b_=b_: e.memset(Vh[b_][:], 1.0), w=[d_V[b_]])
            swp = C.sb(PB, [128, 128], F32)
            C.D(lambda e: e.dma_start(out=swp[:], in_=C.CD["swap_f"]), w=[d_k])
            for b in range(2):
                C.G(lambda e, b=b: e.memset(qT[b][:], 0.0), w=[d_qT[b]])
                C.G(lambda e, b=b: e.memset(kT[b][:], 0.0), w=[d_kT[b]])
            qtok = C.sb(PB, [128, NT, 96], F32); d_qtok = Dep()
            ta = C.sb(PB, [128, NT, 16], F32); tb_ = C.sb(PB, [128, NT, 16], F32)
            tc = C.sb(PB, [128, NT, 16], F32); td = C.sb(PB, [128, NT, 16], F32); d_tt = Dep()
            krr = C.sb(PB, [128, NT, 96], F32); d_krr = Dep()
            C.G(lambda e: e.memset(krr[:], 0.0), w=[d_krr])
            k1 = kr_all[:, :, 0:16]; k2 = kr_all[:, :, 16:32]
            C.V(lambda e: e.tensor_tensor(out=ta[:], in0=k1, in1=cos_t[:], op=ALU.mult), r=[d_kr, d_cs], w=[d_tt])
            C.V(lambda e: e.tensor_tensor(out=tb_[:], in0=k2, in1=sin_t[:], op=ALU.mult), r=[d_kr, d_cs], w=[d_tt])
            C.V(lambda e: e.tensor_tensor(out=krr[:, :, 64:80], in0=ta[:], in1=tb_[:], op=ALU.subtract), r=[d_tt], w=[d_krr])
            C.V(lambda e: e.tensor_tensor(out=tc[:], in0=k2, in1=cos_t[:], op=ALU.mult), r=[d_kr, d_cs], w=[d_tt])
            C.V(lambda e: e.tensor_tensor(out=td[:], in0=k1, in1=sin_t[:], op=ALU.mult), r=[d_kr, d_cs], w=[d_tt])
            C.V(lambda e: e.tensor_tensor(out=krr[:, :, 80:96], in0=tc[:], in1=td[:], op=ALU.add), r=[d_tt], w=[d_krr])
            for t4 in range(4):
                bk = 7
                for i in range(4):
                    C.tr(C.pb(bk)[0:96, i * 128:(i + 1) * 128], krr[:, t4 * 4 + i, :], C.ident_f[:],
                         r=[d_krr, C.d_const], w=[C.psd[bk]])
                for b in range(2):
                    C.V(lambda e, b=b, t4=t4, bk=bk: e.tensor_copy(out=kT[b][64:96, t4 * 512:(t4 + 1) * 512],
                                                                 in_=C.pb(bk)[64:96, :]),
                        r=[C.psd[bk]], w=[d_kT[b]])
            lnd = [C.sb(PB, [128, 512], F32) for _ in range(2)]; rden = [C.sb(PB, [128, 512], F32) for _ in range(2)]
            ot = [C.sb(PB, [128, 512], F32) for _ in range(2)]
            d_nrm = deps(2)
            NPT = 3
            SB = (0, 1, 2)
            PT = [C.sb(PB, [128, 512], BF16) for _ in range(NPT)]; d_PT = deps(NPT)
            d_ot = deps(2)
            QS = 96 ** -0.5

            pcnt = [0]

            def proj_units(h, b):
                for t4 in range(4):
                    bk = 6 + pcnt[0] % 2; pcnt[0] += 1
                    for i in range(4):
                        t = t4 * 4 + i
                        for kc in range(3):
                            C.mm(C.pb(bk)[:, i * 96:(i + 1) * 96], cqnT[:, kc, t * 128:(t + 1) * 128],
                                 wuq[:, kc, h * 96:(h + 1) * 96], kc == 0, kc == 2,
                                 r=[d_cq[t], d_wuq], w=[C.psd[bk]])
                    C.V(lambda e, t4=t4, bk=bk: e.tensor_scalar(
                        out=qtok[:, t4 * 4:(t4 + 1) * 4, :], in0=C.pb(bk)[:, 0:384].rearrange("p (a c) -> p a c", a=4),
                        scalar1=QS, scalar2=None, op0=ALU.mult), r=[C.psd[bk]], w=[d_qtok])
                    yield
                q1 = qtok[:, :, 64:80]; q2 = qtok[:, :, 80:96]
                C.V(lambda e: e.tensor_tensor(out=ta[:], in0=q1, in1=cos_t[:], op=ALU.mult), r=[d_qtok, d_cs], w=[d_tt])
                C.V(lambda e: e.tensor_tensor(out=tb_[:], in0=q2, in1=sin_t[:], op=ALU.mult), r=[d_qtok, d_cs], w=[d_tt])
                C.V(lambda e: e.tensor_tensor(out=tc[:], in0=q2, in1=cos_t[:], op=ALU.mult), r=[d_qtok, d_cs], w=[d_tt])
                C.V(lambda e: e.tensor_tensor(out=td[:], in0=q1, in1=sin_t[:], op=ALU.mult), r=[d_qtok, d_cs], w=[d_tt])
                C.V(lambda e: e.tensor_tensor(out=q1, in0=ta[:], in1=tb_[:], op=ALU.subtract), r=[d_tt], w=[d_qtok])
                C.V(lambda e: e.tensor_tensor(out=q2, in0=tc[:], in1=td[:], op=ALU.add), r=[d_tt], w=[d_qtok])
                yield
                for t4 in range(4):
                    bk = 6 + pcnt[0] % 2; pcnt[0] += 1
                    for i in range(4):
                        C.tr(C.pb(bk)[0:96, i * 128:(i + 1) * 128], qtok[:, t4 * 4 + i, :], C.ident_f[:],
                             r=[d_qtok, C.d_const], w=[C.psd[bk]])
                    C.V(lambda e, t4=t4, bk=bk: e.tensor_copy(out=qT[b][0:96, t4 * 512:(t4 + 1) * 512],
                                                            in_=C.pb(bk)[0:96, :]), r=[C.psd[bk]], w=[d_qT[b]])
                    yield
                for tb in range(4):
                    bk = 6 + pcnt[0] % 2; pcnt[0] += 1
                    for kc in range(2):
                        C.mm(C.pb(bk)[0:64, :], wukv[:, kc, h * 128:h * 128 + 64], ckvT[:, kc, tb * 512:(tb + 1) * 512],
                             kc == 0, kc == 1, r=[d_ckv[4 * tb + i] for i in range(4)] + [d_wukv], w=[C.psd[bk]])
                    C.V(lambda e, tb=tb, bk=bk: e.tensor_copy(out=kT[b][0:64, tb * 512:(tb + 1) * 512],
                                                            in_=C.pb(bk)[0:64, :]), r=[C.psd[bk]], w=[d_kT[b]])
                    yield
                for t8 in range(2):
                    bk = 6 + pcnt[0] % 2; pcnt[0] += 1
                    for i in range(8):
                        t = t8 * 8 + i
                        for kc in range(2):
                            C.mm(C.pb(bk)[:, i * 64:(i + 1) * 64], ckvT[:, kc, t * 128:(t + 1) * 128],
                                 wukv[:, kc, h * 128 + 64:h * 128 + 128], kc == 0, kc == 1,
                                 r=[d_ckv[t], d_wukv], w=[C.psd[bk]])
                    C.V(lambda e, t8=t8, bk=bk: e.tensor_copy(out=Vh[b][:, t8 * 8:(t8 + 1) * 8, b * 64:b * 64 + 64],
                                                            in_=C.pb(bk).rearrange("p (a c) -> p a c", a=8)),
                        r=[C.psd[bk]], w=[d_V[b]])
                    yield

            items = []
            for h in range(H):
                for qc in range(4):
                    nj = 4 * qc + 4
                    for j in range(nj):
                        items.append((h, qc, j, nj))

            def stage_a(i):
                h, qc, j, nj = items[i]
                b = h % 2
                col0 = max(0, j * 128 - qc * 512)
                bS = SB[i % NPT]
                pt = PT[i % NPT]; d_pt = d_PT[i % NPT]
                C.mm(C.pb(bS)[:, col0:512], kT[b][:, j * 128:(j + 1) * 128],
                     qT[b][:, qc * 512 + col0:(qc + 1) * 512], True, True,
                     r=[d_kT[b], d_qT[b]], w=[C.psd[bS]])
                C.act(pt[:, col0:512], C.pb(bS)[:, col0:512], AF.Exp, r=[C.psd[bS]], w=[d_pt])
                if j >= 4 * qc:
                    C.G(lambda e, pt=pt, col0=col0: e.tensor_tensor(out=pt[:, col0:col0 + 128],
                                                                  in0=pt[:, col0:col0 + 128], in1=tri[:], op=ALU.mult),
                        r=[d_pt, d_k], w=[d_pt])

            def stage_b(i):
                h, qc, j, nj = items[i]
                b = h % 2
                r0 = (h % 2) * 64
                rd = 64 - r0
                c = h // 2
                n = h * 4 + qc
                bO = 3 + n % 2
                col0 = max(0, j * 128 - qc * 512)
                pt = PT[i % NPT]; d_pt = d_PT[i % NPT]
                C.mm(C.pb(bO)[:, col0:512], Vh[b][:, j, :], pt[:, col0:512], j == 0, j == nj - 1,
                     r=[d_V[b], d_pt], w=[C.psd[bO]])
                if j == nj - 1:
                    nb = n % 2
                    bW = 5
                    rs = slice(r0, r0 + 64); rq = slice(rd, rd + 64)
                    C.act(lnd[nb][rq, :], C.pb(bO)[rq, :], AF.Ln, r=[C.psd[bO]], w=[d_nrm[nb]])
                    C.act(rden[nb][rq, :], lnd[nb][rq, :], AF.Exp, r=[d_nrm[nb]], w=[d_nrm[nb]], scale=-1.0)
                    def fin(rs=rs, rq=rq, rd=rd, bO=bO, bW=bW, nb=nb, c=c, qc=qc):
                        C.mm(C.pb(bW)[rs, :], swp[rq, rd:rd + 64], rden[nb][rq, :], True, True,
                             r=[d_k, d_nrm[nb]], w=[C.psd[bW]])
                        C.V(lambda e: e.tensor_tensor(
                            out=ot[nb][rs, :], in0=C.pb(bO)[rs, :], in1=sgT[rs, c, qc * 512:(qc + 1) * 512], op=ALU.mult),
                            r=[C.psd[bO], d_sg4[qc]], w=[d_ot[nb]])
                        C.V(lambda e: e.tensor_tensor(
                            out=sgT[rs, c, qc * 512:(qc + 1) * 512], in0=ot[nb][rs, :], in1=C.pb(bW)[rs, :], op=ALU.mult),
                            r=[d_ot[nb], C.psd[bW]], w=[d_sg4[qc]])
                    pending.append([2, fin])

            pending = []

            def run_pending(force=False):
                for p in list(pending):
                    p[0] -= 1
                    if force or p[0] <= 0:
                        p[1]()
                        pending.remove(p)

            for _ in proj_units(0, 0):
                pass
            NI = len(items)
            units = None
            LOOK = 2
            for i in range(min(LOOK, NI)):
                stage_a(i)
            for i in range(NI):
                h, qc, j, nj = items[i]
                if qc == 0 and j == 0 and h + 1 < H:
                    units = proj_units(h + 1, (h + 1) % 2)
                if i + LOOK < NI:
                    if items[i + LOOK][0] != h and units is not None:
                        for _ in units:
                            pass
                        units = None
                    stage_a(i + LOOK)
                run_pending()
                stage_b(i)
                if units is not None:
                    if next(units, "done") == "done":
                        units = None
            run_pending(force=True)
            d_go = [d_sg4[t // 4] for t in range(NT)]
            out_proj(C, sgT, d_go, 8, None, h_in, d_hin, h_out, d_hout, banks=(0, 1, 2, 7), wo_pre=(wo, d_wo))
            C.release()
        C.release()


LAYER_FNS[0] = layer_mla

def layer_gla(C, W, h_in, d_hin, h_out, d_hout):
    w_in = W["gla_w_in"]
    with ExitStack() as L:
        qtT = C.sb(L, [128, 4, S_], BF16); d_qt = deps(4)
        ktT = C.sb(L, [128, 4, S_], BF16); d_kt = deps(4)
        vtok = C.sb(L, [128, NT, 1024], BF16); d_v = deps(NT)
        gsg = C.sb(L, [128, NT, 1024], BF16); d_gsg = deps(NT)
        elast = C.sb(L, [128, 4, 32], F32); d_el = Dep()
        with ExitStack() as PA:
            uT = C.sb(PA, [128, 8, S_], BF16); d_uT = deps(NT)
            norm_T(C, PA, h_in, W["norm_g"], uT, d_uT, d_hin)
            d_uTb = [[d_uT[4 * tb + i] for i in range(4)] for tb in range(4)]
            wgk = C.sb(PA, [128, 8, 16], BF16); d_wgk = Dep()
            C.load_w(wgk[:], w_in[:, 3072:3088], d_wgk)
            gkT = C.sb(PA, [32, S_], BF16); d_gkT = Dep()
            wgk2 = C.sb(PA, [32, 512], BF16); d_wgk2 = Dep()
            C.V(lambda e: e.memset(gkT[:], 0.0), w=[d_gkT])
            C.V(lambda e: e.memset(wgk2[:], 0.0), w=[d_wgk2])
            C.DG(lambda e: e.dma_start(out=wgk2[0:16, :], in_=W["gla_w_gk2"]), w=[d_wgk2])
            for tb in range(4):
                bk = 2 + tb % 2
                for kc in range(8):
                    C.mm(C.pb(bk)[0:16, :], wgk[:, kc, :], uT[:, kc, tb * 512:(tb + 1) * 512], kc == 0, kc == 7,
                         r=d_uTb[tb] + [d_wgk], w=[C.psd[bk]])
                C.V(lambda e, tb=tb, bk=bk: e.tensor_copy(out=gkT[0:16, tb * 512:(tb + 1) * 512], in_=C.pb(bk)[0:16, :]),
                    r=[C.psd[bk]], w=[d_gkT])
            nbgk = C.sb(PA, [128, 4], F32); d_nb = Dep()
            C.load_col(nbgk[:], W["gla_b_gk"], d_nb, 4)
            C.V(lambda e: e.tensor_scalar(out=nbgk[:], in0=nbgk[:], scalar1=-1.0, scalar2=None, op0=ALU.mult),
                r=[d_nb], w=[d_nb])
            rmask = C.sb(PA, [128, S_], BF16); d_rm = Dep()
            C.D(lambda e: e.dma_start(out=rmask[:], in_=C.CD["rmask"]), w=[d_rm])
            Lh = C.sb(PA, [128, S_], F32); d_Lh = Dep()
            Bp = C.sb(PA, [128, S_], F32); d_Bp = Dep()
            wq = [C.sb(PA, [128, 8, 128], BF16) for _ in range(2)]; d_wq = deps(2)
            wk = [C.sb(PA, [128, 8, 128], BF16) for _ in range(2)]; d_wk = deps(2)
            QS = 128 ** -0.5
            for h in range(4):
                b = h % 2
                C.load_w(wq[b][:], w_in[:, h * 128:(h + 1) * 128], d_wq[b])
                C.load_w(wk[b][:], w_in[:, 512 + h * 128:512 + (h + 1) * 128], d_wk[b])
                for tb in range(4):
                    bk = 2 + tb % 2
                    C.mm(C.pb(bk), wgk2[:, h * 128:(h + 1) * 128], gkT[:, tb * 512:(tb + 1) * 512], True, True,
                         r=[d_wgk2, d_gkT], w=[C.psd[bk]])
                    C.act(Lh[:, tb * 512:(tb + 1) * 512], C.pb(bk), AF.Exp, r=[C.psd[bk], d_nb], w=[d_Lh],
                          scale=-1.0, bias=nbgk[:, h:h + 1])
                C.act(Lh[:], Lh[:], AF.Ln, r=[d_Lh], w=[d_Lh], bias=1.0)
                C.V(lambda e: e.tensor_tensor_scan(out=Bp[:], data0=rmask[:], data1=Lh[:], initial=0.0,
                                                   op0=ALU.mult, op1=ALU.add), r=[d_Lh, d_rm], w=[d_Bp])
                C.act(Lh[:], Bp[:], AF.Exp, r=[d_Bp], w=[d_Lh], scale=-1.0 / 16)
                C.act(Bp[:], Bp[:], AF.Exp, r=[d_Bp], w=[d_Bp], scale=1.0 / 16)
                C.G(lambda e, h=h: e.tensor_copy(out=elast[:, h, :],
                                                 in_=Lh[:].rearrange("p (n c) -> p n c", c=64)[:, :, 63]),
                    r=[d_Lh], w=[d_el])
                for tb in range(4):
                    bq = 4 + tb % 2; bkk = 6 + tb % 2
                    for kc in range(8):
                        C.mm(C.pb(bq), wq[b][:, kc, :], uT[:, kc, tb * 512:(tb + 1) * 512], kc == 0, kc == 7,
                             r=d_uTb[tb] + [d_wq[b]], w=[C.psd[bq]])
                    C.V(lambda e, h=h, tb=tb, bq=bq: e.scalar_tensor_tensor(
                        out=qtT[:, h, tb * 512:(tb + 1) * 512], in0=C.pb(bq), scalar=QS,
                        in1=Lh[:, tb * 512:(tb + 1) * 512], op0=ALU.mult, op1=ALU.mult),
                        r=[C.psd[bq], d_Lh], w=[d_qt[tb]])
                    for kc in range(8):
                        C.mm(C.pb(bkk), wk[b][:, kc, :], uT[:, kc, tb * 512:(tb + 1) * 512], kc == 0, kc == 7,
                             r=d_uTb[tb] + [d_wk[b]], w=[C.psd[bkk]])
                    C.V(lambda e, h=h, tb=tb, bkk=bkk: e.tensor_tensor(
                        out=ktT[:, h, tb * 512:(tb + 1) * 512], in0=C.pb(bkk), in1=Bp[:, tb * 512:(tb + 1) * 512],
                        op=ALU.mult), r=[C.psd[bkk], d_Bp], w=[d_kt[tb]])
            wv = [C.sb(PA, [128, 8, 512], BF16) for _ in range(2)]; d_wv = deps(2)
            for i in range(2):
                C.load_w(wv[i][:], w_in[:, 1024 + i * 512:1024 + (i + 1) * 512], d_wv[i])
            i = 0
            for t in range(NT):
                for nb in range(2):
                    bk = (0, 1, 2, 3)[i % 4]; i += 1
                    for kc in range(8):
                        C.mm(C.pb(bk), uT[:, kc, t * 128:(t + 1) * 128], wv[nb][:, kc, :], kc == 0, kc == 7,
                             r=[d_uT[t], d_wv[nb]], w=[C.psd[bk]])
                    C.act(vtok[:, t, nb * 512:(nb + 1) * 512], C.pb(bk), AF.Copy, r=[C.psd[bk]], w=[d_v[t]])
            for i in range(2):
                C.load_w(wv[i][:], w_in[:, 2048 + i * 512:2048 + (i + 1) * 512], d_wv[i])
            go_bc = C.sb(PA, [128, 256], F32); d_gob = Dep()
            C.load_bc(go_bc[:], W["gla_g_o"], d_gob)
            sgt = [C.sb(PA, [128, 1024], F32) for _ in range(2)]; d_sgt = deps(2)
            i = 0
            for t in range(NT):
                b = t % 2
                for nb in range(2):
                    bk = (4, 5, 6, 7)[i % 4]; i += 1
                    for kc in range(8):
                        C.mm(C.pb(bk), uT[:, kc, t * 128:(t + 1) * 128], wv[nb][:, kc, :], kc == 0, kc == 7,
                             r=[d_uT[t], d_wv[nb]], w=[C.psd[bk]])
                    C.act(sgt[b][:, nb * 512:(nb + 1) * 512], C.pb(bk), AF.Silu, r=[C.psd[bk]], w=[d_sgt[b]])
                C.G(lambda e, t=t, b=b: e.tensor_tensor(
                    out=gsg[:, t, :].rearrange("p (a c) -> p a c", a=4), in0=sgt[b][:].rearrange("p (a c) -> p a c", a=4),
                    in1=go_bc[:].unsqueeze(1).to_broadcast([128, 4, 256]), op=ALU.mult),
                    r=[d_sgt[b], d_gob], w=[d_gsg[t]])
            C.release()
        with ExitStack() as PB:
            goT = C.sb(PB, [128, 8, S_], BF16); d_go = deps(NT)
            wo = C.sb(PB, [128, 8, D_], BF16); d_wo = Dep()
            C.load_w(wo[:, 0:4, :], W["gla_w_out"][0:512, :], d_wo)
            C.load_w(wo[:, 4:8, :], W["gla_w_out"][512:1024, :], d_wo)
            mask2 = C.sb(PB, [128, 128], BF16); d_m2 = Dep()
            C.D(lambda e: e.dma_start(out=mask2[:], in_=C.CD["mask2"]), w=[d_m2])
            tmpS = C.sb(PB, [128, 4, 256], F32); d_tmpS = Dep()
            Sbf = [C.sb(PB, [128, 4, 256], BF16) for _ in range(2)]; d_Sbf = deps(2)
            C.V(lambda e: e.memset(Sbf[0][:], 0.0), w=[d_Sbf[0]])
            attm = [C.sb(PB, [128, 4, 128], BF16) for _ in range(2)]; d_attm = deps(2)
            ktk = [C.sb(PB, [128, 4, 128], BF16) for _ in range(2)]; d_ktk = deps(2)
            ss = C.sb(PB, [128, NT, 4], F32); rs = C.sb(PB, [128, NT, 4], F32); d_ss = deps(NT); d_rs = deps(NT)
            junk = C.sb(PB, [128, 256], BF16); d_junk = Dep()
            gotok = [C.sb(PB, [128, 1024], BF16) for _ in range(2)]; d_gotok = deps(2)
            osb = [C.sb(PB, [128, 1024], F32) for _ in range(2)]; d_osb = deps(2)

            def obank(t, h):
                return (2 + 2 * (t % 2) + h // 2, (h % 2) * 256)

            def g0(t):
                b = t % 2
                tb = t // 4
                tsl = slice(t * 128, (t + 1) * 128)
                pv = C.pbh(0)
                for h in range(4):
                    C.tr(pv[:, h * 128:(h + 1) * 128], ktT[:, h, tsl], C.ident_bf[:], r=[d_kt[tb], C.d_const], w=[C.psd[0]])
                for h in range(4):
                    C.mm(C.pb(1)[:, h * 128:(h + 1) * 128], ktT[:, h, tsl], qtT[:, h, tsl], True, True,
                         r=[d_kt[tb], d_qt[tb]], w=[C.psd[1]])
                yield
                C.V(lambda e, b=b, pv=pv: e.tensor_copy(out=ktk[b][:], in_=pv[:, 0:512].rearrange("p (a c) -> p a c", a=4)),
                    r=[C.psd[0]], w=[d_ktk[b]])
                C.V(lambda e, b=b: e.tensor_tensor(out=attm[b][:], in0=C.pb(1).rearrange("p (a c) -> p a c", a=4),
                                                   in1=mask2[:].unsqueeze(1).to_broadcast([128, 4, 128]), op=ALU.mult),
                    r=[C.psd[1], d_m2], w=[d_attm[b]])
                yield
                for h in range(4):
                    bk, c0 = obank(t, h)
                    C.T(lambda e, bk=bk, c0=c0, b=b, h=h, t=t: e.matmul(
                        C.pb(bk)[:, c0:c0 + 256], lhsT=attm[b][:, h, :], rhs=vtok[:, t, h * 256:(h + 1) * 256],
                        start=(h % 2 == 0), stop=False, skip_group_check=True), r=[d_attm[b], d_v[t]], w=[C.psd[bk]])

            def g1(t):
                b = t % 2
                tb = t // 4
                for half in range(2):
                    r0 = half * 64
                    n = 2 * t + half
                    for h in range(4):
                        bk, c0 = obank(t, h)
                        C.T(lambda e, bk=bk, c0=c0, h=h, t=t, r0=r0, half=half: e.matmul(
                            C.pb(bk)[r0:r0 + 64, c0:c0 + 256], lhsT=qtT[:, h, t * 128 + r0:t * 128 + r0 + 64],
                            rhs=Sbf[half][:, h, :], start=False, stop=(half == 1), skip_group_check=True),
                            r=[d_qt[tb], d_Sbf[half]], w=[C.psd[bk]])
                    for h in range(4):
                        bkS = 6 + h // 2
                        C.mm(C.pb(bkS)[:, (h % 2) * 256:(h % 2 + 1) * 256], ktk[b][r0:r0 + 64, h, :],
                             vtok[r0:r0 + 64, t, h * 256:(h + 1) * 256], True, True,
                             r=[d_ktk[b], d_v[t]], w=[C.psd[bkS]])
                    C.V(lambda e, half=half: e.tensor_tensor(out=tmpS[:], in0=C.ps[:, 6 * 512:8 * 512].rearrange("p (a c) -> p a c", a=4),
                                                            in1=Sbf[half][:], op=ALU.add), r=[C.psd[6], C.psd[7], d_Sbf[half]], w=[d_tmpS])
                    C.V(lambda e, n=n, half=half: e.tensor_tensor(out=Sbf[1 - half][:], in0=tmpS[:],
                                                                 in1=elast[:, :, n:n + 1].to_broadcast([128, 4, 256]), op=ALU.mult),
                        r=[d_tmpS, d_el], w=[d_Sbf[1 - half]])
                    yield
                for i_ in range(2):
                    bk = 2 + 2 * (t % 2) + i_
                    C.act(osb[b][:, i_ * 512:(i_ + 1) * 512], C.pb(bk), AF.Copy, r=[C.psd[bk]], w=[d_osb[b]])

            def g2(t):
                b = t % 2
                for h in range(4):
                    C.act(junk[:], osb[b][:, h * 256:(h + 1) * 256], AF.Square, r=[d_osb[b]], w=[d_junk, d_ss[t]],
                          accum_out=ss[:, t, h:h + 1])
                yield
                rstd_ops(C, rs[:, t, :], ss[:, t, :], 256, [d_ss[t]], [d_rs[t]])
                yield
                for h in range(4):
                    C.V(lambda e, t=t, b=b, h=h: e.scalar_tensor_tensor(
                        out=gotok[b][:, h * 256:(h + 1) * 256], in0=osb[b][:, h * 256:(h + 1) * 256], scalar=rs[:, t, h:h + 1],
                        in1=gsg[:, t, h * 256:(h + 1) * 256], op0=ALU.mult, op1=ALU.mult),
                        r=[d_osb[b], d_rs[t], d_gsg[t]], w=[d_gotok[b]])
                yield
                pv = C.pbh(0)
                for c in range(8):
                    C.tr(pv[:, c * 128:(c + 1) * 128], gotok[b][:, c * 128:(c + 1) * 128], C.ident_bf[:],
                         r=[d_gotok[b], C.d_const], w=[C.psd[0]])
                C.act(goT[:, :, t * 128:(t + 1) * 128], pv.rearrange("p (c n) -> p c n", c=8), AF.Copy,
                      r=[C.psd[0]], w=[d_go[t]])

            LV = {1: ("s1", "s0", "s2"), 2: ("s0", "s1", "s2"), 3: ("s1", "s0", "s2"), 4: ("s2",)}
            for s_ in range(NT + 2):
                gens = {}
                if 0 <= s_ - 2 < NT:
                    gens["s2"] = g2(s_ - 2)
                if 0 <= s_ - 1 < NT:
                    gens["s1"] = g1(s_ - 1)
                if s_ < NT:
                    gens["s0"] = g0(s_)
                for lvl in (1, 2, 3, 4):
                    for nm in LV[lvl]:
                        if nm in gens:
                            next(gens[nm], None)
                for gn_ in gens.values():
                    for _ in gn_:
                        pass
            out_proj(C, goT, d_go, 8, None, h_in, d_hin, h_out, d_hout, banks=(1, 2, 3, 4), wo_pre=(wo, d_wo))
            C.release()
        C.release()


LAYER_FNS[1] = layer_gla

def layer_ssd(C, W, h_in, d_hin, h_out, d_hout):
    w_in = W["ssd_w_in"]
    G_ = 8
    with ExitStack() as L:
        goT = C.sb(L, [128, 16, S_], BF16); d_go = deps(NT)
        with ExitStack() as PA:
            uT = C.sb(PA, [128, 8, S_], BF16); d_uT = deps(NT)
            norm_T(C, PA, h_in, W["norm_g"], uT, d_uT, d_hin)
            d_uTb = [[d_uT[4 * tb + i] for i in range(4)] for tb in range(4)]
            U2b = C.sb(PA, [128, 128], BF16); V2b = C.sb(PA, [128, 128], BF16)
            U2f = C.sb(PA, [128, 128], F32); V2f = C.sb(PA, [128, 128], F32)
            onA = C.sb(PA, [128, 128], F32); onB = C.sb(PA, [128, 128], F32); d_k = Dep()
            negm = C.sb(PA, [128, 512], BF16)
            C.D(lambda e: e.dma_start(out=negm[:], in_=C.CD["negm4"]), w=[d_k])
            for dst, nm in ((U2b, "U2b"), (V2b, "mask2"), (U2f, "U2f"), (V2f, "V2f"), (onA, "onesA"), (onB, "onesB")):
                C.D(lambda e, dst=dst, nm=nm: e.dma_start(out=dst[:], in_=C.CD[nm]), w=[d_k])
            dtb = C.sb(PA, [128, 32], F32); alog = C.sb(PA, [128, 32], F32); dsk = C.sb(PA, [128, 32], F32)
            gn = C.sb(PA, [128, 2048], F32); d_p = Dep()
            C.load_bc(dtb[:], W["ssd_dt_bias"], d_p)
            C.load_bc(alog[:], W["ssd_a_log"], d_p)
            C.load_bc(dsk[:], W["ssd_d"], d_p)
            C.load_bc(gn[:], W["ssd_g_norm"], d_p)
            cw = C.sb(PA, [128, 4, 32], F32); cb = C.sb(PA, [128, 32], F32); d_cw = Dep()
            for k in range(4):
                C.load_col(cw[:, k, :], W["ssd_conv_w"][k, :], d_cw, 32)
            C.load_col(cb[:], W["ssd_conv_b"], d_cw, 32)
            wdt = C.sb(PA, [128, 8, 32], BF16); d_wdt = Dep()
            C.load_w(wdt[:], w_in[:, 6144:6176], d_wdt)
            dtt = C.sb(PA, [128, NT, 32], F32); a_tok = C.sb(PA, [128, NT, 32], F32)
            ecs = C.sb(PA, [128, NT, 32], F32); dec = C.sb(PA, [128, NT, 32], F32)
            elast = C.sb(PA, [128, 32, 32], F32); eA = C.sb(PA, [128, 32], F32)
            d_dt = Dep(); d_a = Dep(); d_ecs = Dep(); d_dec = Dep(); d_el = Dep()
            for t in range(NT):
                for kc in range(8):
                    C.mm(C.pb(0)[:, t * 32:(t + 1) * 32], uT[:, kc, t * 128:(t + 1) * 128], wdt[:, kc, :], kc == 0, kc == 7,
                         r=[d_uT[t], d_wdt], w=[C.psd[0]])
            C.V(lambda e: e.tensor_tensor(out=dtt[:], in0=C.pb(0).rearrange("p (a c) -> p a c", a=NT),
                                          in1=dtb[:].unsqueeze(1).to_broadcast([128, NT, 32]), op=ALU.add),
                r=[C.psd[0], d_p], w=[d_dt])
            C.act(dtt[:], dtt[:], AF.Exp, r=[d_dt], w=[d_dt])
            C.act(dtt[:], dtt[:], AF.Ln, r=[d_dt], w=[d_dt], bias=1.0)
            C.act(eA[:], alog[:], AF.Exp, r=[d_p], w=[d_a])
            C.V(lambda e: e.scalar_tensor_tensor(out=a_tok[:], in0=dtt[:], scalar=-1.0,
                                                 in1=eA[:].unsqueeze(1).to_broadcast([128, NT, 32]),
                                                 op0=ALU.mult, op1=ALU.mult), r=[d_dt, d_a], w=[d_a])
            for t in range(NT):
                C.mm(C.pb(1)[:, t * 32:(t + 1) * 32], V2f[:], a_tok[:, t, :], True, True, r=[d_k, d_a], w=[C.psd[1]])
            C.act(ecs[:], C.pb(1).rearrange("p (a c) -> p a c", a=NT), AF.Exp, r=[C.psd[1]], w=[d_ecs])
            for t in range(NT):
                C.mm(C.pb(2)[:, t * 32:(t + 1) * 32], U2f[:], a_tok[:, t, :], True, True, r=[d_k, d_a], w=[C.psd[2]])
            C.act(dec[:], C.pb(2).rearrange("p (a c) -> p a c", a=NT), AF.Exp, r=[C.psd[2]], w=[d_dec])
            for t in range(NT):
                for half in range(2):
                    n = 2 * t + half
                    bk = 3 + n // 16
                    C.mm(C.pb(bk)[:, (n % 16) * 32:(n % 16 + 1) * 32], (onA, onB)[half][:], a_tok[:, t, :], True, True,
                         r=[d_k, d_a], w=[C.psd[bk]])
            C.act(elast[:], C.ps[:, 3 * 512:5 * 512].rearrange("p (a c) -> p a c", a=32), AF.Exp,
                  r=[C.psd[3], C.psd[4]], w=[d_el])
            xTc = [C.sb(PA, [128, S_], BF16) for _ in range(2)]
            BT = C.sb(PA, [128, S_], BF16); CT = C.sb(PA, [128, S_], BF16)
            d_xT = deps(2); d_BT = Dep(); d_CT = Dep()
            x_tok = C.sb(PA, [128, NT, 256], BF16); d_xtok = deps(4)
            B_tok = C.sb(PA, [128, NT, 128], BF16); d_Btok = deps(2)
            sz = C.sb(PA, [128, NT, 256], BF16); d_sz = deps(NT)
            ub = [C.sb(PA, [128, 3 + S_], BF16) for _ in range(2)]; d_ub = deps(2)
            for b in range(2):
                C.V(lambda e, b=b: e.memset(ub[b][:, 0:3], 0.0), w=[d_ub[b]])
            wc = [C.sb(PA, [128, 8, 128], BF16) for _ in range(2)]; d_wc = deps(2)
            dg = [C.sb(PA, [128, 4, 128], BF16) for _ in range(2)]; d_dg = deps(2)
            wz = C.sb(PA, [128, 8, 256], BF16); d_wz = Dep()
            Sst = C.sb(PA, [128, 4, 64], F32); tmpS = C.sb(PA, [128, 4, 64], F32); d_S = Dep(); d_tmpS = Dep()
            Sbf = [C.sb(PA, [128, 256], BF16) for _ in range(2)]; d_Sbf = deps(2)
            cbm = [C.sb(PA, [128, 128], F32) for _ in range(2)]; d_cbm = deps(2)
            aV = [C.sb(PA, [128, 4, 128], BF16) for _ in range(2)]; d_aV = deps(2)
            lm = [C.sb(PA, [128, 4, 128], F32) for _ in range(2)]; d_lm = deps(2)
            Mh = [C.sb(PA, [128, 4, 128], BF16) for _ in range(2)]; d_M = deps(2)
            xdt = [C.sb(PA, [128, 4, 64], BF16) for _ in range(2)]; d_xdt = deps(2)
            xdd = [C.sb(PA, [128, 4, 64], BF16) for _ in range(2)]; d_xdd = deps(2)
            xD = [C.sb(PA, [128, 4, 64], BF16) for _ in range(2)]; d_xD = deps(2)
            t1 = [C.sb(PA, [128, 4, 64], F32) for _ in range(2)]; d_t1 = deps(2)
            yz = [C.sb(PA, [128, 256], F32) for _ in range(2)]; d_yz = deps(2)
            yn = [C.sb(PA, [128, 256], BF16) for _ in range(2)]; d_yn = deps(2)
            ss = C.sb(PA, [128, G_, NT], F32); rs = C.sb(PA, [128, G_, NT], F32)
            junk = C.sb(PA, [128, 256], BF16); d_junk = Dep()
            ci = 0
            d_dS = [deps(2), deps(2)]
            for g in range(G_):
                hs = slice(4 * g, 4 * g + 4)
                chunks = ((2 * g, xTc[0], d_xT[0]), (2 * g + 1, xTc[1], d_xT[1]), (16 + g, BT, d_BT), (24 + g, CT, d_CT))

                def proj_c(k_, b):
                    cc = chunks[k_][0]
                    C.load_w(wc[b][:], w_in[:, 2048 + cc * 128:2048 + (cc + 1) * 128], d_wc[b])
                    for k in range(4):
                        C.V(lambda e, b=b, k=k, cc=cc: e.tensor_scalar(out=dg[b][:, k, :], in0=C.ident_bf[:],
                                                                       scalar1=cw[:, k, cc:cc + 1], scalar2=None, op0=ALU.mult),
                            r=[d_cw, C.d_const], w=[d_dg[b]])
                    for tb in range(4):
                        bk = tb
                        for kc in range(8):
                            C.mm(C.pb(bk), wc[b][:, kc, :], uT[:, kc, tb * 512:(tb + 1) * 512], kc == 0, kc == 7,
                                 r=d_uTb[tb] + [d_wc[b]], w=[C.psd[bk]])
                        C.act(ub[b][:, 3 + tb * 512:3 + (tb + 1) * 512], C.pb(bk), AF.Copy, r=[C.psd[bk]], w=[d_ub[b]])

                def conv_c(k_, b):
                    cc, dst, d_dst = chunks[k_]
                    for tb in range(4):
                        bk = 4 + tb
                        for k in range(4):
                            C.mm(C.pb(bk), dg[b][:, k, :], ub[b][:, tb * 512 + k:tb * 512 + k + 512], k == 0, k == 3,
                                 r=[d_ub[b], d_dg[b]], w=[C.psd[bk]])
                        C.act(dst[:, tb * 512:(tb + 1) * 512], C.pb(bk), AF.Silu, r=[C.psd[bk], d_cw], w=[d_dst],
                              bias=cb[:, cc:cc + 1])

                proj_c(0, 0)
                for k_ in range(4):
                    if k_ + 1 < 4:
                        proj_c(k_ + 1, (k_ + 1) % 2)
                    conv_c(k_, k_ % 2)
                for t4 in range(4):
                    bk = t4 % 2
                    pv = C.pbh(bk)
                    for i in range(4):
                        for c in range(2):
                            C.tr(pv[:, (i * 2 + c) * 128:(i * 2 + c + 1) * 128], xTc[c][:, (t4 * 4 + i) * 128:(t4 * 4 + i + 1) * 128],
                                 C.ident_bf[:], r=[d_xT[c], C.d_const], w=[C.psd[bk]])
                    C.act(x_tok[:, t4 * 4:(t4 + 1) * 4, :], pv.rearrange("p (a c) -> p a c", a=4), AF.Copy,
                          r=[C.psd[bk]], w=[d_xtok[t4]])
                for t8 in range(2):
                    bk = 2 + t8
                    pv = C.pbh(bk)
                    for i in range(8):
                        C.tr(pv[:, i * 128:(i + 1) * 128], BT[:, (t8 * 8 + i) * 128:(t8 * 8 + i + 1) * 128], C.ident_bf[:],
                             r=[d_BT, C.d_const], w=[C.psd[bk]])
                    C.V(lambda e, t8=t8, pv=pv: e.tensor_copy(out=B_tok[:, t8 * 8:(t8 + 1) * 8, :],
                                                             in_=pv.rearrange("p (a c) -> p a c", a=8)),
                        r=[C.psd[bk]], w=[d_Btok[t8]])
                C.load_w(wz[:], w_in[:, g * 256:(g + 1) * 256], d_wz)
                for t2 in range(8):
                    bk = 4 + t2 % 4
                    for i in range(2):
                        t = t2 * 2 + i
                        for kc in range(8):
                            C.mm(C.pb(bk)[:, i * 256:(i + 1) * 256], uT[:, kc, t * 128:(t + 1) * 128], wz[:, kc, :],
                                 kc == 0, kc == 7, r=[d_uT[t], d_wz], w=[C.psd[bk]])
                    C.act(sz[:, t2 * 2:t2 * 2 + 2, :], C.pb(bk).rearrange("p (a c) -> p a c", a=2), AF.Silu,
                          r=[C.psd[bk]], w=[d_sz[t2 * 2], d_sz[t2 * 2 + 1]])
                C.V(lambda e: e.memset(Sbf[0][:], 0.0), w=[d_Sbf[0]])
                def st0(t):
                    b = t % 2
                    tsl = slice(t * 128, (t + 1) * 128)
                    xt_ = x_tok[:, t, :].rearrange("p (a c) -> p a c", a=4)
                    d_x = d_xtok[t // 4]
                    for h_ in range(4):
                        C.act(aV[b][:, h_, :], V2b[:], AF.Copy, r=[d_a, d_k], w=[d_aV[b]],
                              scale=a_tok[:, t, 4 * g + h_:4 * g + h_ + 1])
                    C.G(lambda e, hs=hs, b=b, t=t, xt_=xt_: e.tensor_tensor(out=xdt[b][:], in0=xt_,
                                                                   in1=dtt[:, t, hs].unsqueeze(2).to_broadcast([128, 4, 64]), op=ALU.mult),
                        r=[d_x, d_dt], w=[d_xdt[b]])
                    C.G(lambda e, hs=hs, b=b, t=t: e.tensor_tensor(out=xdd[b][:], in0=xdt[b][:],
                                                           in1=dec[:, t, hs].unsqueeze(2).to_broadcast([128, 4, 64]), op=ALU.mult),
                        r=[d_xdt[b], d_dec], w=[d_xdd[b]])
                    C.G(lambda e, hs=hs, b=b, xt_=xt_: e.tensor_tensor(out=xD[b][:], in0=xt_,
                                                              in1=dsk[:, hs].unsqueeze(2).to_broadcast([128, 4, 64]), op=ALU.mult),
                        r=[d_x, d_p], w=[d_xD[b]])
                    C.mm(C.pb(0)[:, 0:128], BT[:, tsl], CT[:, tsl], True, True, r=[d_BT, d_CT], w=[C.psd[0]])
                    yield
                    C.mm(C.pb(1), U2b[:], aV[b][:].rearrange("p a c -> p (a c)"), True, False, r=[d_k, d_aV[b]], w=[C.psd[1]])
                    C.mm(C.pb(1), C.ident_bf[:], negm[:], False, True, r=[C.d_const, d_k], w=[C.psd[1]])
                    for half in range(2):
                        r0 = half * 64
                        C.mm(C.pb(4 + half)[:, b * 256:(b + 1) * 256], B_tok[r0:r0 + 64, t, :],
                             xdd[b][r0:r0 + 64, :, :].rearrange("p a c -> p (a c)"),
                             True, True, r=[d_Btok[t // 8], d_xdd[b]], w=[C.psd[4 + half]])
                    yield
                    C.act(lm[b][:], C.pb(1).rearrange("p (a c) -> p a c", a=4), AF.Exp, r=[C.psd[1]], w=[d_lm[b]])
                    yield
                    C.V(lambda e, b=b: e.tensor_tensor(out=Mh[b][:], in0=lm[b][:],
                                                       in1=C.pb(0)[:, 0:128].unsqueeze(1).to_broadcast([128, 4, 128]), op=ALU.mult),
                        r=[d_lm[b], C.psd[0]], w=[d_M[b]])

                def st1(t):
                    b = t % 2
                    bY = 2 + b
                    C.mm(C.pb(bY)[:, 0:256], C.ident_bf[:], xD[b][:].rearrange("p a c -> p (a c)"), True, False,
                         r=[C.d_const, d_xD[b]], w=[C.psd[bY]])
                    for h in range(4):
                        C.mm(C.pb(bY)[:, h * 64:(h + 1) * 64], Mh[b][:, h, :], xdt[b][:, h, :], False, h == 3,
                             r=[d_M[b], d_xdt[b]], w=[C.psd[bY]])
                    for half in range(2):
                        r0 = half * 64
                        n = 2 * t + half
                        C.mm(C.pb(bY)[r0:r0 + 64, 256:512], CT[:, t * 128 + r0:t * 128 + r0 + 64], Sbf[half][:], True, True,
                             r=[d_CT, d_Sbf[half]], w=[C.psd[bY]])
                        C.V(lambda e, hs=hs, n=n, half=half: e.tensor_tensor(
                            out=tmpS[:], in0=Sbf[half][:].rearrange("p (a c) -> p a c", a=4),
                            in1=elast[:, n, hs].unsqueeze(2).to_broadcast([128, 4, 64]), op=ALU.mult),
                            r=[d_Sbf[half], d_el], w=[d_tmpS])
                        C.V(lambda e, half=half, b=b: e.tensor_tensor(
                            out=Sbf[1 - half][:].rearrange("p (a c) -> p a c", a=4),
                            in0=C.pb(4 + half)[:, b * 256:(b + 1) * 256].rearrange("p (a c) -> p a c", a=4),
                            in1=tmpS[:], op=ALU.add), r=[C.psd[4 + half], d_tmpS], w=[d_Sbf[1 - half]])
                        if half == 0:
                            yield

                def st2(t, g=g):
                    b = t % 2
                    bY = 2 + b
                    C.V(lambda e, hs=hs, b=b, t=t, bY=bY: e.tensor_tensor(out=t1[b][:], in0=C.pb(bY)[:, 256:512].rearrange("p (a c) -> p a c", a=4),
                                                           in1=ecs[:, t, hs].unsqueeze(2).to_broadcast([128, 4, 64]), op=ALU.mult),
                        r=[C.psd[bY], d_ecs], w=[d_t1[b]])
                    C.V(lambda e, b=b, bY=bY: e.tensor_tensor(out=t1[b][:], in0=C.pb(bY)[:, 0:256].rearrange("p (a c) -> p a c", a=4),
                                                       in1=t1[b][:], op=ALU.add), r=[C.psd[bY], d_t1[b]], w=[d_t1[b]])
                    yield
                    C.G(lambda e, b=b, t=t: e.tensor_tensor(out=yz[b][:], in0=t1[b][:].rearrange("p a c -> p (a c)"),
                                                           in1=sz[:, t, :], op=ALU.mult), r=[d_t1[b], d_sz[t]], w=[d_yz[b]])
                    yield
                    C.act(junk[:], yz[b][:], AF.Square, r=[d_yz[b]], w=[d_junk], accum_out=ss[:, g, t:t + 1])
                    rstd_ops(C, rs[:, g, t:t + 1], ss[:, g, t:t + 1], 256, [d_junk], [d_junk])
                    yield
                    C.V(lambda e, b=b, t=t, g=g: e.scalar_tensor_tensor(out=yn[b][:], in0=yz[b][:], scalar=rs[:, g, t:t + 1],
                                                                       in1=gn[:, g * 256:(g + 1) * 256], op0=ALU.mult, op1=ALU.mult),
                        r=[d_yz[b], d_junk, d_p], w=[d_yn[b]])
                    bk = 6 + (t // 4) % 2
                    pv = C.pbh(bk)
                    i = t % 4
                    for c in range(2):
                        C.tr(pv[:, (c * 4 + i) * 128:(c * 4 + i + 1) * 128], yn[b][:, c * 128:(c + 1) * 128], C.ident_bf[:],
                             r=[d_yn[b], C.d_const], w=[C.psd[bk]])
                    if i == 3:
                        t4 = t // 4
                        C.act(goT[:, 2 * g:2 * g + 2, t4 * 512:(t4 + 1) * 512], pv.rearrange("p (c m) -> p c m", c=2), AF.Copy,
                              r=[C.psd[bk]], w=[d_go[4 * t4 + j] for j in range(4)])

                LV = {1: ("s1", "s0", "s2"), 2: ("s0", "s1", "s2"), 3: ("s0", "s2"), 4: ("s0", "s2")}
                for s_ in range(NT + 2):
                    gens = {}
                    if 0 <= s_ - 2 < NT:
                        gens["s2"] = st2(s_ - 2)
                    if 0 <= s_ - 1 < NT:
                        gens["s1"] = st1(s_ - 1)
                    if s_ < NT:
                        gens["s0"] = st0(s_)
                    for lvl in (1, 2, 3, 4):
                        for nm in LV[lvl]:
                            if nm in gens:
                                next(gens[nm], None)
                    for gn_ in gens.values():
                        for _ in gn_:
                            pass
            C.dbg("dtt", dtt[:], [128, NT, 32], F32, [d_dt])
            C.dbg("a_tok", a_tok[:], [128, NT, 32], F32, [d_a])
            C.dbg("ecs", ecs[:], [128, NT, 32], F32, [d_ecs])
            C.dbg("dec", dec[:], [128, NT, 32], F32, [d_dec])
            C.dbg("elast", elast[:], [128, 32, 32], F32, [d_el])
            C.dbg("x_tok", x_tok[:], [128, NT, 256], BF16, d_xtok)
            C.dbg("B_tok", B_tok[:], [128, NT, 128], BF16, d_Btok)
            C.dbg("CT", CT[:], [128, S_], BF16, [d_CT])
            C.dbg("sz", sz[:], [128, NT, 256], BF16, d_sz)
            C.dbg("goT", goT[:], [128, 16, S_], BF16, d_go)
            C.dbg("Mh", Mh[1][:], [128, 4, 128], BF16, [d_M[1]])
            C.dbg("lm", lm[1][:], [128, 4, 128], F32, [d_lm[1]])
            C.dbg("yz", yz[1][:], [128, 256], F32, [d_yz[1]])
            C.release()
        out_proj(C, goT, d_go, 16, W["ssd_w_out"], h_in, d_hin, h_out, d_hout)
        C.release()


LAYER_FNS[3] = layer_ssd

WSPEC = {
    "norm_g": [4, 1024], "final_g": [1024],
    "mla_w_in": [1024, 1696], "mla_g_q": [384], "mla_w_uq": [384, 1536], "mla_g_kv": [256],
    "mla_w_ukv": [256, 2048], "mla_w_out": [1024, 1024],
    "gla_w_in": [1024, 3088], "gla_w_gk2": [16, 512], "gla_b_gk": [512], "gla_g_o": [256],
    "gla_w_out": [1024, 1024],
    "lru_w_in": [1024, 2560], "lru_conv_w": [4, 1280], "lru_conv_b": [1280], "lru_w_a": [10, 128, 128],
    "lru_b_a": [1280], "lru_w_x": [10, 128, 128], "lru_b_x": [1280], "lru_lam": [1280],
    "lru_w_out": [1280, 1024],
    "ssd_w_in": [1024, 6176], "ssd_conv_w": [4, 4096], "ssd_conv_b": [4096], "ssd_dt_bias": [32],
    "ssd_a_log": [32], "ssd_d": [32], "ssd_g_norm": [2048], "ssd_w_out": [2048, 1024],
}


def host_consts():
    c = {}
    c["ident_bf"] = np.eye(128, dtype=np.float32).astype(ml_dtypes.bfloat16)
    c["ident_f"] = np.eye(128, dtype=np.float32)
    k = np.arange(128)
    c["tri_bf"] = (k[None, :] >= k[:, None]).astype(np.float32).astype(ml_dtypes.bfloat16)
    c["ones_bf"] = np.ones((128, 64), np.float32).astype(ml_dtypes.bfloat16)
    invf = (np.float32(10000.0) ** (-np.arange(0, 32, 2, dtype=np.float32) / np.float32(32))).astype(np.float32)
    rm = np.ones((128, S_), np.float32); rm[:, ::64] = 0.0
    c["rmask"] = rm.astype(ml_dtypes.bfloat16)
    m2 = ((k[None, :] >= k[:, None]) & ((k[None, :] // 64) == (k[:, None] // 64))).astype(np.float32)
    c["mask2"] = m2.astype(ml_dtypes.bfloat16)
    same = (k[None, :] // 64) == (k[:, None] // 64)
    u2 = ((k[:, None] > k[None, :]) & same).astype(np.float32)
    c["U2b"] = u2.astype(ml_dtypes.bfloat16)
    c["U2f"] = u2
    c["V2f"] = m2.astype(np.float32)
    c["onesA"] = np.ascontiguousarray(np.broadcast_to((k[:, None] < 64).astype(np.float32), (128, 128)))
    c["onesB"] = np.ascontiguousarray(np.broadcast_to((k[:, None] >= 64).astype(np.float32), (128, 128)))
    c["negm4"] = np.tile(np.where(m2 > 0, 0.0, -30000.0).astype(np.float32), (1, 4)).astype(ml_dtypes.bfloat16)
    c["swap_f"] = np.eye(128, dtype=np.float32)
    c["invf"] = np.ascontiguousarray(np.broadcast_to(invf[None, :], (128, 16))).astype(np.float32)
    return c


CSPEC = {"ident_bf": ([128, 128], BF16), "ident_f": ([128, 128], F32), "tri_bf": ([128, 128], BF16),
         "ones_bf": ([128, 64], BF16), "invf": ([128, 16], F32), "rmask": ([128, S_], BF16),
         "mask2": ([128, 128], BF16), "U2b": ([128, 128], BF16), "U2f": ([128, 128], F32),
         "V2f": ([128, 128], F32), "swap_f": ([128, 128], F32), "negm4": ([128, 512], BF16), "onesA": ([128, 128], F32), "onesB": ([128, 128], F32)}

def build(layers=(0, 1, 2, 3), final=True):
    nc = bass.Bass("TRN2", target_bir_lowering=False)
    x = nc.dram_tensor("x", [S_, D_], F32, kind="ExternalInput").ap()
    pos = nc.dram_tensor("pos", [128, NT], I32, kind="ExternalInput").ap()
    W = {k: nc.dram_tensor(k, v, F32, kind="ExternalInput").ap() for k, v in WSPEC.items()}
    CD = {k: nc.dram_tensor(k, v[0], v[1], kind="ExternalInput").ap() for k, v in CSPEC.items()}
    out = nc.dram_tensor("out", [S_, D_], F32, kind="ExternalOutput").ap()
    hbuf = [nc.dram_tensor(f"hbuf{i}", [S_, D_], F32).ap() for i in range(2)]
    _BASE["r"] = {}
    with ExitStack() as st:
        C = Ctx(nc, st)
        C.pos = pos
        C.CD = CD
        C.d_const = Dep()
        C.ident_bf = C.sb(st, [128, 128], BF16)
        C.ident_f = C.sb(st, [128, 128], F32)
        C.D(lambda e: e.dma_start(out=C.ident_bf[:], in_=CD["ident_bf"]), w=[C.d_const])
        C.D(lambda e: e.dma_start(out=C.ident_f[:], in_=CD["ident_f"]), w=[C.d_const])
        h_cur, d_cur = x, deps(NT)
        nxt = 0
        for li in layers:
            Wl = dict(W)
            Wl["norm_g"] = W["norm_g"][li, :]
            h_nxt, d_nxt = hbuf[nxt], deps(NT)
            if not final and li == layers[-1]:
                h_nxt = out
            LAYER_FNS[li](C, Wl, h_cur, d_cur, h_nxt, d_nxt)
            h_cur, d_cur = h_nxt, d_nxt
            nxt ^= 1
        if final:
            final_norm(C, h_cur, d_cur, W["final_g"], out)
        else:
            C.out_toks.append(C.last_store)
            for t in range(NT):
                C.out_toks.append(d_cur[t].w)
        for tok in C.out_toks:
            C.S.wait_tok("sync", tok)
        C.S.emit()
    return nc


LAYER_FNS[2] = layer_lru


def make_in_maps(inputs):
    consts = host_consts()
    maps = []
    for b in range(8):
        m = {"x": np.ascontiguousarray(inputs["x"][b]),
             "pos": np.ascontiguousarray(np.asarray(inputs["positions"][b]).astype(np.int32).reshape(NT, 128).T)}
        for k in WSPEC:
            a = np.asarray(inputs[k], dtype=np.float32)
            if k not in ("norm_g", "final_g"):
                a = a[0]
            m[k] = np.ascontiguousarray(a)
        m.update(consts)
        maps.append(m)
    return maps


_NC_CACHE = {}


def kernel(**inputs):
    if "nc" not in _NC_CACHE:
        _NC_CACHE["nc"] = build()
    nc = _NC_CACHE["nc"]
    maps = make_in_maps(inputs)
    res = run_bass_kernel_spmd(nc, maps, core_ids=list(range(8)))
    return np.stack([np.asarray(r["out"], dtype=np.float32) for r in res.results], axis=0)
```

```python
import numpy as np
import ml_dtypes
import concourse.bass as bass
import concourse.mybir as mybir
from concourse.bass_utils import run_bass_kernel_spmd
from contextlib import ExitStack

F32 = mybir.dt.float32
BF16 = mybir.dt.bfloat16
I32 = mybir.dt.int32
AF = mybir.ActivationFunctionType
ALU = mybir.AluOpType

S_ = 2048
D_ = 1024
NT = 16
EPS = 1e-6
LAYER_FNS = {}
DEBUG = False
import os
SSD_SKEW = int(os.environ.get('SSD_SKEW', '1'))

ENGS = ("tensor", "vector", "scalar", "gpsimd", "sync")
EPOCH = 6000
NDMA = 24


_BASE = {"r": {}}


class Dep:
    __slots__ = ("w", "r")

    def __init__(self):
        self.w = None
        self.r = dict(_BASE["r"])


def deps(n):
    return [Dep() for _ in range(n)]


class Sched:
    def __init__(self, nc, stack):
        self.nc = nc
        self.stack = stack
        self.q = {e: [] for e in ENGS}
        self.cnt = {e: 0 for e in ENGS}
        self.sems = {}
        self.known = {e: {} for e in ENGS}
        self.dma_sems = [stack.enter_context(nc.semaphore(f"dma{i}")) for i in range(NDMA)]
        self.dma_i = 0

    def _esem(self, eng, epoch):
        k = (eng, epoch)
        if k not in self.sems:
            self.sems[k] = self.stack.enter_context(self.nc.semaphore(f"s_{eng}_{epoch}"))
        return self.sems[k]

    def _need_wait(self, weng, tok):
        kn = self.known[weng]
        if tok[0] == "c":
            _, eng, ep, val = tok
            cur = kn.get(("c", eng), (-1, 0))
            if cur >= (ep, val):
                return None
            kn[("c", eng)] = (ep, val)
            return (self._esem(eng, ep), val)
        _, idx, val = tok
        cur = kn.get(("d", idx), 0)
        if cur >= val:
            return None
        kn[("d", idx)] = val
        return (self.dma_sems[idx], val)

    def op(self, eng, fn, reads=(), writes=(), dma=False):
        waits = []
        toks = []
        for d in reads:
            if d.w is not None:
                toks.append((d.w, "raw"))
        for d in writes:
            if d.w is not None:
                toks.append((d.w, "waw"))
            for t in d.r.values():
                toks.append((t, "war"))
        for tok, kind in toks:
            if not dma and tok[0] == "c" and tok[1] == eng:
                if eng == "tensor" or kind != "raw":
                    continue
            w = self._need_wait(eng, tok)
            if w is not None:
                waits.append(w)
        if dma:
            idx = self.dma_i % NDMA
            rnd = self.dma_i // NDMA
            self.dma_i += 1
            if rnd > 0:
                w = self._need_wait(eng, ("d", idx, 16 * rnd))
                if w is not None:
                    waits.append(w)
            mytok = ("d", idx, 16 * (rnd + 1))
            inc = (self.dma_sems[idx], 16)
        else:
            c = self.cnt[eng]
            self.cnt[eng] = c + 1
            ep, val = c // EPOCH, c % EPOCH + 1
            mytok = ("c", eng, ep, val)
            inc = (self._esem(eng, ep), 1)
        self.q[eng].append((fn, waits, inc))
        rk = mytok[:2]
        for d in reads:
            d.r[rk] = mytok
        for d in writes:
            d.w = mytok
            d.r = {}
        return mytok

    def snapshot(self):
        snap = {}
        for e in ENGS:
            c = self.cnt[e]
            if c > 0:
                c -= 1
                snap[("c", e)] = ("c", e, c // EPOCH, c % EPOCH + 1)
        for i in range(min(self.dma_i, NDMA)):
            n_i = (self.dma_i - 1 - i) // NDMA
            snap[("d", i)] = ("d", i, 16 * (n_i + 1))
        return snap

    def wait_tok(self, eng, tok):
        w = self._need_wait(eng, tok)
        if w is not None:
            self.q[eng].append((None, [w], None))

    def emit(self):
        nc = self.nc
        with nc.Block() as block:
            def run(engname):
                def body(e):
                    for fn, waits, inc in self.q[engname]:
                        for sem, val in waits:
                            e.wait_ge(sem, val)
                        if fn is not None:
                            ins = fn(e)
                            ins.then_inc(inc[0], inc[1])
                return body
            block.tensor(run("tensor"))
            block.vector(run("vector"))
            block.scalar(run("scalar"))
            block.gpsimd(run("gpsimd"))
            block.sync(run("sync"))


class Ctx:
    def __init__(self, nc, st):
        self.nc = nc
        self.st = st
        self.S = Sched(nc, st)
        self.ps = st.enter_context(nc.psum_tensor("ps", [128, 8 * 512], F32))
        self.psd = deps(8)
        self.uid = 0
        self.out_toks = []

    def release(self):
        _BASE["r"] = self.S.snapshot()

    def sb(self, stack, shape, dt):
        self.uid += 1
        return stack.enter_context(self.nc.sbuf_tensor(f"t{self.uid}", list(shape), dt))

    def pb(self, i, n=512, off=0):
        return self.ps[:, i * 512 + off:i * 512 + off + n]

    def pbh(self, i):
        return self.ps[:, i * 512:(i + 1) * 512].bitcast(BF16)

    def T(self, fn, r=(), w=()):
        return self.S.op("tensor", fn, r, w)

    def V(self, fn, r=(), w=()):
        return self.S.op("vector", fn, r, w)

    def A(self, fn, r=(), w=()):
        return self.S.op("scalar", fn, r, w)

    def G(self, fn, r=(), w=()):
        return self.S.op("gpsimd", fn, r, w)

    def D(self, fn, r=(), w=()):
        return self.S.op("sync", fn, r, w, dma=True)

    def DG(self, fn, r=(), w=()):
        return self.S.op("gpsimd", fn, r, w, dma=True)

    def mm(self, out, lhsT, rhs, start, stop, r=(), w=()):
        return self.T(lambda e: e.matmul(out, lhsT=lhsT, rhs=rhs, start=start, stop=stop), r, w)

    def tr(self, out, in_, ident, r=(), w=()):
        return self.T(lambda e: e.transpose(out=out, in_=in_, identity=ident), r, w)

    def act(self, out, in_, func, r=(), w=(), **kw):
        return self.A(lambda e: e.activation(out=out, in_=in_, func=func, **kw), r, w)

    def dbg(self, name, ap, shape, dt, r):
        if not DEBUG:
            return
        d = self.nc.dram_tensor("dbg_" + name, list(shape), dt, kind="ExternalOutput").ap()
        tok = self.D(lambda e: e.dma_start(out=d, in_=ap), r=r)
        self.out_toks.append(tok)

    def load_w(self, dst, src2d, dep):
        n = src2d.shape[1]
        tok = None
        for c0 in range(0, n, 512):
            c1 = min(n, c0 + 512)
            assert (c1 - c0) in (16, 32, 64, 128, 256, 512), (c0, c1)
            tok = self.DG(lambda e, c0=c0, c1=c1: e.dma_start(
                out=dst[:, :, c0:c1], in_=src2d[:, c0:c1].rearrange("(c p) n -> p c n", p=128)), w=[dep])
        return tok

    def load_bc(self, dst, src1d, dep, n=128):
        return self.DG(lambda e: e.dma_start(out=dst, in_=src1d.partition_broadcast(n)), w=[dep])

    def load_col(self, dst, src1d, dep, nchunks):
        return self.DG(lambda e: e.dma_start(out=dst, in_=src1d.rearrange("(c p) -> p c", p=128),
                                             allow_slow_non_contiguous=True), w=[dep])


def rstd_ops(C, rs_ap, ss_ap, n, r, w):
    C.act(rs_ap, ss_ap, AF.Ln, r=r, w=w, scale=1.0 / n, bias=EPS)
    C.act(rs_ap, rs_ap, AF.Exp, r=w, w=w, scale=-0.5)


def norm_T(C, stk, h_ap, g_row, uT, d_uT, d_h, banks=(0, 1)):
    with ExitStack() as s:
        gt = C.sb(s, [128, D_], F32); d_gt = Dep()
        C.load_bc(gt[:], g_row, d_gt)
        hb = [C.sb(s, [128, D_], F32) for _ in range(2)]; d_hb = deps(2)
        junk = C.sb(s, [128, D_], BF16); d_junk = Dep()
        ss = C.sb(s, [128, NT], F32); rs = C.sb(s, [128, NT], F32)
        d_ss = deps(NT); d_rs = deps(NT)
        xn = [C.sb(s, [128, D_], BF16) for _ in range(2)]; d_xn = deps(2)
        for t in range(NT):
            b = t % 2
            C.D(lambda e, t=t, b=b: e.dma_start(out=hb[b][:], in_=h_ap[t * 128:(t + 1) * 128, :]),
                r=[d_h[t]], w=[d_hb[b]])
            C.act(junk[:], hb[b][:], AF.Square, r=[d_hb[b]], w=[d_junk, d_ss[t]], accum_out=ss[:, t:t + 1])
            rstd_ops(C, rs[:, t:t + 1], ss[:, t:t + 1], D_, [d_ss[t]], [d_rs[t]])
            C.V(lambda e, t=t, b=b: e.scalar_tensor_tensor(out=xn[b][:], in0=hb[b][:], scalar=rs[:, t:t + 1],
                                                           in1=gt[:], op0=ALU.mult, op1=ALU.mult),
                r=[d_hb[b], d_rs[t], d_gt], w=[d_xn[b]])
            bk = banks[t % len(banks)]
            pv = C.pbh(bk)
            for c in range(8):
                C.tr(pv[:, c * 128:(c + 1) * 128], xn[b][:, c * 128:(c + 1) * 128], C.ident_bf[:],
                     r=[d_xn[b], C.d_const], w=[C.psd[bk]])
            C.V(lambda e, t=t, pv=pv: e.tensor_copy(out=uT[:, :, t * 128:(t + 1) * 128],
                                                   in_=pv.rearrange("p (c n) -> p c n", c=8)),
                r=[C.psd[bk]], w=[d_uT[t]])
        C.release()


def out_proj(C, goT, d_go, nk, w_out, h_in, d_hin, h_out, d_hout, banks=(0, 1, 2, 3), wo_pre=None):
    with ExitStack() as s:
        if wo_pre is None:
            wo = C.sb(s, [128, nk, D_], BF16); d_wo = Dep()
            half = nk // 2
            C.load_w(wo[:, 0:half, :], w_out[0:half * 128, :], d_wo)
            C.load_w(wo[:, half:nk, :], w_out[half * 128:nk * 128, :], d_wo)
        else:
            wo, d_wo = wo_pre
        hb = [C.sb(s, [128, D_], F32) for _ in range(2)]; d_hb = deps(2)
        ho = [C.sb(s, [128, D_], F32) for _ in range(2)]; d_ho = deps(2)
        i = 0
        for t in range(NT):
            b = t % 2
            C.D(lambda e, t=t, b=b: e.dma_start(out=hb[b][:], in_=h_in[t * 128:(t + 1) * 128, :]),
                r=[d_hin[t]], w=[d_hb[b]])
            for nb in range(2):
                bk = banks[i % len(banks)]; i += 1
                for c in range(nk):
                    C.mm(C.pb(bk), goT[:, c, t * 128:(t + 1) * 128], wo[:, c, nb * 512:(nb + 1) * 512],
                         c == 0, c == nk - 1, r=[d_go[t], d_wo], w=[C.psd[bk]])
                C.V(lambda e, b=b, nb=nb, bk=bk: e.tensor_tensor(out=ho[b][:, nb * 512:(nb + 1) * 512], in0=C.pb(bk),
                                                               in1=hb[b][:, nb * 512:(nb + 1) * 512], op=ALU.add),
                    r=[C.psd[bk], d_hb[b]], w=[d_ho[b]])
            tok = C.D(lambda e, t=t, b=b: e.dma_start(out=h_out[t * 128:(t + 1) * 128, :], in_=ho[b][:]),
                      r=[d_ho[b]], w=[d_hout[t]])
            C.last_store = tok
        C.release()


def final_norm(C, h_ap, d_h, g_row, out_ap):
    with ExitStack() as s:
        gt = C.sb(s, [128, D_], F32); d_gt = Dep()
        C.load_bc(gt[:], g_row, d_gt)
        hb = [C.sb(s, [128, D_], F32) for _ in range(2)]; d_hb = deps(2)
        junk = C.sb(s, [128, D_], BF16); d_junk = Dep()
        ss = C.sb(s, [128, NT], F32); rs = C.sb(s, [128, NT], F32)
        d_ss = deps(NT); d_rs = deps(NT)
        ob = [C.sb(s, [128, D_], F32) for _ in range(2)]; d_ob = deps(2)
        for t in range(NT):
            b = t % 2
            C.D(lambda e, t=t, b=b: e.dma_start(out=hb[b][:], in_=h_ap[t * 128:(t + 1) * 128, :]),
                r=[d_h[t]], w=[d_hb[b]])
            C.act(junk[:], hb[b][:], AF.Square, r=[d_hb[b]], w=[d_junk, d_ss[t]], accum_out=ss[:, t:t + 1])
            rstd_ops(C, rs[:, t:t + 1], ss[:, t:t + 1], D_, [d_ss[t]], [d_rs[t]])
            C.V(lambda e, t=t, b=b: e.scalar_tensor_tensor(out=ob[b][:], in0=hb[b][:], scalar=rs[:, t:t + 1],
                                                           in1=gt[:], op0=ALU.mult, op1=ALU.mult),
                r=[d_hb[b], d_rs[t], d_gt], w=[d_ob[b]])
            tok = C.D(lambda e, t=t, b=b: e.dma_start(out=out_ap[t * 128:(t + 1) * 128, :], in_=ob[b][:]),
                      r=[d_ob[b]])
            C.out_toks.append(tok)
        C.release()


def layer_lru(C, W, h_in, d_hin, h_out, d_hout):
    NCH = 10
    with ExitStack() as L:
        uT = C.sb(L, [128, 8, S_], BF16); d_uT = deps(NT)
        norm_T(C, L, h_in, W["norm_g"], uT, d_uT, d_hin)
        d_uTb = [[d_uT[4 * tb + i] for i in range(4)] for tb in range(4)]
        goT = C.sb(L, [128, NCH, S_], BF16); d_go = deps(NT)
        cw = C.sb(L, [128, 4, NCH], F32); d_cw = Dep()
        for k in range(4):
            C.load_col(cw[:, k, :], W["lru_conv_w"][k, :], d_cw, NCH)
        cb = C.sb(L, [128, NCH], F32); ba = C.sb(L, [128, NCH], F32); bx = C.sb(L, [128, NCH], F32)
        lam = C.sb(L, [128, NCH], F32); d_p = Dep()
        C.load_col(cb[:], W["lru_conv_b"], d_p, NCH)
        C.load_col(ba[:], W["lru_b_a"], d_p, NCH)
        C.load_col(bx[:], W["lru_b_x"], d_p, NCH)
        C.load_col(lam[:], W["lru_lam"], d_p, NCH)
        cA = C.sb(L, [128, NCH], F32); cA2 = C.sb(L, [128, NCH], F32); d_cA = Dep()
        C.act(cA[:], lam[:], AF.Exp, r=[d_p], w=[d_cA], scale=-1.0)
        C.act(cA[:], cA[:], AF.Ln, r=[d_cA], w=[d_cA], bias=1.0)
        C.V(lambda e: e.tensor_scalar(out=cA2[:], in0=cA[:], scalar1=-16.0, scalar2=None, op0=ALU.mult),
            r=[d_cA], w=[d_cA])
        C.V(lambda e: e.tensor_scalar(out=cA[:], in0=cA[:], scalar1=-8.0, scalar2=None, op0=ALU.mult),
            r=[d_cA], w=[d_cA])
        wa = C.sb(L, [128, NCH, 128], BF16); wx = C.sb(L, [128, NCH, 128], BF16); d_wg = Dep()
        C.DG(lambda e: e.dma_start(out=wa[:], in_=W["lru_w_a"].rearrange("n i j -> i n j")), w=[d_wg])
        C.DG(lambda e: e.dma_start(out=wx[:], in_=W["lru_w_x"].rearrange("n i j -> i n j")), w=[d_wg])
        dg = C.sb(L, [128, NCH, 4, 128], BF16); d_dg = Dep()
        for c in range(NCH):
            for k in range(4):
                C.V(lambda e, c=c, k=k: e.tensor_scalar(out=dg[:, c, k, :], in0=C.ident_bf[:],
                                                        scalar1=cw[:, k, c:c + 1], scalar2=None, op0=ALU.mult),
                    r=[d_cw, C.d_const], w=[d_dg])
        wu = [C.sb(L, [128, 8, 128], BF16) for _ in range(2)]; d_wu = deps(2)
        wg = [C.sb(L, [128, 8, 128], BF16) for _ in range(2)]; d_wgt = deps(2)
        ub2 = [C.sb(L, [128, 3 + S_], BF16) for _ in range(2)]; d_ub2 = deps(2)
        for b_ in range(2):
            C.V(lambda e, b_=b_: e.memset(ub2[b_][:, 0:3], 0.0), w=[d_ub2[b_]])
        uc = C.sb(L, [128, S_], F32); d_uc = Dep()
        ucb = C.sb(L, [128, S_], BF16); d_ucb = Dep()
        rr = C.sb(L, [128, S_], F32); d_rr = Dep()
        ig = C.sb(L, [128, S_], F32); d_ig = Dep()
        tmp = C.sb(L, [128, S_], F32); d_tmp = Dep()
        hs = C.sb(L, [128, S_], BF16); d_hs = Dep()
        sg = C.sb(L, [128, S_], BF16); d_sg = Dep()
        w_in = W["lru_w_in"]

        def proj_u(c):
            b = c % 2
            C.load_w(wu[b][:], w_in[:, 1280 + c * 128:1280 + (c + 1) * 128], d_wu[b])
            C.load_w(wg[b][:], w_in[:, c * 128:(c + 1) * 128], d_wgt[b])
            for tb in range(4):
                bk = tb
                for kc in range(8):
                    C.mm(C.pb(bk), wu[b][:, kc, :], uT[:, kc, tb * 512:(tb + 1) * 512], kc == 0, kc == 7,
                         r=d_uTb[tb] + [d_wu[b]], w=[C.psd[bk]])
                C.act(ub2[b][:, 3 + tb * 512:3 + (tb + 1) * 512], C.pb(bk), AF.Copy, r=[C.psd[bk]], w=[d_ub2[b]])

        proj_u(0)
        for c in range(NCH):
            b = c % 2
            for tb in range(4):
                bk = 4 + tb
                for k in range(4):
                    C.mm(C.pb(bk), dg[:, c, k, :], ub2[b][:, tb * 512 + k:tb * 512 + k + 512], k == 0, k == 3,
                         r=[d_ub2[b], d_dg], w=[C.psd[bk]])
                C.act(uc[:, tb * 512:(tb + 1) * 512], C.pb(bk), AF.Identity, r=[C.psd[bk], d_p], w=[d_uc],
                      bias=cb[:, c:c + 1])
            C.V(lambda e: e.tensor_copy(out=ucb[:], in_=uc[:]), r=[d_uc], w=[d_ucb])
            for tb in range(4):
                bk = tb
                for kc in range(8):
                    C.mm(C.pb(bk), wg[b][:, kc, :], uT[:, kc, tb * 512:(tb + 1) * 512], kc == 0, kc == 7,
                         r=d_uTb[tb] + [d_wgt[b]], w=[C.psd[bk]])
                C.act(sg[:, tb * 512:(tb + 1) * 512], C.pb(bk), AF.Silu, r=[C.psd[bk]], w=[d_sg])
            for tb in range(4):
                bk = 4 + tb
                C.mm(C.pb(bk), wa[:, c, :], ucb[:, tb * 512:(tb + 1) * 512], True, True,
                     r=[d_ucb, d_wg], w=[C.psd[bk]])
                C.act(rr[:, tb * 512:(tb + 1) * 512], C.pb(bk), AF.Sigmoid, r=[C.psd[bk], d_p], w=[d_rr],
                      bias=ba[:, c:c + 1])
            for tb in range(4):
                bk = 4 + tb
                C.mm(C.pb(bk), wx[:, c, :], ucb[:, tb * 512:(tb + 1) * 512], True, True,
                     r=[d_ucb, d_wg], w=[C.psd[bk]])
                C.act(ig[:, tb * 512:(tb + 1) * 512], C.pb(bk), AF.Sigmoid, r=[C.psd[bk], d_p], w=[d_ig],
                      bias=bx[:, c:c + 1])
            if c + 1 < NCH:
                proj_u(c + 1)
            C.act(tmp[:], rr[:], AF.Exp, r=[d_rr, d_cA], w=[d_tmp], scale=cA2[:, c:c + 1])
            C.act(rr[:], rr[:], AF.Exp, r=[d_rr, d_cA], w=[d_rr], scale=cA[:, c:c + 1])
            C.act(tmp[:], tmp[:], AF.Sqrt, r=[d_tmp], w=[d_tmp], scale=-1.0, bias=1.0)
            C.V(lambda e: e.tensor_tensor(out=ig[:], in0=ig[:], in1=uc[:], op=ALU.mult), r=[d_ig, d_uc], w=[d_ig])
            C.V(lambda e: e.tensor_tensor(out=tmp[:], in0=tmp[:], in1=ig[:], op=ALU.mult), r=[d_tmp, d_ig], w=[d_tmp])
            C.V(lambda e: e.tensor_tensor_scan(out=hs[:], data0=rr[:], data1=tmp[:], initial=0.0,
                                               op0=ALU.mult, op1=ALU.add), r=[d_rr, d_tmp], w=[d_hs])
            C.G(lambda e, c=c: e.tensor_tensor(out=goT[:, c, :], in0=hs[:], in1=sg[:], op=ALU.mult),
                r=[d_hs, d_sg], w=d_go)
        out_proj(C, goT, d_go, NCH, W["lru_w_out"], h_in, d_hin, h_out, d_hout)
        C.release()


def layer_mla(C, W, h_in, d_hin, h_out, d_hout):
    H = 16
    PI = float(np.pi)
    with ExitStack() as L:
        cqnT = C.sb(L, [128, 3, S_], BF16); d_cq = deps(NT)
        ckvT = C.sb(L, [128, 2, S_], BF16); d_ckv = deps(NT)
        sgT = C.sb(L, [128, 8, S_], BF16); d_sg4 = deps(4)
        kr_all = C.sb(L, [128, NT, 32], F32); d_kr = Dep()
        w_in = W["mla_w_in"]
        with ExitStack() as PA:
            uT = C.sb(PA, [128, 8, S_], BF16); d_uT = deps(NT)
            norm_T(C, PA, h_in, W["norm_g"], uT, d_uT, d_hin)
            d_uTb = [[d_uT[4 * tb + i] for i in range(4)] for tb in range(4)]
            w1 = C.sb(PA, [128, 8, 768], BF16); d_w1 = Dep()
            C.load_w(w1[:], w_in[:, 0:768], d_w1)
            gq = C.sb(PA, [128, 384], F32); gkv = C.sb(PA, [128, 256], F32); d_g = Dep()
            C.load_bc(gq[:], W["mla_g_q"], d_g)
            C.load_bc(gkv[:], W["mla_g_kv"], d_g)
            ctok = [C.sb(PA, [128, 672], F32) for _ in range(2)]; d_ct = deps(2)
            ss2 = C.sb(PA, [128, NT, 2], F32); rs2 = C.sb(PA, [128, NT, 2], F32)
            d_ss = deps(NT); d_rs = deps(NT)
            junk = C.sb(PA, [128, 384], BF16); d_junk = Dep()
            cn = [C.sb(PA, [128, 640], BF16) for _ in range(2)]; d_cn = deps(2)
            wgt = [C.sb(PA, [128, 8, 512], BF16) for _ in range(2)]; d_wgt = deps(2)
            for hf in range(2):
                C.load_w(wgt[hf][:], w_in[:, 672 + hf * 512:672 + (hf + 1) * 512], d_wgt[hf])
            for t in range(NT):
                b = t % 2
                for (c0, c1, bk) in ((0, 512, 2), (512, 672, 3)):
                    for kc in range(8):
                        C.mm(C.pb(bk, c1 - c0), uT[:, kc, t * 128:(t + 1) * 128], w1[:, kc, c0:c1], kc == 0, kc == 7,
                             r=[d_uT[t], d_w1], w=[C.psd[bk]])
                    C.act(ctok[b][:, c0:c1], C.pb(bk, c1 - c0), AF.Copy, r=[C.psd[bk]], w=[d_ct[b]])
                C.act(junk[:, 0:384], ctok[b][:, 0:384], AF.Square, r=[d_ct[b]], w=[d_junk, d_ss[t]],
                      accum_out=ss2[:, t, 0:1])
                C.act(junk[:, 0:256], ctok[b][:, 384:640], AF.Square, r=[d_ct[b]], w=[d_junk, d_ss[t]],
                      accum_out=ss2[:, t, 1:2])
                rstd_ops(C, rs2[:, t, 0:1], ss2[:, t, 0:1], 384, [d_ss[t]], [d_rs[t]])
                rstd_ops(C, rs2[:, t, 1:2], ss2[:, t, 1:2], 256, [d_ss[t]], [d_rs[t]])
                C.V(lambda e, t=t, b=b: e.scalar_tensor_tensor(out=cn[b][:, 0:384], in0=ctok[b][:, 0:384],
                                                               scalar=rs2[:, t, 0:1], in1=gq[:], op0=ALU.mult, op1=ALU.mult),
                    r=[d_ct[b], d_rs[t], d_g], w=[d_cn[b]])
                C.V(lambda e, t=t, b=b: e.scalar_tensor_tensor(out=cn[b][:, 384:640], in0=ctok[b][:, 384:640],
                                                               scalar=rs2[:, t, 1:2], in1=gkv[:], op0=ALU.mult, op1=ALU.mult),
                    r=[d_ct[b], d_rs[t], d_g], w=[d_cn[b]])
                C.G(lambda e, t=t, b=b: e.tensor_copy(out=kr_all[:, t, :], in_=ctok[b][:, 640:672]),
                    r=[d_ct[b]], w=[d_kr])
                bk = 4 + b
                pv = C.pbh(bk)
                for c in range(5):
                    C.tr(pv[:, c * 128:(c + 1) * 128], cn[b][:, c * 128:(c + 1) * 128], C.ident_bf[:],
                         r=[d_cn[b], C.d_const], w=[C.psd[bk]])
                C.V(lambda e, t=t, pv=pv: e.tensor_copy(out=cqnT[:, :, t * 128:(t + 1) * 128],
                                                       in_=pv[:, 0:384].rearrange("p (c n) -> p c n", c=3)),
                    r=[C.psd[bk]], w=[d_cq[t]])
                C.V(lambda e, t=t, pv=pv: e.tensor_copy(out=ckvT[:, :, t * 128:(t + 1) * 128],
                                                       in_=pv[:, 384:640].rearrange("p (c n) -> p c n", c=2)),
                    r=[C.psd[bk]], w=[d_ckv[t]])
            C.dbg("ctok", ctok[1][:], [128, 672], F32, [d_ct[1]])
            C.dbg("w1", w1[:], [128, 8, 768], BF16, [d_w1])
            C.dbg("cn", cn[1][:], [128, 640], BF16, [d_cn[1]])
            C.dbg("rs2", rs2[:], [128, NT, 2], F32, d_rs)
            i = 0
            for c in range(8):
                for tb in range(4):
                    bk = (6, 7, 0, 1)[i % 4]; i += 1
                    for kc in range(8):
                        C.mm(C.pb(bk), wgt[c // 4][:, kc, (c % 4) * 128:(c % 4 + 1) * 128],
                             uT[:, kc, tb * 512:(tb + 1) * 512], kc == 0, kc == 7,
                             r=d_uTb[tb] + [d_wgt[c // 4]], w=[C.psd[bk]])
                    C.act(sgT[:, c, tb * 512:(tb + 1) * 512], C.pb(bk), AF.Silu, r=[C.psd[bk]], w=[d_sg4[tb]])
            C.release()
        C.dbg("cqnT", cqnT[:], [128, 3, S_], BF16, d_cq)
        C.dbg("ckvT", ckvT[:], [128, 2, S_], BF16, d_ckv)
        C.dbg("sgT", sgT[:], [128, 8, S_], BF16, d_sg4)
        C.dbg("kr", kr_all[:], [128, NT, 32], F32, [d_kr])
        with ExitStack() as PB:
            wuq = C.sb(PB, [128, 3, 1536], BF16); d_wuq = Dep()
            C.load_w(wuq[:], W["mla_w_uq"], d_wuq)
            wukv = C.sb(PB, [128, 2, 2048], BF16); d_wukv = Dep()
            C.load_w(wukv[:], W["mla_w_ukv"], d_wukv)
            wo = C.sb(PB, [128, 8, D_], BF16); d_wo = Dep()
            C.load_w(wo[:, 0:4, :], W["mla_w_out"][0:512, :], d_wo)
            C.load_w(wo[:, 4:8, :], W["mla_w_out"][512:1024, :], d_wo)
            tri = C.sb(PB, [128, 128], BF16); ones64 = C.sb(PB, [128, 64], BF16)
            invf = C.sb(PB, [128, 16], F32); posi = C.sb(PB, [128, NT], I32); d_k = Dep()
            C.D(lambda e: e.dma_start(out=tri[:], in_=C.CD["tri_bf"]), w=[d_k])
            C.D(lambda e: e.dma_start(out=ones64[:], in_=C.CD["ones_bf"]), w=[d_k])
            C.D(lambda e: e.dma_start(out=invf[:], in_=C.CD["invf"]), w=[d_k])
            C.D(lambda e: e.dma_start(out=posi[:], in_=C.pos), w=[d_k])
            posf = C.sb(PB, [128, NT], F32); d_t = Dep()
            ang = C.sb(PB, [128, NT, 16], F32); tmpf = C.sb(PB, [128, NT, 16], F32)
            ki = C.sb(PB, [128, NT, 16], I32); kff = C.sb(PB, [128, NT, 16], F32)
            rr = C.sb(PB, [128, NT, 16], F32); yy = C.sb(PB, [128, NT, 16], F32); mm_ = C.sb(PB, [128, NT, 16], F32)
            cos_t = C.sb(PB, [128, NT, 16], F32); sin_t = C.sb(PB, [128, NT, 16], F32); d_cs = Dep()
            C.V(lambda e: e.tensor_copy(out=posf[:], in_=posi[:]), r=[d_k], w=[d_t])
            C.V(lambda e: e.tensor_tensor(out=ang[:], in0=posf[:].unsqueeze(2).to_broadcast([128, NT, 16]),
                                          in1=invf[:].unsqueeze(1).to_broadcast([128, NT, 16]), op=ALU.mult),
                r=[d_t, d_k], w=[d_t])
            C.V(lambda e: e.tensor_scalar(out=tmpf[:], in0=ang[:], scalar1=1.0 / (2 * PI), scalar2=None, op0=ALU.mult),
                r=[d_t], w=[d_t])
            C.V(lambda e: e.tensor_copy(out=ki[:], in_=tmpf[:]), r=[d_t], w=[d_t])
            C.V(lambda e: e.tensor_copy(out=kff[:], in_=ki[:]), r=[d_t], w=[d_t])
            C1 = 6.28125
            C2 = float(2 * np.pi - 6.28125)
            C.V(lambda e: e.scalar_tensor_tensor(out=rr[:], in0=kff[:], scalar=-C1, in1=ang[:], op0=ALU.mult, op1=ALU.add),
                r=[d_t], w=[d_t])
            C.V(lambda e: e.scalar_tensor_tensor(out=rr[:], in0=kff[:], scalar=-C2, in1=rr[:], op0=ALU.mult, op1=ALU.add),
                r=[d_t], w=[d_t])
            for dst, shift in ((sin_t, 0.0), (cos_t, PI / 2)):
                C.V(lambda e, shift=shift: e.tensor_scalar(out=yy[:], in0=rr[:], scalar1=shift, scalar2=None, op0=ALU.add),
                    r=[d_t], w=[d_t])
                C.V(lambda e: e.tensor_scalar(out=mm_[:], in0=yy[:], scalar1=PI, scalar2=-2 * PI, op0=ALU.is_gt, op1=ALU.mult),
                    r=[d_t], w=[d_t])
                C.V(lambda e: e.tensor_tensor(out=yy[:], in0=yy[:], in1=mm_[:], op=ALU.add), r=[d_t], w=[d_t])
                C.V(lambda e: e.tensor_scalar(out=mm_[:], in0=yy[:], scalar1=-PI, scalar2=2 * PI, op0=ALU.is_lt, op1=ALU.mult),
                    r=[d_t], w=[d_t])
                C.V(lambda e: e.tensor_tensor(out=yy[:], in0=yy[:], in1=mm_[:], op=ALU.add), r=[d_t], w=[d_t])
                C.act(dst[:], yy[:], AF.Sin, r=[d_t], w=[d_cs, d_t])
            qT = [C.sb(PB, [128, S_], BF16) for _ in range(2)]; d_qT = deps(2)
            kT = [C.sb(PB, [128, S_], BF16) for _ in range(2)]; d_kT = deps(2)
            Vh = [C.sb(PB, [128, NT, 128], BF16) for _ in range(2)]; d_V = deps(2)
            for b_ in range(2):
                C.G(lambda e, b_=b_: e.memset(Vh[b_][:], 1.0), w=[d_V[b_]])
            swp = C.sb(PB, [128, 128], F32)
            C.D(lambda e: e.dma_start(out=swp[:], in_=C.CD["swap_f"]), w=[d_k])
            for b in range(2):
                C.G(lambda e, b=b: e.memset(qT[b][:], 0.0), w=[d_qT[b]])
                C.G(lambda e, b=b: e.memset(kT[b][:], 0.0), w=[d_kT[b]])
            qtok = C.sb(PB, [128, NT, 96], F32); d_qtok = Dep()
            ta = C.sb(PB, [128, NT, 16], F32); tb_ = C.sb(PB, [128, NT, 16], F32)
            tc = C.sb(PB, [128, NT, 16], F32); td = C.sb(PB, [128, NT, 16], F32); d_tt = Dep()
            krr = C.sb(PB, [128, NT, 96], F32); d_krr = Dep()
            C.G(lambda e: e.memset(krr[:], 0.0), w=[d_krr])
            k1 = kr_all[:, :, 0:16]; k2 = kr_all[:, :, 16:32]
            C.V(lambda e: e.tensor_tensor(out=ta[:], in0=k1, in1=cos_t[:], op=ALU.mult), r=[d_kr, d_cs], w=[d_tt])
            C.V(lambda e: e.tensor_tensor(out=tb_[:], in0=k2, in1=sin_t[:], op=ALU.mult), r=[d_kr, d_cs], w=[d_tt])
            C.V(lambda e: e.tensor_tensor(out=krr[:, :, 64:80], in0=ta[:], in1=tb_[:], op=ALU.subtract), r=[d_tt], w=[d_krr])
            C.V(lambda e: e.tensor_tensor(out=tc[:], in0=k2, in1=cos_t[:], op=ALU.mult), r=[d_kr, d_cs], w=[d_tt])
            C.V(lambda e: e.tensor_tensor(out=td[:], in0=k1, in1=sin_t[:], op=ALU.mult), r=[d_kr, d_cs], w=[d_tt])
            C.V(lambda e: e.tensor_tensor(out=krr[:, :, 80:96], in0=tc[:], in1=td[:], op=ALU.add), r=[d_tt], w=[d_krr])
            for t4 in range(4):
                bk = 7
                for i in range(4):
                    C.tr(C.pb(bk)[0:96, i * 128:(i + 1) * 128], krr[:, t4 * 4 + i, :], C.ident_f[:],
                         r=[d_krr, C.d_const], w=[C.psd[bk]])
                for b in range(2):
                    C.V(lambda e, b=b, t4=t4, bk=bk: e.tensor_copy(out=kT[b][64:96, t4 * 512:(t4 + 1) * 512],
                                                                 in_=C.pb(bk)[64:96, :]),
                        r=[C.psd[bk]], w=[d_kT[b]])
            lnd = [C.sb(PB, [128, 512], F32) for _ in range(2)]; rden = [C.sb(PB, [128, 512], F32) for _ in range(2)]
            ot = [C.sb(PB, [128, 512], F32) for _ in range(2)]
            d_nrm = deps(2)
            for b_ in range(2):
                C.G(lambda e, b_=b_: e.memset(rden[b_][:], 1.0), w=[d_nrm[b_]])
            NPT = 3
            SB = (0, 1, 2)
            PT = [C.sb(PB, [128, 512], BF16) for _ in range(NPT)]; d_PT = deps(NPT)
            d_ot = deps(2)
            QS = 96 ** -0.5

            pcnt = [0]

            def proj_units(h, b):
                for t4 in range(4):
                    bk = 6 + pcnt[0] % 2; pcnt[0] += 1
                    for i in range(4):
                        t = t4 * 4 + i
                        for kc in range(3):
                            C.mm(C.pb(bk)[:, i * 96:(i + 1) * 96], cqnT[:, kc, t * 128:(t + 1) * 128],
                                 wuq[:, kc, h * 96:(h + 1) * 96], kc == 0, kc == 2,
                                 r=[d_cq[t], d_wuq], w=[C.psd[bk]])
                    C.V(lambda e, t4=t4, bk=bk: e.tensor_scalar(
                        out=qtok[:, t4 * 4:(t4 + 1) * 4, :], in0=C.pb(bk)[:, 0:384].rearrange("p (a c) -> p a c", a=4),
                        scalar1=QS, scalar2=None, op0=ALU.mult), r=[C.psd[bk]], w=[d_qtok])
                    yield
                q1 = qtok[:, :, 64:80]; q2 = qtok[:, :, 80:96]
                C.V(lambda e: e.tensor_tensor(out=ta[:], in0=q1, in1=cos_t[:], op=ALU.mult), r=[d_qtok, d_cs], w=[d_tt])
                C.V(lambda e: e.tensor_tensor(out=tb_[:], in0=q2, in1=sin_t[:], op=ALU.mult), r=[d_qtok, d_cs], w=[d_tt])
                C.V(lambda e: e.tensor_tensor(out=tc[:], in0=q2, in1=cos_t[:], op=ALU.mult), r=[d_qtok, d_cs], w=[d_tt])
                C.V(lambda e: e.tensor_tensor(out=td[:], in0=q1, in1=sin_t[:], op=ALU.mult), r=[d_qtok, d_cs], w=[d_tt])
                C.V(lambda e: e.tensor_tensor(out=q1, in0=ta[:], in1=tb_[:], op=ALU.subtract), r=[d_tt], w=[d_qtok])
                C.V(lambda e: e.tensor_tensor(out=q2, in0=tc[:], in1=td[:], op=ALU.add), r=[d_tt], w=[d_qtok])
                yield
                for t4 in range(4):
                    bk = 6 + pcnt[0] % 2; pcnt[0] += 1
                    for i in range(4):
                        C.tr(C.pb(bk)[0:96, i * 128:(i + 1) * 128], qtok[:, t4 * 4 + i, :], C.ident_f[:],
                             r=[d_qtok, C.d_const], w=[C.psd[bk]])
                    C.V(lambda e, t4=t4, bk=bk: e.tensor_copy(out=qT[b][0:96, t4 * 512:(t4 + 1) * 512],
                                                            in_=C.pb(bk)[0:96, :]), r=[C.psd[bk]], w=[d_qT[b]])
                    yield
                for tb in range(4):
                    bk = 6 + pcnt[0] % 2; pcnt[0] += 1
                    for kc in range(2):
                        C.mm(C.pb(bk)[0:64, :], wukv[:, kc, h * 128:h * 128 + 64], ckvT[:, kc, tb * 512:(tb + 1) * 512],
                             kc == 0, kc == 1, r=[d_ckv[4 * tb + i] for i in range(4)] + [d_wukv], w=[C.psd[bk]])
                    C.V(lambda e, tb=tb, bk=bk: e.tensor_copy(out=kT[b][0:64, tb * 512:(tb + 1) * 512],
                                                            in_=C.pb(bk)[0:64, :]), r=[C.psd[bk]], w=[d_kT[b]])
                    yield
                for t8 in range(2):
                    bk = 6 + pcnt[0] % 2; pcnt[0] += 1
                    for i in range(8):
                        t = t8 * 8 + i
                        for kc in range(2):
                            C.mm(C.pb(bk)[:, i * 64:(i + 1) * 64], ckvT[:, kc, t * 128:(t + 1) * 128],
                                 wukv[:, kc, h * 128 + 64:h * 128 + 128], kc == 0, kc == 1,
                                 r=[d_ckv[t], d_wukv], w=[C.psd[bk]])
                    C.V(lambda e, t8=t8, bk=bk: e.tensor_copy(out=Vh[b][:, t8 * 8:(t8 + 1) * 8, b * 64:b * 64 + 64],
                                                            in_=C.pb(bk).rearrange("p (a c) -> p a c", a=8)),
                        r=[C.psd[bk]], w=[d_V[b]])
                    yield

            items = []
            for h in range(H):
                for qc in range(4):
                    nj = 4 * qc + 4
                    for j in range(nj):
                        items.append((h, qc, j, nj))

            def stage_a(i):
                h, qc, j, nj = items[i]
                b = h % 2
                col0 = max(0, j * 128 - qc * 512)
                bS = SB[i % NPT]
                pt = PT[i % NPT]; d_pt = d_PT[i % NPT]
                C.mm(C.pb(bS)[:, col0:512], kT[b][:, j * 128:(j + 1) * 128],
                     qT[b][:, qc * 512 + col0:(qc + 1) * 512], True, True,
                     r=[d_kT[b], d_qT[b]], w=[C.psd[bS]])
                C.act(pt[:, col0:512], C.pb(bS)[:, col0:512], AF.Exp, r=[C.psd[bS]], w=[d_pt])
                if j >= 4 * qc:
                    C.G(lambda e, pt=pt, col0=col0: e.tensor_tensor(out=pt[:, col0:col0 + 128],
                                                                  in0=pt[:, col0:col0 + 128], in1=tri[:], op=ALU.mult),
                        r=[d_pt, d_k], w=[d_pt])

            def stage_b(i):
                h, qc, j, nj = items[i]
                b = h % 2
                r0 = (h % 2) * 64
                rd = 64 - r0
                c = h // 2
                n = h * 4 + qc
                bO = 3 + n % 2
                col0 = max(0, j * 128 - qc * 512)
                pt = PT[i % NPT]; d_pt = d_PT[i % NPT]
                C.mm(C.pb(bO)[:, col0:512], Vh[b][:, j, :], pt[:, col0:512], j == 0, j == nj - 1,
                     r=[d_V[b], d_pt], w=[C.psd[bO]])
                if j == nj - 1:
                    nb = n % 2
                    bW = 5
                    rs = slice(r0, r0 + 64); rq = slice(rd, rd + 64)
                    C.act(lnd[nb][rq, :], C.pb(bO)[rq, :], AF.Ln, r=[C.psd[bO]], w=[d_nrm[nb]])
                    C.act(rden[nb][rq, :], lnd[nb][rq, :], AF.Exp, r=[d_nrm[nb]], w=[d_nrm[nb]], scale=-1.0)
                    def fin(rs=rs, rq=rq, rd=rd, bO=bO, bW=bW, nb=nb, c=c, qc=qc):
                        C.mm(C.pb(bW), swp[:], rden[nb][:], True, True,
                             r=[d_k, d_nrm[nb]], w=[C.psd[bW]])
                        C.V(lambda e: e.tensor_tensor(
                            out=ot[nb][rs, :], in0=C.pb(bO)[rs, :], in1=sgT[rs, c, qc * 512:(qc + 1) * 512], op=ALU.mult),
                            r=[C.psd[bO], d_sg4[qc]], w=[d_ot[nb]])
                        C.V(lambda e: e.tensor_tensor(
                            out=sgT[rs, c, qc * 512:(qc + 1) * 512], in0=ot[nb][rs, :], in1=C.pb(bW)[rs, :], op=ALU.mult),
                            r=[d_ot[nb], C.psd[bW]], w=[d_sg4[qc]])
                    pending.append([2, fin])

            pending = []

            def run_pending(force=False):
                for p in list(pending):
                    p[0] -= 1
                    if force or p[0] <= 0:
                        p[1]()
                        pending.remove(p)

            for _ in proj_units(0, 0):
                pass
            NI = len(items)
            units = None
            LOOK = 2
            for i in range(min(LOOK, NI)):
                stage_a(i)
            for i in range(NI):
                h, qc, j, nj = items[i]
                if qc == 0 and j == 0 and h + 1 < H:
                    units = proj_units(h + 1, (h + 1) % 2)
                if i + LOOK < NI:
                    if items[i + LOOK][0] != h and units is not None:
                        for _ in units:
                            pass
                        units = None
                    stage_a(i + LOOK)
                run_pending()
                stage_b(i)
                if units is not None:
                    if next(units, "done") == "done":
                        units = None
            run_pending(force=True)
            d_go = [d_sg4[t // 4] for t in range(NT)]
            out_proj(C, sgT, d_go, 8, None, h_in, d_hin, h_out, d_hout, banks=(0, 1, 2, 7), wo_pre=(wo, d_wo))
            C.release()
        C.release()


LAYER_FNS[0] = layer_mla

def layer_gla(C, W, h_in, d_hin, h_out, d_hout):
    w_in = W["gla_w_in"]
    with ExitStack() as L:
        qtT = C.sb(L, [128, 4, S_], BF16); d_qt = deps(4)
        ktT = C.sb(L, [128, 4, S_], BF16); d_kt = deps(4)
        vtok = C.sb(L, [128, NT, 1024], BF16); d_v = deps(NT)
        gsg = C.sb(L, [128, NT, 1024], BF16); d_gsg = deps(NT)
        elast = C.sb(L, [128, 4, 32], F32); d_el = Dep()
        with ExitStack() as PA:
            uT = C.sb(PA, [128, 8, S_], BF16); d_uT = deps(NT)
            norm_T(C, PA, h_in, W["norm_g"], uT, d_uT, d_hin)
            d_uTb = [[d_uT[4 * tb + i] for i in range(4)] for tb in range(4)]
            wgk = C.sb(PA, [128, 8, 16], BF16); d_wgk = Dep()
            C.load_w(wgk[:], w_in[:, 3072:3088], d_wgk)
            gkT = C.sb(PA, [32, S_], BF16); d_gkT = Dep()
            wgk2 = C.sb(PA, [32, 512], BF16); d_wgk2 = Dep()
            C.V(lambda e: e.memset(gkT[:], 0.0), w=[d_gkT])
            C.V(lambda e: e.memset(wgk2[:], 0.0), w=[d_wgk2])
            C.DG(lambda e: e.dma_start(out=wgk2[0:16, :], in_=W["gla_w_gk2"]), w=[d_wgk2])
            for tb in range(4):
                bk = 2 + tb % 2
                for kc in range(8):
                    C.mm(C.pb(bk)[0:16, :], wgk[:, kc, :], uT[:, kc, tb * 512:(tb + 1) * 512], kc == 0, kc == 7,
                         r=d_uTb[tb] + [d_wgk], w=[C.psd[bk]])
                C.V(lambda e, tb=tb, bk=bk: e.tensor_copy(out=gkT[0:16, tb * 512:(tb + 1) * 512], in_=C.pb(bk)[0:16, :]),
                    r=[C.psd[bk]], w=[d_gkT])
            nbgk = C.sb(PA, [128, 4], F32); d_nb = Dep()
            C.load_col(nbgk[:], W["gla_b_gk"], d_nb, 4)
            C.V(lambda e: e.tensor_scalar(out=nbgk[:], in0=nbgk[:], scalar1=-1.0, scalar2=None, op0=ALU.mult),
                r=[d_nb], w=[d_nb])
            rmask = C.sb(PA, [128, S_], BF16); d_rm = Dep()
            C.D(lambda e: e.dma_start(out=rmask[:], in_=C.CD["rmask"]), w=[d_rm])
            Lh = C.sb(PA, [128, S_], F32); d_Lh = Dep()
            Bp = C.sb(PA, [128, S_], F32); d_Bp = Dep()
            wq = [C.sb(PA, [128, 8, 128], BF16) for _ in range(2)]; d_wq = deps(2)
            wk = [C.sb(PA, [128, 8, 128], BF16) for _ in range(2)]; d_wk = deps(2)
            QS = 128 ** -0.5
            for h in range(4):
                b = h % 2
                C.load_w(wq[b][:], w_in[:, h * 128:(h + 1) * 128], d_wq[b])
                C.load_w(wk[b][:], w_in[:, 512 + h * 128:512 + (h + 1) * 128], d_wk[b])
                for tb in range(4):
                    bk = 2 + tb % 2
                    C.mm(C.pb(bk), wgk2[:, h * 128:(h + 1) * 128], gkT[:, tb * 512:(tb + 1) * 512], True, True,
                         r=[d_wgk2, d_gkT], w=[C.psd[bk]])
                    C.act(Lh[:, tb * 512:(tb + 1) * 512], C.pb(bk), AF.Exp, r=[C.psd[bk], d_nb], w=[d_Lh],
                          scale=-1.0, bias=nbgk[:, h:h + 1])
                C.act(Lh[:], Lh[:], AF.Ln, r=[d_Lh], w=[d_Lh], bias=1.0)
                C.V(lambda e: e.tensor_tensor_scan(out=Bp[:], data0=rmask[:], data1=Lh[:], initial=0.0,
                                                   op0=ALU.mult, op1=ALU.add), r=[d_Lh, d_rm], w=[d_Bp])
                C.act(Lh[:], Bp[:], AF.Exp, r=[d_Bp], w=[d_Lh], scale=-1.0 / 16)
                C.act(Bp[:], Bp[:], AF.Exp, r=[d_Bp], w=[d_Bp], scale=1.0 / 16)
                C.G(lambda e, h=h: e.tensor_copy(out=elast[:, h, :],
                                                 in_=Lh[:].rearrange("p (n c) -> p n c", c=64)[:, :, 63]),
                    r=[d_Lh], w=[d_el])
                for tb in range(4):
                    bq = 4 + tb % 2; bkk = 6 + tb % 2
                    for kc in range(8):
                        C.mm(C.pb(bq), wq[b][:, kc, :], uT[:, kc, tb * 512:(tb + 1) * 512], kc == 0, kc == 7,
                             r=d_uTb[tb] + [d_wq[b]], w=[C.psd[bq]])
                    C.V(lambda e, h=h, tb=tb, bq=bq: e.scalar_tensor_tensor(
                        out=qtT[:, h, tb * 512:(tb + 1) * 512], in0=C.pb(bq), scalar=QS,
                        in1=Lh[:, tb * 512:(tb + 1) * 512], op0=ALU.mult, op1=ALU.mult),
                        r=[C.psd[bq], d_Lh], w=[d_qt[tb]])
                    for kc in range(8):
                        C.mm(C.pb(bkk), wk[b][:, kc, :], uT[:, kc, tb * 512:(tb + 1) * 512], kc == 0, kc == 7,
                             r=d_uTb[tb] + [d_wk[b]], w=[C.psd[bkk]])
                    C.V(lambda e, h=h, tb=tb, bkk=bkk: e.tensor_tensor(
                        out=ktT[:, h, tb * 512:(tb + 1) * 512], in0=C.pb(bkk), in1=Bp[:, tb * 512:(tb + 1) * 512],
                        op=ALU.mult), r=[C.psd[bkk], d_Bp], w=[d_kt[tb]])
            wv = [C.sb(PA, [128, 8, 512], BF16) for _ in range(2)]; d_wv = deps(2)
            for i in range(2):
                C.load_w(wv[i][:], w_in[:, 1024 + i * 512:1024 + (i + 1) * 512], d_wv[i])
            i = 0
            for t in range(NT):
                for nb in range(2):
                    bk = (0, 1, 2, 3)[i % 4]; i += 1
                    for kc in range(8):
                        C.mm(C.pb(bk), uT[:, kc, t * 128:(t + 1) * 128], wv[nb][:, kc, :], kc == 0, kc == 7,
                             r=[d_uT[t], d_wv[nb]], w=[C.psd[bk]])
                    C.act(vtok[:, t, nb * 512:(nb + 1) * 512], C.pb(bk), AF.Copy, r=[C.psd[bk]], w=[d_v[t]])
            for i in range(2):
                C.load_w(wv[i][:], w_in[:, 2048 + i * 512:2048 + (i + 1) * 512], d_wv[i])
            go_bc = C.sb(PA, [128, 256], F32); d_gob = Dep()
            C.load_bc(go_bc[:], W["gla_g_o"], d_gob)
            sgt = [C.sb(PA, [128, 1024], F32) for _ in range(2)]; d_sgt = deps(2)
            i = 0
            for t in range(NT):
                b = t % 2
                for nb in range(2):
                    bk = (4, 5, 6, 7)[i % 4]; i += 1
                    for kc in range(8):
                        C.mm(C.pb(bk), uT[:, kc, t * 128:(t + 1) * 128], wv[nb][:, kc, :], kc == 0, kc == 7,
                             r=[d_uT[t], d_wv[nb]], w=[C.psd[bk]])
                    C.act(sgt[b][:, nb * 512:(nb + 1) * 512], C.pb(bk), AF.Silu, r=[C.psd[bk]], w=[d_sgt[b]])
                C.G(lambda e, t=t, b=b: e.tensor_tensor(
                    out=gsg[:, t, :].rearrange("p (a c) -> p a c", a=4), in0=sgt[b][:].rearrange("p (a c) -> p a c", a=4),
                    in1=go_bc[:].unsqueeze(1).to_broadcast([128, 4, 256]), op=ALU.mult),
                    r=[d_sgt[b], d_gob], w=[d_gsg[t]])
            C.release()
        with ExitStack() as PB:
            goT = C.sb(PB, [128, 8, S_], BF16); d_go = deps(NT)
            wo = C.sb(PB, [128, 8, D_], BF16); d_wo = Dep()
            C.load_w(wo[:, 0:4, :], W["gla_w_out"][0:512, :], d_wo)
            C.load_w(wo[:, 4:8, :], W["gla_w_out"][512:1024, :], d_wo)
            mask2 = C.sb(PB, [128, 128], BF16); d_m2 = Dep()
            C.D(lambda e: e.dma_start(out=mask2[:], in_=C.CD["mask2"]), w=[d_m2])
            tmpS = C.sb(PB, [128, 4, 256], F32); d_tmpS = Dep()
            Sbf = [C.sb(PB, [128, 4, 256], BF16) for _ in range(2)]; d_Sbf = deps(2)
            C.V(lambda e: e.memset(Sbf[0][:], 0.0), w=[d_Sbf[0]])
            attm = [C.sb(PB, [128, 4, 128], BF16) for _ in range(2)]; d_attm = deps(2)
            ktk = [C.sb(PB, [128, 4, 128], BF16) for _ in range(2)]; d_ktk = deps(2)
            ss = C.sb(PB, [128, NT, 4], F32); rs = C.sb(PB, [128, NT, 4], F32); d_ss = deps(NT); d_rs = deps(NT)
            junk = C.sb(PB, [128, 256], BF16); d_junk = Dep()
            gotok = [C.sb(PB, [128, 1024], BF16) for _ in range(2)]; d_gotok = deps(2)
            osb = [C.sb(PB, [128, 1024], F32) for _ in range(2)]; d_osb = deps(2)

            def obank(t, h):
                return (2 + 2 * (t % 2) + h // 2, (h % 2) * 256)

            def g0(t):
                b = t % 2
                tb = t // 4
                tsl = slice(t * 128, (t + 1) * 128)
                pv = C.pbh(0)
                for h in range(4):
                    C.tr(pv[:, h * 128:(h + 1) * 128], ktT[:, h, tsl], C.ident_bf[:], r=[d_kt[tb], C.d_const], w=[C.psd[0]])
                for h in range(4):
                    C.mm(C.pb(1)[:, h * 128:(h + 1) * 128], ktT[:, h, tsl], qtT[:, h, tsl], True, True,
                         r=[d_kt[tb], d_qt[tb]], w=[C.psd[1]])
                yield
                C.V(lambda e, b=b, pv=pv: e.tensor_copy(out=ktk[b][:], in_=pv[:, 0:512].rearrange("p (a c) -> p a c", a=4)),
                    r=[C.psd[0]], w=[d_ktk[b]])
                C.V(lambda e, b=b: e.tensor_tensor(out=attm[b][:], in0=C.pb(1).rearrange("p (a c) -> p a c", a=4),
                                                   in1=mask2[:].unsqueeze(1).to_broadcast([128, 4, 128]), op=ALU.mult),
                    r=[C.psd[1], d_m2], w=[d_attm[b]])
                yield
                for h in range(4):
                    bk, c0 = obank(t, h)
                    C.T(lambda e, bk=bk, c0=c0, b=b, h=h, t=t: e.matmul(
                        C.pb(bk)[:, c0:c0 + 256], lhsT=attm[b][:, h, :], rhs=vtok[:, t, h * 256:(h + 1) * 256],
                        start=(h % 2 == 0), stop=False, skip_group_check=True), r=[d_attm[b], d_v[t]], w=[C.psd[bk]])

            def g1(t):
                b = t % 2
                tb = t // 4
                for half in range(2):
                    r0 = half * 64
                    n = 2 * t + half
                    for h in range(4):
                        bk, c0 = obank(t, h)
                        C.T(lambda e, bk=bk, c0=c0, h=h, t=t, r0=r0, half=half: e.matmul(
                            C.pb(bk)[r0:r0 + 64, c0:c0 + 256], lhsT=qtT[:, h, t * 128 + r0:t * 128 + r0 + 64],
                            rhs=Sbf[half][:, h, :], start=False, stop=(half == 1), skip_group_check=True),
                            r=[d_qt[tb], d_Sbf[half]], w=[C.psd[bk]])
                    for h in range(4):
                        bkS = 6 + h // 2
                        C.mm(C.pb(bkS)[:, (h % 2) * 256:(h % 2 + 1) * 256], ktk[b][r0:r0 + 64, h, :],
                             vtok[r0:r0 + 64, t, h * 256:(h + 1) * 256], True, True,
                             r=[d_ktk[b], d_v[t]], w=[C.psd[bkS]])
                    C.V(lambda e, half=half: e.tensor_tensor(out=tmpS[:], in0=C.ps[:, 6 * 512:8 * 512].rearrange("p (a c) -> p a c", a=4),
                                                            in1=Sbf[half][:], op=ALU.add), r=[C.psd[6], C.psd[7], d_Sbf[half]], w=[d_tmpS])
                    C.V(lambda e, n=n, half=half: e.tensor_tensor(out=Sbf[1 - half][:], in0=tmpS[:],
                                                                 in1=elast[:, :, n:n + 1].to_broadcast([128, 4, 256]), op=ALU.mult),
                        r=[d_tmpS, d_el], w=[d_Sbf[1 - half]])
                    yield
                for i_ in range(2):
                    bk = 2 + 2 * (t % 2) + i_
                    C.act(osb[b][:, i_ * 512:(i_ + 1) * 512], C.pb(bk), AF.Copy, r=[C.psd[bk]], w=[d_osb[b]])

            def g2(t):
                b = t % 2
                for h in range(4):
                    C.act(junk[:], osb[b][:, h * 256:(h + 1) * 256], AF.Square, r=[d_osb[b]], w=[d_junk, d_ss[t]],
                          accum_out=ss[:, t, h:h + 1])
                yield
                rstd_ops(C, rs[:, t, :], ss[:, t, :], 256, [d_ss[t]], [d_rs[t]])
                yield
                for h in range(4):
                    C.V(lambda e, t=t, b=b, h=h: e.scalar_tensor_tensor(
                        out=gotok[b][:, h * 256:(h + 1) * 256], in0=osb[b][:, h * 256:(h + 1) * 256], scalar=rs[:, t, h:h + 1],
                        in1=gsg[:, t, h * 256:(h + 1) * 256], op0=ALU.mult, op1=ALU.mult),
                        r=[d_osb[b], d_rs[t], d_gsg[t]], w=[d_gotok[b]])
                yield
                pv = C.pbh(0)
                for c in range(8):
                    C.tr(pv[:, c * 128:(c + 1) * 128], gotok[b][:, c * 128:(c + 1) * 128], C.ident_bf[:],
                         r=[d_gotok[b], C.d_const], w=[C.psd[0]])
                C.act(goT[:, :, t * 128:(t + 1) * 128], pv.rearrange("p (c n) -> p c n", c=8), AF.Copy,
                      r=[C.psd[0]], w=[d_go[t]])

            LV = {1: ("s1", "s0", "s2"), 2: ("s0", "s1", "s2"), 3: ("s1", "s0", "s2"), 4: ("s2",)}
            for s_ in range(NT + 2):
                gens = {}
                if 0 <= s_ - 2 < NT:
                    gens["s2"] = g2(s_ - 2)
                if 0 <= s_ - 1 < NT:
                    gens["s1"] = g1(s_ - 1)
                if s_ < NT:
                    gens["s0"] = g0(s_)
                for lvl in (1, 2, 3, 4):
                    for nm in LV[lvl]:
                        if nm in gens:
                            next(gens[nm], None)
                for gn_ in gens.values():
                    for _ in gn_:
                        pass
            out_proj(C, goT, d_go, 8, None, h_in, d_hin, h_out, d_hout, banks=(1, 2, 3, 4), wo_pre=(wo, d_wo))
            C.release()
        C.release()


LAYER_FNS[1] = layer_gla

def layer_ssd(C, W, h_in, d_hin, h_out, d_hout):
    w_in = W["ssd_w_in"]
    G_ = 8
    with ExitStack() as L:
        goT = C.sb(L, [128, 16, S_], BF16); d_go = deps(NT)
        with ExitStack() as PA:
            uT = C.sb(PA, [128, 8, S_], BF16); d_uT = deps(NT)
            norm_T(C, PA, h_in, W["norm_g"], uT, d_uT, d_hin)
            d_uTb = [[d_uT[4 * tb + i] for i in range(4)] for tb in range(4)]
            U2b = C.sb(PA, [128, 128], BF16); V2b = C.sb(PA, [128, 128], BF16)
            U2f = C.sb(PA, [128, 128], F32); V2f = C.sb(PA, [128, 128], F32)
            onA = C.sb(PA, [128, 128], F32); onB = C.sb(PA, [128, 128], F32); d_k = Dep()
            negm = C.sb(PA, [128, 512], BF16)
            C.D(lambda e: e.dma_start(out=negm[:], in_=C.CD["negm4"]), w=[d_k])
            for dst, nm in ((U2b, "U2b"), (V2b, "mask2"), (U2f, "U2f"), (V2f, "V2f"), (onA, "onesA"), (onB, "onesB")):
                C.D(lambda e, dst=dst, nm=nm: e.dma_start(out=dst[:], in_=C.CD[nm]), w=[d_k])
            dtb = C.sb(PA, [128, 32], F32); alog = C.sb(PA, [128, 32], F32); dsk = C.sb(PA, [128, 32], F32)
            gn = C.sb(PA, [128, 2048], F32); d_p = Dep()
            C.load_bc(dtb[:], W["ssd_dt_bias"], d_p)
            C.load_bc(alog[:], W["ssd_a_log"], d_p)
            C.load_bc(dsk[:], W["ssd_d"], d_p)
            C.load_bc(gn[:], W["ssd_g_norm"], d_p)
            cw = C.sb(PA, [128, 4, 32], F32); cb = C.sb(PA, [128, 32], F32); d_cw = Dep()
            for k in range(4):
                C.load_col(cw[:, k, :], W["ssd_conv_w"][k, :], d_cw, 32)
            C.load_col(cb[:], W["ssd_conv_b"], d_cw, 32)
            wdt = C.sb(PA, [128, 8, 32], BF16); d_wdt = Dep()
            C.load_w(wdt[:], w_in[:, 6144:6176], d_wdt)
            dtt = C.sb(PA, [128, NT, 32], F32); a_tok = C.sb(PA, [128, NT, 32], F32)
            ecs = C.sb(PA, [128, NT, 32], F32); dec = C.sb(PA, [128, NT, 32], F32)
            elast = C.sb(PA, [128, 32, 32], F32); eA = C.sb(PA, [128, 32], F32)
            d_dt = Dep(); d_a = Dep(); d_ecs = Dep(); d_dec = Dep(); d_el = Dep()
            for t in range(NT):
                for kc in range(8):
                    C.mm(C.pb(0)[:, t * 32:(t + 1) * 32], uT[:, kc, t * 128:(t + 1) * 128], wdt[:, kc, :], kc == 0, kc == 7,
                         r=[d_uT[t], d_wdt], w=[C.psd[0]])
            C.V(lambda e: e.tensor_tensor(out=dtt[:], in0=C.pb(0).rearrange("p (a c) -> p a c", a=NT),
                                          in1=dtb[:].unsqueeze(1).to_broadcast([128, NT, 32]), op=ALU.add),
                r=[C.psd[0], d_p], w=[d_dt])
            C.act(dtt[:], dtt[:], AF.Exp, r=[d_dt], w=[d_dt])
            C.act(dtt[:], dtt[:], AF.Ln, r=[d_dt], w=[d_dt], bias=1.0)
            C.act(eA[:], alog[:], AF.Exp, r=[d_p], w=[d_a])
            C.V(lambda e: e.scalar_tensor_tensor(out=a_tok[:], in0=dtt[:], scalar=-1.0,
                                                 in1=eA[:].unsqueeze(1).to_broadcast([128, NT, 32]),
                                                 op0=ALU.mult, op1=ALU.mult), r=[d_dt, d_a], w=[d_a])
            for t in range(NT):
                C.mm(C.pb(1)[:, t * 32:(t + 1) * 32], V2f[:], a_tok[:, t, :], True, True, r=[d_k, d_a], w=[C.psd[1]])
            C.act(ecs[:], C.pb(1).rearrange("p (a c) -> p a c", a=NT), AF.Exp, r=[C.psd[1]], w=[d_ecs])
            for t in range(NT):
                C.mm(C.pb(2)[:, t * 32:(t + 1) * 32], U2f[:], a_tok[:, t, :], True, True, r=[d_k, d_a], w=[C.psd[2]])
            C.act(dec[:], C.pb(2).rearrange("p (a c) -> p a c", a=NT), AF.Exp, r=[C.psd[2]], w=[d_dec])
            for t in range(NT):
                for half in range(2):
                    n = 2 * t + half
                    bk = 3 + n // 16
                    C.mm(C.pb(bk)[:, (n % 16) * 32:(n % 16 + 1) * 32], (onA, onB)[half][:], a_tok[:, t, :], True, True,
                         r=[d_k, d_a], w=[C.psd[bk]])
            C.act(elast[:], C.ps[:, 3 * 512:5 * 512].rearrange("p (a c) -> p a c", a=32), AF.Exp,
                  r=[C.psd[3], C.psd[4]], w=[d_el])
            xTc = [C.sb(PA, [128, S_], BF16) for _ in range(2)]
            BT = C.sb(PA, [128, S_], BF16); CT = C.sb(PA, [128, S_], BF16)
            d_xT = deps(2); d_BT = Dep(); d_CT = Dep()
            x_tok = C.sb(PA, [128, NT, 256], BF16); d_xtok = deps(4)
            B_tok = C.sb(PA, [128, NT, 128], BF16); d_Btok = deps(2)
            sz = C.sb(PA, [128, NT, 256], BF16); d_sz = deps(NT)
            ub = [C.sb(PA, [128, 3 + S_], BF16) for _ in range(2)]; d_ub = deps(2)
            for b in range(2):
                C.V(lambda e, b=b: e.memset(ub[b][:, 0:3], 0.0), w=[d_ub[b]])
            wc = [C.sb(PA, [128, 8, 128], BF16) for _ in range(2)]; d_wc = deps(2)
            dg = [C.sb(PA, [128, 4, 128], BF16) for _ in range(2)]; d_dg = deps(2)
            wz = C.sb(PA, [128, 8, 256], BF16); d_wz = Dep()
            Sst = C.sb(PA, [128, 4, 64], F32); tmpS = C.sb(PA, [128, 4, 64], F32); d_S = Dep(); d_tmpS = Dep()
            Sbf = [C.sb(PA, [128, 256], BF16) for _ in range(2)]; d_Sbf = deps(2)
            cbm = [C.sb(PA, [128, 128], F32) for _ in range(2)]; d_cbm = deps(2)
            aV = [C.sb(PA, [128, 4, 128], BF16) for _ in range(2)]; d_aV = deps(2)
            lm = [C.sb(PA, [128, 4, 128], F32) for _ in range(2)]; d_lm = deps(2)
            Mh = [C.sb(PA, [128, 4, 128], BF16) for _ in range(2)]; d_M = deps(2)
            xdt = [C.sb(PA, [128, 4, 64], BF16) for _ in range(2)]; d_xdt = deps(2)
            xdd = [C.sb(PA, [128, 4, 64], BF16) for _ in range(2)]; d_xdd = deps(2)
            xD = [C.sb(PA, [128, 4, 64], BF16) for _ in range(2)]; d_xD = deps(2)
            t1 = [C.sb(PA, [128, 4, 64], F32) for _ in range(2)]; d_t1 = deps(2)
            yz = [C.sb(PA, [128, 256], F32) for _ in range(2)]; d_yz = deps(2)
            yn = [C.sb(PA, [128, 256], BF16) for _ in range(2)]; d_yn = deps(2)
            ss = C.sb(PA, [128, G_, NT], F32); rs = C.sb(PA, [128, G_, NT], F32)
            junk = C.sb(PA, [128, 256], BF16); d_junk = Dep()
            ci = 0
            d_dS = [deps(2), deps(2)]
            for g in range(G_):
                hs = slice(4 * g, 4 * g + 4)
                chunks = ((2 * g, xTc[0], d_xT[0]), (2 * g + 1, xTc[1], d_xT[1]), (16 + g, BT, d_BT), (24 + g, CT, d_CT))

                def proj_c(k_, b):
                    cc = chunks[k_][0]
                    C.load_w(wc[b][:], w_in[:, 2048 + cc * 128:2048 + (cc + 1) * 128], d_wc[b])
                    for k in range(4):
                        C.V(lambda e, b=b, k=k, cc=cc: e.tensor_scalar(out=dg[b][:, k, :], in0=C.ident_bf[:],
                                                                       scalar1=cw[:, k, cc:cc + 1], scalar2=None, op0=ALU.mult),
                            r=[d_cw, C.d_const], w=[d_dg[b]])
                    for tb in range(4):
                        bk = tb
                        for kc in range(8):
                            C.mm(C.pb(bk), wc[b][:, kc, :], uT[:, kc, tb * 512:(tb + 1) * 512], kc == 0, kc == 7,
                                 r=d_uTb[tb] + [d_wc[b]], w=[C.psd[bk]])
                        C.act(ub[b][:, 3 + tb * 512:3 + (tb + 1) * 512], C.pb(bk), AF.Copy, r=[C.psd[bk]], w=[d_ub[b]])

                def conv_c(k_, b):
                    cc, dst, d_dst = chunks[k_]
                    for tb in range(4):
                        bk = 4 + tb
                        for k in range(4):
                            C.mm(C.pb(bk), dg[b][:, k, :], ub[b][:, tb * 512 + k:tb * 512 + k + 512], k == 0, k == 3,
                                 r=[d_ub[b], d_dg[b]], w=[C.psd[bk]])
                        C.act(dst[:, tb * 512:(tb + 1) * 512], C.pb(bk), AF.Silu, r=[C.psd[bk], d_cw], w=[d_dst],
                              bias=cb[:, cc:cc + 1])

                proj_c(0, 0)
                for k_ in range(4):
                    if k_ + 1 < 4:
                        proj_c(k_ + 1, (k_ + 1) % 2)
                    conv_c(k_, k_ % 2)
                for t4 in range(4):
                    bk = t4 % 2
                    pv = C.pbh(bk)
                    for i in range(4):
                        for c in range(2):
                            C.tr(pv[:, (i * 2 + c) * 128:(i * 2 + c + 1) * 128], xTc[c][:, (t4 * 4 + i) * 128:(t4 * 4 + i + 1) * 128],
                                 C.ident_bf[:], r=[d_xT[c], C.d_const], w=[C.psd[bk]])
                    C.act(x_tok[:, t4 * 4:(t4 + 1) * 4, :], pv.rearrange("p (a c) -> p a c", a=4), AF.Copy,
                          r=[C.psd[bk]], w=[d_xtok[t4]])
                for t8 in range(2):
                    bk = 2 + t8
                    pv = C.pbh(bk)
                    for i in range(8):
                        C.tr(pv[:, i * 128:(i + 1) * 128], BT[:, (t8 * 8 + i) * 128:(t8 * 8 + i + 1) * 128], C.ident_bf[:],
                             r=[d_BT, C.d_const], w=[C.psd[bk]])
                    C.V(lambda e, t8=t8, pv=pv: e.tensor_copy(out=B_tok[:, t8 * 8:(t8 + 1) * 8, :],
                                                             in_=pv.rearrange("p (a c) -> p a c", a=8)),
                        r=[C.psd[bk]], w=[d_Btok[t8]])
                C.load_w(wz[:], w_in[:, g * 256:(g + 1) * 256], d_wz)
                for t2 in range(8):
                    bk = 4 + t2 % 4
                    for i in range(2):
                        t = t2 * 2 + i
                        for kc in range(8):
                            C.mm(C.pb(bk)[:, i * 256:(i + 1) * 256], uT[:, kc, t * 128:(t + 1) * 128], wz[:, kc, :],
                                 kc == 0, kc == 7, r=[d_uT[t], d_wz], w=[C.psd[bk]])
                    C.act(sz[:, t2 * 2:t2 * 2 + 2, :], C.pb(bk).rearrange("p (a c) -> p a c", a=2), AF.Silu,
                          r=[C.psd[bk]], w=[d_sz[t2 * 2], d_sz[t2 * 2 + 1]])
                C.V(lambda e: e.memset(Sbf[0][:], 0.0), w=[d_Sbf[0]])
                def st0(t):
                    b = t % 2
                    tsl = slice(t * 128, (t + 1) * 128)
                    xt_ = x_tok[:, t, :].rearrange("p (a c) -> p a c", a=4)
                    d_x = d_xtok[t // 4]
                    for h_ in range(4):
                        C.act(aV[b][:, h_, :], V2b[:], AF.Copy, r=[d_a, d_k], w=[d_aV[b]],
                              scale=a_tok[:, t, 4 * g + h_:4 * g + h_ + 1])
                    C.G(lambda e, hs=hs, b=b, t=t, xt_=xt_: e.tensor_tensor(out=xdt[b][:], in0=xt_,
                                                                   in1=dtt[:, t, hs].unsqueeze(2).to_broadcast([128, 4, 64]), op=ALU.mult),
                        r=[d_x, d_dt], w=[d_xdt[b]])
                    C.G(lambda e, hs=hs, b=b, t=t: e.tensor_tensor(out=xdd[b][:], in0=xdt[b][:],
                                                           in1=dec[:, t, hs].unsqueeze(2).to_broadcast([128, 4, 64]), op=ALU.mult),
                        r=[d_xdt[b], d_dec], w=[d_xdd[b]])
                    C.G(lambda e, hs=hs, b=b, xt_=xt_: e.tensor_tensor(out=xD[b][:], in0=xt_,
                                                              in1=dsk[:, hs].unsqueeze(2).to_broadcast([128, 4, 64]), op=ALU.mult),
                        r=[d_x, d_p], w=[d_xD[b]])
                    C.mm(C.pb(0)[:, 0:128], BT[:, tsl], CT[:, tsl], True, True, r=[d_BT, d_CT], w=[C.psd[0]])
                    yield
                    C.mm(C.pb(1), U2b[:], aV[b][:].rearrange("p a c -> p (a c)"), True, False, r=[d_k, d_aV[b]], w=[C.psd[1]])
                    C.mm(C.pb(1), C.ident_bf[:], negm[:], False, True, r=[C.d_const, d_k], w=[C.psd[1]])
                    for half in range(2):
                        r0 = half * 64
                        C.mm(C.pb(4 + half)[:, b * 256:(b + 1) * 256], B_tok[r0:r0 + 64, t, :],
                             xdd[b][r0:r0 + 64, :, :].rearrange("p a c -> p (a c)"),
                             True, True, r=[d_Btok[t // 8], d_xdd[b]], w=[C.psd[4 + half]])
                    yield
                    C.act(lm[b][:], C.pb(1).rearrange("p (a c) -> p a c", a=4), AF.Exp, r=[C.psd[1]], w=[d_lm[b]])
                    yield
                    C.V(lambda e, b=b: e.tensor_tensor(out=Mh[b][:], in0=lm[b][:],
                                                       in1=C.pb(0)[:, 0:128].unsqueeze(1).to_broadcast([128, 4, 128]), op=ALU.mult),
                        r=[d_lm[b], C.psd[0]], w=[d_M[b]])

                def st1(t):
                    b = t % 2
                    bY = 2 + b
                    C.mm(C.pb(bY)[:, 0:256], C.ident_bf[:], xD[b][:].rearrange("p a c -> p (a c)"), True, False,
                         r=[C.d_const, d_xD[b]], w=[C.psd[bY]])
                    for h in range(4):
                        C.mm(C.pb(bY)[:, h * 64:(h + 1) * 64], Mh[b][:, h, :], xdt[b][:, h, :], False, h == 3,
                             r=[d_M[b], d_xdt[b]], w=[C.psd[bY]])
                    for half in range(2):
                        r0 = half * 64
                        n = 2 * t + half
                        C.mm(C.pb(bY)[r0:r0 + 64, 256:512], CT[:, t * 128 + r0:t * 128 + r0 + 64], Sbf[half][:], True, True,
                             r=[d_CT, d_Sbf[half]], w=[C.psd[bY]])
                        C.V(lambda e, hs=hs, n=n, half=half: e.tensor_tensor(
                            out=tmpS[:], in0=Sbf[half][:].rearrange("p (a c) -> p a c", a=4),
                            in1=elast[:, n, hs].unsqueeze(2).to_broadcast([128, 4, 64]), op=ALU.mult),
                            r=[d_Sbf[half], d_el], w=[d_tmpS])
                        C.V(lambda e, half=half, b=b: e.tensor_tensor(
                            out=Sbf[1 - half][:].rearrange("p (a c) -> p a c", a=4),
                            in0=C.pb(4 + half)[:, b * 256:(b + 1) * 256].rearrange("p (a c) -> p a c", a=4),
                            in1=tmpS[:], op=ALU.add), r=[C.psd[4 + half], d_tmpS], w=[d_Sbf[1 - half]])
                        if half == 0:
                            yield

                def st2(t, g=g):
                    b = t % 2
                    bY = 2 + b
                    C.V(lambda e, hs=hs, b=b, t=t, bY=bY: e.tensor_tensor(out=t1[b][:], in0=C.pb(bY)[:, 256:512].rearrange("p (a c) -> p a c", a=4),
                                                           in1=ecs[:, t, hs].unsqueeze(2).to_broadcast([128, 4, 64]), op=ALU.mult),
                        r=[C.psd[bY], d_ecs], w=[d_t1[b]])
                    C.V(lambda e, b=b, bY=bY: e.tensor_tensor(out=t1[b][:], in0=C.pb(bY)[:, 0:256].rearrange("p (a c) -> p a c", a=4),
                                                       in1=t1[b][:], op=ALU.add), r=[C.psd[bY], d_t1[b]], w=[d_t1[b]])
                    yield
                    C.G(lambda e, b=b, t=t: e.tensor_tensor(out=yz[b][:], in0=t1[b][:].rearrange("p a c -> p (a c)"),
                                                           in1=sz[:, t, :], op=ALU.mult), r=[d_t1[b], d_sz[t]], w=[d_yz[b]])
                    yield
                    C.act(junk[:], yz[b][:], AF.Square, r=[d_yz[b]], w=[d_junk], accum_out=ss[:, g, t:t + 1])
                    rstd_ops(C, rs[:, g, t:t + 1], ss[:, g, t:t + 1], 256, [d_junk], [d_junk])
                    yield
                    C.V(lambda e, b=b, t=t, g=g: e.scalar_tensor_tensor(out=yn[b][:], in0=yz[b][:], scalar=rs[:, g, t:t + 1],
                                                                       in1=gn[:, g * 256:(g + 1) * 256], op0=ALU.mult, op1=ALU.mult),
                        r=[d_yz[b], d_junk, d_p], w=[d_yn[b]])
                    bk = 6 + (t // 4) % 2
                    pv = C.pbh(bk)
                    i = t % 4
                    for c in range(2):
                        C.tr(pv[:, (c * 4 + i) * 128:(c * 4 + i + 1) * 128], yn[b][:, c * 128:(c + 1) * 128], C.ident_bf[:],
                             r=[d_yn[b], C.d_const], w=[C.psd[bk]])
                    if i == 3:
                        t4 = t // 4
                        C.act(goT[:, 2 * g:2 * g + 2, t4 * 512:(t4 + 1) * 512], pv.rearrange("p (c m) -> p c m", c=2), AF.Copy,
                              r=[C.psd[bk]], w=[d_go[4 * t4 + j] for j in range(4)])

                LV = {1: ("s1", "s0", "s2"), 2: ("s0", "s1", "s2"), 3: ("s0", "s2"), 4: ("s0", "s2")}
                for s_ in range(NT + 2):
                    gens = {}
                    if 0 <= s_ - 2 < NT:
                        gens["s2"] = st2(s_ - 2)
                    if 0 <= s_ - 1 < NT:
                        gens["s1"] = st1(s_ - 1)
                    if s_ < NT:
                        gens["s0"] = st0(s_)
                    for lvl in (1, 2, 3, 4):
                        for nm in LV[lvl]:
                            if nm in gens:
                                next(gens[nm], None)
                    for gn_ in gens.values():
                        for _ in gn_:
                            pass
            C.dbg("dtt", dtt[:], [128, NT, 32], F32, [d_dt])
            C.dbg("a_tok", a_tok[:], [128, NT, 32], F32, [d_a])
            C.dbg("ecs", ecs[:], [128, NT, 32], F32, [d_ecs])
            C.dbg("dec", dec[:], [128, NT, 32], F32, [d_dec])
            C.dbg("elast", elast[:], [128, 32, 32], F32, [d_el])
            C.dbg("x_tok", x_tok[:], [128, NT, 256], BF16, d_xtok)
            C.dbg("B_tok", B_tok[:], [128, NT, 128], BF16, d_Btok)
            C.dbg("CT", CT[:], [128, S_], BF16, [d_CT])
            C.dbg("sz", sz[:], [128, NT, 256], BF16, d_sz)
            C.dbg("goT", goT[:], [128, 16, S_], BF16, d_go)
            C.dbg("Mh", Mh[1][:], [128, 4, 128], BF16, [d_M[1]])
            C.dbg("lm", lm[1][:], [128, 4, 128], F32, [d_lm[1]])
            C.dbg("yz", yz[1][:], [128, 256], F32, [d_yz[1]])
            C.release()
        out_proj(C, goT, d_go, 16, W["ssd_w_out"], h_in, d_hin, h_out, d_hout)
        C.release()


LAYER_FNS[3] = layer_ssd

WSPEC = {
    "norm_g": [4, 1024], "final_g": [1024],
    "mla_w_in": [1024, 1696], "mla_g_q": [384], "mla_w_uq": [384, 1536], "mla_g_kv": [256],
    "mla_w_ukv": [256, 2048], "mla_w_out": [1024, 1024],
    "gla_w_in": [1024, 3088], "gla_w_gk2": [16, 512], "gla_b_gk": [512], "gla_g_o": [256],
    "gla_w_out": [1024, 1024],
    "lru_w_in": [1024, 2560], "lru_conv_w": [4, 1280], "lru_conv_b": [1280], "lru_w_a": [10, 128, 128],
    "lru_b_a": [1280], "lru_w_x": [10, 128, 128], "lru_b_x": [1280], "lru_lam": [1280],
    "lru_w_out": [1280, 1024],
    "ssd_w_in": [1024, 6176], "ssd_conv_w": [4, 4096], "ssd_conv_b": [4096], "ssd_dt_bias": [32],
    "ssd_a_log": [32], "ssd_d": [32], "ssd_g_norm": [2048], "ssd_w_out": [2048, 1024],
}


def host_consts():
    c = {}
    c["ident_bf"] = np.eye(128, dtype=np.float32).astype(ml_dtypes.bfloat16)
    c["ident_f"] = np.eye(128, dtype=np.float32)
    k = np.arange(128)
    c["tri_bf"] = (k[None, :] >= k[:, None]).astype(np.float32).astype(ml_dtypes.bfloat16)
    c["ones_bf"] = np.ones((128, 64), np.float32).astype(ml_dtypes.bfloat16)
    invf = (np.float32(10000.0) ** (-np.arange(0, 32, 2, dtype=np.float32) / np.float32(32))).astype(np.float32)
    rm = np.ones((128, S_), np.float32); rm[:, ::64] = 0.0
    c["rmask"] = rm.astype(ml_dtypes.bfloat16)
    m2 = ((k[None, :] >= k[:, None]) & ((k[None, :] // 64) == (k[:, None] // 64))).astype(np.float32)
    c["mask2"] = m2.astype(ml_dtypes.bfloat16)
    same = (k[None, :] // 64) == (k[:, None] // 64)
    u2 = ((k[:, None] > k[None, :]) & same).astype(np.float32)
    c["U2b"] = u2.astype(ml_dtypes.bfloat16)
    c["U2f"] = u2
    c["V2f"] = m2.astype(np.float32)
    c["onesA"] = np.ascontiguousarray(np.broadcast_to((k[:, None] < 64).astype(np.float32), (128, 128)))
    c["onesB"] = np.ascontiguousarray(np.broadcast_to((k[:, None] >= 64).astype(np.float32), (128, 128)))
    c["negm4"] = np.tile(np.where(m2 > 0, 0.0, -30000.0).astype(np.float32), (1, 4)).astype(ml_dtypes.bfloat16)
    c["swap_f"] = np.roll(np.eye(128, dtype=np.float32), 64, axis=0)
    c["invf"] = np.ascontiguousarray(np.broadcast_to(invf[None, :], (128, 16))).astype(np.float32)
    return c


CSPEC = {"ident_bf": ([128, 128], BF16), "ident_f": ([128, 128], F32), "tri_bf": ([128, 128], BF16),
         "ones_bf": ([128, 64], BF16), "invf": ([128, 16], F32), "rmask": ([128, S_], BF16),
         "mask2": ([128, 128], BF16), "U2b": ([128, 128], BF16), "U2f": ([128, 128], F32),
         "V2f": ([128, 128], F32), "swap_f": ([128, 128], F32), "negm4": ([128, 512], BF16), "onesA": ([128, 128], F32), "onesB": ([128, 128], F32)}

def build(layers=(0, 1, 2, 3), final=True):
    nc = bass.Bass("TRN2", target_bir_lowering=False)
    x = nc.dram_tensor("x", [S_, D_], F32, kind="ExternalInput").ap()
    pos = nc.dram_tensor("pos", [128, NT], I32, kind="ExternalInput").ap()
    W = {k: nc.dram_tensor(k, v, F32, kind="ExternalInput").ap() for k, v in WSPEC.items()}
    CD = {k: nc.dram_tensor(k, v[0], v[1], kind="ExternalInput").ap() for k, v in CSPEC.items()}
    out = nc.dram_tensor("out", [S_, D_], F32, kind="ExternalOutput").ap()
    hbuf = [nc.dram_tensor(f"hbuf{i}", [S_, D_], F32).ap() for i in range(2)]
    _BASE["r"] = {}
    with ExitStack() as st:
        C = Ctx(nc, st)
        C.pos = pos
        C.CD = CD
        C.d_const = Dep()
        C.ident_bf = C.sb(st, [128, 128], BF16)
        C.ident_f = C.sb(st, [128, 128], F32)
        C.D(lambda e: e.dma_start(out=C.ident_bf[:], in_=CD["ident_bf"]), w=[C.d_const])
        C.D(lambda e: e.dma_start(out=C.ident_f[:], in_=CD["ident_f"]), w=[C.d_const])
        h_cur, d_cur = x, deps(NT)
        nxt = 0
        for li in layers:
            Wl = dict(W)
            Wl["norm_g"] = W["norm_g"][li, :]
            h_nxt, d_nxt = hbuf[nxt], deps(NT)
            if not final and li == layers[-1]:
                h_nxt = out
            LAYER_FNS[li](C, Wl, h_cur, d_cur, h_nxt, d_nxt)
            h_cur, d_cur = h_nxt, d_nxt
            nxt ^= 1
        if final:
            final_norm(C, h_cur, d_cur, W["final_g"], out)
        else:
            C.out_toks.append(C.last_store)
            for t in range(NT):
                C.out_toks.append(d_cur[t].w)
        for tok in C.out_toks:
            C.S.wait_tok("sync", tok)
        C.S.emit()
    return nc


LAYER_FNS[2] = layer_lru


def make_in_maps(inputs):
    consts = host_consts()
    maps = []
    for b in range(8):
        m = {"x": np.ascontiguousarray(inputs["x"][b]),
             "pos": np.ascontiguousarray(np.asarray(inputs["positions"][b]).astype(np.int32).reshape(NT, 128).T)}
        for k in WSPEC:
            a = np.asarray(inputs[k], dtype=np.float32)
            if k not in ("norm_g", "final_g"):
                a = a[0]
            m[k] = np.ascontiguousarray(a)
        m.update(consts)
        maps.append(m)
    return maps


_NC_CACHE = {}


def kernel(**inputs):
    if "nc" not in _NC_CACHE:
        _NC_CACHE["nc"] = build()
    nc = _NC_CACHE["nc"]
    maps = make_in_maps(inputs)
    res = run_bass_kernel_spmd(nc, maps, core_ids=list(range(8)))
    return np.stack([np.asarray(r["out"], dtype=np.float32) for r in res.results], axis=0)
```

```python
import numpy as np
import ml_dtypes
import concourse.bass as bass
import concourse.mybir as mybir
from concourse.bass_utils import run_bass_kernel_spmd
from contextlib import ExitStack

F32 = mybir.dt.float32
BF16 = mybir.dt.bfloat16
I32 = mybir.dt.int32
AF = mybir.ActivationFunctionType
ALU = mybir.AluOpType

S_ = 2048
D_ = 1024
NT = 16
EPS = 1e-6
LAYER_FNS = {}
DEBUG = False
import os
SSD_SKEW = int(os.environ.get('SSD_SKEW', '1'))

ENGS = ("tensor", "vector", "scalar", "gpsimd", "sync")
EPOCH = 6000
NDMA = 24


_BASE = {"r": {}}


class Dep:
    __slots__ = ("w", "r")

    def __init__(self):
        self.w = None
        self.r = dict(_BASE["r"])


def deps(n):
    return [Dep() for _ in range(n)]


class Sched:
    def __init__(self, nc, stack):
        self.nc = nc
        self.stack = stack
        self.q = {e: [] for e in ENGS}
        self.cnt = {e: 0 for e in ENGS}
        self.sems = {}
        self.known = {e: {} for e in ENGS}
        self.dma_sems = [stack.enter_context(nc.semaphore(f"dma{i}")) for i in range(NDMA)]
        self.dma_i = 0

    def _esem(self, eng, epoch):
        k = (eng, epoch)
        if k not in self.sems:
            self.sems[k] = self.stack.enter_context(self.nc.semaphore(f"s_{eng}_{epoch}"))
        return self.sems[k]

    def _need_wait(self, weng, tok):
        kn = self.known[weng]
        if tok[0] == "c":
            _, eng, ep, val = tok
            cur = kn.get(("c", eng), (-1, 0))
            if cur >= (ep, val):
                return None
            kn[("c", eng)] = (ep, val)
            return (self._esem(eng, ep), val)
        _, idx, val = tok
        cur = kn.get(("d", idx), 0)
        if cur >= val:
            return None
        kn[("d", idx)] = val
        return (self.dma_sems[idx], val)

    def op(self, eng, fn, reads=(), writes=(), dma=False):
        waits = []
        toks = []
        for d in reads:
            if d.w is not None:
                toks.append((d.w, "raw"))
        for d in writes:
            if d.w is not None:
                toks.append((d.w, "waw"))
            for t in d.r.values():
                toks.append((t, "war"))
        for tok, kind in toks:
            if not dma and tok[0] == "c" and tok[1] == eng:
                if eng == "tensor" or kind != "raw":
                    continue
            w = self._need_wait(eng, tok)
            if w is not None:
                waits.append(w)
        if dma:
            idx = self.dma_i % NDMA
            rnd = self.dma_i // NDMA
            self.dma_i += 1
            if rnd > 0:
                w = self._need_wait(eng, ("d", idx, 16 * rnd))
                if w is not None:
                    waits.append(w)
            mytok = ("d", idx, 16 * (rnd + 1))
            inc = (self.dma_sems[idx], 16)
        else:
            c = self.cnt[eng]
            self.cnt[eng] = c + 1
            ep, val = c // EPOCH, c % EPOCH + 1
            mytok = ("c", eng, ep, val)
            inc = (self._esem(eng, ep), 1)
        self.q[eng].append((fn, waits, inc))
        rk = mytok[:2]
        for d in reads:
            d.r[rk] = mytok
        for d in writes:
            d.w = mytok
            d.r = {}
        return mytok

    def snapshot(self):
        snap = {}
        for e in ENGS:
            c = self.cnt[e]
            if c > 0:
                c -= 1
                snap[("c", e)] = ("c", e, c // EPOCH, c % EPOCH + 1)
        for i in range(min(self.dma_i, NDMA)):
            n_i = (self.dma_i - 1 - i) // NDMA
            snap[("d", i)] = ("d", i, 16 * (n_i + 1))
        return snap

    def wait_tok(self, eng, tok):
        w = self._need_wait(eng, tok)
        if w is not None:
            self.q[eng].append((None, [w], None))

    def emit(self):
        nc = self.nc
        with nc.Block() as block:
            def run(engname):
                def body(e):
                    for fn, waits, inc in self.q[engname]:
                        for sem, val in waits:
                            e.wait_ge(sem, val)
                        if fn is not None:
                            ins = fn(e)
                            ins.then_inc(inc[0], inc[1])
                return body
            block.tensor(run("tensor"))
            block.vector(run("vector"))
            block.scalar(run("scalar"))
            block.gpsimd(run("gpsimd"))
            block.sync(run("sync"))


class Ctx:
    def __init__(self, nc, st):
        self.nc = nc
        self.st = st
        self.S = Sched(nc, st)
        self.ps = st.enter_context(nc.psum_tensor("ps", [128, 8 * 512], F32))
        self.psd = deps(8)
        self.uid = 0
        self.out_toks = []

    def release(self):
        _BASE["r"] = self.S.snapshot()

    def sb(self, stack, shape, dt):
        self.uid += 1
        return stack.enter_context(self.nc.sbuf_tensor(f"t{self.uid}", list(shape), dt))

    def pb(self, i, n=512, off=0):
        return self.ps[:, i * 512 + off:i * 512 + off + n]

    def pbh(self, i):
        return self.ps[:, i * 512:(i + 1) * 512].bitcast(BF16)

    def T(self, fn, r=(), w=()):
        return self.S.op("tensor", fn, r, w)

    def V(self, fn, r=(), w=()):
        return self.S.op("vector", fn, r, w)

    def A(self, fn, r=(), w=()):
        return self.S.op("scalar", fn, r, w)

    def G(self, fn, r=(), w=()):
        return self.S.op("gpsimd", fn, r, w)

    def D(self, fn, r=(), w=()):
        return self.S.op("sync", fn, r, w, dma=True)

    def DG(self, fn, r=(), w=()):
        return self.S.op("gpsimd", fn, r, w, dma=True)

    def mm(self, out, lhsT, rhs, start, stop, r=(), w=()):
        return self.T(lambda e: e.matmul(out, lhsT=lhsT, rhs=rhs, start=start, stop=stop), r, w)

    def tr(self, out, in_, ident, r=(), w=()):
        return self.T(lambda e: e.transpose(out=out, in_=in_, identity=ident), r, w)

    def act(self, out, in_, func, r=(), w=(), **kw):
        return self.A(lambda e: e.activation(out=out, in_=in_, func=func, **kw), r, w)

    def dbg(self, name, ap, shape, dt, r):
        if not DEBUG:
            return
        d = self.nc.dram_tensor("dbg_" + name, list(shape), dt, kind="ExternalOutput").ap()
        tok = self.D(lambda e: e.dma_start(out=d, in_=ap), r=r)
        self.out_toks.append(tok)

    def load_w(self, dst, src2d, dep):
        n = src2d.shape[1]
        tok = None
        for c0 in range(0, n, 512):
            c1 = min(n, c0 + 512)
            assert (c1 - c0) in (16, 32, 64, 128, 256, 512), (c0, c1)
            tok = self.DG(lambda e, c0=c0, c1=c1: e.dma_start(
                out=dst[:, :, c0:c1], in_=src2d[:, c0:c1].rearrange("(c p) n -> p c n", p=128)), w=[dep])
        return tok

    def load_bc(self, dst, src1d, dep, n=128):
        return self.DG(lambda e: e.dma_start(out=dst, in_=src1d.partition_broadcast(n)), w=[dep])

    def load_col(self, dst, src1d, dep, nchunks):
        return self.DG(lambda e: e.dma_start(out=dst, in_=src1d.rearrange("(c p) -> p c", p=128),
                                             allow_slow_non_contiguous=True), w=[dep])


def rstd_ops(C, rs_ap, ss_ap, n, r, w):
    C.act(rs_ap, ss_ap, AF.Ln, r=r, w=w, scale=1.0 / n, bias=EPS)
    C.act(rs_ap, rs_ap, AF.Exp, r=w, w=w, scale=-0.5)


def norm_T(C, stk, h_ap, g_row, uT, d_uT, d_h, banks=(0, 1)):
    NH, NX = 4, 3
    with ExitStack() as s:
        gt = C.sb(s, [128, D_], F32); d_gt = Dep()
        C.load_bc(gt[:], g_row, d_gt)
        hb = [C.sb(s, [128, D_], F32) for _ in range(NH)]; d_hb = deps(NH)
        junk = C.sb(s, [128, D_], BF16); d_junk = Dep()
        ss = C.sb(s, [128, NT], F32); rs = C.sb(s, [128, NT], F32)
        d_ss = deps(NT); d_rs = deps(NT)
        xn = [C.sb(s, [128, D_], BF16) for _ in range(NX)]; d_xn = deps(NX)

        def load(t):
            b = t % NH
            C.DG(lambda e, t=t, b=b: e.dma_start(out=hb[b][:], in_=h_ap[t * 128:(t + 1) * 128, :]),
                 r=[d_h[t]], w=[d_hb[b]])

        for t in range(NH - 1):
            load(t)
        for t in range(NT):
            b = t % NH
            x = t % NX
            if t + NH - 1 < NT:
                load(t + NH - 1)
            C.act(junk[:], hb[b][:], AF.Square, r=[d_hb[b]], w=[d_junk, d_ss[t]], accum_out=ss[:, t:t + 1])
            rstd_ops(C, rs[:, t:t + 1], ss[:, t:t + 1], D_, [d_ss[t]], [d_rs[t]])
            C.act(hb[b][:], hb[b][:], AF.Copy, r=[d_hb[b], d_rs[t]], w=[d_hb[b]], scale=rs[:, t:t + 1])
            C.V(lambda e, b=b, x=x: e.tensor_tensor(out=xn[x][:], in0=hb[b][:], in1=gt[:], op=ALU.mult),
                r=[d_hb[b], d_gt], w=[d_xn[x]])
            bk = banks[t % len(banks)]
            pv = C.pbh(bk)
            for c in range(8):
                C.tr(pv[:, c * 128:(c + 1) * 128], xn[x][:, c * 128:(c + 1) * 128], C.ident_bf[:],
                     r=[d_xn[x], C.d_const], w=[C.psd[bk]])
            C.V(lambda e, t=t, pv=pv: e.tensor_copy(out=uT[:, :, t * 128:(t + 1) * 128],
                                                   in_=pv.rearrange("p (c n) -> p c n", c=8)),
                r=[C.psd[bk]], w=[d_uT[t]])
        C.release()


def out_proj(C, goT, d_go, nk, w_out, h_in, d_hin, h_out, d_hout, banks=(0, 1, 2, 3), wo_pre=None):
    with ExitStack() as s:
        if wo_pre is None:
            wo = C.sb(s, [128, nk, D_], BF16); d_wo = Dep()
            half = nk // 2
            C.load_w(wo[:, 0:half, :], w_out[0:half * 128, :], d_wo)
            C.load_w(wo[:, half:nk, :], w_out[half * 128:nk * 128, :], d_wo)
        else:
            wo, d_wo = wo_pre
        NHB = 3
        hb = [C.sb(s, [128, D_], F32) for _ in range(NHB)]; d_hb = deps(NHB)
        ho = [C.sb(s, [128, D_], F32) for _ in range(2)]; d_ho = deps(2)
        i = 0

        def load_h(t):
            bb = t % NHB
            C.DG(lambda e, t=t, bb=bb: e.dma_start(out=hb[bb][:], in_=h_in[t * 128:(t + 1) * 128, :]),
                 r=[d_hin[t]], w=[d_hb[bb]])

        for t in range(min(NHB - 1, NT)):
            load_h(t)
        for t in range(NT):
            b = t % 2
            bb = t % NHB
            if t + NHB - 1 < NT:
                load_h(t + NHB - 1)
            for nb in range(2):
                bk = banks[i % len(banks)]; i += 1
                for c in range(nk):
                    C.mm(C.pb(bk), goT[:, c, t * 128:(t + 1) * 128], wo[:, c, nb * 512:(nb + 1) * 512],
                         c == 0, c == nk - 1, r=[d_go[t], d_wo], w=[C.psd[bk]])
                C.V(lambda e, b=b, bb=bb, nb=nb, bk=bk: e.tensor_tensor(out=ho[b][:, nb * 512:(nb + 1) * 512], in0=C.pb(bk),
                                                                      in1=hb[bb][:, nb * 512:(nb + 1) * 512], op=ALU.add),
                    r=[C.psd[bk], d_hb[bb]], w=[d_ho[b]])
            tok = C.D(lambda e, t=t, b=b: e.dma_start(out=h_out[t * 128:(t + 1) * 128, :], in_=ho[b][:]),
                      r=[d_ho[b]], w=[d_hout[t]])
            C.last_store = tok
        C.release()


def final_norm(C, h_ap, d_h, g_row, out_ap):
    NH = 4
    with ExitStack() as s:
        gt = C.sb(s, [128, D_], F32); d_gt = Dep()
        C.load_bc(gt[:], g_row, d_gt)
        hb = [C.sb(s, [128, D_], F32) for _ in range(NH)]; d_hb = deps(NH)
        junk = C.sb(s, [128, D_], BF16); d_junk = Dep()
        ss = C.sb(s, [128, NT], F32); rs = C.sb(s, [128, NT], F32)
        d_ss = deps(NT); d_rs = deps(NT)
        ob = [C.sb(s, [128, D_], F32) for _ in range(3)]; d_ob = deps(3)

        def load(t):
            b = t % NH
            C.DG(lambda e, t=t, b=b: e.dma_start(out=hb[b][:], in_=h_ap[t * 128:(t + 1) * 128, :]),
                 r=[d_h[t]], w=[d_hb[b]])

        for t in range(NH - 1):
            load(t)
        for t in range(NT):
            b = t % NH
            o = t % 3
            if t + NH - 1 < NT:
                load(t + NH - 1)
            C.act(junk[:], hb[b][:], AF.Square, r=[d_hb[b]], w=[d_junk, d_ss[t]], accum_out=ss[:, t:t + 1])
            rstd_ops(C, rs[:, t:t + 1], ss[:, t:t + 1], D_, [d_ss[t]], [d_rs[t]])
            C.act(hb[b][:], hb[b][:], AF.Copy, r=[d_hb[b], d_rs[t]], w=[d_hb[b]], scale=rs[:, t:t + 1])
            C.V(lambda e, b=b, o=o: e.tensor_tensor(out=ob[o][:], in0=hb[b][:], in1=gt[:], op=ALU.mult),
                r=[d_hb[b], d_gt], w=[d_ob[o]])
            tok = C.D(lambda e, t=t, o=o: e.dma_start(out=out_ap[t * 128:(t + 1) * 128, :], in_=ob[o][:]),
                      r=[d_ob[o]])
            C.out_toks.append(tok)
        C.release()


def layer_lru(C, W, h_in, d_hin, h_out, d_hout):
    NCH = 10
    with ExitStack() as L:
        uT = C.sb(L, [128, 8, S_], BF16); d_uT = deps(NT)
        norm_T(C, L, h_in, W["norm_g"], uT, d_uT, d_hin)
        d_uTb = [[d_uT[4 * tb + i] for i in range(4)] for tb in range(4)]
        goT = C.sb(L, [128, NCH, S_], BF16); d_go = deps(NT)
        cw = C.sb(L, [128, 4, NCH], F32); d_cw = Dep()
        for k in range(4):
            C.load_col(cw[:, k, :], W["lru_conv_w"][k, :], d_cw, NCH)
        cb = C.sb(L, [128, NCH], F32); ba = C.sb(L, [128, NCH], F32); bx = C.sb(L, [128, NCH], F32)
        lam = C.sb(L, [128, NCH], F32); d_p = Dep()
        C.load_col(cb[:], W["lru_conv_b"], d_p, NCH)
        C.load_col(ba[:], W["lru_b_a"], d_p, NCH)
        C.load_col(bx[:], W["lru_b_x"], d_p, NCH)
        C.load_col(lam[:], W["lru_lam"], d_p, NCH)
        cA = C.sb(L, [128, NCH], F32); cA2 = C.sb(L, [128, NCH], F32); d_cA = Dep()
        C.act(cA[:], lam[:], AF.Exp, r=[d_p], w=[d_cA], scale=-1.0)
        C.act(cA[:], cA[:], AF.Ln, r=[d_cA], w=[d_cA], bias=1.0)
        C.V(lambda e: e.tensor_scalar(out=cA2[:], in0=cA[:], scalar1=-16.0, scalar2=None, op0=ALU.mult),
            r=[d_cA], w=[d_cA])
        C.V(lambda e: e.tensor_scalar(out=cA[:], in0=cA[:], scalar1=-8.0, scalar2=None, op0=ALU.mult),
            r=[d_cA], w=[d_cA])
        wa = C.sb(L, [128, NCH, 128], BF16); wx = C.sb(L, [128, NCH, 128], BF16); d_wg = Dep()
        C.DG(lambda e: e.dma_start(out=wa[:], in_=W["lru_w_a"].rearrange("n i j -> i n j")), w=[d_wg])
        C.DG(lambda e: e.dma_start(out=wx[:], in_=W["lru_w_x"].rearrange("n i j -> i n j")), w=[d_wg])
        dg = C.sb(L, [128, NCH, 4, 128], BF16); d_dg = Dep()
        for c in range(NCH):
            for k in range(4):
                C.V(lambda e, c=c, k=k: e.tensor_scalar(out=dg[:, c, k, :], in0=C.ident_bf[:],
                                                        scalar1=cw[:, k, c:c + 1], scalar2=None, op0=ALU.mult),
                    r=[d_cw, C.d_const], w=[d_dg])
        wu = [C.sb(L, [128, 8, 128], BF16) for _ in range(2)]; d_wu = deps(2)
        wg = [C.sb(L, [128, 8, 128], BF16) for _ in range(2)]; d_wgt = deps(2)
        ub2 = [C.sb(L, [128, 3 + S_], BF16) for _ in range(2)]; d_ub2 = deps(2)
        for b_ in range(2):
            C.V(lambda e, b_=b_: e.memset(ub2[b_][:, 0:3], 0.0), w=[d_ub2[b_]])
        uc = C.sb(L, [128, S_], F32); d_uc = Dep()
        ucb = C.sb(L, [128, S_], BF16); d_ucb = Dep()
        rr = C.sb(L, [128, S_], F32); d_rr = Dep()
        ig = C.sb(L, [128, S_], F32); d_ig = Dep()
        tmp = C.sb(L, [128, S_], F32); d_tmp = Dep()
        hs = C.sb(L, [128, S_], BF16); d_hs = Dep()
        sg = C.sb(L, [128, S_], BF16); d_sg = Dep()
        w_in = W["lru_w_in"]

        def proj_u(c):
            b = c % 2
            C.load_w(wu[b][:], w_in[:, 1280 + c * 128:1280 + (c + 1) * 128], d_wu[b])
            C.load_w(wg[b][:], w_in[:, c * 128:(c + 1) * 128], d_wgt[b])
            for tb in range(4):
                bk = tb
                for kc in range(8):
                    C.mm(C.pb(bk), wu[b][:, kc, :], uT[:, kc, tb * 512:(tb + 1) * 512], kc == 0, kc == 7,
                         r=d_uTb[tb] + [d_wu[b]], w=[C.psd[bk]])
                C.act(ub2[b][:, 3 + tb * 512:3 + (tb + 1) * 512], C.pb(bk), AF.Copy, r=[C.psd[bk]], w=[d_ub2[b]])

        proj_u(0)
        for c in range(NCH):
            b = c % 2
            for tb in range(4):
                bk = 4 + tb
                for k in range(4):
                    C.mm(C.pb(bk), dg[:, c, k, :], ub2[b][:, tb * 512 + k:tb * 512 + k + 512], k == 0, k == 3,
                         r=[d_ub2[b], d_dg], w=[C.psd[bk]])
                C.act(uc[:, tb * 512:(tb + 1) * 512], C.pb(bk), AF.Identity, r=[C.psd[bk], d_p], w=[d_uc],
                      bias=cb[:, c:c + 1])
            C.V(lambda e: e.tensor_copy(out=ucb[:], in_=uc[:]), r=[d_uc], w=[d_ucb])
            for tb in range(4):
                bk = tb
                for kc in range(8):
                    C.mm(C.pb(bk), wg[b][:, kc, :], uT[:, kc, tb * 512:(tb + 1) * 512], kc == 0, kc == 7,
                         r=d_uTb[tb] + [d_wgt[b]], w=[C.psd[bk]])
                C.act(sg[:, tb * 512:(tb + 1) * 512], C.pb(bk), AF.Silu, r=[C.psd[bk]], w=[d_sg])
            for tb in range(4):
                bk = 4 + tb
                C.mm(C.pb(bk), wa[:, c, :], ucb[:, tb * 512:(tb + 1) * 512], True, True,
                     r=[d_ucb, d_wg], w=[C.psd[bk]])
                C.act(rr[:, tb * 512:(tb + 1) * 512], C.pb(bk), AF.Sigmoid, r=[C.psd[bk], d_p], w=[d_rr],
                      bias=ba[:, c:c + 1])
            for tb in range(4):
                bk = 4 + tb
                C.mm(C.pb(bk), wx[:, c, :], ucb[:, tb * 512:(tb + 1) * 512], True, True,
                     r=[d_ucb, d_wg], w=[C.psd[bk]])
                C.act(ig[:, tb * 512:(tb + 1) * 512], C.pb(bk), AF.Sigmoid, r=[C.psd[bk], d_p], w=[d_ig],
                      bias=bx[:, c:c + 1])
            if c + 1 < NCH:
                proj_u(c + 1)
            C.act(tmp[:], rr[:], AF.Exp, r=[d_rr, d_cA], w=[d_tmp], scale=cA2[:, c:c + 1])
            C.act(rr[:], rr[:], AF.Exp, r=[d_rr, d_cA], w=[d_rr], scale=cA[:, c:c + 1])
            C.act(tmp[:], tmp[:], AF.Sqrt, r=[d_tmp], w=[d_tmp], scale=-1.0, bias=1.0)
            C.V(lambda e: e.tensor_tensor(out=ig[:], in0=ig[:], in1=uc[:], op=ALU.mult), r=[d_ig, d_uc], w=[d_ig])
            C.V(lambda e: e.tensor_tensor(out=tmp[:], in0=tmp[:], in1=ig[:], op=ALU.mult), r=[d_tmp, d_ig], w=[d_tmp])
            C.V(lambda e: e.tensor_tensor_scan(out=hs[:], data0=rr[:], data1=tmp[:], initial=0.0,
                                               op0=ALU.mult, op1=ALU.add), r=[d_rr, d_tmp], w=[d_hs])
            C.G(lambda e, c=c: e.tensor_tensor(out=goT[:, c, :], in0=hs[:], in1=sg[:], op=ALU.mult),
                r=[d_hs, d_sg], w=d_go)
        out_proj(C, goT, d_go, NCH, W["lru_w_out"], h_in, d_hin, h_out, d_hout)
        C.release()


def layer_mla(C, W, h_in, d_hin, h_out, d_hout):
    H = 16
    PI = float(np.pi)
    with ExitStack() as L:
        cqnT = C.sb(L, [128, 3, S_], BF16); d_cq = deps(NT)
        ckvT = C.sb(L, [128, 2, S_], BF16); d_ckv = deps(NT)
        sgT = C.sb(L, [128, 8, S_], BF16); d_sg4 = deps(4)
        kr_all = C.sb(L, [128, NT, 32], F32); d_kr = Dep()
        w_in = W["mla_w_in"]
        with ExitStack() as PA:
            uT = C.sb(PA, [128, 8, S_], BF16); d_uT = deps(NT)
            norm_T(C, PA, h_in, W["norm_g"], uT, d_uT, d_hin)
            d_uTb = [[d_uT[4 * tb + i] for i in range(4)] for tb in range(4)]
            w1 = C.sb(PA, [128, 8, 768], BF16); d_w1 = Dep()
            C.load_w(w1[:], w_in[:, 0:768], d_w1)
            gq = C.sb(PA, [128, 384], F32); gkv = C.sb(PA, [128, 256], F32); d_g = Dep()
            C.load_bc(gq[:], W["mla_g_q"], d_g)
            C.load_bc(gkv[:], W["mla_g_kv"], d_g)
            ctok = [C.sb(PA, [128, 672], F32) for _ in range(2)]; d_ct = deps(2)
            ss2 = C.sb(PA, [128, NT, 2], F32); rs2 = C.sb(PA, [128, NT, 2], F32)
            d_ss = deps(NT); d_rs = deps(NT)
            junk = C.sb(PA, [128, 384], BF16); d_junk = Dep()
            cn = [C.sb(PA, [128, 640], BF16) for _ in range(2)]; d_cn = deps(2)
            wgt = [C.sb(PA, [128, 8, 512], BF16) for _ in range(2)]; d_wgt = deps(2)
            for hf in range(2):
                C.load_w(wgt[hf][:], w_in[:, 672 + hf * 512:672 + (hf + 1) * 512], d_wgt[hf])
            for t in range(NT):
                b = t % 2
                for (c0, c1, bk) in ((0, 512, 2), (512, 672, 3)):
                    for kc in range(8):
                        C.mm(C.pb(bk, c1 - c0), uT[:, kc, t * 128:(t + 1) * 128], w1[:, kc, c0:c1], kc == 0, kc == 7,
                             r=[d_uT[t], d_w1], w=[C.psd[bk]])
                    C.act(ctok[b][:, c0:c1], C.pb(bk, c1 - c0), AF.Copy, r=[C.psd[bk]], w=[d_ct[b]])
                C.act(junk[:, 0:384], ctok[b][:, 0:384], AF.Square, r=[d_ct[b]], w=[d_junk, d_ss[t]],
                      accum_out=ss2[:, t, 0:1])
                C.act(junk[:, 0:256], ctok[b][:, 384:640], AF.Square, r=[d_ct[b]], w=[d_junk, d_ss[t]],
                      accum_out=ss2[:, t, 1:2])
                rstd_ops(C, rs2[:, t, 0:1], ss2[:, t, 0:1], 384, [d_ss[t]], [d_rs[t]])
                rstd_ops(C, rs2[:, t, 1:2], ss2[:, t, 1:2], 256, [d_ss[t]], [d_rs[t]])
                C.V(lambda e, t=t, b=b: e.scalar_tensor_tensor(out=cn[b][:, 0:384], in0=ctok[b][:, 0:384],
                                                               scalar=rs2[:, t, 0:1], in1=gq[:], op0=ALU.mult, op1=ALU.mult),
                    r=[d_ct[b], d_rs[t], d_g], w=[d_cn[b]])
                C.V(lambda e, t=t, b=b: e.scalar_tensor_tensor(out=cn[b][:, 384:640], in0=ctok[b][:, 384:640],
                                                               scalar=rs2[:, t, 1:2], in1=gkv[:], op0=ALU.mult, op1=ALU.mult),
                    r=[d_ct[b], d_rs[t], d_g], w=[d_cn[b]])
                C.G(lambda e, t=t, b=b: e.tensor_copy(out=kr_all[:, t, :], in_=ctok[b][:, 640:672]),
                    r=[d_ct[b]], w=[d_kr])
                bk = 4 + b
                pv = C.pbh(bk)
                for c in range(5):
                    C.tr(pv[:, c * 128:(c + 1) * 128], cn[b][:, c * 128:(c + 1) * 128], C.ident_bf[:],
                         r=[d_cn[b], C.d_const], w=[C.psd[bk]])
                C.V(lambda e, t=t, pv=pv: e.tensor_copy(out=cqnT[:, :, t * 128:(t + 1) * 128],
                                                       in_=pv[:, 0:384].rearrange("p (c n) -> p c n", c=3)),
                    r=[C.psd[bk]], w=[d_cq[t]])
                C.V(lambda e, t=t, pv=pv: e.tensor_copy(out=ckvT[:, :, t * 128:(t + 1) * 128],
                                                       in_=pv[:, 384:640].rearrange("p (c n) -> p c n", c=2)),
                    r=[C.psd[bk]], w=[d_ckv[t]])
            C.dbg("ctok", ctok[1][:], [128, 672], F32, [d_ct[1]])
            C.dbg("w1", w1[:], [128, 8, 768], BF16, [d_w1])
            C.dbg("cn", cn[1][:], [128, 640], BF16, [d_cn[1]])
            C.dbg("rs2", rs2[:], [128, NT, 2], F32, d_rs)
            i = 0
            for c in range(8):
                for tb in range(4):
                    bk = (6, 7, 0, 1)[i % 4]; i += 1
                    for kc in range(8):
                        C.mm(C.pb(bk), wgt[c // 4][:, kc, (c % 4) * 128:(c % 4 + 1) * 128],
                             uT[:, kc, tb * 512:(tb + 1) * 512], kc == 0, kc == 7,
                             r=d_uTb[tb] + [d_wgt[c // 4]], w=[C.psd[bk]])
                    C.act(sgT[:, c, tb * 512:(tb + 1) * 512], C.pb(bk), AF.Silu, r=[C.psd[bk]], w=[d_sg4[tb]])
            C.release()
        C.dbg("cqnT", cqnT[:], [128, 3, S_], BF16, d_cq)
        C.dbg("ckvT", ckvT[:], [128, 2, S_], BF16, d_ckv)
        C.dbg("sgT", sgT[:], [128, 8, S_], BF16, d_sg4)
        C.dbg("kr", kr_all[:], [128, NT, 32], F32, [d_kr])
        with ExitStack() as PB:
            wuq = C.sb(PB, [128, 3, 1536], BF16); d_wuq = Dep()
            C.load_w(wuq[:], W["mla_w_uq"], d_wuq)
            wukv = C.sb(PB, [128, 2, 2048], BF16); d_wukv = Dep()
            C.load_w(wukv[:], W["mla_w_ukv"], d_wukv)
            wo = C.sb(PB, [128, 8, D_], BF16); d_wo = Dep()
            C.load_w(wo[:, 0:4, :], W["mla_w_out"][0:512, :], d_wo)
            C.load_w(wo[:, 4:8, :], W["mla_w_out"][512:1024, :], d_wo)
            tri = C.sb(PB, [128, 128], BF16); ones64 = C.sb(PB, [128, 64], BF16)
            invf = C.sb(PB, [128, 16], F32); posi = C.sb(PB, [128, NT], I32); d_k = Dep()
            C.D(lambda e: e.dma_start(out=tri[:], in_=C.CD["tri_bf"]), w=[d_k])
            C.D(lambda e: e.dma_start(out=ones64[:], in_=C.CD["ones_bf"]), w=[d_k])
            C.D(lambda e: e.dma_start(out=invf[:], in_=C.CD["invf"]), w=[d_k])
            C.D(lambda e: e.dma_start(out=posi[:], in_=C.pos), w=[d_k])
            posf = C.sb(PB, [128, NT], F32); d_t = Dep()
            ang = C.sb(PB, [128, NT, 16], F32); tmpf = C.sb(PB, [128, NT, 16], F32)
            ki = C.sb(PB, [128, NT, 16], I32); kff = C.sb(PB, [128, NT, 16], F32)
            rr = C.sb(PB, [128, NT, 16], F32); yy = C.sb(PB, [128, NT, 16], F32); mm_ = C.sb(PB, [128, NT, 16], F32)
            cos_t = C.sb(PB, [128, NT, 16], F32); sin_t = C.sb(PB, [128, NT, 16], F32); d_cs = Dep()
            C.V(lambda e: e.tensor_copy(out=posf[:], in_=posi[:]), r=[d_k], w=[d_t])
            C.V(lambda e: e.tensor_tensor(out=ang[:], in0=posf[:].unsqueeze(2).to_broadcast([128, NT, 16]),
                                          in1=invf[:].unsqueeze(1).to_broadcast([128, NT, 16]), op=ALU.mult),
                r=[d_t, d_k], w=[d_t])
            C.V(lambda e: e.tensor_scalar(out=tmpf[:], in0=ang[:], scalar1=1.0 / (2 * PI), scalar2=None, op0=ALU.mult),
                r=[d_t], w=[d_t])
            C.V(lambda e: e.tensor_copy(out=ki[:], in_=tmpf[:]), r=[d_t], w=[d_t])
            C.V(lambda e: e.tensor_copy(out=kff[:], in_=ki[:]), r=[d_t], w=[d_t])
            C1 = 6.28125
            C2 = float(2 * np.pi - 6.28125)
            C.V(lambda e: e.scalar_tensor_tensor(out=rr[:], in0=kff[:], scalar=-C1, in1=ang[:], op0=ALU.mult, op1=ALU.add),
                r=[d_t], w=[d_t])
            C.V(lambda e: e.scalar_tensor_tensor(out=rr[:], in0=kff[:], scalar=-C2, in1=rr[:], op0=ALU.mult, op1=ALU.add),
                r=[d_t], w=[d_t])
            for dst, shift in ((sin_t, 0.0), (cos_t, PI / 2)):
                C.V(lambda e, shift=shift: e.tensor_scalar(out=yy[:], in0=rr[:], scalar1=shift, scalar2=None, op0=ALU.add),
                    r=[d_t], w=[d_t])
                C.V(lambda e: e.tensor_scalar(out=mm_[:], in0=yy[:], scalar1=PI, scalar2=-2 * PI, op0=ALU.is_gt, op1=ALU.mult),
                    r=[d_t], w=[d_t])
                C.V(lambda e: e.tensor_tensor(out=yy[:], in0=yy[:], in1=mm_[:], op=ALU.add), r=[d_t], w=[d_t])
                C.V(lambda e: e.tensor_scalar(out=mm_[:], in0=yy[:], scalar1=-PI, scalar2=2 * PI, op0=ALU.is_lt, op1=ALU.mult),
                    r=[d_t], w=[d_t])
                C.V(lambda e: e.tensor_tensor(out=yy[:], in0=yy[:], in1=mm_[:], op=ALU.add), r=[d_t], w=[d_t])
                C.act(dst[:], yy[:], AF.Sin, r=[d_t], w=[d_cs, d_t])
            qT = [C.sb(PB, [128, S_], BF16) for _ in range(2)]; d_qT = deps(2)
            kT = [C.sb(PB, [128, S_], BF16) for _ in range(2)]; d_kT = deps(2)
            Vh = [C.sb(PB, [128, NT, 128], BF16) for _ in range(2)]; d_V = deps(2)
            for b_ in range(2):
                C.G(lambda e, b_=b_: e.memset(Vh[b_][:], 1.0), w=[d_V[b_]])
            swp = C.sb(PB, [128, 128], F32)
            C.D(lambda e: e.dma_start(out=swp[:], in_=C.CD["swap_f"]), w=[d_k])
            for b in range(2):
                C.G(lambda e, b=b: e.memset(qT[b][:], 0.0), w=[d_qT[b]])
                C.G(lambda e, b=b: e.memset(kT[b][:], 0.0), w=[d_kT[b]])
            qtok = C.sb(PB, [128, NT, 96], F32); d_qtok = Dep()
            ta = C.sb(PB, [128, NT, 16], F32); tb_ = C.sb(PB, [128, NT, 16], F32)
            tc = C.sb(PB, [128, NT, 16], F32); td = C.sb(PB, [128, NT, 16], F32); d_tt = Dep()
            krr = C.sb(PB, [128, NT, 96], F32); d_krr = Dep()
            C.G(lambda e: e.memset(krr[:], 0.0), w=[d_krr])
            k1 = kr_all[:, :, 0:16]; k2 = kr_all[:, :, 16:32]
            C.V(lambda e: e.tensor_tensor(out=ta[:], in0=k1, in1=cos_t[:], op=ALU.mult), r=[d_kr, d_cs], w=[d_tt])
            C.V(lambda e: e.tensor_tensor(out=tb_[:], in0=k2, in1=sin_t[:], op=ALU.mult), r=[d_kr, d_cs], w=[d_tt])
            C.V(lambda e: e.tensor_tensor(out=krr[:, :, 64:80], in0=ta[:], in1=tb_[:], op=ALU.subtract), r=[d_tt], w=[d_krr])
            C.V(lambda e: e.tensor_tensor(out=tc[:], in0=k2, in1=cos_t[:], op=ALU.mult), r=[d_kr, d_cs], w=[d_tt])
            C.V(lambda e: e.tensor_tensor(out=td[:], in0=k1, in1=sin_t[:], op=ALU.mult), r=[d_kr, d_cs], w=[d_tt])
            C.V(lambda e: e.tensor_tensor(out=krr[:, :, 80:96], in0=tc[:], in1=td[:], op=ALU.add), r=[d_tt], w=[d_krr])
            for t4 in range(4):
                bk = 7
                for i in range(4):
                    C.tr(C.pb(bk)[0:96, i * 128:(i + 1) * 128], krr[:, t4 * 4 + i, :], C.ident_f[:],
                         r=[d_krr, C.d_const], w=[C.psd[bk]])
                for b in range(2):
                    C.V(lambda e, b=b, t4=t4, bk=bk: e.tensor_copy(out=kT[b][64:96, t4 * 512:(t4 + 1) * 512],
                                                                 in_=C.pb(bk)[64:96, :]),
                        r=[C.psd[bk]], w=[d_kT[b]])
            lnd = [C.sb(PB, [128, 512], F32) for _ in range(2)]; rden = [C.sb(PB, [128, 512], F32) for _ in range(2)]
            ot = [C.sb(PB, [128, 512], F32) for _ in range(2)]
            d_nrm = deps(2)
            for b_ in range(2):
                C.G(lambda e, b_=b_: e.memset(rden[b_][:], 1.0), w=[d_nrm[b_]])
            NPT = 3
            SB = (0, 1, 2)
            PT = [C.sb(PB, [128, 512], BF16) for _ in range(NPT)]; d_PT = deps(NPT)
            d_ot = deps(2)
            QS = 96 ** -0.5

            pcnt = [0]

            def proj_units(h, b):
                for t4 in range(4):
                    bk = 6 + pcnt[0] % 2; pcnt[0] += 1
                    for i in range(4):
                        t = t4 * 4 + i
                        for kc in range(3):
                            C.mm(C.pb(bk)[:, i * 96:(i + 1) * 96], cqnT[:, kc, t * 128:(t + 1) * 128],
                                 wuq[:, kc, h * 96:(h + 1) * 96], kc == 0, kc == 2,
                                 r=[d_cq[t], d_wuq], w=[C.psd[bk]])
                    C.V(lambda e, t4=t4, bk=bk: e.tensor_scalar(
                        out=qtok[:, t4 * 4:(t4 + 1) * 4, :], in0=C.pb(bk)[:, 0:384].rearrange("p (a c) -> p a c", a=4),
                        scalar1=QS, scalar2=None, op0=ALU.mult), r=[C.psd[bk]], w=[d_qtok])
                    yield
                q1 = qtok[:, :, 64:80]; q2 = qtok[:, :, 80:96]
                C.V(lambda e: e.tensor_tensor(out=ta[:], in0=q1, in1=cos_t[:], op=ALU.mult), r=[d_qtok, d_cs], w=[d_tt])
                C.V(lambda e: e.tensor_tensor(out=tb_[:], in0=q2, in1=sin_t[:], op=ALU.mult), r=[d_qtok, d_cs], w=[d_tt])
                C.V(lambda e: e.tensor_tensor(out=tc[:], in0=q2, in1=cos_t[:], op=ALU.mult), r=[d_qtok, d_cs], w=[d_tt])
                C.V(lambda e: e.tensor_tensor(out=td[:], in0=q1, in1=sin_t[:], op=ALU.mult), r=[d_qtok, d_cs], w=[d_tt])
                C.V(lambda e: e.tensor_tensor(out=q1, in0=ta[:], in1=tb_[:], op=ALU.subtract), r=[d_tt], w=[d_qtok])
                C.V(lambda e: e.tensor_tensor(out=q2, in0=tc[:], in1=td[:], op=ALU.add), r=[d_tt], w=[d_qtok])
                yield
                for t4 in range(4):
                    bk = 6 + pcnt[0] % 2; pcnt[0] += 1
                    for i in range(4):
                        C.tr(C.pb(bk)[0:96, i * 128:(i + 1) * 128], qtok[:, t4 * 4 + i, :], C.ident_f[:],
                             r=[d_qtok, C.d_const], w=[C.psd[bk]])
                    C.V(lambda e, t4=t4, bk=bk: e.tensor_copy(out=qT[b][0:96, t4 * 512:(t4 + 1) * 512],
                                                            in_=C.pb(bk)[0:96, :]), r=[C.psd[bk]], w=[d_qT[b]])
                    yield
                for tb in range(4):
                    bk = 6 + pcnt[0] % 2; pcnt[0] += 1
                    for kc in range(2):
                        C.mm(C.pb(bk)[0:64, :], wukv[:, kc, h * 128:h * 128 + 64], ckvT[:, kc, tb * 512:(tb + 1) * 512],
                             kc == 0, kc == 1, r=[d_ckv[4 * tb + i] for i in range(4)] + [d_wukv], w=[C.psd[bk]])
                    C.V(lambda e, tb=tb, bk=bk: e.tensor_copy(out=kT[b][0:64, tb * 512:(tb + 1) * 512],
                                                            in_=C.pb(bk)[0:64, :]), r=[C.psd[bk]], w=[d_kT[b]])
                    yield
                for t8 in range(2):
                    bk = 6 + pcnt[0] % 2; pcnt[0] += 1
                    for i in range(8):
                        t = t8 * 8 + i
                        for kc in range(2):
                            C.mm(C.pb(bk)[:, i * 64:(i + 1) * 64], ckvT[:, kc, t * 128:(t + 1) * 128],
                                 wukv[:, kc, h * 128 + 64:h * 128 + 128], kc == 0, kc == 1,
                                 r=[d_ckv[t], d_wukv], w=[C.psd[bk]])
                    C.V(lambda e, t8=t8, bk=bk: e.tensor_copy(out=Vh[b][:, t8 * 8:(t8 + 1) * 8, b * 64:b * 64 + 64],
                                                            in_=C.pb(bk).rearrange("p (a c) -> p a c", a=8)),
                        r=[C.psd[bk]], w=[d_V[b]])
                    yield

            items = []
            for h in range(H):
                for qc in range(4):
                    nj = 4 * qc + 4
                    for j in range(nj):
                        items.append((h, qc, j, nj))

            def stage_a(i):
                h, qc, j, nj = items[i]
                b = h % 2
                col0 = max(0, j * 128 - qc * 512)
                bS = SB[i % NPT]
                pt = PT[i % NPT]; d_pt = d_PT[i % NPT]
                C.mm(C.pb(bS)[:, col0:512], kT[b][:, j * 128:(j + 1) * 128],
                     qT[b][:, qc * 512 + col0:(qc + 1) * 512], True, True,
                     r=[d_kT[b], d_qT[b]], w=[C.psd[bS]])
                C.act(pt[:, col0:512], C.pb(bS)[:, col0:512], AF.Exp, r=[C.psd[bS]], w=[d_pt])
                if j >= 4 * qc:
                    C.G(lambda e, pt=pt, col0=col0: e.tensor_tensor(out=pt[:, col0:col0 + 128],
                                                                  in0=pt[:, col0:col0 + 128], in1=tri[:], op=ALU.mult),
                        r=[d_pt, d_k], w=[d_pt])

            def stage_b(i):
                h, qc, j, nj = items[i]
                b = h % 2
                r0 = (h % 2) * 64
                rd = 64 - r0
                c = h // 2
                n = h * 4 + qc
                bO = 3 + n % 2
                col0 = max(0, j * 128 - qc * 512)
                pt = PT[i % NPT]; d_pt = d_PT[i % NPT]
                C.mm(C.pb(bO)[:, col0:512], Vh[b][:, j, :], pt[:, col0:512], j == 0, j == nj - 1,
                     r=[d_V[b], d_pt], w=[C.psd[bO]])
                if j == nj - 1:
                    nb = n % 2
                    bW = 5
                    rs = slice(r0, r0 + 64); rq = slice(rd, rd + 64)
                    C.act(lnd[nb][rq, :], C.pb(bO)[rq, :], AF.Ln, r=[C.psd[bO]], w=[d_nrm[nb]])
                    C.act(rden[nb][rq, :], lnd[nb][rq, :], AF.Exp, r=[d_nrm[nb]], w=[d_nrm[nb]], scale=-1.0)
                    def fin(rs=rs, rq=rq, rd=rd, bO=bO, bW=bW, nb=nb, c=c, qc=qc):
                        C.mm(C.pb(bW), swp[:], rden[nb][:], True, True,
                             r=[d_k, d_nrm[nb]], w=[C.psd[bW]])
                        C.V(lambda e: e.tensor_tensor(
                            out=ot[nb][rs, :], in0=C.pb(bO)[rs, :], in1=sgT[rs, c, qc * 512:(qc + 1) * 512], op=ALU.mult),
                            r=[C.psd[bO], d_sg4[qc]], w=[d_ot[nb]])
                        C.V(lambda e: e.tensor_tensor(
                            out=sgT[rs, c, qc * 512:(qc + 1) * 512], in0=ot[nb][rs, :], in1=C.pb(bW)[rs, :], op=ALU.mult),
                            r=[d_ot[nb], C.psd[bW]], w=[d_sg4[qc]])
                    pending.append([2, fin])

            pending = []

            def run_pending(force=False):
                for p in list(pending):
                    p[0] -= 1
                    if force or p[0] <= 0:
                        p[1]()
                        pending.remove(p)

            for _ in proj_units(0, 0):
                pass
            NI = len(items)
            units = None
            LOOK = 2
            for i in range(min(LOOK, NI)):
                stage_a(i)
            for i in range(NI):
                h, qc, j, nj = items[i]
                if qc == 0 and j == 0 and h + 1 < H:
                    units = proj_units(h + 1, (h + 1) % 2)
                if i + LOOK < NI:
                    if items[i + LOOK][0] != h and units is not None:
                        for _ in units:
                            pass
                        units = None
                    stage_a(i + LOOK)
                run_pending()
                stage_b(i)
                if units is not None:
                    if next(units, "done") == "done":
                        units = None
            run_pending(force=True)
            d_go = [d_sg4[t // 4] for t in range(NT)]
            out_proj(C, sgT, d_go, 8, None, h_in, d_hin, h_out, d_hout, banks=(0, 1, 2, 7), wo_pre=(wo, d_wo))
            C.release()
        C.release()


LAYER_FNS[0] = layer_mla

def layer_gla(C, W, h_in, d_hin, h_out, d_hout):
    w_in = W["gla_w_in"]
    with ExitStack() as L:
        qtT = C.sb(L, [128, 4, S_], BF16); d_qt = deps(4)
        ktT = C.sb(L, [128, 4, S_], BF16); d_kt = deps(4)
        vtok = C.sb(L, [128, NT, 1024], BF16); d_v = deps(NT)
        gsg = C.sb(L, [128, NT, 1024], BF16); d_gsg = deps(NT)
        elast = C.sb(L, [128, 4, 32], F32); d_el = Dep()
        with ExitStack() as PA:
            uT = C.sb(PA, [128, 8, S_], BF16); d_uT = deps(NT)
            norm_T(C, PA, h_in, W["norm_g"], uT, d_uT, d_hin)
            d_uTb = [[d_uT[4 * tb + i] for i in range(4)] for tb in range(4)]
            wgk = C.sb(PA, [128, 8, 16], BF16); d_wgk = Dep()
            C.load_w(wgk[:], w_in[:, 3072:3088], d_wgk)
            gkT = C.sb(PA, [32, S_], BF16); d_gkT = Dep()
            wgk2 = C.sb(PA, [32, 512], BF16); d_wgk2 = Dep()
            C.V(lambda e: e.memset(gkT[:], 0.0), w=[d_gkT])
            C.V(lambda e: e.memset(wgk2[:], 0.0), w=[d_wgk2])
            C.DG(lambda e: e.dma_start(out=wgk2[0:16, :], in_=W["gla_w_gk2"]), w=[d_wgk2])
            for tb in range(4):
                bk = 2 + tb % 2
                for kc in range(8):
                    C.mm(C.pb(bk)[0:16, :], wgk[:, kc, :], uT[:, kc, tb * 512:(tb + 1) * 512], kc == 0, kc == 7,
                         r=d_uTb[tb] + [d_wgk], w=[C.psd[bk]])
                C.V(lambda e, tb=tb, bk=bk: e.tensor_copy(out=gkT[0:16, tb * 512:(tb + 1) * 512], in_=C.pb(bk)[0:16, :]),
                    r=[C.psd[bk]], w=[d_gkT])
            nbgk = C.sb(PA, [128, 4], F32); d_nb = Dep()
            C.load_col(nbgk[:], W["gla_b_gk"], d_nb, 4)
            C.V(lambda e: e.tensor_scalar(out=nbgk[:], in0=nbgk[:], scalar1=-1.0, scalar2=None, op0=ALU.mult),
                r=[d_nb], w=[d_nb])
            rmask = C.sb(PA, [128, S_], BF16); d_rm = Dep()
            C.D(lambda e: e.dma_start(out=rmask[:], in_=C.CD["rmask"]), w=[d_rm])
            Lh = C.sb(PA, [128, S_], F32); d_Lh = Dep()
            Bp = C.sb(PA, [128, S_], F32); d_Bp = Dep()
            wq = [C.sb(PA, [128, 8, 128], BF16) for _ in range(2)]; d_wq = deps(2)
            wk = [C.sb(PA, [128, 8, 128], BF16) for _ in range(2)]; d_wk = deps(2)
            QS = 128 ** -0.5
            for h in range(4):
                b = h % 2
                C.load_w(wq[b][:], w_in[:, h * 128:(h + 1) * 128], d_wq[b])
                C.load_w(wk[b][:], w_in[:, 512 + h * 128:512 + (h + 1) * 128], d_wk[b])
                for tb in range(4):
                    bk = 2 + tb % 2
                    C.mm(C.pb(bk), wgk2[:, h * 128:(h + 1) * 128], gkT[:, tb * 512:(tb + 1) * 512], True, True,
                         r=[d_wgk2, d_gkT], w=[C.psd[bk]])
                    C.act(Lh[:, tb * 512:(tb + 1) * 512], C.pb(bk), AF.Exp, r=[C.psd[bk], d_nb], w=[d_Lh],
                          scale=-1.0, bias=nbgk[:, h:h + 1])
                C.act(Lh[:], Lh[:], AF.Ln, r=[d_Lh], w=[d_Lh], bias=1.0)
                C.V(lambda e: e.tensor_tensor_scan(out=Bp[:], data0=rmask[:], data1=Lh[:], initial=0.0,
                                                   op0=ALU.mult, op1=ALU.add), r=[d_Lh, d_rm], w=[d_Bp])
                C.act(Lh[:], Bp[:], AF.Exp, r=[d_Bp], w=[d_Lh], scale=-1.0 / 16)
                C.act(Bp[:], Bp[:], AF.Exp, r=[d_Bp], w=[d_Bp], scale=1.0 / 16)
                C.G(lambda e, h=h: e.tensor_copy(out=elast[:, h, :],
                                                 in_=Lh[:].rearrange("p (n c) -> p n c", c=64)[:, :, 63]),
                    r=[d_Lh], w=[d_el])
                for tb in range(4):
                    bq = 4 + tb % 2; bkk = 6 + tb % 2
                    for kc in range(8):
                        C.mm(C.pb(bq), wq[b][:, kc, :], uT[:, kc, tb * 512:(tb + 1) * 512], kc == 0, kc == 7,
                             r=d_uTb[tb] + [d_wq[b]], w=[C.psd[bq]])
                    C.V(lambda e, h=h, tb=tb, bq=bq: e.scalar_tensor_tensor(
                        out=qtT[:, h, tb * 512:(tb + 1) * 512], in0=C.pb(bq), scalar=QS,
                        in1=Lh[:, tb * 512:(tb + 1) * 512], op0=ALU.mult, op1=ALU.mult),
                        r=[C.psd[bq], d_Lh], w=[d_qt[tb]])
                    for kc in range(8):
                        C.mm(C.pb(bkk), wk[b][:, kc, :], uT[:, kc, tb * 512:(tb + 1) * 512], kc == 0, kc == 7,
                             r=d_uTb[tb] + [d_wk[b]], w=[C.psd[bkk]])
                    C.V(lambda e, h=h, tb=tb, bkk=bkk: e.tensor_tensor(
                        out=ktT[:, h, tb * 512:(tb + 1) * 512], in0=C.pb(bkk), in1=Bp[:, tb * 512:(tb + 1) * 512],
                        op=ALU.mult), r=[C.psd[bkk], d_Bp], w=[d_kt[tb]])
            wv = [C.sb(PA, [128, 8, 512], BF16) for _ in range(2)]; d_wv = deps(2)
            for i in range(2):
                C.load_w(wv[i][:], w_in[:, 1024 + i * 512:1024 + (i + 1) * 512], d_wv[i])
            i = 0
            for t in range(NT):
                for nb in range(2):
                    bk = (0, 1, 2, 3)[i % 4]; i += 1
                    for kc in range(8):
                        C.mm(C.pb(bk), uT[:, kc, t * 128:(t + 1) * 128], wv[nb][:, kc, :], kc == 0, kc == 7,
                             r=[d_uT[t], d_wv[nb]], w=[C.psd[bk]])
                    C.act(vtok[:, t, nb * 512:(nb + 1) * 512], C.pb(bk), AF.Copy, r=[C.psd[bk]], w=[d_v[t]])
            for i in range(2):
                C.load_w(wv[i][:], w_in[:, 2048 + i * 512:2048 + (i + 1) * 512], d_wv[i])
            go_bc = C.sb(PA, [128, 256], F32); d_gob = Dep()
            C.load_bc(go_bc[:], W["gla_g_o"], d_gob)
            sgt = [C.sb(PA, [128, 1024], F32) for _ in range(2)]; d_sgt = deps(2)
            i = 0
            for t in range(NT):
                b = t % 2
                for nb in range(2):
                    bk = (4, 5, 6, 7)[i % 4]; i += 1
                    for kc in range(8):
                        C.mm(C.pb(bk), uT[:, kc, t * 128:(t + 1) * 128], wv[nb][:, kc, :], kc == 0, kc == 7,
                             r=[d_uT[t], d_wv[nb]], w=[C.psd[bk]])
                    C.act(sgt[b][:, nb * 512:(nb + 1) * 512], C.pb(bk), AF.Silu, r=[C.psd[bk]], w=[d_sgt[b]])
                C.G(lambda e, t=t, b=b: e.tensor_tensor(
                    out=gsg[:, t, :].rearrange("p (a c) -> p a c", a=4), in0=sgt[b][:].rearrange("p (a c) -> p a c", a=4),
                    in1=go_bc[:].unsqueeze(1).to_broadcast([128, 4, 256]), op=ALU.mult),
                    r=[d_sgt[b], d_gob], w=[d_gsg[t]])
            C.release()
        with ExitStack() as PB:
            goT = C.sb(PB, [128, 8, S_], BF16); d_go = deps(NT)
            wo = C.sb(PB, [128, 8, D_], BF16); d_wo = Dep()
            C.load_w(wo[:, 0:4, :], W["gla_w_out"][0:512, :], d_wo)
            C.load_w(wo[:, 4:8, :], W["gla_w_out"][512:1024, :], d_wo)
            mask2 = C.sb(PB, [128, 128], BF16); d_m2 = Dep()
            C.D(lambda e: e.dma_start(out=mask2[:], in_=C.CD["mask2"]), w=[d_m2])
            tmpS = C.sb(PB, [128, 4, 256], F32); d_tmpS = Dep()
            Sbf = [C.sb(PB, [128, 4, 256], BF16) for _ in range(2)]; d_Sbf = deps(2)
            C.V(lambda e: e.memset(Sbf[0][:], 0.0), w=[d_Sbf[0]])
            attm = [C.sb(PB, [128, 4, 128], BF16) for _ in range(2)]; d_attm = deps(2)
            ktk = [C.sb(PB, [128, 4, 128], BF16) for _ in range(2)]; d_ktk = deps(2)
            ss = C.sb(PB, [128, NT, 4], F32); rs = C.sb(PB, [128, NT, 4], F32); d_ss = deps(NT); d_rs = deps(NT)
            junk = C.sb(PB, [128, 256], BF16); d_junk = Dep()
            gotok = [C.sb(PB, [128, 1024], BF16) for _ in range(2)]; d_gotok = deps(2)
            osb = [C.sb(PB, [128, 1024], F32) for _ in range(2)]; d_osb = deps(2)

            def obank(t, h):
                return (2 + 2 * (t % 2) + h // 2, (h % 2) * 256)

            def g0(t):
                b = t % 2
                tb = t // 4
                tsl = slice(t * 128, (t + 1) * 128)
                pv = C.pbh(0)
                for h in range(4):
                    C.tr(pv[:, h * 128:(h + 1) * 128], ktT[:, h, tsl], C.ident_bf[:], r=[d_kt[tb], C.d_const], w=[C.psd[0]])
                for h in range(4):
                    C.mm(C.pb(1)[:, h * 128:(h + 1) * 128], ktT[:, h, tsl], qtT[:, h, tsl], True, True,
                         r=[d_kt[tb], d_qt[tb]], w=[C.psd[1]])
                yield
                C.V(lambda e, b=b, pv=pv: e.tensor_copy(out=ktk[b][:], in_=pv[:, 0:512].rearrange("p (a c) -> p a c", a=4)),
                    r=[C.psd[0]], w=[d_ktk[b]])
                C.V(lambda e, b=b: e.tensor_tensor(out=attm[b][:], in0=C.pb(1).rearrange("p (a c) -> p a c", a=4),
                                                   in1=mask2[:].unsqueeze(1).to_broadcast([128, 4, 128]), op=ALU.mult),
                    r=[C.psd[1], d_m2], w=[d_attm[b]])
                yield
                for h in range(4):
                    bk, c0 = obank(t, h)
                    C.T(lambda e, bk=bk, c0=c0, b=b, h=h, t=t: e.matmul(
                        C.pb(bk)[:, c0:c0 + 256], lhsT=attm[b][:, h, :], rhs=vtok[:, t, h * 256:(h + 1) * 256],
                        start=(h % 2 == 0), stop=False, skip_group_check=True), r=[d_attm[b], d_v[t]], w=[C.psd[bk]])

            def g1(t):
                b = t % 2
                tb = t // 4
                for half in range(2):
                    r0 = half * 64
                    n = 2 * t + half
                    for h in range(4):
                        bk, c0 = obank(t, h)
                        C.T(lambda e, bk=bk, c0=c0, h=h, t=t, r0=r0, half=half: e.matmul(
                            C.pb(bk)[r0:r0 + 64, c0:c0 + 256], lhsT=qtT[:, h, t * 128 + r0:t * 128 + r0 + 64],
                            rhs=Sbf[half][:, h, :], start=False, stop=(half == 1), skip_group_check=True),
                            r=[d_qt[tb], d_Sbf[half]], w=[C.psd[bk]])
                    for h in range(4):
                        bkS = 6 + h // 2
                        C.mm(C.pb(bkS)[:, (h % 2) * 256:(h % 2 + 1) * 256], ktk[b][r0:r0 + 64, h, :],
                             vtok[r0:r0 + 64, t, h * 256:(h + 1) * 256], True, True,
                             r=[d_ktk[b], d_v[t]], w=[C.psd[bkS]])
                    C.V(lambda e, half=half: e.tensor_tensor(out=tmpS[:], in0=C.ps[:, 6 * 512:8 * 512].rearrange("p (a c) -> p a c", a=4),
                                                            in1=Sbf[half][:], op=ALU.add), r=[C.psd[6], C.psd[7], d_Sbf[half]], w=[d_tmpS])
                    C.V(lambda e, n=n, half=half: e.tensor_tensor(out=Sbf[1 - half][:], in0=tmpS[:],
                                                                 in1=elast[:, :, n:n + 1].to_broadcast([128, 4, 256]), op=ALU.mult),
                        r=[d_tmpS, d_el], w=[d_Sbf[1 - half]])
                    yield
                for i_ in range(2):
                    bk = 2 + 2 * (t % 2) + i_
                    C.act(osb[b][:, i_ * 512:(i_ + 1) * 512], C.pb(bk), AF.Copy, r=[C.psd[bk]], w=[d_osb[b]])

            def g2(t):
                b = t % 2
                for h in range(4):
                    C.act(junk[:], osb[b][:, h * 256:(h + 1) * 256], AF.Square, r=[d_osb[b]], w=[d_junk, d_ss[t]],
                          accum_out=ss[:, t, h:h + 1])
                yield
                rstd_ops(C, rs[:, t, :], ss[:, t, :], 256, [d_ss[t]], [d_rs[t]])
                yield
                for h in range(4):
                    C.V(lambda e, t=t, b=b, h=h: e.scalar_tensor_tensor(
                        out=gotok[b][:, h * 256:(h + 1) * 256], in0=osb[b][:, h * 256:(h + 1) * 256], scalar=rs[:, t, h:h + 1],
                        in1=gsg[:, t, h * 256:(h + 1) * 256], op0=ALU.mult, op1=ALU.mult),
                        r=[d_osb[b], d_rs[t], d_gsg[t]], w=[d_gotok[b]])
                yield
                pv = C.pbh(0)
                for c in range(8):
                    C.tr(pv[:, c * 128:(c + 1) * 128], gotok[b][:, c * 128:(c + 1) * 128], C.ident_bf[:],
                         r=[d_gotok[b], C.d_const], w=[C.psd[0]])
                C.act(goT[:, :, t * 128:(t + 1) * 128], pv.rearrange("p (c n) -> p c n", c=8), AF.Copy,
                      r=[C.psd[0]], w=[d_go[t]])

            LV = {1: ("s1", "s0", "s2"), 2: ("s0", "s1", "s2"), 3: ("s1", "s0", "s2"), 4: ("s2",)}
            for s_ in range(NT + 2):
                gens = {}
                if 0 <= s_ - 2 < NT:
                    gens["s2"] = g2(s_ - 2)
                if 0 <= s_ - 1 < NT:
                    gens["s1"] = g1(s_ - 1)
                if s_ < NT:
                    gens["s0"] = g0(s_)
                for lvl in (1, 2, 3, 4):
                    for nm in LV[lvl]:
                        if nm in gens:
                            next(gens[nm], None)
                for gn_ in gens.values():
                    for _ in gn_:
                        pass
            out_proj(C, goT, d_go, 8, None, h_in, d_hin, h_out, d_hout, banks=(1, 2, 3, 4), wo_pre=(wo, d_wo))
            C.release()
        C.release()


LAYER_FNS[1] = layer_gla

def layer_ssd(C, W, h_in, d_hin, h_out, d_hout):
    w_in = W["ssd_w_in"]
    G_ = 8
    with ExitStack() as L:
        goT = C.sb(L, [128, 16, S_], BF16); d_go = deps(NT)
        with ExitStack() as PA:
            uT = C.sb(PA, [128, 8, S_], BF16); d_uT = deps(NT)
            norm_T(C, PA, h_in, W["norm_g"], uT, d_uT, d_hin)
            d_uTb = [[d_uT[4 * tb + i] for i in range(4)] for tb in range(4)]
            U2b = C.sb(PA, [128, 128], BF16); V2b = C.sb(PA, [128, 128], BF16)
            U2f = C.sb(PA, [128, 128], F32); V2f = C.sb(PA, [128, 128], F32)
            onA = C.sb(PA, [128, 128], F32); onB = C.sb(PA, [128, 128], F32); d_k = Dep()
            negm = C.sb(PA, [128, 512], BF16)
            C.D(lambda e: e.dma_start(out=negm[:], in_=C.CD["negm4"]), w=[d_k])
            for dst, nm in ((U2b, "U2b"), (V2b, "mask2"), (U2f, "U2f"), (V2f, "V2f"), (onA, "onesA"), (onB, "onesB")):
                C.D(lambda e, dst=dst, nm=nm: e.dma_start(out=dst[:], in_=C.CD[nm]), w=[d_k])
            dtb = C.sb(PA, [128, 32], F32); alog = C.sb(PA, [128, 32], F32); dsk = C.sb(PA, [128, 32], F32)
            gn = C.sb(PA, [128, 2048], F32); d_p = Dep()
            C.load_bc(dtb[:], W["ssd_dt_bias"], d_p)
            C.load_bc(alog[:], W["ssd_a_log"], d_p)
            C.load_bc(dsk[:], W["ssd_d"], d_p)
            C.load_bc(gn[:], W["ssd_g_norm"], d_p)
            cw = C.sb(PA, [128, 4, 32], F32); cb = C.sb(PA, [128, 32], F32); d_cw = Dep()
            for k in range(4):
                C.load_col(cw[:, k, :], W["ssd_conv_w"][k, :], d_cw, 32)
            C.load_col(cb[:], W["ssd_conv_b"], d_cw, 32)
            wdt = C.sb(PA, [128, 8, 32], BF16); d_wdt = Dep()
            C.load_w(wdt[:], w_in[:, 6144:6176], d_wdt)
            dtt = C.sb(PA, [128, NT, 32], F32); a_tok = C.sb(PA, [128, NT, 32], F32)
            ecs = C.sb(PA, [128, NT, 32], F32); dec = C.sb(PA, [128, NT, 32], F32)
            elast = C.sb(PA, [128, 32, 32], F32); eA = C.sb(PA, [128, 32], F32)
            d_dt = Dep(); d_a = Dep(); d_ecs = Dep(); d_dec = Dep(); d_el = Dep()
            for t in range(NT):
                for kc in range(8):
                    C.mm(C.pb(0)[:, t * 32:(t + 1) * 32], uT[:, kc, t * 128:(t + 1) * 128], wdt[:, kc, :], kc == 0, kc == 7,
                         r=[d_uT[t], d_wdt], w=[C.psd[0]])
            C.V(lambda e: e.tensor_tensor(out=dtt[:], in0=C.pb(0).rearrange("p (a c) -> p a c", a=NT),
                                          in1=dtb[:].unsqueeze(1).to_broadcast([128, NT, 32]), op=ALU.add),
                r=[C.psd[0], d_p], w=[d_dt])
            C.act(dtt[:], dtt[:], AF.Exp, r=[d_dt], w=[d_dt])
            C.act(dtt[:], dtt[:], AF.Ln, r=[d_dt], w=[d_dt], bias=1.0)
            C.act(eA[:], alog[:], AF.Exp, r=[d_p], w=[d_a])
            C.V(lambda e: e.scalar_tensor_tensor(out=a_tok[:], in0=dtt[:], scalar=-1.0,
                                                 in1=eA[:].unsqueeze(1).to_broadcast([128, NT, 32]),
                                                 op0=ALU.mult, op1=ALU.mult), r=[d_dt, d_a], w=[d_a])
            for t in range(NT):
                C.mm(C.pb(1)[:, t * 32:(t + 1) * 32], V2f[:], a_tok[:, t, :], True, True, r=[d_k, d_a], w=[C.psd[1]])
            C.act(ecs[:], C.pb(1).rearrange("p (a c) -> p a c", a=NT), AF.Exp, r=[C.psd[1]], w=[d_ecs])
            for t in range(NT):
                C.mm(C.pb(2)[:, t * 32:(t + 1) * 32], U2f[:], a_tok[:, t, :], True, True, r=[d_k, d_a], w=[C.psd[2]])
            C.act(dec[:], C.pb(2).rearrange("p (a c) -> p a c", a=NT), AF.Exp, r=[C.psd[2]], w=[d_dec])
            for t in range(NT):
                for half in range(2):
                    n = 2 * t + half
                    bk = 3 + n // 16
                    C.mm(C.pb(bk)[:, (n % 16) * 32:(n % 16 + 1) * 32], (onA, onB)[half][:], a_tok[:, t, :], True, True,
                         r=[d_k, d_a], w=[C.psd[bk]])
            C.act(elast[:], C.ps[:, 3 * 512:5 * 512].rearrange("p (a c) -> p a c", a=32), AF.Exp,
                  r=[C.psd[3], C.psd[4]], w=[d_el])
            xTc = [C.sb(PA, [128, S_], BF16) for _ in range(2)]
            BT = C.sb(PA, [128, S_], BF16); CT = C.sb(PA, [128, S_], BF16)
            d_xT = deps(2); d_BT = Dep(); d_CT = Dep()
            x_tok = C.sb(PA, [128, NT, 256], BF16); d_xtok = deps(4)
            B_tok = C.sb(PA, [128, NT, 128], BF16); d_Btok = deps(2)
            sz = C.sb(PA, [128, NT, 256], BF16); d_sz = deps(NT)
            ub = [C.sb(PA, [128, 3 + S_], BF16) for _ in range(2)]; d_ub = deps(2)
            for b in range(2):
                C.V(lambda e, b=b: e.memset(ub[b][:, 0:3], 0.0), w=[d_ub[b]])
            wc = [C.sb(PA, [128, 8, 128], BF16) for _ in range(2)]; d_wc = deps(2)
            dg = [C.sb(PA, [128, 4, 128], BF16) for _ in range(2)]; d_dg = deps(2)
            wz = C.sb(PA, [128, 8, 256], BF16); d_wz = Dep()
            Sst = C.sb(PA, [128, 4, 64], F32); tmpS = C.sb(PA, [128, 4, 64], F32); d_S = Dep(); d_tmpS = Dep()
            Sbf = [C.sb(PA, [128, 256], BF16) for _ in range(2)]; d_Sbf = deps(2)
            cbm = [C.sb(PA, [128, 128], F32) for _ in range(2)]; d_cbm = deps(2)
            aV = [C.sb(PA, [128, 4, 128], BF16) for _ in range(2)]; d_aV = deps(2)
            lm = [C.sb(PA, [128, 4, 128], F32) for _ in range(2)]; d_lm = deps(2)
            Mh = [C.sb(PA, [128, 4, 128], BF16) for _ in range(2)]; d_M = deps(2)
            xdt = [C.sb(PA, [128, 4, 64], BF16) for _ in range(2)]; d_xdt = deps(2)
            xdd = [C.sb(PA, [128, 4, 64], BF16) for _ in range(2)]; d_xdd = deps(2)
            xD = [C.sb(PA, [128, 4, 64], BF16) for _ in range(2)]; d_xD = deps(2)
            t1 = [C.sb(PA, [128, 4, 64], F32) for _ in range(2)]; d_t1 = deps(2)
            yz = [C.sb(PA, [128, 256], F32) for _ in range(2)]; d_yz = deps(2)
            yn = [C.sb(PA, [128, 256], BF16) for _ in range(2)]; d_yn = deps(2)
            ss = C.sb(PA, [128, G_, NT], F32); rs = C.sb(PA, [128, G_, NT], F32)
            junk = C.sb(PA, [128, 256], BF16); d_junk = Dep()
            ci = 0
            d_dS = [deps(2), deps(2)]
            for g in range(G_):
                hs = slice(4 * g, 4 * g + 4)
                chunks = ((2 * g, xTc[0], d_xT[0]), (2 * g + 1, xTc[1], d_xT[1]), (16 + g, BT, d_BT), (24 + g, CT, d_CT))

                def proj_c(k_, b):
                    cc = chunks[k_][0]
                    C.load_w(wc[b][:], w_in[:, 2048 + cc * 128:2048 + (cc + 1) * 128], d_wc[b])
                    for k in range(4):
                        C.V(lambda e, b=b, k=k, cc=cc: e.tensor_scalar(out=dg[b][:, k, :], in0=C.ident_bf[:],
                                                                       scalar1=cw[:, k, cc:cc + 1], scalar2=None, op0=ALU.mult),
                            r=[d_cw, C.d_const], w=[d_dg[b]])
                    for tb in range(4):
                        bk = tb
                        for kc in range(8):
                            C.mm(C.pb(bk), wc[b][:, kc, :], uT[:, kc, tb * 512:(tb + 1) * 512], kc == 0, kc == 7,
                                 r=d_uTb[tb] + [d_wc[b]], w=[C.psd[bk]])
                        C.act(ub[b][:, 3 + tb * 512:3 + (tb + 1) * 512], C.pb(bk), AF.Copy, r=[C.psd[bk]], w=[d_ub[b]])

                def conv_c(k_, b):
                    cc, dst, d_dst = chunks[k_]
                    for tb in range(4):
                        bk = 4 + tb
                        for k in range(4):
                            C.mm(C.pb(bk), dg[b][:, k, :], ub[b][:, tb * 512 + k:tb * 512 + k + 512], k == 0, k == 3,
                                 r=[d_ub[b], d_dg[b]], w=[C.psd[bk]])
                        C.act(dst[:, tb * 512:(tb + 1) * 512], C.pb(bk), AF.Silu, r=[C.psd[bk], d_cw], w=[d_dst],
                              bias=cb[:, cc:cc + 1])

                proj_c(0, 0)
                for k_ in range(4):
                    if k_ + 1 < 4:
                        proj_c(k_ + 1, (k_ + 1) % 2)
                    conv_c(k_, k_ % 2)
                for t4 in range(4):
                    bk = t4 % 2
                    pv = C.pbh(bk)
                    for i in range(4):
                        for c in range(2):
                            C.tr(pv[:, (i * 2 + c) * 128:(i * 2 + c + 1) * 128], xTc[c][:, (t4 * 4 + i) * 128:(t4 * 4 + i + 1) * 128],
                                 C.ident_bf[:], r=[d_xT[c], C.d_const], w=[C.psd[bk]])
                    C.act(x_tok[:, t4 * 4:(t4 + 1) * 4, :], pv.rearrange("p (a c) -> p a c", a=4), AF.Copy,
                          r=[C.psd[bk]], w=[d_xtok[t4]])
                for t8 in range(2):
                    bk = 2 + t8
                    pv = C.pbh(bk)
                    for i in range(8):
                        C.tr(pv[:, i * 128:(i + 1) * 128], BT[:, (t8 * 8 + i) * 128:(t8 * 8 + i + 1) * 128], C.ident_bf[:],
                             r=[d_BT, C.d_const], w=[C.psd[bk]])
                    C.V(lambda e, t8=t8, pv=pv: e.tensor_copy(out=B_tok[:, t8 * 8:(t8 + 1) * 8, :],
                                                             in_=pv.rearrange("p (a c) -> p a c", a=8)),
                        r=[C.psd[bk]], w=[d_Btok[t8]])
                C.load_w(wz[:], w_in[:, g * 256:(g + 1) * 256], d_wz)
                for t2 in range(8):
                    bk = 4 + t2 % 4
                    for i in range(2):
                        t = t2 * 2 + i
                        for kc in range(8):
                            C.mm(C.pb(bk)[:, i * 256:(i + 1) * 256], uT[:, kc, t * 128:(t + 1) * 128], wz[:, kc, :],
                                 kc == 0, kc == 7, r=[d_uT[t], d_wz], w=[C.psd[bk]])
                    C.act(sz[:, t2 * 2:t2 * 2 + 2, :], C.pb(bk).rearrange("p (a c) -> p a c", a=2), AF.Silu,
                          r=[C.psd[bk]], w=[d_sz[t2 * 2], d_sz[t2 * 2 + 1]])
                C.V(lambda e: e.memset(Sbf[0][:], 0.0), w=[d_Sbf[0]])
                def st0(t):
                    b = t % 2
                    tsl = slice(t * 128, (t + 1) * 128)
                    xt_ = x_tok[:, t, :].rearrange("p (a c) -> p a c", a=4)
                    d_x = d_xtok[t // 4]
                    for h_ in range(4):
                        C.act(aV[b][:, h_, :], V2b[:], AF.Copy, r=[d_a, d_k], w=[d_aV[b]],
                              scale=a_tok[:, t, 4 * g + h_:4 * g + h_ + 1])
                    C.G(lambda e, hs=hs, b=b, t=t, xt_=xt_: e.tensor_tensor(out=xdt[b][:], in0=xt_,
                                                                   in1=dtt[:, t, hs].unsqueeze(2).to_broadcast([128, 4, 64]), op=ALU.mult),
                        r=[d_x, d_dt], w=[d_xdt[b]])
                    C.G(lambda e, hs=hs, b=b, t=t: e.tensor_tensor(out=xdd[b][:], in0=xdt[b][:],
                                                           in1=dec[:, t, hs].unsqueeze(2).to_broadcast([128, 4, 64]), op=ALU.mult),
                        r=[d_xdt[b], d_dec], w=[d_xdd[b]])
                    C.G(lambda e, hs=hs, b=b, xt_=xt_: e.tensor_tensor(out=xD[b][:], in0=xt_,
                                                              in1=dsk[:, hs].unsqueeze(2).to_broadcast([128, 4, 64]), op=ALU.mult),
                        r=[d_x, d_p], w=[d_xD[b]])
                    C.mm(C.pb(0)[:, 0:128], BT[:, tsl], CT[:, tsl], True, True, r=[d_BT, d_CT], w=[C.psd[0]])
                    yield
                    C.mm(C.pb(1), U2b[:], aV[b][:].rearrange("p a c -> p (a c)"), True, False, r=[d_k, d_aV[b]], w=[C.psd[1]])
                    C.mm(C.pb(1), C.ident_bf[:], negm[:], False, True, r=[C.d_const, d_k], w=[C.psd[1]])
                    for half in range(2):
                        r0 = half * 64
                        C.mm(C.pb(4 + half)[:, b * 256:(b + 1) * 256], B_tok[r0:r0 + 64, t, :],
                             xdd[b][r0:r0 + 64, :, :].rearrange("p a c -> p (a c)"),
                             True, True, r=[d_Btok[t // 8], d_xdd[b]], w=[C.psd[4 + half]])
                    yield
                    C.act(lm[b][:], C.pb(1).rearrange("p (a c) -> p a c", a=4), AF.Exp, r=[C.psd[1]], w=[d_lm[b]])
                    yield
                    C.V(lambda e, b=b: e.tensor_tensor(out=Mh[b][:], in0=lm[b][:],
                                                       in1=C.pb(0)[:, 0:128].unsqueeze(1).to_broadcast([128, 4, 128]), op=ALU.mult),
                        r=[d_lm[b], C.psd[0]], w=[d_M[b]])

                def st1(t):
                    b = t % 2
                    bY = 2 + b
                    C.mm(C.pb(bY)[:, 0:256], C.ident_bf[:], xD[b][:].rearrange("p a c -> p (a c)"), True, False,
                         r=[C.d_const, d_xD[b]], w=[C.psd[bY]])
                    for h in range(4):
                        C.mm(C.pb(bY)[:, h * 64:(h + 1) * 64], Mh[b][:, h, :], xdt[b][:, h, :], False, h == 3,
                             r=[d_M[b], d_xdt[b]], w=[C.psd[bY]])
                    for half in range(2):
                        r0 = half * 64
                        n = 2 * t + half
                        C.mm(C.pb(bY)[r0:r0 + 64, 256:512], CT[:, t * 128 + r0:t * 128 + r0 + 64], Sbf[half][:], True, True,
                             r=[d_CT, d_Sbf[half]], w=[C.psd[bY]])
                        C.V(lambda e, hs=hs, n=n, half=half: e.tensor_tensor(
                            out=tmpS[:], in0=Sbf[half][:].rearrange("p (a c) -> p a c", a=4),
                            in1=elast[:, n, hs].unsqueeze(2).to_broadcast([128, 4, 64]), op=ALU.mult),
                            r=[d_Sbf[half], d_el], w=[d_tmpS])
                        C.V(lambda e, half=half, b=b: e.tensor_tensor(
                            out=Sbf[1 - half][:].rearrange("p (a c) -> p a c", a=4),
                            in0=C.pb(4 + half)[:, b * 256:(b + 1) * 256].rearrange("p (a c) -> p a c", a=4),
                            in1=tmpS[:], op=ALU.add), r=[C.psd[4 + half], d_tmpS], w=[d_Sbf[1 - half]])
                        if half == 0:
                            yield

                def st2(t, g=g):
                    b = t % 2
                    bY = 2 + b
                    C.V(lambda e, hs=hs, b=b, t=t, bY=bY: e.tensor_tensor(out=t1[b][:], in0=C.pb(bY)[:, 256:512].rearrange("p (a c) -> p a c", a=4),
                                                           in1=ecs[:, t, hs].unsqueeze(2).to_broadcast([128, 4, 64]), op=ALU.mult),
                        r=[C.psd[bY], d_ecs], w=[d_t1[b]])
                    C.V(lambda e, b=b, bY=bY: e.tensor_tensor(out=t1[b][:], in0=C.pb(bY)[:, 0:256].rearrange("p (a c) -> p a c", a=4),
                                                       in1=t1[b][:], op=ALU.add), r=[C.psd[bY], d_t1[b]], w=[d_t1[b]])
                    yield
                    C.G(lambda e, b=b, t=t: e.tensor_tensor(out=yz[b][:], in0=t1[b][:].rearrange("p a c -> p (a c)"),
                                                           in1=sz[:, t, :], op=ALU.mult), r=[d_t1[b], d_sz[t]], w=[d_yz[b]])
                    yield
                    C.act(junk[:], yz[b][:], AF.Square, r=[d_yz[b]], w=[d_junk], accum_out=ss[:, g, t:t + 1])
                    rstd_ops(C, rs[:, g, t:t + 1], ss[:, g, t:t + 1], 256, [d_junk], [d_junk])
                    yield
                    C.V(lambda e, b=b, t=t, g=g: e.scalar_tensor_tensor(out=yn[b][:], in0=yz[b][:], scalar=rs[:, g, t:t + 1],
                                                                       in1=gn[:, g * 256:(g + 1) * 256], op0=ALU.mult, op1=ALU.mult),
                        r=[d_yz[b], d_junk, d_p], w=[d_yn[b]])
                    bk = 6 + (t // 4) % 2
                    pv = C.pbh(bk)
                    i = t % 4
                    for c in range(2):
                        C.tr(pv[:, (c * 4 + i) * 128:(c * 4 + i + 1) * 128], yn[b][:, c * 128:(c + 1) * 128], C.ident_bf[:],
                             r=[d_yn[b], C.d_const], w=[C.psd[bk]])
                    if i == 3:
                        t4 = t // 4
                        C.act(goT[:, 2 * g:2 * g + 2, t4 * 512:(t4 + 1) * 512], pv.rearrange("p (c m) -> p c m", c=2), AF.Copy,
                              r=[C.psd[bk]], w=[d_go[4 * t4 + j] for j in range(4)])

                LV = {1: ("s1", "s0", "s2"), 2: ("s0", "s1", "s2"), 3: ("s0", "s2"), 4: ("s0", "s2")}
                for s_ in range(NT + 2):
                    gens = {}
                    if 0 <= s_ - 2 < NT:
                        gens["s2"] = st2(s_ - 2)
                    if 0 <= s_ - 1 < NT:
                        gens["s1"] = st1(s_ - 1)
                    if s_ < NT:
                        gens["s0"] = st0(s_)
                    for lvl in (1, 2, 3, 4):
                        for nm in LV[lvl]:
                            if nm in gens:
                                next(gens[nm], None)
                    for gn_ in gens.values():
                        for _ in gn_:
                            pass
            C.dbg("dtt", dtt[:], [128, NT, 32], F32, [d_dt])
            C.dbg("a_tok", a_tok[:], [128, NT, 32], F32, [d_a])
            C.dbg("ecs", ecs[:], [128, NT, 32], F32, [d_ecs])
            C.dbg("dec", dec[:], [128, NT, 32], F32, [d_dec])
            C.dbg("elast", elast[:], [128, 32, 32], F32, [d_el])
            C.dbg("x_tok", x_tok[:], [128, NT, 256], BF16, d_xtok)
            C.dbg("B_tok", B_tok[:], [128, NT, 128], BF16, d_Btok)
            C.dbg("CT", CT[:], [128, S_], BF16, [d_CT])
            C.dbg("sz", sz[:], [128, NT, 256], BF16, d_sz)
            C.dbg("goT", goT[:], [128, 16, S_], BF16, d_go)
            C.dbg("Mh", Mh[1][:], [128, 4, 128], BF16, [d_M[1]])
            C.dbg("lm", lm[1][:], [128, 4, 128], F32, [d_lm[1]])
            C.dbg("yz", yz[1][:], [128, 256], F32, [d_yz[1]])
            C.release()
        out_proj(C, goT, d_go, 16, W["ssd_w_out"], h_in, d_hin, h_out, d_hout)
        C.release()


LAYER_FNS[3] = layer_ssd

WSPEC = {
    "norm_g": [4, 1024], "final_g": [1024],
    "mla_w_in": [1024, 1696], "mla_g_q": [384], "mla_w_uq": [384, 1536], "mla_g_kv": [256],
    "mla_w_ukv": [256, 2048], "mla_w_out": [1024, 1024],
    "gla_w_in": [1024, 3088], "gla_w_gk2": [16, 512], "gla_b_gk": [512], "gla_g_o": [256],
    "gla_w_out": [1024, 1024],
    "lru_w_in": [1024, 2560], "lru_conv_w": [4, 1280], "lru_conv_b": [1280], "lru_w_a": [10, 128, 128],
    "lru_b_a": [1280], "lru_w_x": [10, 128, 128], "lru_b_x": [1280], "lru_lam": [1280],
    "lru_w_out": [1280, 1024],
    "ssd_w_in": [1024, 6176], "ssd_conv_w": [4, 4096], "ssd_conv_b": [4096], "ssd_dt_bias": [32],
    "ssd_a_log": [32], "ssd_d": [32], "ssd_g_norm": [2048], "ssd_w_out": [2048, 1024],
}


def host_consts():
    c = {}
    c["ident_bf"] = np.eye(128, dtype=np.float32).astype(ml_dtypes.bfloat16)
    c["ident_f"] = np.eye(128, dtype=np.float32)
    k = np.arange(128)
    c["tri_bf"] = (k[None, :] >= k[:, None]).astype(np.float32).astype(ml_dtypes.bfloat16)
    c["ones_bf"] = np.ones((128, 64), np.float32).astype(ml_dtypes.bfloat16)
    invf = (np.float32(10000.0) ** (-np.arange(0, 32, 2, dtype=np.float32) / np.float32(32))).astype(np.float32)
    rm = np.ones((128, S_), np.float32); rm[:, ::64] = 0.0
    c["rmask"] = rm.astype(ml_dtypes.bfloat16)
    m2 = ((k[None, :] >= k[:, None]) & ((k[None, :] // 64) == (k[:, None] // 64))).astype(np.float32)
    c["mask2"] = m2.astype(ml_dtypes.bfloat16)
    same = (k[None, :] // 64) == (k[:, None] // 64)
    u2 = ((k[:, None] > k[None, :]) & same).astype(np.float32)
    c["U2b"] = u2.astype(ml_dtypes.bfloat16)
    c["U2f"] = u2
    c["V2f"] = m2.astype(np.float32)
    c["onesA"] = np.ascontiguousarray(np.broadcast_to((k[:, None] < 64).astype(np.float32), (128, 128)))
    c["onesB"] = np.ascontiguousarray(np.broadcast_to((k[:, None] >= 64).astype(np.float32), (128, 128)))
    c["negm4"] = np.tile(np.where(m2 > 0, 0.0, -30000.0).astype(np.float32), (1, 4)).astype(ml_dtypes.bfloat16)
    c["swap_f"] = np.roll(np.eye(128, dtype=np.float32), 64, axis=0)
    c["invf"] = np.ascontiguousarray(np.broadcast_to(invf[None, :], (128, 16))).astype(np.float32)
    return c


CSPEC = {"ident_bf": ([128, 128], BF16), "ident_f": ([128, 128], F32), "tri_bf": ([128, 128], BF16),
         "ones_bf": ([128, 64], BF16), "invf": ([128, 16], F32), "rmask": ([128, S_], BF16),
         "mask2": ([128, 128], BF16), "U2b": ([128, 128], BF16), "U2f": ([128, 128], F32),
         "V2f": ([128, 128], F32), "swap_f": ([128, 128], F32), "negm4": ([128, 512], BF16), "onesA": ([128, 128], F32), "onesB": ([128, 128], F32)}

def build(layers=(0, 1, 2, 3), final=True):
    nc = bass.Bass("TRN2", target_bir_lowering=False)
    x = nc.dram_tensor("x", [S_, D_], F32, kind="ExternalInput").ap()
    pos = nc.dram_tensor("pos", [128, NT], I32, kind="ExternalInput").ap()
    W = {k: nc.dram_tensor(k, v, F32, kind="ExternalInput").ap() for k, v in WSPEC.items()}
    CD = {k: nc.dram_tensor(k, v[0], v[1], kind="ExternalInput").ap() for k, v in CSPEC.items()}
    out = nc.dram_tensor("out", [S_, D_], F32, kind="ExternalOutput").ap()
    hbuf = [nc.dram_tensor(f"hbuf{i}", [S_, D_], F32).ap() for i in range(2)]
    _BASE["r"] = {}
    with ExitStack() as st:
        C = Ctx(nc, st)
        C.pos = pos
        C.CD = CD
        C.d_const = Dep()
        C.ident_bf = C.sb(st, [128, 128], BF16)
        C.ident_f = C.sb(st, [128, 128], F32)
        C.D(lambda e: e.dma_start(out=C.ident_bf[:], in_=CD["ident_bf"]), w=[C.d_const])
        C.D(lambda e: e.dma_start(out=C.ident_f[:], in_=CD["ident_f"]), w=[C.d_const])
        h_cur, d_cur = x, deps(NT)
        nxt = 0
        for li in layers:
            Wl = dict(W)
            Wl["norm_g"] = W["norm_g"][li, :]
            h_nxt, d_nxt = hbuf[nxt], deps(NT)
            if not final and li == layers[-1]:
                h_nxt = out
            LAYER_FNS[li](C, Wl, h_cur, d_cur, h_nxt, d_nxt)
            h_cur, d_cur = h_nxt, d_nxt
            nxt ^= 1
        if final:
            final_norm(C, h_cur, d_cur, W["final_g"], out)
        else:
            C.out_toks.append(C.last_store)
            for t in range(NT):
                C.out_toks.append(d_cur[t].w)
        for tok in C.out_toks:
            C.S.wait_tok("sync", tok)
        C.S.emit()
    return nc


LAYER_FNS[2] = layer_lru


def make_in_maps(inputs):
    consts = host_consts()
    maps = []
    for b in range(8):
        m = {"x": np.ascontiguousarray(inputs["x"][b]),
             "pos": np.ascontiguousarray(np.asarray(inputs["positions"][b]).astype(np.int32).reshape(NT, 128).T)}
        for k in WSPEC:
            a = np.asarray(inputs[k], dtype=np.float32)
            if k not in ("norm_g", "final_g"):
                a = a[0]
            m[k] = np.ascontiguousarray(a)
        m.update(consts)
        maps.append(m)
    return maps


_NC_CACHE = {}


def kernel(**inputs):
    if "nc" not in _NC_CACHE:
        _NC_CACHE["nc"] = build()
    nc = _NC_CACHE["nc"]
    maps = make_in_maps(inputs)
    res = run_bass_kernel_spmd(nc, maps, core_ids=list(range(8)))
    return np.stack([np.asarray(r["out"], dtype=np.float32) for r in res.results], axis=0)
```

```python
import numpy as np
import ml_dtypes
import concourse.bass as bass
import concourse.mybir as mybir
from concourse.bass_utils import run_bass_kernel_spmd
from contextlib import ExitStack

F32 = mybir.dt.float32
BF16 = mybir.dt.bfloat16
I32 = mybir.dt.int32
AF = mybir.ActivationFunctionType
ALU = mybir.AluOpType

S_ = 2048
D_ = 1024
NT = 16
EPS = 1e-6
LAYER_FNS = {}
DEBUG = False
import os
SSD_SKEW = int(os.environ.get('SSD_SKEW', '1'))

ENGS = ("tensor", "vector", "scalar", "gpsimd", "sync")
EPOCH = 6000
NDMA = 24


_BASE = {"r": {}}


class Dep:
    __slots__ = ("w", "r")

    def __init__(self):
        self.w = None
        self.r = dict(_BASE["r"])


def deps(n):
    return [Dep() for _ in range(n)]


class Sched:
    def __init__(self, nc, stack):
        self.nc = nc
        self.stack = stack
        self.q = {e: [] for e in ENGS}
        self.cnt = {e: 0 for e in ENGS}
        self.sems = {}
        self.known = {e: {} for e in ENGS}
        self.dma_sems = [stack.enter_context(nc.semaphore(f"dma{i}")) for i in range(2 * NDMA)]
        self.dma_i = {"sync": 0, "gpsimd": 0}
        self.dma_base = {"sync": 0, "gpsimd": NDMA}

    def _esem(self, eng, epoch):
        k = (eng, epoch)
        if k not in self.sems:
            self.sems[k] = self.stack.enter_context(self.nc.semaphore(f"s_{eng}_{epoch}"))
        return self.sems[k]

    def _need_wait(self, weng, tok):
        kn = self.known[weng]
        if tok[0] == "c":
            _, eng, ep, val = tok
            cur = kn.get(("c", eng), (-1, 0))
            if cur >= (ep, val):
                return None
            kn[("c", eng)] = (ep, val)
            return (self._esem(eng, ep), val)
        _, idx, val = tok
        cur = kn.get(("d", idx), 0)
        if cur >= val:
            return None
        kn[("d", idx)] = val
        return (self.dma_sems[idx], val)

    def op(self, eng, fn, reads=(), writes=(), dma=False):
        waits = []
        toks = []
        for d in reads:
            if d.w is not None:
                toks.append((d.w, "raw"))
        for d in writes:
            if d.w is not None:
                toks.append((d.w, "waw"))
            for t in d.r.values():
                toks.append((t, "war"))
        for tok, kind in toks:
            if not dma and tok[0] == "c" and tok[1] == eng:
                if eng == "tensor" or kind != "raw":
                    continue
            w = self._need_wait(eng, tok)
            if w is not None:
                waits.append(w)
        if dma:
            n_ = self.dma_i[eng]
            idx = self.dma_base[eng] + n_ % NDMA
            rnd = n_ // NDMA
            self.dma_i[eng] = n_ + 1
            if rnd > 0:
                w = self._need_wait(eng, ("d", idx, 16 * rnd))
                if w is not None:
                    waits.append(w)
            mytok = ("d", idx, 16 * (rnd + 1))
            inc = (self.dma_sems[idx], 16)
        else:
            c = self.cnt[eng]
            self.cnt[eng] = c + 1
            ep, val = c // EPOCH, c % EPOCH + 1
            mytok = ("c", eng, ep, val)
            inc = (self._esem(eng, ep), 1)
        self.q[eng].append((fn, waits, inc))
        rk = mytok[:2]
        for d in reads:
            d.r[rk] = mytok
        for d in writes:
            d.w = mytok
            d.r = {}
        return mytok

    def snapshot(self):
        snap = {}
        for e in ENGS:
            c = self.cnt[e]
            if c > 0:
                c -= 1
                snap[("c", e)] = ("c", e, c // EPOCH, c % EPOCH + 1)
        for q in ("sync", "gpsimd"):
            n_ = self.dma_i[q]
            for i in range(min(n_, NDMA)):
                n_i = (n_ - 1 - i) // NDMA
                snap[("d", self.dma_base[q] + i)] = ("d", self.dma_base[q] + i, 16 * (n_i + 1))
        return snap

    def wait_tok(self, eng, tok):
        w = self._need_wait(eng, tok)
        if w is not None:
            self.q[eng].append((None, [w], None))

    def emit(self):
        nc = self.nc
        with nc.Block() as block:
            def run(engname):
                def body(e):
                    for fn, waits, inc in self.q[engname]:
                        for sem, val in waits:
                            e.wait_ge(sem, val)
                        if fn is not None:
                            ins = fn(e)
                            ins.then_inc(inc[0], inc[1])
                return body
            block.tensor(run("tensor"))
            block.vector(run("vector"))
            block.scalar(run("scalar"))
            block.gpsimd(run("gpsimd"))
            block.sync(run("sync"))


class Ctx:
    def __init__(self, nc, st):
        self.nc = nc
        self.st = st
        self.S = Sched(nc, st)
        self.ps = st.enter_context(nc.psum_tensor("ps", [128, 8 * 512], F32))
        self.psd = deps(8)
        self.uid = 0
        self.out_toks = []

    def release(self):
        _BASE["r"] = self.S.snapshot()

    def sb(self, stack, shape, dt):
        self.uid += 1
        return stack.enter_context(self.nc.sbuf_tensor(f"t{self.uid}", list(shape), dt))

    def pb(self, i, n=512, off=0):
        return self.ps[:, i * 512 + off:i * 512 + off + n]

    def pbh(self, i):
        return self.ps[:, i * 512:(i + 1) * 512].bitcast(BF16)

    def T(self, fn, r=(), w=()):
        return self.S.op("tensor", fn, r, w)

    def V(self, fn, r=(), w=()):
        return self.S.op("vector", fn, r, w)

    def A(self, fn, r=(), w=()):
        return self.S.op("scalar", fn, r, w)

    def G(self, fn, r=(), w=()):
        return self.S.op("gpsimd", fn, r, w)

    def D(self, fn, r=(), w=()):
        return self.S.op("sync", fn, r, w, dma=True)

    def DG(self, fn, r=(), w=()):
        return self.S.op("gpsimd", fn, r, w, dma=True)

    def mm(self, out, lhsT, rhs, start, stop, r=(), w=()):
        return self.T(lambda e: e.matmul(out, lhsT=lhsT, rhs=rhs, start=start, stop=stop), r, w)

    def tr(self, out, in_, ident, r=(), w=()):
        return self.T(lambda e: e.transpose(out=out, in_=in_, identity=ident), r, w)

    def act(self, out, in_, func, r=(), w=(), **kw):
        return self.A(lambda e: e.activation(out=out, in_=in_, func=func, **kw), r, w)

    def dbg(self, name, ap, shape, dt, r):
        if not DEBUG:
            return
        d = self.nc.dram_tensor("dbg_" + name, list(shape), dt, kind="ExternalOutput").ap()
        tok = self.D(lambda e: e.dma_start(out=d, in_=ap), r=r)
        self.out_toks.append(tok)

    def load_w(self, dst, src2d, dep):
        n = src2d.shape[1]
        tok = None
        for c0 in range(0, n, 512):
            c1 = min(n, c0 + 512)
            assert (c1 - c0) in (16, 32, 64, 128, 256, 512), (c0, c1)
            tok = self.DG(lambda e, c0=c0, c1=c1: e.dma_start(
                out=dst[:, :, c0:c1], in_=src2d[:, c0:c1].rearrange("(c p) n -> p c n", p=128)), w=[dep])
        return tok

    def load_bc(self, dst, src1d, dep, n=128):
        return self.DG(lambda e: e.dma_start(out=dst, in_=src1d.partition_broadcast(n)), w=[dep])

    def load_col(self, dst, src1d, dep, nchunks):
        return self.DG(lambda e: e.dma_start(out=dst, in_=src1d.rearrange("(c p) -> p c", p=128),
                                             allow_slow_non_contiguous=True), w=[dep])


def rstd_ops(C, rs_ap, ss_ap, n, r, w):
    C.act(rs_ap, ss_ap, AF.Ln, r=r, w=w, scale=1.0 / n, bias=EPS)
    C.act(rs_ap, rs_ap, AF.Exp, r=w, w=w, scale=-0.5)


def norm_T(C, stk, h_ap, g_row, uT, d_uT, d_h, banks=(0, 1)):
    NH, NX = 4, 3
    with ExitStack() as s:
        gt = C.sb(s, [128, D_], F32); d_gt = Dep()
        C.load_bc(gt[:], g_row, d_gt)
        hb = [C.sb(s, [128, D_], F32) for _ in range(NH)]; d_hb = deps(NH)
        junk = C.sb(s, [128, D_], BF16); d_junk = Dep()
        ss = C.sb(s, [128, NT], F32); rs = C.sb(s, [128, NT], F32)
        d_ss = deps(NT); d_rs = deps(NT)
        xn = [C.sb(s, [128, D_], BF16) for _ in range(NX)]; d_xn = deps(NX)

        def load(t):
            b = t % NH
            C.DG(lambda e, t=t, b=b: e.dma_start(out=hb[b][:], in_=h_ap[t * 128:(t + 1) * 128, :]),
                 r=[d_h[t]], w=[d_hb[b]])

        for t in range(NH - 1):
            load(t)
        for t in range(NT):
            b = t % NH
            x = t % NX
            if t + NH - 1 < NT:
                load(t + NH - 1)
            C.act(junk[:], hb[b][:], AF.Square, r=[d_hb[b]], w=[d_junk, d_ss[t]], accum_out=ss[:, t:t + 1])
            rstd_ops(C, rs[:, t:t + 1], ss[:, t:t + 1], D_, [d_ss[t]], [d_rs[t]])
            C.act(hb[b][:], hb[b][:], AF.Copy, r=[d_hb[b], d_rs[t]], w=[d_hb[b]], scale=rs[:, t:t + 1])
            C.V(lambda e, b=b, x=x: e.tensor_tensor(out=xn[x][:], in0=hb[b][:], in1=gt[:], op=ALU.mult),
                r=[d_hb[b], d_gt], w=[d_xn[x]])
            bk = banks[t % len(banks)]
            pv = C.pbh(bk)
            for c in range(8):
                C.tr(pv[:, c * 128:(c + 1) * 128], xn[x][:, c * 128:(c + 1) * 128], C.ident_bf[:],
                     r=[d_xn[x], C.d_const], w=[C.psd[bk]])
            C.V(lambda e, t=t, pv=pv: e.tensor_copy(out=uT[:, :, t * 128:(t + 1) * 128],
                                                   in_=pv.rearrange("p (c n) -> p c n", c=8)),
                r=[C.psd[bk]], w=[d_uT[t]])
        C.release()


def out_proj(C, goT, d_go, nk, w_out, h_in, d_hin, h_out, d_hout, banks=(0, 1, 2, 3), wo_pre=None):
    with ExitStack() as s:
        if wo_pre is None:
            wo = C.sb(s, [128, nk, D_], BF16); d_wo = Dep()
            half = nk // 2
            C.load_w(wo[:, 0:half, :], w_out[0:half * 128, :], d_wo)
            C.load_w(wo[:, half:nk, :], w_out[half * 128:nk * 128, :], d_wo)
        else:
            wo, d_wo = wo_pre
        NHB = 3
        hb = [C.sb(s, [128, D_], F32) for _ in range(NHB)]; d_hb = deps(NHB)
        ho = [C.sb(s, [128, D_], F32) for _ in range(2)]; d_ho = deps(2)
        i = 0

        def load_h(t):
            bb = t % NHB
            C.DG(lambda e, t=t, bb=bb: e.dma_start(out=hb[bb][:], in_=h_in[t * 128:(t + 1) * 128, :]),
                 r=[d_hin[t]], w=[d_hb[bb]])

        for t in range(min(NHB - 1, NT)):
            load_h(t)
        for t in range(NT):
            b = t % 2
            bb = t % NHB
            if t + NHB - 1 < NT:
                load_h(t + NHB - 1)
            for nb in range(2):
                bk = banks[i % len(banks)]; i += 1
                for c in range(nk):
                    C.mm(C.pb(bk), goT[:, c, t * 128:(t + 1) * 128], wo[:, c, nb * 512:(nb + 1) * 512],
                         c == 0, c == nk - 1, r=[d_go[t], d_wo], w=[C.psd[bk]])
                C.V(lambda e, b=b, bb=bb, nb=nb, bk=bk: e.tensor_tensor(out=ho[b][:, nb * 512:(nb + 1) * 512], in0=C.pb(bk),
                                                                      in1=hb[bb][:, nb * 512:(nb + 1) * 512], op=ALU.add),
                    r=[C.psd[bk], d_hb[bb]], w=[d_ho[b]])
            tok = C.D(lambda e, t=t, b=b: e.dma_start(out=h_out[t * 128:(t + 1) * 128, :], in_=ho[b][:]),
                      r=[d_ho[b]], w=[d_hout[t]])
            C.last_store = tok
        C.release()


def final_norm(C, h_ap, d_h, g_row, out_ap):
    NH = 4
    with ExitStack() as s:
        gt = C.sb(s, [128, D_], F32); d_gt = Dep()
        C.load_bc(gt[:], g_row, d_gt)
        hb = [C.sb(s, [128, D_], F32) for _ in range(NH)]; d_hb = deps(NH)
        junk = C.sb(s, [128, D_], BF16); d_junk = Dep()
        ss = C.sb(s, [128, NT], F32); rs = C.sb(s, [128, NT], F32)
        d_ss = deps(NT); d_rs = deps(NT)
        ob = [C.sb(s, [128, D_], F32) for _ in range(3)]; d_ob = deps(3)

        def load(t):
            b = t % NH
            C.DG(lambda e, t=t, b=b: e.dma_start(out=hb[b][:], in_=h_ap[t * 128:(t + 1) * 128, :]),
                 r=[d_h[t]], w=[d_hb[b]])

        for t in range(NH - 1):
            load(t)
        for t in range(NT):
            b = t % NH
            o = t % 3
            if t + NH - 1 < NT:
                load(t + NH - 1)
            C.act(junk[:], hb[b][:], AF.Square, r=[d_hb[b]], w=[d_junk, d_ss[t]], accum_out=ss[:, t:t + 1])
            rstd_ops(C, rs[:, t:t + 1], ss[:, t:t + 1], D_, [d_ss[t]], [d_rs[t]])
            C.act(hb[b][:], hb[b][:], AF.Copy, r=[d_hb[b], d_rs[t]], w=[d_hb[b]], scale=rs[:, t:t + 1])
            C.V(lambda e, b=b, o=o: e.tensor_tensor(out=ob[o][:], in0=hb[b][:], in1=gt[:], op=ALU.mult),
                r=[d_hb[b], d_gt], w=[d_ob[o]])
            tok = C.D(lambda e, t=t, o=o: e.dma_start(out=out_ap[t * 128:(t + 1) * 128, :], in_=ob[o][:]),
                      r=[d_ob[o]])
            C.out_toks.append(tok)
        C.release()


def layer_lru(C, W, h_in, d_hin, h_out, d_hout):
    NCH = 10
    with ExitStack() as L:
        uT = C.sb(L, [128, 8, S_], BF16); d_uT = deps(NT)
        norm_T(C, L, h_in, W["norm_g"], uT, d_uT, d_hin)
        d_uTb = [[d_uT[4 * tb + i] for i in range(4)] for tb in range(4)]
        goT = C.sb(L, [128, NCH, S_], BF16); d_go = deps(NT)
        cw = C.sb(L, [128, 4, NCH], F32); d_cw = Dep()
        for k in range(4):
            C.load_col(cw[:, k, :], W["lru_conv_w"][k, :], d_cw, NCH)
        cb = C.sb(L, [128, NCH], F32); ba = C.sb(L, [128, NCH], F32); bx = C.sb(L, [128, NCH], F32)
        lam = C.sb(L, [128, NCH], F32); d_p = Dep()
        C.load_col(cb[:], W["lru_conv_b"], d_p, NCH)
        C.load_col(ba[:], W["lru_b_a"], d_p, NCH)
        C.load_col(bx[:], W["lru_b_x"], d_p, NCH)
        C.load_col(lam[:], W["lru_lam"], d_p, NCH)
        cA = C.sb(L, [128, NCH], F32); cA2 = C.sb(L, [128, NCH], F32); d_cA = Dep()
        C.act(cA[:], lam[:], AF.Exp, r=[d_p], w=[d_cA], scale=-1.0)
        C.act(cA[:], cA[:], AF.Ln, r=[d_cA], w=[d_cA], bias=1.0)
        C.V(lambda e: e.tensor_scalar(out=cA2[:], in0=cA[:], scalar1=-16.0, scalar2=None, op0=ALU.mult),
            r=[d_cA], w=[d_cA])
        C.V(lambda e: e.tensor_scalar(out=cA[:], in0=cA[:], scalar1=-8.0, scalar2=None, op0=ALU.mult),
            r=[d_cA], w=[d_cA])
        wa = C.sb(L, [128, NCH, 128], BF16); wx = C.sb(L, [128, NCH, 128], BF16); d_wg = Dep()
        C.DG(lambda e: e.dma_start(out=wa[:], in_=W["lru_w_a"].rearrange("n i j -> i n j")), w=[d_wg])
        C.DG(lambda e: e.dma_start(out=wx[:], in_=W["lru_w_x"].rearrange("n i j -> i n j")), w=[d_wg])
        dg = C.sb(L, [128, NCH, 4, 128], BF16); d_dg = Dep()
        for c in range(NCH):
            for k in range(4):
                C.V(lambda e, c=c, k=k: e.tensor_scalar(out=dg[:, c, k, :], in0=C.ident_bf[:],
                                                        scalar1=cw[:, k, c:c + 1], scalar2=None, op0=ALU.mult),
                    r=[d_cw, C.d_const], w=[d_dg])
        wu = [C.sb(L, [128, 8, 128], BF16) for _ in range(2)]; d_wu = deps(2)
        wg = [C.sb(L, [128, 8, 128], BF16) for _ in range(2)]; d_wgt = deps(2)
        ub2 = [C.sb(L, [128, 3 + S_], BF16) for _ in range(2)]; d_ub2 = deps(2)
        for b_ in range(2):
            C.V(lambda e, b_=b_: e.memset(ub2[b_][:, 0:3], 0.0), w=[d_ub2[b_]])
        uc = C.sb(L, [128, S_], F32); d_uc = Dep()
        ucb = C.sb(L, [128, S_], BF16); d_ucb = Dep()
        rr = C.sb(L, [128, S_], F32); d_rr = Dep()
        ig = C.sb(L, [128, S_], F32); d_ig = Dep()
        tmp = C.sb(L, [128, S_], F32); d_tmp = Dep()
        hs = C.sb(L, [128, S_], BF16); d_hs = Dep()
        sg = C.sb(L, [128, S_], BF16); d_sg = Dep()
        w_in = W["lru_w_in"]

        def proj_u(c):
            b = c % 2
            C.load_w(wu[b][:], w_in[:, 1280 + c * 128:1280 + (c + 1) * 128], d_wu[b])
            C.load_w(wg[b][:], w_in[:, c * 128:(c + 1) * 128], d_wgt[b])
            for tb in range(4):
                bk = tb
                for kc in range(8):
                    C.mm(C.pb(bk), wu[b][:, kc, :], uT[:, kc, tb * 512:(tb + 1) * 512], kc == 0, kc == 7,
                         r=d_uTb[tb] + [d_wu[b]], w=[C.psd[bk]])
                C.act(ub2[b][:, 3 + tb * 512:3 + (tb + 1) * 512], C.pb(bk), AF.Copy, r=[C.psd[bk]], w=[d_ub2[b]])

        proj_u(0)
        for c in range(NCH):
            b = c % 2
            for tb in range(4):
                bk = 4 + tb
                for k in range(4):
                    C.mm(C.pb(bk), dg[:, c, k, :], ub2[b][:, tb * 512 + k:tb * 512 + k + 512], k == 0, k == 3,
                         r=[d_ub2[b], d_dg], w=[C.psd[bk]])
                C.act(uc[:, tb * 512:(tb + 1) * 512], C.pb(bk), AF.Identity, r=[C.psd[bk], d_p], w=[d_uc],
                      bias=cb[:, c:c + 1])
            C.V(lambda e: e.tensor_copy(out=ucb[:], in_=uc[:]), r=[d_uc], w=[d_ucb])
            for tb in range(4):
                bk = tb
                for kc in range(8):
                    C.mm(C.pb(bk), wg[b][:, kc, :], uT[:, kc, tb * 512:(tb + 1) * 512], kc == 0, kc == 7,
                         r=d_uTb[tb] + [d_wgt[b]], w=[C.psd[bk]])
                C.act(sg[:, tb * 512:(tb + 1) * 512], C.pb(bk), AF.Silu, r=[C.psd[bk]], w=[d_sg])
            for tb in range(4):
                bk = 4 + tb
                C.mm(C.pb(bk), wa[:, c, :], ucb[:, tb * 512:(tb + 1) * 512], True, True,
                     r=[d_ucb, d_wg], w=[C.psd[bk]])
                C.act(rr[:, tb * 512:(tb + 1) * 512], C.pb(bk), AF.Sigmoid, r=[C.psd[bk], d_p], w=[d_rr],
                      bias=ba[:, c:c + 1])
            for tb in range(4):
                bk = 4 + tb
                C.mm(C.pb(bk), wx[:, c, :], ucb[:, tb * 512:(tb + 1) * 512], True, True,
                     r=[d_ucb, d_wg], w=[C.psd[bk]])
                C.act(ig[:, tb * 512:(tb + 1) * 512], C.pb(bk), AF.Sigmoid, r=[C.psd[bk], d_p], w=[d_ig],
                      bias=bx[:, c:c + 1])
            if c + 1 < NCH:
                proj_u(c + 1)
            C.act(tmp[:], rr[:], AF.Exp, r=[d_rr, d_cA], w=[d_tmp], scale=cA2[:, c:c + 1])
            C.act(rr[:], rr[:], AF.Exp, r=[d_rr, d_cA], w=[d_rr], scale=cA[:, c:c + 1])
            C.act(tmp[:], tmp[:], AF.Sqrt, r=[d_tmp], w=[d_tmp], scale=-1.0, bias=1.0)
            C.V(lambda e: e.tensor_tensor(out=ig[:], in0=ig[:], in1=uc[:], op=ALU.mult), r=[d_ig, d_uc], w=[d_ig])
            C.V(lambda e: e.tensor_tensor(out=tmp[:], in0=tmp[:], in1=ig[:], op=ALU.mult), r=[d_tmp, d_ig], w=[d_tmp])
            C.V(lambda e: e.tensor_tensor_scan(out=hs[:], data0=rr[:], data1=tmp[:], initial=0.0,
                                               op0=ALU.mult, op1=ALU.add), r=[d_rr, d_tmp], w=[d_hs])
            C.G(lambda e, c=c: e.tensor_tensor(out=goT[:, c, :], in0=hs[:], in1=sg[:], op=ALU.mult),
                r=[d_hs, d_sg], w=d_go)
        out_proj(C, goT, d_go, NCH, W["lru_w_out"], h_in, d_hin, h_out, d_hout)
        C.release()


def layer_mla(C, W, h_in, d_hin, h_out, d_hout):
    H = 16
    PI = float(np.pi)
    with ExitStack() as L:
        cqnT = C.sb(L, [128, 3, S_], BF16); d_cq = deps(NT)
        ckvT = C.sb(L, [128, 2, S_], BF16); d_ckv = deps(NT)
        sgT = C.sb(L, [128, 8, S_], BF16); d_sg4 = deps(4)
        kr_all = C.sb(L, [128, NT, 32], F32); d_kr = Dep()
        w_in = W["mla_w_in"]
        wuq = C.sb(L, [128, 3, 1536], BF16); d_wuq = Dep()
        wukv = C.sb(L, [128, 2, 2048], BF16); d_wukv = Dep()
        cos_t = C.sb(L, [128, NT, 16], F32); sin_t = C.sb(L, [128, NT, 16], F32)
        qT = [C.sb(L, [128, S_], BF16) for _ in range(2)]; d_qT = deps(2)
        kT = [C.sb(L, [128, S_], BF16) for _ in range(2)]; d_kT = deps(2)
        Vh = [C.sb(L, [128, NT, 128], BF16) for _ in range(2)]; d_V = deps(2)
        swp = C.sb(L, [128, 128], F32); d_swp = Dep()
        qtok = C.sb(L, [128, NT, 96], F32); d_qtok = Dep()
        ta = C.sb(L, [128, NT, 16], F32); tb_ = C.sb(L, [128, NT, 16], F32)
        tc = C.sb(L, [128, NT, 16], F32); td = C.sb(L, [128, NT, 16], F32); d_tt = Dep()
        krr = C.sb(L, [128, NT, 96], F32); d_krr = Dep()
        QS = 96 ** -0.5
        tri = C.sb(L, [128, 128], BF16); d_k = Dep(); d_cs = Dep()

        def early_ops(TMPX):
            C.load_w(wuq[:], W["mla_w_uq"], d_wuq)
            C.load_w(wukv[:], W["mla_w_ukv"], d_wukv)
            C.D(lambda e: e.dma_start(out=swp[:], in_=C.CD["swap_f"]), w=[d_swp])
            for b_ in range(2):
                C.G(lambda e, b_=b_: e.memset(Vh[b_][:], 1.0), w=[d_V[b_]])
                C.G(lambda e, b_=b_: e.memset(qT[b_][:], 0.0), w=[d_qT[b_]])
                C.G(lambda e, b_=b_: e.memset(kT[b_][:], 0.0), w=[d_kT[b_]])
            invf = C.sb(TMPX, [128, 16], F32); posi = C.sb(TMPX, [128, NT], I32); d_k2 = Dep()
            C.D(lambda e: e.dma_start(out=tri[:], in_=C.CD["tri_bf"]), w=[d_k])
            C.D(lambda e: e.dma_start(out=invf[:], in_=C.CD["invf"]), w=[d_k2])
            C.D(lambda e: e.dma_start(out=posi[:], in_=C.pos), w=[d_k2])
            posf = C.sb(TMPX, [128, NT], F32); d_t = Dep()
            ang = C.sb(TMPX, [128, NT, 16], F32); tmpf = C.sb(TMPX, [128, NT, 16], F32)
            ki = C.sb(TMPX, [128, NT, 16], I32); kff = C.sb(TMPX, [128, NT, 16], F32)
            rr = C.sb(TMPX, [128, NT, 16], F32); yy = C.sb(TMPX, [128, NT, 16], F32); mm_ = C.sb(TMPX, [128, NT, 16], F32)
            C.V(lambda e: e.tensor_copy(out=posf[:], in_=posi[:]), r=[d_k2], w=[d_t])
            C.V(lambda e: e.tensor_tensor(out=ang[:], in0=posf[:].unsqueeze(2).to_broadcast([128, NT, 16]),
                                          in1=invf[:].unsqueeze(1).to_broadcast([128, NT, 16]), op=ALU.mult),
                r=[d_t, d_k2], w=[d_t])
            C.V(lambda e: e.tensor_scalar(out=tmpf[:], in0=ang[:], scalar1=1.0 / (2 * PI), scalar2=None, op0=ALU.mult),
                r=[d_t], w=[d_t])
            C.V(lambda e: e.tensor_copy(out=ki[:], in_=tmpf[:]), r=[d_t], w=[d_t])
            C.V(lambda e: e.tensor_copy(out=kff[:], in_=ki[:]), r=[d_t], w=[d_t])
            C1 = 6.28125
            C2 = float(2 * np.pi - 6.28125)
            C.V(lambda e: e.scalar_tensor_tensor(out=rr[:], in0=kff[:], scalar=-C1, in1=ang[:], op0=ALU.mult, op1=ALU.add),
                r=[d_t], w=[d_t])
            C.V(lambda e: e.scalar_tensor_tensor(out=rr[:], in0=kff[:], scalar=-C2, in1=rr[:], op0=ALU.mult, op1=ALU.add),
                r=[d_t], w=[d_t])
            for dst, shift in ((sin_t, 0.0), (cos_t, PI / 2)):
                C.V(lambda e, shift=shift: e.tensor_scalar(out=yy[:], in0=rr[:], scalar1=shift, scalar2=None, op0=ALU.add),
                    r=[d_t], w=[d_t])
                C.V(lambda e: e.tensor_scalar(out=mm_[:], in0=yy[:], scalar1=PI, scalar2=-2 * PI, op0=ALU.is_gt, op1=ALU.mult),
                    r=[d_t], w=[d_t])
                C.V(lambda e: e.tensor_tensor(out=yy[:], in0=yy[:], in1=mm_[:], op=ALU.add), r=[d_t], w=[d_t])
                C.V(lambda e: e.tensor_scalar(out=mm_[:], in0=yy[:], scalar1=-PI, scalar2=2 * PI, op0=ALU.is_lt, op1=ALU.mult),
                    r=[d_t], w=[d_t])
                C.V(lambda e: e.tensor_tensor(out=yy[:], in0=yy[:], in1=mm_[:], op=ALU.add), r=[d_t], w=[d_t])
                C.act(dst[:], yy[:], AF.Sin, r=[d_t], w=[d_cs, d_t])

        with ExitStack() as PA:
            uT = C.sb(PA, [128, 8, S_], BF16); d_uT = deps(NT)
            norm_T(C, PA, h_in, W["norm_g"], uT, d_uT, d_hin)
            early_ops(PA)
            d_uTb = [[d_uT[4 * tb + i] for i in range(4)] for tb in range(4)]
            w1 = C.sb(PA, [128, 8, 768], BF16); d_w1 = Dep()
            C.load_w(w1[:], w_in[:, 0:768], d_w1)
            gq = C.sb(PA, [128, 384], F32); gkv = C.sb(PA, [128, 256], F32); d_g = Dep()
            C.load_bc(gq[:], W["mla_g_q"], d_g)
            C.load_bc(gkv[:], W["mla_g_kv"], d_g)
            ctok = [C.sb(PA, [128, 672], F32) for _ in range(2)]; d_ct = deps(2)
            ss2 = C.sb(PA, [128, NT, 2], F32); rs2 = C.sb(PA, [128, NT, 2], F32)
            d_ss = deps(NT); d_rs = deps(NT)
            junk = C.sb(PA, [128, 384], BF16); d_junk = Dep()
            cn = [C.sb(PA, [128, 640], BF16) for _ in range(2)]; d_cn = deps(2)
            wgt = [C.sb(PA, [128, 8, 512], BF16) for _ in range(2)]; d_wgt = deps(2)
            for hf in range(2):
                C.load_w(wgt[hf][:], w_in[:, 672 + hf * 512:672 + (hf + 1) * 512], d_wgt[hf])
            for t in range(NT):
                b = t % 2
                for (c0, c1, bk) in ((0, 512, 2), (512, 672, 3)):
                    for kc in range(8):
                        C.mm(C.pb(bk, c1 - c0), uT[:, kc, t * 128:(t + 1) * 128], w1[:, kc, c0:c1], kc == 0, kc == 7,
                             r=[d_uT[t], d_w1], w=[C.psd[bk]])
                    C.act(ctok[b][:, c0:c1], C.pb(bk, c1 - c0), AF.Copy, r=[C.psd[bk]], w=[d_ct[b]])
                C.act(junk[:, 0:384], ctok[b][:, 0:384], AF.Square, r=[d_ct[b]], w=[d_junk, d_ss[t]],
                      accum_out=ss2[:, t, 0:1])
                C.act(junk[:, 0:256], ctok[b][:, 384:640], AF.Square, r=[d_ct[b]], w=[d_junk, d_ss[t]],
                      accum_out=ss2[:, t, 1:2])
                rstd_ops(C, rs2[:, t, 0:1], ss2[:, t, 0:1], 384, [d_ss[t]], [d_rs[t]])
                rstd_ops(C, rs2[:, t, 1:2], ss2[:, t, 1:2], 256, [d_ss[t]], [d_rs[t]])
                C.V(lambda e, t=t, b=b: e.scalar_tensor_tensor(out=cn[b][:, 0:384], in0=ctok[b][:, 0:384],
                                                               scalar=rs2[:, t, 0:1], in1=gq[:], op0=ALU.mult, op1=ALU.mult),
                    r=[d_ct[b], d_rs[t], d_g], w=[d_cn[b]])
                C.V(lambda e, t=t, b=b: e.scalar_tensor_tensor(out=cn[b][:, 384:640], in0=ctok[b][:, 384:640],
                                                               scalar=rs2[:, t, 1:2], in1=gkv[:], op0=ALU.mult, op1=ALU.mult),
                    r=[d_ct[b], d_rs[t], d_g], w=[d_cn[b]])
                C.G(lambda e, t=t, b=b: e.tensor_copy(out=kr_all[:, t, :], in_=ctok[b][:, 640:672]),
                    r=[d_ct[b]], w=[d_kr])
                bk = 4 + b
                pv = C.pbh(bk)
                for c in range(5):
                    C.tr(pv[:, c * 128:(c + 1) * 128], cn[b][:, c * 128:(c + 1) * 128], C.ident_bf[:],
                         r=[d_cn[b], C.d_const], w=[C.psd[bk]])
                C.V(lambda e, t=t, pv=pv: e.tensor_copy(out=cqnT[:, :, t * 128:(t + 1) * 128],
                                                       in_=pv[:, 0:384].rearrange("p (c n) -> p c n", c=3)),
                    r=[C.psd[bk]], w=[d_cq[t]])
                C.V(lambda e, t=t, pv=pv: e.tensor_copy(out=ckvT[:, :, t * 128:(t + 1) * 128],
                                                       in_=pv[:, 384:640].rearrange("p (c n) -> p c n", c=2)),
                    r=[C.psd[bk]], w=[d_ckv[t]])
            C.dbg("ctok", ctok[1][:], [128, 672], F32, [d_ct[1]])
            C.dbg("w1", w1[:], [128, 8, 768], BF16, [d_w1])
            C.dbg("cn", cn[1][:], [128, 640], BF16, [d_cn[1]])
            C.dbg("rs2", rs2[:], [128, NT, 2], F32, d_rs)
            C.G(lambda e: e.memset(krr[:], 0.0), w=[d_krr])
            k1 = kr_all[:, :, 0:16]; k2 = kr_all[:, :, 16:32]
            C.V(lambda e: e.tensor_tensor(out=ta[:], in0=k1, in1=cos_t[:], op=ALU.mult), r=[d_kr, d_cs], w=[d_tt])
            C.V(lambda e: e.tensor_tensor(out=tb_[:], in0=k2, in1=sin_t[:], op=ALU.mult), r=[d_kr, d_cs], w=[d_tt])
            C.V(lambda e: e.tensor_tensor(out=krr[:, :, 64:80], in0=ta[:], in1=tb_[:], op=ALU.subtract), r=[d_tt], w=[d_krr])
            C.V(lambda e: e.tensor_tensor(out=tc[:], in0=k2, in1=cos_t[:], op=ALU.mult), r=[d_kr, d_cs], w=[d_tt])
            C.V(lambda e: e.tensor_tensor(out=td[:], in0=k1, in1=sin_t[:], op=ALU.mult), r=[d_kr, d_cs], w=[d_tt])
            C.V(lambda e: e.tensor_tensor(out=krr[:, :, 80:96], in0=tc[:], in1=td[:], op=ALU.add), r=[d_tt], w=[d_krr])
            for t4 in range(4):
                bk = 7
                for i in range(4):
                    C.tr(C.pb(bk)[0:96, i * 128:(i + 1) * 128], krr[:, t4 * 4 + i, :], C.ident_f[:],
                         r=[d_krr, C.d_const], w=[C.psd[bk]])
                for b in range(2):
                    C.V(lambda e, b=b, t4=t4, bk=bk: e.tensor_copy(out=kT[b][64:96, t4 * 512:(t4 + 1) * 512],
                                                                 in_=C.pb(bk)[64:96, :]),
                        r=[C.psd[bk]], w=[d_kT[b]])
            pcnt = [0]

            def proj_units(h, b):
                for t4 in range(4):
                    bk = 6 + pcnt[0] % 2; pcnt[0] += 1
                    for i in range(4):
                        t = t4 * 4 + i
                        for kc in range(3):
                            C.mm(C.pb(bk)[:, i * 96:(i + 1) * 96], cqnT[:, kc, t * 128:(t + 1) * 128],
                                 wuq[:, kc, h * 96:(h + 1) * 96], kc == 0, kc == 2,
                                 r=[d_cq[t], d_wuq], w=[C.psd[bk]])
                    C.V(lambda e, t4=t4, bk=bk: e.tensor_scalar(
                        out=qtok[:, t4 * 4:(t4 + 1) * 4, :], in0=C.pb(bk)[:, 0:384].rearrange("p (a c) -> p a c", a=4),
                        scalar1=QS, scalar2=None, op0=ALU.mult), r=[C.psd[bk]], w=[d_qtok])
                    yield
                q1 = qtok[:, :, 64:80]; q2 = qtok[:, :, 80:96]
                C.V(lambda e: e.tensor_tensor(out=ta[:], in0=q1, in1=cos_t[:], op=ALU.mult), r=[d_qtok, d_cs], w=[d_tt])
                C.V(lambda e: e.tensor_tensor(out=tb_[:], in0=q2, in1=sin_t[:], op=ALU.mult), r=[d_qtok, d_cs], w=[d_tt])
                C.V(lambda e: e.tensor_tensor(out=tc[:], in0=q2, in1=cos_t[:], op=ALU.mult), r=[d_qtok, d_cs], w=[d_tt])
                C.V(lambda e: e.tensor_tensor(out=td[:], in0=q1, in1=sin_t[:], op=ALU.mult), r=[d_qtok, d_cs], w=[d_tt])
                C.V(lambda e: e.tensor_tensor(out=q1, in0=ta[:], in1=tb_[:], op=ALU.subtract), r=[d_tt], w=[d_qtok])
                C.V(lambda e: e.tensor_tensor(out=q2, in0=tc[:], in1=td[:], op=ALU.add), r=[d_tt], w=[d_qtok])
                yield
                for t4 in range(4):
                    bk = 6 + pcnt[0] % 2; pcnt[0] += 1
                    for i in range(4):
                        C.tr(C.pb(bk)[0:96, i * 128:(i + 1) * 128], qtok[:, t4 * 4 + i, :], C.ident_f[:],
                             r=[d_qtok, C.d_const], w=[C.psd[bk]])
                    C.V(lambda e, t4=t4, bk=bk: e.tensor_copy(out=qT[b][0:96, t4 * 512:(t4 + 1) * 512],
                                                            in_=C.pb(bk)[0:96, :]), r=[C.psd[bk]], w=[d_qT[b]])
                    yield
                for tb in range(4):
                    bk = 6 + pcnt[0] % 2; pcnt[0] += 1
                    for kc in range(2):
                        C.mm(C.pb(bk)[0:64, :], wukv[:, kc, h * 128:h * 128 + 64], ckvT[:, kc, tb * 512:(tb + 1) * 512],
                             kc == 0, kc == 1, r=[d_ckv[4 * tb + i] for i in range(4)] + [d_wukv], w=[C.psd[bk]])
                    C.V(lambda e, tb=tb, bk=bk: e.tensor_copy(out=kT[b][0:64, tb * 512:(tb + 1) * 512],
                                                            in_=C.pb(bk)[0:64, :]), r=[C.psd[bk]], w=[d_kT[b]])
                    yield
                for t8 in range(2):
                    bk = 6 + pcnt[0] % 2; pcnt[0] += 1
                    for i in range(8):
                        t = t8 * 8 + i
                        for kc in range(2):
                            C.mm(C.pb(bk)[:, i * 64:(i + 1) * 64], ckvT[:, kc, t * 128:(t + 1) * 128],
                                 wukv[:, kc, h * 128 + 64:h * 128 + 128], kc == 0, kc == 1,
                                 r=[d_ckv[t], d_wukv], w=[C.psd[bk]])
                    C.V(lambda e, t8=t8, bk=bk: e.tensor_copy(out=Vh[b][:, t8 * 8:(t8 + 1) * 8, b * 64:b * 64 + 64],
                                                            in_=C.pb(bk).rearrange("p (a c) -> p a c", a=8)),
                        r=[C.psd[bk]], w=[d_V[b]])
                    yield

            units0 = proj_units(0, 0)
            i = 0
            for c in range(8):
                for tb in range(4):
                    bk = (0, 1, 2, 3)[i % 4]; i += 1
                    for kc in range(8):
                        C.mm(C.pb(bk), wgt[c // 4][:, kc, (c % 4) * 128:(c % 4 + 1) * 128],
                             uT[:, kc, tb * 512:(tb + 1) * 512], kc == 0, kc == 7,
                             r=d_uTb[tb] + [d_wgt[c // 4]], w=[C.psd[bk]])
                    C.act(sgT[:, c, tb * 512:(tb + 1) * 512], C.pb(bk), AF.Silu, r=[C.psd[bk]], w=[d_sg4[tb]])
                    next(units0, None)
            for _ in units0:
                pass
            C.release()
        C.dbg("cqnT", cqnT[:], [128, 3, S_], BF16, d_cq)
        C.dbg("ckvT", ckvT[:], [128, 2, S_], BF16, d_ckv)
        C.dbg("sgT", sgT[:], [128, 8, S_], BF16, d_sg4)
        C.dbg("kr", kr_all[:], [128, NT, 32], F32, [d_kr])
        with ExitStack() as PB:
            wo = C.sb(PB, [128, 8, D_], BF16); d_wo = Dep()
            C.load_w(wo[:, 0:4, :], W["mla_w_out"][0:512, :], d_wo)
            C.load_w(wo[:, 4:8, :], W["mla_w_out"][512:1024, :], d_wo)
            lnd = [C.sb(PB, [128, 512], F32) for _ in range(2)]; rden = [C.sb(PB, [128, 512], F32) for _ in range(2)]
            ot = [C.sb(PB, [128, 512], F32) for _ in range(2)]
            d_nrm = deps(2)
            for b_ in range(2):
                C.G(lambda e, b_=b_: e.memset(rden[b_][:], 1.0), w=[d_nrm[b_]])
            NPT = 3
            SB = (0, 1, 2)
            PT = [C.sb(PB, [128, 512], BF16) for _ in range(NPT)]; d_PT = deps(NPT)
            d_ot = deps(2)

            items = []
            for h in range(H):
                for qc in range(4):
                    nj = 4 * qc + 4
                    for j in range(nj):
                        items.append((h, qc, j, nj))

            def stage_a(i):
                h, qc, j, nj = items[i]
                b = h % 2
                col0 = max(0, j * 128 - qc * 512)
                bS = SB[i % NPT]
                pt = PT[i % NPT]; d_pt = d_PT[i % NPT]
                C.mm(C.pb(bS)[:, col0:512], kT[b][:, j * 128:(j + 1) * 128],
                     qT[b][:, qc * 512 + col0:(qc + 1) * 512], True, True,
                     r=[d_kT[b], d_qT[b]], w=[C.psd[bS]])
                C.act(pt[:, col0:512], C.pb(bS)[:, col0:512], AF.Exp, r=[C.psd[bS]], w=[d_pt])
                if j >= 4 * qc:
                    C.G(lambda e, pt=pt, col0=col0: e.tensor_tensor(out=pt[:, col0:col0 + 128],
                                                                  in0=pt[:, col0:col0 + 128], in1=tri[:], op=ALU.mult),
                        r=[d_pt, d_k], w=[d_pt])

            def stage_b(i):
                h, qc, j, nj = items[i]
                b = h % 2
                r0 = (h % 2) * 64
                rd = 64 - r0
                c = h // 2
                n = h * 4 + qc
                bO = 3 + n % 2
                col0 = max(0, j * 128 - qc * 512)
                pt = PT[i % NPT]; d_pt = d_PT[i % NPT]
                C.mm(C.pb(bO)[:, col0:512], Vh[b][:, j, :], pt[:, col0:512], j == 0, j == nj - 1,
                     r=[d_V[b], d_pt], w=[C.psd[bO]])
                if j == nj - 1:
                    nb = n % 2
                    bW = 5
                    rs = slice(r0, r0 + 64); rq = slice(rd, rd + 64)
                    C.act(lnd[nb][rq, :], C.pb(bO)[rq, :], AF.Ln, r=[C.psd[bO]], w=[d_nrm[nb]])
                    C.act(rden[nb][rq, :], lnd[nb][rq, :], AF.Exp, r=[d_nrm[nb]], w=[d_nrm[nb]], scale=-1.0)
                    def fin(rs=rs, rq=rq, rd=rd, bO=bO, bW=bW, nb=nb, c=c, qc=qc):
                        C.mm(C.pb(bW), swp[:], rden[nb][:], True, True,
                             r=[d_swp, d_nrm[nb]], w=[C.psd[bW]])
                        C.V(lambda e: e.tensor_tensor(
                            out=ot[nb][rs, :], in0=C.pb(bO)[rs, :], in1=sgT[rs, c, qc * 512:(qc + 1) * 512], op=ALU.mult),
                            r=[C.psd[bO], d_sg4[qc]], w=[d_ot[nb]])
                        C.V(lambda e: e.tensor_tensor(
                            out=sgT[rs, c, qc * 512:(qc + 1) * 512], in0=ot[nb][rs, :], in1=C.pb(bW)[rs, :], op=ALU.mult),
                            r=[d_ot[nb], C.psd[bW]], w=[d_sg4[qc]])
                    pending.append([2, fin])

            pending = []

            def run_pending(force=False):
                for p in list(pending):
                    p[0] -= 1
                    if force or p[0] <= 0:
                        p[1]()
                        pending.remove(p)

            NI = len(items)
            units = None
            LOOK = 2
            for i in range(min(LOOK, NI)):
                stage_a(i)
            for i in range(NI):
                h, qc, j, nj = items[i]
                if qc == 0 and j == 0 and h + 1 < H:
                    units = proj_units(h + 1, (h + 1) % 2)
                if i + LOOK < NI:
                    if items[i + LOOK][0] != h and units is not None:
                        for _ in units:
                            pass
                        units = None
                    stage_a(i + LOOK)
                run_pending()
                stage_b(i)
                if units is not None:
                    if next(units, "done") == "done":
                        units = None
            run_pending(force=True)
            d_go = [d_sg4[t // 4] for t in range(NT)]
            out_proj(C, sgT, d_go, 8, None, h_in, d_hin, h_out, d_hout, banks=(0, 1, 2, 7), wo_pre=(wo, d_wo))
            C.release()
        C.release()


LAYER_FNS[0] = layer_mla

def layer_gla(C, W, h_in, d_hin, h_out, d_hout):
    w_in = W["gla_w_in"]
    with ExitStack() as L:
        qtT = C.sb(L, [128, 4, S_], BF16); d_qt = deps(4)
        ktT = C.sb(L, [128, 4, S_], BF16); d_kt = deps(4)
        vtok = C.sb(L, [128, NT, 1024], BF16); d_v = deps(NT)
        gsg = C.sb(L, [128, NT, 1024], BF16); d_gsg = deps(NT)
        elast = C.sb(L, [128, 4, 32], F32); d_el = Dep()
        with ExitStack() as PA:
            uT = C.sb(PA, [128, 8, S_], BF16); d_uT = deps(NT)
            norm_T(C, PA, h_in, W["norm_g"], uT, d_uT, d_hin)
            d_uTb = [[d_uT[4 * tb + i] for i in range(4)] for tb in range(4)]
            wgk = C.sb(PA, [128, 8, 16], BF16); d_wgk = Dep()
            C.load_w(wgk[:], w_in[:, 3072:3088], d_wgk)
            gkT = C.sb(PA, [32, S_], BF16); d_gkT = Dep()
            wgk2 = C.sb(PA, [32, 512], BF16); d_wgk2 = Dep()
            C.V(lambda e: e.memset(gkT[:], 0.0), w=[d_gkT])
            C.V(lambda e: e.memset(wgk2[:], 0.0), w=[d_wgk2])
            C.DG(lambda e: e.dma_start(out=wgk2[0:16, :], in_=W["gla_w_gk2"]), w=[d_wgk2])
            for tb in range(4):
                bk = 2 + tb % 2
                for kc in range(8):
                    C.mm(C.pb(bk)[0:16, :], wgk[:, kc, :], uT[:, kc, tb * 512:(tb + 1) * 512], kc == 0, kc == 7,
                         r=d_uTb[tb] + [d_wgk], w=[C.psd[bk]])
                C.V(lambda e, tb=tb, bk=bk: e.tensor_copy(out=gkT[0:16, tb * 512:(tb + 1) * 512], in_=C.pb(bk)[0:16, :]),
                    r=[C.psd[bk]], w=[d_gkT])
            nbgk = C.sb(PA, [128, 4], F32); d_nb = Dep()
            C.load_col(nbgk[:], W["gla_b_gk"], d_nb, 4)
            C.V(lambda e: e.tensor_scalar(out=nbgk[:], in0=nbgk[:], scalar1=-1.0, scalar2=None, op0=ALU.mult),
                r=[d_nb], w=[d_nb])
            rmask = C.sb(PA, [128, S_], BF16); d_rm = Dep()
            C.D(lambda e: e.dma_start(out=rmask[:], in_=C.CD["rmask"]), w=[d_rm])
            Lh = C.sb(PA, [128, S_], F32); d_Lh = Dep()
            Bp = C.sb(PA, [128, S_], F32); d_Bp = Dep()
            wq = [C.sb(PA, [128, 8, 128], BF16) for _ in range(2)]; d_wq = deps(2)
            wk = [C.sb(PA, [128, 8, 128], BF16) for _ in range(2)]; d_wk = deps(2)
            QS = 128 ** -0.5
            for h in range(4):
                b = h % 2
                C.load_w(wq[b][:], w_in[:, h * 128:(h + 1) * 128], d_wq[b])
                C.load_w(wk[b][:], w_in[:, 512 + h * 128:512 + (h + 1) * 128], d_wk[b])
                for tb in range(4):
                    bk = 2 + tb % 2
                    C.mm(C.pb(bk), wgk2[:, h * 128:(h + 1) * 128], gkT[:, tb * 512:(tb + 1) * 512], True, True,
                         r=[d_wgk2, d_gkT], w=[C.psd[bk]])
                    C.act(Lh[:, tb * 512:(tb + 1) * 512], C.pb(bk), AF.Exp, r=[C.psd[bk], d_nb], w=[d_Lh],
                          scale=-1.0, bias=nbgk[:, h:h + 1])
                C.act(Lh[:], Lh[:], AF.Ln, r=[d_Lh], w=[d_Lh], bias=1.0)
                C.V(lambda e: e.tensor_tensor_scan(out=Bp[:], data0=rmask[:], data1=Lh[:], initial=0.0,
                                                   op0=ALU.mult, op1=ALU.add), r=[d_Lh, d_rm], w=[d_Bp])
                C.act(Lh[:], Bp[:], AF.Exp, r=[d_Bp], w=[d_Lh], scale=-1.0 / 16)
                C.act(Bp[:], Bp[:], AF.Exp, r=[d_Bp], w=[d_Bp], scale=1.0 / 16)
                C.G(lambda e, h=h: e.tensor_copy(out=elast[:, h, :],
                                                 in_=Lh[:].rearrange("p (n c) -> p n c", c=64)[:, :, 63]),
                    r=[d_Lh], w=[d_el])
                for tb in range(4):
                    bq = 4 + tb % 2; bkk = 6 + tb % 2
                    for kc in range(8):
                        C.mm(C.pb(bq), wq[b][:, kc, :], uT[:, kc, tb * 512:(tb + 1) * 512], kc == 0, kc == 7,
                             r=d_uTb[tb] + [d_wq[b]], w=[C.psd[bq]])
                    C.V(lambda e, h=h, tb=tb, bq=bq: e.scalar_tensor_tensor(
                        out=qtT[:, h, tb * 512:(tb + 1) * 512], in0=C.pb(bq), scalar=QS,
                        in1=Lh[:, tb * 512:(tb + 1) * 512], op0=ALU.mult, op1=ALU.mult),
                        r=[C.psd[bq], d_Lh], w=[d_qt[tb]])
                    for kc in range(8):
                        C.mm(C.pb(bkk), wk[b][:, kc, :], uT[:, kc, tb * 512:(tb + 1) * 512], kc == 0, kc == 7,
                             r=d_uTb[tb] + [d_wk[b]], w=[C.psd[bkk]])
                    C.V(lambda e, h=h, tb=tb, bkk=bkk: e.tensor_tensor(
                        out=ktT[:, h, tb * 512:(tb + 1) * 512], in0=C.pb(bkk), in1=Bp[:, tb * 512:(tb + 1) * 512],
                        op=ALU.mult), r=[C.psd[bkk], d_Bp], w=[d_kt[tb]])
            wv = [C.sb(PA, [128, 8, 512], BF16) for _ in range(2)]; d_wv = deps(2)
            for i in range(2):
                C.load_w(wv[i][:], w_in[:, 1024 + i * 512:1024 + (i + 1) * 512], d_wv[i])
            i = 0
            for t in range(NT):
                for nb in range(2):
                    bk = (0, 1, 2, 3)[i % 4]; i += 1
                    for kc in range(8):
                        C.mm(C.pb(bk), uT[:, kc, t * 128:(t + 1) * 128], wv[nb][:, kc, :], kc == 0, kc == 7,
                             r=[d_uT[t], d_wv[nb]], w=[C.psd[bk]])
                    C.act(vtok[:, t, nb * 512:(nb + 1) * 512], C.pb(bk), AF.Copy, r=[C.psd[bk]], w=[d_v[t]])
            for i in range(2):
                C.load_w(wv[i][:], w_in[:, 2048 + i * 512:2048 + (i + 1) * 512], d_wv[i])
            go_bc = C.sb(PA, [128, 256], F32); d_gob = Dep()
            C.load_bc(go_bc[:], W["gla_g_o"], d_gob)
            sgt = [C.sb(PA, [128, 1024], F32) for _ in range(2)]; d_sgt = deps(2)
            i = 0
            for t in range(NT):
                b = t % 2
                for nb in range(2):
                    bk = (4, 5, 6, 7)[i % 4]; i += 1
                    for kc in range(8):
                        C.mm(C.pb(bk), uT[:, kc, t * 128:(t + 1) * 128], wv[nb][:, kc, :], kc == 0, kc == 7,
                             r=[d_uT[t], d_wv[nb]], w=[C.psd[bk]])
                    C.act(sgt[b][:, nb * 512:(nb + 1) * 512], C.pb(bk), AF.Silu, r=[C.psd[bk]], w=[d_sgt[b]])
                C.G(lambda e, t=t, b=b: e.tensor_tensor(
                    out=gsg[:, t, :].rearrange("p (a c) -> p a c", a=4), in0=sgt[b][:].rearrange("p (a c) -> p a c", a=4),
                    in1=go_bc[:].unsqueeze(1).to_broadcast([128, 4, 256]), op=ALU.mult),
                    r=[d_sgt[b], d_gob], w=[d_gsg[t]])
            C.release()
        with ExitStack() as PB:
            goT = C.sb(PB, [128, 8, S_], BF16); d_go = deps(NT)
            wo = C.sb(PB, [128, 8, D_], BF16); d_wo = Dep()
            C.load_w(wo[:, 0:4, :], W["gla_w_out"][0:512, :], d_wo)
            C.load_w(wo[:, 4:8, :], W["gla_w_out"][512:1024, :], d_wo)
            mask2 = C.sb(PB, [128, 128], BF16); d_m2 = Dep()
            C.D(lambda e: e.dma_start(out=mask2[:], in_=C.CD["mask2"]), w=[d_m2])
            tmpS = C.sb(PB, [128, 4, 256], F32); d_tmpS = Dep()
            Sbf = [C.sb(PB, [128, 4, 256], BF16) for _ in range(2)]; d_Sbf = deps(2)
            C.V(lambda e: e.memset(Sbf[0][:], 0.0), w=[d_Sbf[0]])
            attm = [C.sb(PB, [128, 4, 128], BF16) for _ in range(2)]; d_attm = deps(2)
            ktk = [C.sb(PB, [128, 4, 128], BF16) for _ in range(2)]; d_ktk = deps(2)
            ss = C.sb(PB, [128, NT, 4], F32); rs = C.sb(PB, [128, NT, 4], F32); d_ss = deps(NT); d_rs = deps(NT)
            junk = C.sb(PB, [128, 256], BF16); d_junk = Dep()
            gotok = [C.sb(PB, [128, 1024], BF16) for _ in range(2)]; d_gotok = deps(2)
            osb = [C.sb(PB, [128, 1024], F32) for _ in range(2)]; d_osb = deps(2)

            def obank(t, h):
                return (2 + 2 * (t % 2) + h // 2, (h % 2) * 256)

            def g0(t):
                b = t % 2
                tb = t // 4
                tsl = slice(t * 128, (t + 1) * 128)
                pv = C.pbh(0)
                for h in range(4):
                    C.tr(pv[:, h * 128:(h + 1) * 128], ktT[:, h, tsl], C.ident_bf[:], r=[d_kt[tb], C.d_const], w=[C.psd[0]])
                for h in range(4):
                    C.mm(C.pb(1)[:, h * 128:(h + 1) * 128], ktT[:, h, tsl], qtT[:, h, tsl], True, True,
                         r=[d_kt[tb], d_qt[tb]], w=[C.psd[1]])
                yield
                C.V(lambda e, b=b, pv=pv: e.tensor_copy(out=ktk[b][:], in_=pv[:, 0:512].rearrange("p (a c) -> p a c", a=4)),
                    r=[C.psd[0]], w=[d_ktk[b]])
                C.V(lambda e, b=b: e.tensor_tensor(out=attm[b][:], in0=C.pb(1).rearrange("p (a c) -> p a c", a=4),
                                                   in1=mask2[:].unsqueeze(1).to_broadcast([128, 4, 128]), op=ALU.mult),
                    r=[C.psd[1], d_m2], w=[d_attm[b]])
                yield
                for h in range(4):
                    bk, c0 = obank(t, h)
                    C.T(lambda e, bk=bk, c0=c0, b=b, h=h, t=t: e.matmul(
                        C.pb(bk)[:, c0:c0 + 256], lhsT=attm[b][:, h, :], rhs=vtok[:, t, h * 256:(h + 1) * 256],
                        start=(h % 2 == 0), stop=False, skip_group_check=True), r=[d_attm[b], d_v[t]], w=[C.psd[bk]])

            def g1(t):
                b = t % 2
                tb = t // 4
                for half in range(2):
                    r0 = half * 64
                    n = 2 * t + half
                    for h in range(4):
                        bk, c0 = obank(t, h)
                        C.T(lambda e, bk=bk, c0=c0, h=h, t=t, r0=r0, half=half: e.matmul(
                            C.pb(bk)[r0:r0 + 64, c0:c0 + 256], lhsT=qtT[:, h, t * 128 + r0:t * 128 + r0 + 64],
                            rhs=Sbf[half][:, h, :], start=False, stop=(half == 1), skip_group_check=True),
                            r=[d_qt[tb], d_Sbf[half]], w=[C.psd[bk]])
                    for h in range(4):
                        bkS = 6 + h // 2
                        C.mm(C.pb(bkS)[:, (h % 2) * 256:(h % 2 + 1) * 256], ktk[b][r0:r0 + 64, h, :],
                             vtok[r0:r0 + 64, t, h * 256:(h + 1) * 256], True, True,
                             r=[d_ktk[b], d_v[t]], w=[C.psd[bkS]])
                    C.V(lambda e, half=half: e.tensor_tensor(out=tmpS[:], in0=C.ps[:, 6 * 512:8 * 512].rearrange("p (a c) -> p a c", a=4),
                                                            in1=Sbf[half][:], op=ALU.add), r=[C.psd[6], C.psd[7], d_Sbf[half]], w=[d_tmpS])
                    C.V(lambda e, n=n, half=half: e.tensor_tensor(out=Sbf[1 - half][:], in0=tmpS[:],
                                                                 in1=elast[:, :, n:n + 1].to_broadcast([128, 4, 256]), op=ALU.mult),
                        r=[d_tmpS, d_el], w=[d_Sbf[1 - half]])
                    yield
                for i_ in range(2):
                    bk = 2 + 2 * (t % 2) + i_
                    C.act(osb[b][:, i_ * 512:(i_ + 1) * 512], C.pb(bk), AF.Copy, r=[C.psd[bk]], w=[d_osb[b]])

            def g2(t):
                b = t % 2
                for h in range(4):
                    C.act(junk[:], osb[b][:, h * 256:(h + 1) * 256], AF.Square, r=[d_osb[b]], w=[d_junk, d_ss[t]],
                          accum_out=ss[:, t, h:h + 1])
                yield
                rstd_ops(C, rs[:, t, :], ss[:, t, :], 256, [d_ss[t]], [d_rs[t]])
                yield
                for h in range(4):
                    C.V(lambda e, t=t, b=b, h=h: e.scalar_tensor_tensor(
                        out=gotok[b][:, h * 256:(h + 1) * 256], in0=osb[b][:, h * 256:(h + 1) * 256], scalar=rs[:, t, h:h + 1],
                        in1=gsg[:, t, h * 256:(h + 1) * 256], op0=ALU.mult, op1=ALU.mult),
                        r=[d_osb[b], d_rs[t], d_gsg[t]], w=[d_gotok[b]])
                yield
                pv = C.pbh(0)
                for c in range(8):
                    C.tr(pv[:, c * 128:(c + 1) * 128], gotok[b][:, c * 128:(c + 1) * 128], C.ident_bf[:],
                         r=[d_gotok[b], C.d_const], w=[C.psd[0]])
                C.act(goT[:, :, t * 128:(t + 1) * 128], pv.rearrange("p (c n) -> p c n", c=8), AF.Copy,
                      r=[C.psd[0]], w=[d_go[t]])

            LV = {1: ("s1", "s0", "s2"), 2: ("s0", "s1", "s2"), 3: ("s1", "s0", "s2"), 4: ("s2",)}
            for s_ in range(NT + 2):
                gens = {}
                if 0 <= s_ - 2 < NT:
                    gens["s2"] = g2(s_ - 2)
                if 0 <= s_ - 1 < NT:
                    gens["s1"] = g1(s_ - 1)
                if s_ < NT:
                    gens["s0"] = g0(s_)
                for lvl in (1, 2, 3, 4):
                    for nm in LV[lvl]:
                        if nm in gens:
                            next(gens[nm], None)
                for gn_ in gens.values():
                    for _ in gn_:
                        pass
            out_proj(C, goT, d_go, 8, None, h_in, d_hin, h_out, d_hout, banks=(1, 2, 3, 4), wo_pre=(wo, d_wo))
            C.release()
        C.release()


LAYER_FNS[1] = layer_gla

def layer_ssd(C, W, h_in, d_hin, h_out, d_hout):
    w_in = W["ssd_w_in"]
    G_ = 8
    with ExitStack() as L:
        goT = C.sb(L, [128, 16, S_], BF16); d_go = deps(NT)
        with ExitStack() as PA:
            uT = C.sb(PA, [128, 8, S_], BF16); d_uT = deps(NT)
            norm_T(C, PA, h_in, W["norm_g"], uT, d_uT, d_hin)
            d_uTb = [[d_uT[4 * tb + i] for i in range(4)] for tb in range(4)]
            U2b = C.sb(PA, [128, 128], BF16); V2b = C.sb(PA, [128, 128], BF16)
            U2f = C.sb(PA, [128, 128], F32); V2f = C.sb(PA, [128, 128], F32)
            onA = C.sb(PA, [128, 128], F32); onB = C.sb(PA, [128, 128], F32); d_k = Dep()
            negm = C.sb(PA, [128, 512], BF16)
            C.D(lambda e: e.dma_start(out=negm[:], in_=C.CD["negm4"]), w=[d_k])
            for dst, nm in ((U2b, "U2b"), (V2b, "mask2"), (U2f, "U2f"), (V2f, "V2f"), (onA, "onesA"), (onB, "onesB")):
                C.D(lambda e, dst=dst, nm=nm: e.dma_start(out=dst[:], in_=C.CD[nm]), w=[d_k])
            dtb = C.sb(PA, [128, 32], F32); alog = C.sb(PA, [128, 32], F32); dsk = C.sb(PA, [128, 32], F32)
            gn = C.sb(PA, [128, 2048], F32); d_p = Dep()
            C.load_bc(dtb[:], W["ssd_dt_bias"], d_p)
            C.load_bc(alog[:], W["ssd_a_log"], d_p)
            C.load_bc(dsk[:], W["ssd_d"], d_p)
            C.load_bc(gn[:], W["ssd_g_norm"], d_p)
            cw = C.sb(PA, [128, 4, 32], F32); cb = C.sb(PA, [128, 32], F32); d_cw = Dep()
            for k in range(4):
                C.load_col(cw[:, k, :], W["ssd_conv_w"][k, :], d_cw, 32)
            C.load_col(cb[:], W["ssd_conv_b"], d_cw, 32)
            wdt = C.sb(PA, [128, 8, 32], BF16); d_wdt = Dep()
            C.load_w(wdt[:], w_in[:, 6144:6176], d_wdt)
            dtt = C.sb(PA, [128, NT, 32], F32); a_tok = C.sb(PA, [128, NT, 32], F32)
            ecs = C.sb(PA, [128, NT, 32], F32); dec = C.sb(PA, [128, NT, 32], F32)
            elast = C.sb(PA, [128, 32, 32], F32); eA = C.sb(PA, [128, 32], F32)
            d_dt = Dep(); d_a = Dep(); d_ecs = Dep(); d_dec = Dep(); d_el = Dep()
            for t in range(NT):
                for kc in range(8):
                    C.mm(C.pb(0)[:, t * 32:(t + 1) * 32], uT[:, kc, t * 128:(t + 1) * 128], wdt[:, kc, :], kc == 0, kc == 7,
                         r=[d_uT[t], d_wdt], w=[C.psd[0]])
            C.V(lambda e: e.tensor_tensor(out=dtt[:], in0=C.pb(0).rearrange("p (a c) -> p a c", a=NT),
                                          in1=dtb[:].unsqueeze(1).to_broadcast([128, NT, 32]), op=ALU.add),
                r=[C.psd[0], d_p], w=[d_dt])
            C.act(dtt[:], dtt[:], AF.Exp, r=[d_dt], w=[d_dt])
            C.act(dtt[:], dtt[:], AF.Ln, r=[d_dt], w=[d_dt], bias=1.0)
            C.act(eA[:], alog[:], AF.Exp, r=[d_p], w=[d_a])
            C.V(lambda e: e.scalar_tensor_tensor(out=a_tok[:], in0=dtt[:], scalar=-1.0,
                                                 in1=eA[:].unsqueeze(1).to_broadcast([128, NT, 32]),
                                                 op0=ALU.mult, op1=ALU.mult), r=[d_dt, d_a], w=[d_a])
            for t in range(NT):
                C.mm(C.pb(1)[:, t * 32:(t + 1) * 32], V2f[:], a_tok[:, t, :], True, True, r=[d_k, d_a], w=[C.psd[1]])
            C.act(ecs[:], C.pb(1).rearrange("p (a c) -> p a c", a=NT), AF.Exp, r=[C.psd[1]], w=[d_ecs])
            for t in range(NT):
                C.mm(C.pb(2)[:, t * 32:(t + 1) * 32], U2f[:], a_tok[:, t, :], True, True, r=[d_k, d_a], w=[C.psd[2]])
            C.act(dec[:], C.pb(2).rearrange("p (a c) -> p a c", a=NT), AF.Exp, r=[C.psd[2]], w=[d_dec])
            for t in range(NT):
                for half in range(2):
                    n = 2 * t + half
                    bk = 3 + n // 16
                    C.mm(C.pb(bk)[:, (n % 16) * 32:(n % 16 + 1) * 32], (onA, onB)[half][:], a_tok[:, t, :], True, True,
                         r=[d_k, d_a], w=[C.psd[bk]])
            C.act(elast[:], C.ps[:, 3 * 512:5 * 512].rearrange("p (a c) -> p a c", a=32), AF.Exp,
                  r=[C.psd[3], C.psd[4]], w=[d_el])
            xTc = [C.sb(PA, [128, S_], BF16) for _ in range(2)]
            BT = C.sb(PA, [128, S_], BF16); CT = C.sb(PA, [128, S_], BF16)
            d_xT = deps(2); d_BT = Dep(); d_CT = Dep()
            x_tok = C.sb(PA, [128, NT, 256], BF16); d_xtok = deps(4)
            B_tok = C.sb(PA, [128, NT, 128], BF16); d_Btok = deps(2)
            sz = C.sb(PA, [128, NT, 256], BF16); d_sz = deps(NT)
            ub = [C.sb(PA, [128, 3 + S_], BF16) for _ in range(2)]; d_ub = deps(2)
            for b in range(2):
                C.V(lambda e, b=b: e.memset(ub[b][:, 0:3], 0.0), w=[d_ub[b]])
            wc = [C.sb(PA, [128, 8, 128], BF16) for _ in range(2)]; d_wc = deps(2)
            dg = [C.sb(PA, [128, 4, 128], BF16) for _ in range(2)]; d_dg = deps(2)
            wz = C.sb(PA, [128, 8, 256], BF16); d_wz = Dep()
            Sst = C.sb(PA, [128, 4, 64], F32); tmpS = C.sb(PA, [128, 4, 64], F32); d_S = Dep(); d_tmpS = Dep()
            Sbf = [C.sb(PA, [128, 256], BF16) for _ in range(2)]; d_Sbf = deps(2)
            cbm = [C.sb(PA, [128, 128], F32) for _ in range(2)]; d_cbm = deps(2)
            aV = [C.sb(PA, [128, 4, 128], BF16) for _ in range(2)]; d_aV = deps(2)
            lm = [C.sb(PA, [128, 4, 128], F32) for _ in range(2)]; d_lm = deps(2)
            Mh = [C.sb(PA, [128, 4, 128], BF16) for _ in range(2)]; d_M = deps(2)
            xdt = [C.sb(PA, [128, 4, 64], BF16) for _ in range(2)]; d_xdt = deps(2)
            xdd = [C.sb(PA, [128, 4, 64], BF16) for _ in range(2)]; d_xdd = deps(2)
            xD = [C.sb(PA, [128, 4, 64], BF16) for _ in range(2)]; d_xD = deps(2)
            t1 = [C.sb(PA, [128, 4, 64], F32) for _ in range(2)]; d_t1 = deps(2)
            yz = [C.sb(PA, [128, 256], F32) for _ in range(2)]; d_yz = deps(2)
            yn = [C.sb(PA, [128, 256], BF16) for _ in range(2)]; d_yn = deps(2)
            ss = C.sb(PA, [128, G_, NT], F32); rs = C.sb(PA, [128, G_, NT], F32)
            junk = C.sb(PA, [128, 256], BF16); d_junk = Dep()
            ci = 0
            d_dS = [deps(2), deps(2)]
            for g in range(G_):
                hs = slice(4 * g, 4 * g + 4)
                chunks = ((2 * g, xTc[0], d_xT[0]), (2 * g + 1, xTc[1], d_xT[1]), (16 + g, BT, d_BT), (24 + g, CT, d_CT))

                def proj_c(k_, b):
                    cc = chunks[k_][0]
                    C.load_w(wc[b][:], w_in[:, 2048 + cc * 128:2048 + (cc + 1) * 128], d_wc[b])
                    for k in range(4):
                        C.V(lambda e, b=b, k=k, cc=cc: e.tensor_scalar(out=dg[b][:, k, :], in0=C.ident_bf[:],
                                                                       scalar1=cw[:, k, cc:cc + 1], scalar2=None, op0=ALU.mult),
                            r=[d_cw, C.d_const], w=[d_dg[b]])
                    for tb in range(4):
                        bk = tb
                        for kc in range(8):
                            C.mm(C.pb(bk), wc[b][:, kc, :], uT[:, kc, tb * 512:(tb + 1) * 512], kc == 0, kc == 7,
                                 r=d_uTb[tb] + [d_wc[b]], w=[C.psd[bk]])
                        C.act(ub[b][:, 3 + tb * 512:3 + (tb + 1) * 512], C.pb(bk), AF.Copy, r=[C.psd[bk]], w=[d_ub[b]])

                def conv_c(k_, b):
                    cc, dst, d_dst = chunks[k_]
                    for tb in range(4):
                        bk = 4 + tb
                        for k in range(4):
                            C.mm(C.pb(bk), dg[b][:, k, :], ub[b][:, tb * 512 + k:tb * 512 + k + 512], k == 0, k == 3,
                                 r=[d_ub[b], d_dg[b]], w=[C.psd[bk]])
                        C.act(dst[:, tb * 512:(tb + 1) * 512], C.pb(bk), AF.Silu, r=[C.psd[bk], d_cw], w=[d_dst],
                              bias=cb[:, cc:cc + 1])

                proj_c(0, 0)
                for k_ in range(4):
                    if k_ + 1 < 4:
                        proj_c(k_ + 1, (k_ + 1) % 2)
                    conv_c(k_, k_ % 2)
                for t4 in range(4):
                    bk = t4 % 2
                    pv = C.pbh(bk)
                    for i in range(4):
                        for c in range(2):
                            C.tr(pv[:, (i * 2 + c) * 128:(i * 2 + c + 1) * 128], xTc[c][:, (t4 * 4 + i) * 128:(t4 * 4 + i + 1) * 128],
                                 C.ident_bf[:], r=[d_xT[c], C.d_const], w=[C.psd[bk]])
                    C.act(x_tok[:, t4 * 4:(t4 + 1) * 4, :], pv.rearrange("p (a c) -> p a c", a=4), AF.Copy,
                          r=[C.psd[bk]], w=[d_xtok[t4]])
                for t8 in range(2):
                    bk = 2 + t8
                    pv = C.pbh(bk)
                    for i in range(8):
                        C.tr(pv[:, i * 128:(i + 1) * 128], BT[:, (t8 * 8 + i) * 128:(t8 * 8 + i + 1) * 128], C.ident_bf[:],
                             r=[d_BT, C.d_const], w=[C.psd[bk]])
                    C.V(lambda e, t8=t8, pv=pv: e.tensor_copy(out=B_tok[:, t8 * 8:(t8 + 1) * 8, :],
                                                             in_=pv.rearrange("p (a c) -> p a c", a=8)),
                        r=[C.psd[bk]], w=[d_Btok[t8]])
                C.load_w(wz[:], w_in[:, g * 256:(g + 1) * 256], d_wz)
                for t2 in range(8):
                    bk = 4 + t2 % 4
                    for i in range(2):
                        t = t2 * 2 + i
                        for kc in range(8):
                            C.mm(C.pb(bk)[:, i * 256:(i + 1) * 256], uT[:, kc, t * 128:(t + 1) * 128], wz[:, kc, :],
                                 kc == 0, kc == 7, r=[d_uT[t], d_wz], w=[C.psd[bk]])
                    C.act(sz[:, t2 * 2:t2 * 2 + 2, :], C.pb(bk).rearrange("p (a c) -> p a c", a=2), AF.Silu,
                          r=[C.psd[bk]], w=[d_sz[t2 * 2], d_sz[t2 * 2 + 1]])
                C.V(lambda e: e.memset(Sbf[0][:], 0.0), w=[d_Sbf[0]])
                def st0(t):
                    b = t % 2
                    tsl = slice(t * 128, (t + 1) * 128)
                    xt_ = x_tok[:, t, :].rearrange("p (a c) -> p a c", a=4)
                    d_x = d_xtok[t // 4]
                    for h_ in range(4):
                        C.act(aV[b][:, h_, :], V2b[:], AF.Copy, r=[d_a, d_k], w=[d_aV[b]],
                              scale=a_tok[:, t, 4 * g + h_:4 * g + h_ + 1])
                    C.G(lambda e, hs=hs, b=b, t=t, xt_=xt_: e.tensor_tensor(out=xdt[b][:], in0=xt_,
                                                                   in1=dtt[:, t, hs].unsqueeze(2).to_broadcast([128, 4, 64]), op=ALU.mult),
                        r=[d_x, d_dt], w=[d_xdt[b]])
                    C.G(lambda e, hs=hs, b=b, t=t: e.tensor_tensor(out=xdd[b][:], in0=xdt[b][:],
                                                           in1=dec[:, t, hs].unsqueeze(2).to_broadcast([128, 4, 64]), op=ALU.mult),
                        r=[d_xdt[b], d_dec], w=[d_xdd[b]])
                    C.G(lambda e, hs=hs, b=b, xt_=xt_: e.tensor_tensor(out=xD[b][:], in0=xt_,
                                                              in1=dsk[:, hs].unsqueeze(2).to_broadcast([128, 4, 64]), op=ALU.mult),
                        r=[d_x, d_p], w=[d_xD[b]])
                    C.mm(C.pb(0)[:, 0:128], BT[:, tsl], CT[:, tsl], True, True, r=[d_BT, d_CT], w=[C.psd[0]])
                    yield
                    C.mm(C.pb(1), U2b[:], aV[b][:].rearrange("p a c -> p (a c)"), True, False, r=[d_k, d_aV[b]], w=[C.psd[1]])
                    C.mm(C.pb(1), C.ident_bf[:], negm[:], False, True, r=[C.d_const, d_k], w=[C.psd[1]])
                    for half in range(2):
                        r0 = half * 64
                        C.mm(C.pb(4 + half)[:, b * 256:(b + 1) * 256], B_tok[r0:r0 + 64, t, :],
                             xdd[b][r0:r0 + 64, :, :].rearrange("p a c -> p (a c)"),
                             True, True, r=[d_Btok[t // 8], d_xdd[b]], w=[C.psd[4 + half]])
                    yield
                    C.act(lm[b][:], C.pb(1).rearrange("p (a c) -> p a c", a=4), AF.Exp, r=[C.psd[1]], w=[d_lm[b]])
                    yield
                    C.V(lambda e, b=b: e.tensor_tensor(out=Mh[b][:], in0=lm[b][:],
                                                       in1=C.pb(0)[:, 0:128].unsqueeze(1).to_broadcast([128, 4, 128]), op=ALU.mult),
                        r=[d_lm[b], C.psd[0]], w=[d_M[b]])

                def st1(t):
                    b = t % 2
                    bY = 2 + b
                    C.mm(C.pb(bY)[:, 0:256], C.ident_bf[:], xD[b][:].rearrange("p a c -> p (a c)"), True, False,
                         r=[C.d_const, d_xD[b]], w=[C.psd[bY]])
                    for h in range(4):
                        C.mm(C.pb(bY)[:, h * 64:(h + 1) * 64], Mh[b][:, h, :], xdt[b][:, h, :], False, h == 3,
                             r=[d_M[b], d_xdt[b]], w=[C.psd[bY]])
                    for half in range(2):
                        r0 = half * 64
                        n = 2 * t + half
                        C.mm(C.pb(bY)[r0:r0 + 64, 256:512], CT[:, t * 128 + r0:t * 128 + r0 + 64], Sbf[half][:], True, True,
                             r=[d_CT, d_Sbf[half]], w=[C.psd[bY]])
                        C.V(lambda e, hs=hs, n=n, half=half: e.tensor_tensor(
                            out=tmpS[:], in0=Sbf[half][:].rearrange("p (a c) -> p a c", a=4),
                            in1=elast[:, n, hs].unsqueeze(2).to_broadcast([128, 4, 64]), op=ALU.mult),
                            r=[d_Sbf[half], d_el], w=[d_tmpS])
                        C.V(lambda e, half=half, b=b: e.tensor_tensor(
                            out=Sbf[1 - half][:].rearrange("p (a c) -> p a c", a=4),
                            in0=C.pb(4 + half)[:, b * 256:(b + 1) * 256].rearrange("p (a c) -> p a c", a=4),
                            in1=tmpS[:], op=ALU.add), r=[C.psd[4 + half], d_tmpS], w=[d_Sbf[1 - half]])
                        if half == 0:
                            yield

                def st2(t, g=g):
                    b = t % 2
                    bY = 2 + b
                    C.V(lambda e, hs=hs, b=b, t=t, bY=bY: e.tensor_tensor(out=t1[b][:], in0=C.pb(bY)[:, 256:512].rearrange("p (a c) -> p a c", a=4),
                                                           in1=ecs[:, t, hs].unsqueeze(2).to_broadcast([128, 4, 64]), op=ALU.mult),
                        r=[C.psd[bY], d_ecs], w=[d_t1[b]])
                    C.V(lambda e, b=b, bY=bY: e.tensor_tensor(out=t1[b][:], in0=C.pb(bY)[:, 0:256].rearrange("p (a c) -> p a c", a=4),
                                                       in1=t1[b][:], op=ALU.add), r=[C.psd[bY], d_t1[b]], w=[d_t1[b]])
                    yield
                    C.G(lambda e, b=b, t=t: e.tensor_tensor(out=yz[b][:], in0=t1[b][:].rearrange("p a c -> p (a c)"),
                                                           in1=sz[:, t, :], op=ALU.mult), r=[d_t1[b], d_sz[t]], w=[d_yz[b]])
                    yield
                    C.act(junk[:], yz[b][:], AF.Square, r=[d_yz[b]], w=[d_junk], accum_out=ss[:, g, t:t + 1])
                    rstd_ops(C, rs[:, g, t:t + 1], ss[:, g, t:t + 1], 256, [d_junk], [d_junk])
                    yield
                    C.V(lambda e, b=b, t=t, g=g: e.scalar_tensor_tensor(out=yn[b][:], in0=yz[b][:], scalar=rs[:, g, t:t + 1],
                                                                       in1=gn[:, g * 256:(g + 1) * 256], op0=ALU.mult, op1=ALU.mult),
                        r=[d_yz[b], d_junk, d_p], w=[d_yn[b]])
                    bk = 6 + (t // 4) % 2
                    pv = C.pbh(bk)
                    i = t % 4
                    for c in range(2):
                        C.tr(pv[:, (c * 4 + i) * 128:(c * 4 + i + 1) * 128], yn[b][:, c * 128:(c + 1) * 128], C.ident_bf[:],
                             r=[d_yn[b], C.d_const], w=[C.psd[bk]])
                    if i == 3:
                        t4 = t // 4
                        C.act(goT[:, 2 * g:2 * g + 2, t4 * 512:(t4 + 1) * 512], pv.rearrange("p (c m) -> p c m", c=2), AF.Copy,
                              r=[C.psd[bk]], w=[d_go[4 * t4 + j] for j in range(4)])

                LV = {1: ("s1", "s0", "s2"), 2: ("s0", "s1", "s2"), 3: ("s0", "s2"), 4: ("s0", "s2")}
                for s_ in range(NT + 2):
                    gens = {}
                    if 0 <= s_ - 2 < NT:
                        gens["s2"] = st2(s_ - 2)
                    if 0 <= s_ - 1 < NT:
                        gens["s1"] = st1(s_ - 1)
                    if s_ < NT:
                        gens["s0"] = st0(s_)
                    for lvl in (1, 2, 3, 4):
                        for nm in LV[lvl]:
                            if nm in gens:
                                next(gens[nm], None)
                    for gn_ in gens.values():
                        for _ in gn_:
                            pass
            C.dbg("dtt", dtt[:], [128, NT, 32], F32, [d_dt])
            C.dbg("a_tok", a_tok[:], [128, NT, 32], F32, [d_a])
            C.dbg("ecs", ecs[:], [128, NT, 32], F32, [d_ecs])
            C.dbg("dec", dec[:], [128, NT, 32], F32, [d_dec])
            C.dbg("elast", elast[:], [128, 32, 32], F32, [d_el])
            C.dbg("x_tok", x_tok[:], [128, NT, 256], BF16, d_xtok)
            C.dbg("B_tok", B_tok[:], [128, NT, 128], BF16, d_Btok)
            C.dbg("CT", CT[:], [128, S_], BF16, [d_CT])
            C.dbg("sz", sz[:], [128, NT, 256], BF16, d_sz)
            C.dbg("goT", goT[:], [128, 16, S_], BF16, d_go)
            C.dbg("Mh", Mh[1][:], [128, 4, 128], BF16, [d_M[1]])
            C.dbg("lm", lm[1][:], [128, 4, 128], F32, [d_lm[1]])
            C.dbg("yz", yz[1][:], [128, 256], F32, [d_yz[1]])
            C.release()
        out_proj(C, goT, d_go, 16, W["ssd_w_out"], h_in, d_hin, h_out, d_hout)
        C.release()


LAYER_FNS[3] = layer_ssd

WSPEC = {
    "norm_g": [4, 1024], "final_g": [1024],
    "mla_w_in": [1024, 1696], "mla_g_q": [384], "mla_w_uq": [384, 1536], "mla_g_kv": [256],
    "mla_w_ukv": [256, 2048], "mla_w_out": [1024, 1024],
    "gla_w_in": [1024, 3088], "gla_w_gk2": [16, 512], "gla_b_gk": [512], "gla_g_o": [256],
    "gla_w_out": [1024, 1024],
    "lru_w_in": [1024, 2560], "lru_conv_w": [4, 1280], "lru_conv_b": [1280], "lru_w_a": [10, 128, 128],
    "lru_b_a": [1280], "lru_w_x": [10, 128, 128], "lru_b_x": [1280], "lru_lam": [1280],
    "lru_w_out": [1280, 1024],
    "ssd_w_in": [1024, 6176], "ssd_conv_w": [4, 4096], "ssd_conv_b": [4096], "ssd_dt_bias": [32],
    "ssd_a_log": [32], "ssd_d": [32], "ssd_g_norm": [2048], "ssd_w_out": [2048, 1024],
}


def host_consts():
    c = {}
    c["ident_bf"] = np.eye(128, dtype=np.float32).astype(ml_dtypes.bfloat16)
    c["ident_f"] = np.eye(128, dtype=np.float32)
    k = np.arange(128)
    c["tri_bf"] = (k[None, :] >= k[:, None]).astype(np.float32).astype(ml_dtypes.bfloat16)
    c["ones_bf"] = np.ones((128, 64), np.float32).astype(ml_dtypes.bfloat16)
    invf = (np.float32(10000.0) ** (-np.arange(0, 32, 2, dtype=np.float32) / np.float32(32))).astype(np.float32)
    rm = np.ones((128, S_), np.float32); rm[:, ::64] = 0.0
    c["rmask"] = rm.astype(ml_dtypes.bfloat16)
    m2 = ((k[None, :] >= k[:, None]) & ((k[None, :] // 64) == (k[:, None] // 64))).astype(np.float32)
    c["mask2"] = m2.astype(ml_dtypes.bfloat16)
    same = (k[None, :] // 64) == (k[:, None] // 64)
    u2 = ((k[:, None] > k[None, :]) & same).astype(np.float32)
    c["U2b"] = u2.astype(ml_dtypes.bfloat16)
    c["U2f"] = u2
    c["V2f"] = m2.astype(np.float32)
    c["onesA"] = np.ascontiguousarray(np.broadcast_to((k[:, None] < 64).astype(np.float32), (128, 128)))
    c["onesB"] = np.ascontiguousarray(np.broadcast_to((k[:, None] >= 64).astype(np.float32), (128, 128)))
    c["negm4"] = np.tile(np.where(m2 > 0, 0.0, -30000.0).astype(np.float32), (1, 4)).astype(ml_dtypes.bfloat16)
    c["swap_f"] = np.roll(np.eye(128, dtype=np.float32), 64, axis=0)
    c["invf"] = np.ascontiguousarray(np.broadcast_to(invf[None, :], (128, 16))).astype(np.float32)
    return c


CSPEC = {"ident_bf": ([128, 128], BF16), "ident_f": ([128, 128], F32), "tri_bf": ([128, 128], BF16),
         "ones_bf": ([128, 64], BF16), "invf": ([128, 16], F32), "rmask": ([128, S_], BF16),
         "mask2": ([128, 128], BF16), "U2b": ([128, 128], BF16), "U2f": ([128, 128], F32),
         "V2f": ([128, 128], F32), "swap_f": ([128, 128], F32), "negm4": ([128, 512], BF16), "onesA": ([128, 128], F32), "onesB": ([128, 128], F32)}

def build(layers=(0, 1, 2, 3), final=True):
    nc = bass.Bass("TRN2", target_bir_lowering=False)
    x = nc.dram_tensor("x", [S_, D_], F32, kind="ExternalInput").ap()
    pos = nc.dram_tensor("pos", [128, NT], I32, kind="ExternalInput").ap()
    W = {k: nc.dram_tensor(k, v, F32, kind="ExternalInput").ap() for k, v in WSPEC.items()}
    CD = {k: nc.dram_tensor(k, v[0], v[1], kind="ExternalInput").ap() for k, v in CSPEC.items()}
    out = nc.dram_tensor("out", [S_, D_], F32, kind="ExternalOutput").ap()
    hbuf = [nc.dram_tensor(f"hbuf{i}", [S_, D_], F32).ap() for i in range(2)]
    _BASE["r"] = {}
    with ExitStack() as st:
        C = Ctx(nc, st)
        C.pos = pos
        C.CD = CD
        C.d_const = Dep()
        C.ident_bf = C.sb(st, [128, 128], BF16)
        C.ident_f = C.sb(st, [128, 128], F32)
        C.D(lambda e: e.dma_start(out=C.ident_bf[:], in_=CD["ident_bf"]), w=[C.d_const])
        C.D(lambda e: e.dma_start(out=C.ident_f[:], in_=CD["ident_f"]), w=[C.d_const])
        h_cur, d_cur = x, deps(NT)
        nxt = 0
        for li in layers:
            Wl = dict(W)
            Wl["norm_g"] = W["norm_g"][li, :]
            h_nxt, d_nxt = hbuf[nxt], deps(NT)
            if not final and li == layers[-1]:
                h_nxt = out
            LAYER_FNS[li](C, Wl, h_cur, d_cur, h_nxt, d_nxt)
            h_cur, d_cur = h_nxt, d_nxt
            nxt ^= 1
        if final:
            final_norm(C, h_cur, d_cur, W["final_g"], out)
        else:
            C.out_toks.append(C.last_store)
            for t in range(NT):
                C.out_toks.append(d_cur[t].w)
        for tok in C.out_toks:
            C.S.wait_tok("sync", tok)
        C.S.emit()
    return nc


LAYER_FNS[2] = layer_lru


def make_in_maps(inputs):
    consts = host_consts()
    maps = []
    for b in range(8):
        m = {"x": np.ascontiguousarray(inputs["x"][b]),
             "pos": np.ascontiguousarray(np.asarray(inputs["positions"][b]).astype(np.int32).reshape(NT, 128).T)}
        for k in WSPEC:
            a = np.asarray(inputs[k], dtype=np.float32)
            if k not in ("norm_g", "final_g"):
                a = a[0]
            m[k] = np.ascontiguousarray(a)
        m.update(consts)
        maps.append(m)
    return maps


_NC_CACHE = {}


def kernel(**inputs):
    if "nc" not in _NC_CACHE:
        _NC_CACHE["nc"] = build()
    nc = _NC_CACHE["nc"]
    maps = make_in_maps(inputs)
    res = run_bass_kernel_spmd(nc, maps, core_ids=list(range(8)))
    return np.stack([np.asarray(r["out"], dtype=np.float32) for r in res.results], axis=0)
```
